# Optimizing a Trainium2 kernel written in Bass

```python
import jax, jax.numpy as jnp
from jax import lax
import numpy as np

D_MODEL = 1024
BATCH = 2
SEQ = 16384
DEPTH = 1
DEC_BATCH = 8
DEC_SEQ = 8192
PAST_LEN = 128

MIX_WIDTH = D_MODEL
HEAD_DIM = 64
M_WIDTH = MIX_WIDTH // 2
M_HEADS = M_WIDTH // HEAD_DIM
M_GROUPS = 2
M_STATE = 128
M_CONV = 5
M_CHUNK = 128
M_CONV_CH = M_WIDTH + 2 * M_GROUPS * M_STATE
M_IN = M_WIDTH + M_CONV_CH + M_HEADS
R_WIDTH = MIX_WIDTH - M_WIDTH
R_HEADS = R_WIDTH // HEAD_DIM
R_DECAY_RANK = 64
R_ICLR_RANK = 64
R_GATE_RANK = 128
R_IN = 3 * R_WIDTH + R_DECAY_RANK + R_ICLR_RANK + R_GATE_RANK
IN_COLS = M_IN + R_IN
D_FF = ((8 * D_MODEL // 3 + 255) // 256) * 256
DEEPNORM_ALPHA = (2.0 * DEPTH) ** 0.25
DEEPNORM_BETA = (8.0 * DEPTH) ** -0.25
LN_EPS = 1e-5
RMS_EPS = 1e-5
GN_EPS = 64e-5

kernel_name = "hymba_ssd_rwkv7_bidir_deepnorm_encoder"


def layer_norm(x, w, b):
    xf = x.astype(jnp.float32)
    mu = jnp.mean(xf, -1, keepdims=True)
    var = jnp.mean(jnp.square(xf - mu), -1, keepdims=True)
    return ((xf - mu) * lax.rsqrt(var + LN_EPS) * w + b).astype(x.dtype)


def centred_depthwise_conv(x, w, b):
    k = w.shape[0]
    out = lax.conv_general_dilated(
        x, w.astype(x.dtype)[:, None, :], window_strides=(1,),
        padding=[(k // 2, k // 2)], dimension_numbers=('NWC', 'WIO', 'NWC'),
        feature_group_count=x.shape[-1])
    return out + b


def ssd_chunked(xh, dt, A, Bm, Cm):
    b, T, H, P = xh.shape
    G, N = Bm.shape[-2:]
    R = H // G
    nc = T // M_CHUNK
    x = (xh * dt[..., None]).reshape(b, nc, M_CHUNK, G, R, P)
    a = (dt * A).reshape(b, nc, M_CHUNK, G, R)
    Bc = Bm.reshape(b, nc, M_CHUNK, G, N)
    Cc = Cm.reshape(b, nc, M_CHUNK, G, N)
    a_cs = jnp.cumsum(a, axis=2)
    seg = a_cs[:, :, :, None] - a_cs[:, :, None, :]
    mask = jnp.tril(jnp.ones((M_CHUNK, M_CHUNK), bool))[None, None, :, :, None, None]
    Lmat = jnp.exp(jnp.where(mask, seg, -jnp.inf))
    scores = jnp.einsum('bclgn,bcsgn->bclsg', Cc, Bc)
    y_diag = jnp.einsum('bclsg,bclsgr,bcsgrp->bclgrp', scores, Lmat, x)
    decay_to_end = jnp.exp(a_cs[:, :, -1:] - a_cs)
    chunk_states = jnp.einsum('bclgn,bclgr,bclgrp->bcgrpn', Bc, decay_to_end, x)
    chunk_decay = jnp.exp(a_cs[:, :, -1])

    def chunk_step(h, inp):
        dec, st = inp
        return dec[..., None, None] * h + st, h

    h0 = jnp.zeros((b, G, R, P, N), jnp.float32)
    _, states_in = lax.scan(chunk_step, h0,
                            (jnp.moveaxis(chunk_decay, 1, 0), jnp.moveaxis(chunk_states, 1, 0)))
    states_in = jnp.moveaxis(states_in, 0, 1)
    y_off = jnp.einsum('bclgn,bcgrpn,bclgr->bclgrp', Cc, states_in, jnp.exp(a_cs))
    return (y_diag + y_off).reshape(b, T, H, P)


def mamba2_group(u, p):
    b, T, _ = u.shape
    z, xbc, dt_raw = jnp.split(u, [M_WIDTH, M_WIDTH + M_CONV_CH], axis=-1)
    xbc = jax.nn.silu(centred_depthwise_conv(xbc, p['conv_w'], p['conv_b']))
    xs, Bm, Cm = jnp.split(xbc, [M_WIDTH, M_WIDTH + M_GROUPS * M_STATE], axis=-1)
    xh = xs.reshape(b, T, M_HEADS, HEAD_DIM)
    Bm = Bm.reshape(b, T, M_GROUPS, M_STATE)
    Cm = Cm.reshape(b, T, M_GROUPS, M_STATE)
    flip = lambda t: jnp.flip(t, axis=1)
    dt_f = jax.nn.softplus(dt_raw + p['m_dt_bias_f'])
    dt_b = jax.nn.softplus(dt_raw + p['m_dt_bias_b'])
    A_f = -jnp.exp(p['m_a_log_f'].astype(jnp.float32))
    A_b = -jnp.exp(p['m_a_log_b'].astype(jnp.float32))
    y_f = ssd_chunked(xh, dt_f, A_f, Bm, Cm)
    y_b = flip(ssd_chunked(flip(xh), flip(dt_b), A_b, flip(Bm), flip(Cm)))
    y = (y_f + y_b + p['m_d'][:, None] * xh).reshape(b, T, M_WIDTH)
    yg = (y * jax.nn.silu(z)).reshape(b, T, M_GROUPS, M_WIDTH // M_GROUPS)
    yg = yg * lax.rsqrt(jnp.mean(jnp.square(yg), -1, keepdims=True) + RMS_EPS)
    return yg.reshape(b, T, M_WIDTH) * p['m_norm_w']


def centred_token_shift(u, mu_prev, mu_next):
    zero = jnp.zeros_like(u[:, :1])
    prev = jnp.concatenate([zero, u[:, :-1]], axis=1)
    nxt = jnp.concatenate([u[:, 1:], zero], axis=1)
    return u + mu_prev * (prev - u) + mu_next * (nxt - u)


def rwkv7_decay(wl, w0, w2):
    logw = -jax.nn.softplus(-(w0 + jnp.tanh(wl) @ w2)) - 0.5
    return jnp.exp(-jnp.exp(logw))


def rwkv7_step(S, inp):
    r, w, k, v, a, bb = inp
    sa = jnp.einsum('bhij,bhj->bhi', S, a)
    S = S * w[:, :, None, :] + sa[..., None] * bb[:, :, None, :] + v[..., None] * k[:, :, None, :]
    y = jnp.einsum('bhij,bhj->bhi', S, r)
    return S, y


def rwkv7_scan(r, w, k, v, a, bb):
    bsz, T, H, K = r.shape
    xs = tuple(jnp.moveaxis(t, 1, 0) for t in (r, w, k, v, a, bb))
    S0 = jnp.zeros((bsz, H, K, K), jnp.float32)
    _, y = lax.scan(rwkv7_step, S0, xs)
    return jnp.moveaxis(y, 0, 1)


def rwkv7_group(u, p):
    b, T, _ = u.shape
    u = centred_token_shift(u, p['r_mu_prev'], p['r_mu_next'])
    r, k, v, wl, al, gl = jnp.split(
        u, [R_WIDTH, 2 * R_WIDTH, 3 * R_WIDTH, 3 * R_WIDTH + R_DECAY_RANK,
            3 * R_WIDTH + R_DECAY_RANK + R_ICLR_RANK], axis=-1)
    w_f = rwkv7_decay(wl, p['r_w0_f'], p['r_w2_f'])
    w_b = rwkv7_decay(wl, p['r_w0_b'], p['r_w2_b'])
    iclr = jax.nn.sigmoid(p['r_a0'] + al @ p['r_a2'])
    g = jax.nn.sigmoid(gl) @ p['r_g2']
    heads = lambda t: t.reshape(b, T, R_HEADS, HEAD_DIM)
    kk = heads(k * p['r_k_k'])
    kk = kk * lax.rsqrt(jnp.maximum(jnp.sum(jnp.square(kk), -1, keepdims=True), 1e-24))
    k = k * (1.0 + (iclr - 1.0) * p['r_k_a'])
    rh, kh, vh, ah = heads(r), heads(k), heads(v), heads(iclr)
    a_vec, b_vec = -kk, kk * ah
    flip = lambda t: jnp.flip(t, axis=1)
    y_f = rwkv7_scan(rh, heads(w_f), kh, vh, a_vec, b_vec)
    y_b = flip(rwkv7_scan(flip(rh), flip(heads(w_b)), flip(kh), flip(vh), flip(a_vec), flip(b_vec)))
    y = y_f + y_b
    mu = jnp.mean(y, -1, keepdims=True)
    var = jnp.mean(jnp.square(y - mu), -1, keepdims=True)
    y = ((y - mu) * lax.rsqrt(var + GN_EPS)).reshape(b, T, R_WIDTH) * p['r_lnx_w'] + p['r_lnx_b']
    bonus = jnp.sum(rh * kh * p['r_r_k'], -1, keepdims=True) * vh
    return (y + bonus.reshape(b, T, R_WIDTH)) * g


def encoder_layer(x, p):
    u = jnp.einsum('btd,dc->btc', x, p['w_in']).astype(jnp.float32)
    y_m = mamba2_group(u[..., :M_IN], p)
    y_r = rwkv7_group(u[..., M_IN:], p)
    mix = jnp.concatenate([y_m, y_r], axis=-1).astype(x.dtype)
    h = layer_norm(DEEPNORM_ALPHA * x + jnp.einsum('btc,cd->btd', mix, p['w_out']),
                   p['ln1_w'], p['ln1_b'])
    gu = jnp.einsum('btd,df->btf', h, p['w_ffn_in'])
    gate, up = jnp.split(gu, 2, axis=-1)
    f = jnp.einsum('btf,fd->btd', jax.nn.silu(gate) * up, p['w_ffn_out'])
    return layer_norm(DEEPNORM_ALPHA * h + f, p['ln2_w'], p['ln2_b'])


def setup_inputs(seed: int = 0) -> dict:
    key = jax.random.key(seed)
    ks = jax.random.split(key, 40)
    L = DEPTH
    f32 = jnp.float32
    nrm = lambda k, s, sc: jax.random.normal(k, s, f32) * sc
    unif = lambda k, s, lo, hi: jax.random.uniform(k, s, f32, lo, hi)

    def dt_bias(k):
        dt0 = jnp.exp(unif(k, (L, M_HEADS), float(np.log(1e-3)), float(np.log(1e-1))))
        return dt0 + jnp.log(-jnp.expm1(-dt0))

    return {
        'x_prompt': nrm(ks[0], (BATCH, SEQ, D_MODEL), 1.0),
        'x_sample': nrm(ks[1], (DEC_BATCH, DEC_SEQ, D_MODEL), 1.0),
        'w_in': nrm(ks[2], (L, D_MODEL, IN_COLS), D_MODEL ** -0.5),
        'conv_w': nrm(ks[3], (L, M_CONV, M_CONV_CH), M_CONV ** -0.5),
        'conv_b': nrm(ks[4], (L, M_CONV_CH), 0.02),
        'm_dt_bias_f': dt_bias(ks[5]),
        'm_dt_bias_b': dt_bias(ks[6]),
        'm_a_log_f': jnp.log(unif(ks[7], (L, M_HEADS), 1.0, 16.0)),
        'm_a_log_b': jnp.log(unif(ks[8], (L, M_HEADS), 1.0, 16.0)),
        'm_d': 1.0 + nrm(ks[9], (L, M_HEADS), 0.02),
        'm_norm_w': 1.0 + nrm(ks[10], (L, M_WIDTH), 0.02),
        'r_mu_prev': unif(ks[11], (L, R_IN), 0.0, 0.5),
        'r_mu_next': unif(ks[12], (L, R_IN), 0.0, 0.5),
        'r_w0_f': unif(ks[13], (L, R_WIDTH), -6.0, 1.0),
        'r_w0_b': unif(ks[14], (L, R_WIDTH), -6.0, 1.0),
        'r_w2_f': nrm(ks[15], (L, R_DECAY_RANK, R_WIDTH), 0.1 * R_DECAY_RANK ** -0.5),
        'r_w2_b': nrm(ks[16], (L, R_DECAY_RANK, R_WIDTH), 0.1 * R_DECAY_RANK ** -0.5),
        'r_a0': nrm(ks[17], (L, R_WIDTH), 0.1),
        'r_a2': nrm(ks[18], (L, R_ICLR_RANK, R_WIDTH), 0.1 * R_ICLR_RANK ** -0.5),
        'r_g2': nrm(ks[19], (L, R_GATE_RANK, R_WIDTH), R_GATE_RANK ** -0.5),
        'r_k_k': 0.85 + nrm(ks[20], (L, R_WIDTH), 0.02),
        'r_k_a': 1.0 + nrm(ks[21], (L, R_WIDTH), 0.02),
        'r_r_k': nrm(ks[22], (L, R_HEADS, HEAD_DIM), 0.1),
        'r_lnx_w': 1.0 + nrm(ks[23], (L, R_WIDTH), 0.02),
        'r_lnx_b': nrm(ks[24], (L, R_WIDTH), 0.02),
        'w_out': nrm(ks[25], (L, MIX_WIDTH, D_MODEL), MIX_WIDTH ** -0.5 * DEEPNORM_BETA),
        'ln1_w': 1.0 + nrm(ks[26], (L, D_MODEL), 0.02),
        'ln1_b': nrm(ks[27], (L, D_MODEL), 0.02),
        'w_ffn_in': nrm(ks[28], (L, D_MODEL, 2 * D_FF), D_MODEL ** -0.5),
        'w_ffn_out': nrm(ks[29], (L, D_FF, D_MODEL), D_FF ** -0.5 * DEEPNORM_BETA),
        'ln2_w': 1.0 + nrm(ks[30], (L, D_MODEL), 0.02),
        'ln2_b': nrm(ks[31], (L, D_MODEL), 0.02),
    }


def reference(x_prompt, x_sample, w_in, conv_w, conv_b, m_dt_bias_f, m_dt_bias_b,
              m_a_log_f, m_a_log_b, m_d, m_norm_w, r_mu_prev, r_mu_next, r_w0_f, r_w0_b,
              r_w2_f, r_w2_b, r_a0, r_a2, r_g2, r_k_k, r_k_a, r_r_k, r_lnx_w, r_lnx_b,
              w_out, ln1_w, ln1_b, w_ffn_in, w_ffn_out, ln2_w, ln2_b):
    params = dict(w_in=w_in, conv_w=conv_w, conv_b=conv_b, m_dt_bias_f=m_dt_bias_f,
                  m_dt_bias_b=m_dt_bias_b, m_a_log_f=m_a_log_f, m_a_log_b=m_a_log_b, m_d=m_d,
                  m_norm_w=m_norm_w, r_mu_prev=r_mu_prev, r_mu_next=r_mu_next, r_w0_f=r_w0_f,
                  r_w0_b=r_w0_b, r_w2_f=r_w2_f, r_w2_b=r_w2_b, r_a0=r_a0, r_a2=r_a2, r_g2=r_g2,
                  r_k_k=r_k_k, r_k_a=r_k_a, r_r_k=r_r_k, r_lnx_w=r_lnx_w, r_lnx_b=r_lnx_b,
                  w_out=w_out, ln1_w=ln1_w, ln1_b=ln1_b, w_ffn_in=w_ffn_in,
                  w_ffn_out=w_ffn_out, ln2_w=ln2_w, ln2_b=ln2_b)
    y_prompt = x_prompt
    y_sample = x_sample
    for l in range(DEPTH):
        p = {name: arr[l] for name, arr in params.items()}
        y_prompt = encoder_layer(y_prompt, p)
        y_sample = encoder_layer(y_sample, p)
    return (y_prompt, y_sample)
```

```python
import numpy as np
from contextlib import ExitStack
import concourse.bass as bass
import concourse.mybir as mybir
from concourse.bass_utils import run_bass_kernel_spmd

F32 = mybir.dt.float32
BF16 = mybir.dt.bfloat16
AF = mybir.ActivationFunctionType
ALU = mybir.AluOpType

D = 1024
C = 128
NCH_IN = 28
DFF = 2816
NFC = DFF // 128
ALPHA = 2.0 ** 0.25
LN_EPS = 1e-5
RMS_EPS = 1e-5
GN_EPS = 64e-5
CDEC = float(np.exp(-0.5))

CZ, CXS, CB, CC_, CR, CK, CV, CWL, CAL, CGL, CDT = 0, 4, 8, 10, 12, 16, 20, 24, 25, 26, 27

PV = {}
_o = 0
for _n, _w in [("cw", 40), ("cb", 8), ("dch", 4), ("nw", 4), ("mp", 15), ("mn", 15), ("w0f", 4), ("w0b", 4),
               ("a0", 4), ("kk", 4), ("ka", 4), ("rk", 4), ("lw", 4), ("lb", 4), ("l1w", 8), ("l1b", 8),
               ("l2w", 8), ("l2b", 8), ("link", 1)]:
    PV[_n] = (_o, _w)
    _o += _w
NPV = _o
CST = {n: i for i, n in enumerate(["ident", "J", "msl", "msu", "mui", "bmask", "bo64", "bo64m", "o1024", "o256", "ones"])}
NCST = len(CST)


class Buf:
    __slots__ = ("name", "w", "r", "psum")

    def __init__(self, name="", psum=False):
        self.name = name
        self.w = None
        self.r = []
        self.psum = psum


class Ctx:
    ENGS = ("pe", "act", "dve", "pool", "sp")

    def __init__(self, nc, es, n_dma_slots=6):
        self.nc = nc
        self.ops = {e: [] for e in self.ENGS}
        self.sems = {}
        self.count = {}
        for e in self.ENGS:
            self.sems[e] = es.enter_context(nc.semaphore("s_" + e))
            self.count[e] = 0
        self.waited = {e: {} for e in self.ENGS}
        self.slots = {}
        for q in ("sp", "pool"):
            self.slots[q] = []
            for i in range(n_dma_slots):
                key = "d_%s_%d" % (q, i)
                self.sems[key] = es.enter_context(nc.semaphore(key))
                self.count[key] = 0
                self.slots[q].append(key)
        self.slot_rr = {"sp": 0, "pool": 0}
        self.rr = 0
        import os
        self.PEX = Buf("PEX")
        self.pex_on = os.environ.get("MK_PEX", "1") == "1"
        self.cut = int(os.environ.get("MK_CUT", "0"))
        self.nops = 0
        self.last_lines = None

    def _cut(self):
        self.nops += 1
        if self.cut and self.nops >= self.cut:
            if self.nops == self.cut:
                import sys
                f = sys._getframe(2)
                lines = []
                while f is not None and len(lines) < 5:
                    lines.append(f.f_lineno)
                    f = f.f_back
                print("MK_CUT: first dropped op #%d at lines %s" % (self.nops, lines), flush=True)
            return True
        return False

    def _deps(self, eng, reads, writes):
        need = {}
        for b in reads:
            ev = b.w
            if ev is not None and need.get(ev[0], 0) < ev[1]:
                need[ev[0]] = ev[1]
        for b in writes:
            ev = b.w
            if ev is not None and need.get(ev[0], 0) < ev[1]:
                need[ev[0]] = ev[1]
            for ev in b.r:
                if need.get(ev[0], 0) < ev[1]:
                    need[ev[0]] = ev[1]
        waits = []
        wd = self.waited[eng]
        for k, v in need.items():
            if k == eng and eng == "pe":
                continue
            if wd.get(k, 0) >= v:
                continue
            wd[k] = v
            waits.append((k, v))
        return waits

    def _mark(self, ev, reads, writes):
        for b in reads:
            b.r.append(ev)
            if len(b.r) > 24:
                m = {}
                for k, v in b.r:
                    if m.get(k, 0) < v:
                        m[k] = v
                b.r = list(m.items())
        for b in writes:
            b.w = ev
            b.r = []

    @staticmethod
    def _flat(bs):
        out = []
        for b in bs:
            if isinstance(b, (list, tuple)):
                out.extend(Ctx._flat(b))
            else:
                out.append(b)
        return out

    def op(self, eng, fn, reads=(), writes=()):
        if self._cut():
            return
        reads, writes = self._flat(reads), self._flat(writes)
        pr = [b for b in reads if b.psum]
        if pr:
            reads = [b for b in reads if not b.psum]
            writes = list(writes) + pr
        if self.pex_on:
            if eng == "pe":
                reads = list(reads) + [self.PEX]
            elif eng == "dve" and any(b.psum for b in writes):
                writes = list(writes) + [self.PEX]
        waits = self._deps(eng, reads, writes)
        self.count[eng] += 1
        ev = (eng, self.count[eng])
        self.ops[eng].append((waits, fn, eng, 1))
        self._mark(ev, reads, writes)

    def dma(self, q, fn, reads=(), writes=()):
        if self._cut():
            return
        slot = self.slots[q][self.slot_rr[q] % len(self.slots[q])]
        self.slot_rr[q] += 1
        reads, writes = self._flat(reads), self._flat(writes)
        waits = self._deps(q, reads, writes)
        prev = self.count[slot]
        if prev > 0 and self.waited[q].get(slot, 0) < prev:
            self.waited[q][slot] = prev
            waits.append((slot, prev))
        self.count[slot] += 16
        ev = (slot, self.count[slot])
        self.ops[q].append((waits, fn, slot, 16))
        self._mark(ev, reads, writes)

    def wait_all(self, eng, bufs):
        if self.cut and self.nops >= self.cut:
            return
        waits = self._deps(eng, self._flat(bufs), ())
        self.ops[eng].append((waits, None, None, 0))

    def ew(self, fn, reads=(), writes=(), engs=("dve", "pool")):
        e = engs[self.rr % len(engs)]
        self.rr += 1
        self.op(e, fn, reads, writes)

    def full_barrier(self):
        for eng in self.ENGS:
            waits = []
            for k, v in self.count.items():
                if v > 0 and self.waited[eng].get(k, 0) < v and not (k == eng):
                    self.waited[eng][k] = v
                    waits.append((k, v))
            self.ops[eng].append((waits, None, None, 0))

    def replay(self, engname, e):
        sems = self.sems
        for waits, fn, semkey, inc in self.ops[engname]:
            for k, v in waits:
                e.wait_ge(sems[k], v)
            if fn is not None:
                fn(e).then_inc(sems[semkey], inc)

    def run_block(self):
        nc = self.nc
        with nc.Block() as block:
            @block.tensor
            def _(e):
                self.replay("pe", e)

            @block.scalar
            def _(e):
                self.replay("act", e)

            @block.vector
            def _(e):
                self.replay("dve", e)

            @block.gpsimd
            def _(e):
                self.replay("pool", e)

            @block.sync
            def _(e):
                self.replay("sp", e)
        self.ops = {e: [] for e in self.ENGS}


class Rot:
    def __init__(self, nc, es, name, shape, dt, n=2):
        self.t = [es.enter_context(nc.sbuf_tensor("%s_%d" % (name, i), shape, dt)) for i in range(n)]
        self.b = [Buf(name) for _ in range(n)]
        self.i = 0

    def next(self):
        k = self.i % len(self.t)
        self.i += 1
        return self.t[k], self.b[k]


class PsumPool:
    def __init__(self, nc, es, names):
        self.t = [es.enter_context(nc.psum_tensor(n, [128, 512], F32)) for n in names]
        self.b = [[Buf(n, psum=True)] * 4 for n in names]
        self.i = 0

    def next(self):
        k = self.i % len(self.t)
        self.i += 1
        return self.t[k], self.b[k]


def build(TU, stage=3):
    import os
    SUB = float(os.environ.get('MK_SUB', '9'))
    NTL = int(os.environ.get('MK_NT', '999'))
    NTU = TU // C
    NT = 2 * NTU
    TT = 2 * TU
    XW = 2 * (TU + 4)
    nc = bass.Bass("TRN2", target_bir_lowering=False)
    dram = lambda n, s, dt=F32, kind="ExternalInput": nc.dram_tensor(n, s, dt, kind=kind).ap()
    xT = dram("xT", [D, XW])
    xTr = dram("xTr", [D, XW])
    w_in = dram("w_in", [D, NCH_IN * 128])
    w_out = dram("w_out", [D, D])
    w_fi = dram("w_fi", [D, 2 * DFF])
    w_fo = dram("w_fo", [DFF, D])
    pvec = dram("pvec", [128, NPV])
    p8 = dram("p8", [8, 8])
    lowr = dram("lowr", [128, 4 * 512])
    cst = dram("cst", [128, NCST * 128])
    m8 = dram("m8", [8, 1024])
    yT = dram("yT", [D, TT], kind="ExternalOutput")
    ysc = {k: dram(k, [NT * 128, 512], BF16, kind="Internal") for k in ("ybr_h", "ybr_l", "ybs_h", "ybs_l")}
    hT = dram("hT", [D, TT], kind="Internal")

    with ExitStack() as es0:
        c = Ctx(nc, es0)
        B_ysc = {k: [Buf() for _ in range(NT)] for k in ysc}
        B_hT = [Buf() for _ in range(NT)]
        B_out = []

        def mk_mm(c):
            def mm(out, lhsT, rhs, start, stop, reads, writes):
                c.op("pe", lambda e: e.matmul(out, lhsT=lhsT, rhs=rhs, start=start, stop=stop), reads=reads, writes=writes)
            return mm
        mm = mk_mm(c)

        def split2(src, hi, lo, reads, Bhi, Blo, psum=False):
            c.op("act", lambda e: e.activation(out=hi, in_=src, func=AF.Copy), reads=reads, writes=[Bhi])
            if psum:
                c.op("dve", lambda e: e.tensor_tensor(out=lo, in0=src, in1=hi, op=ALU.subtract), reads=list(reads) + [Bhi], writes=[Blo])
            else:
                c.ew(lambda e: e.tensor_tensor(out=lo, in0=src, in1=hi, op=ALU.subtract), reads=list(reads) + [Bhi], writes=[Blo])

        def rsqrt_eps(out, in_, eps, reads, writes):
            c.op("dve", lambda e: e.tensor_scalar(out=out, in0=in_, scalar1=float(eps), scalar2=None, op0=ALU.add), reads=reads, writes=writes)
            c.op("act", lambda e: e.activation(out=out, in_=out, func=AF.Sqrt), reads=writes, writes=writes)
            c.op("dve", lambda e: e.reciprocal(out=out, in_=out), reads=writes, writes=writes)

        def layer_norm(pp, Kb, pre, Bpre, out, Bout, sq, Bsq, r_st, r_sp, P, wn, bn, Bcsb, Bpv, N):
            def stat(src, Bsrc):
                red, Bred = r_st.next()
                c.op("dve", lambda e: e.tensor_reduce(out=red[:, 0:N], in_=src.rearrange("p a n -> p n a"), axis=mybir.AxisListType.X, op=ALU.add),
                     reads=[Bsrc], writes=[Bred])
                hi, Bhi = r_sp.next()
                lo, Blo = r_sp.next()
                split2(red[:, 0:N], hi[:, 0:N], lo[:, 0:N], [Bred], Bhi, Blo)
                ps, pb = pp.next()
                mm(ps[:, 0:N], Kb("o1024"), hi[:, 0:N], True, False, [Bcsb, Bhi], pb)
                mm(ps[:, 0:N], Kb("o1024"), lo[:, 0:N], False, True, [Bcsb, Blo], pb)
                return ps, pb
            ps, pb = stat(pre, Bpre)
            mu, Bmu = r_st.next()
            c.op("act", lambda e: e.activation(out=mu[:, 0:N], in_=ps[:, 0:N], func=AF.Copy), reads=pb, writes=[Bmu])
            for dc in range(8):
                c.ew(lambda e, dc=dc: e.tensor_tensor(out=pre[:, dc, :], in0=pre[:, dc, :], in1=mu[:, 0:N], op=ALU.subtract),
                     reads=[Bpre, Bmu], writes=[Bpre])
            c.ew(lambda e: e.tensor_tensor(out=sq, in0=pre, in1=pre, op=ALU.mult), reads=[Bpre], writes=[Bsq])
            ps2, pb2 = stat(sq, Bsq)
            rs, Brs = r_st.next()
            rsqrt_eps(rs[:, 0:N], ps2[:, 0:N], LN_EPS, pb2, [Brs])
            for dc in range(8):
                c.ew(lambda e, dc=dc: e.tensor_tensor(out=pre[:, dc, :], in0=pre[:, dc, :], in1=rs[:, 0:N], op=ALU.mult),
                     reads=[Bpre, Brs], writes=[Bpre])
                c.ew(lambda e, dc=dc: e.tensor_scalar(out=out[:, dc, :], in0=pre[:, dc, :], scalar1=P(wn, dc), scalar2=P(bn, dc),
                                                      op0=ALU.mult, op1=ALU.add), reads=[Bpre, Bpv], writes=[Bout])

        def mixer_phase(fwd):
            with ExitStack() as es:
                sfx = "F" if fwd else "B"
                sb = lambda n, s, dt=F32: es.enter_context(nc.sbuf_tensor(n + sfx, s, dt))
                rot = lambda n, s, dt=F32, k=2: Rot(nc, es, n + sfx, s, dt, k)
                pp = PsumPool(nc, es, ["pp%d%s" % (i, sfx) for i in range(6)])
                YR_t = es.enter_context(nc.psum_tensor("YR" + sfx, [128, 512], F32)); YR_b = Buf("YR", psum=True)
                YS_t = es.enter_context(nc.psum_tensor("YS" + sfx, [128, 512], F32)); YS_b = Buf("YS", psum=True)
                xsrc = xT if fwd else xTr

                class Fix:
                    def __init__(self, t, b):
                        self.t, self.b = t, b

                    def next(self):
                        return self.t, self.b
                one = lambda n, s, dt=F32: Fix(sb(n, s, dt), Buf(n))

                win = sb("win", [128, 8, NCH_IN * 128], BF16); Bwin = Buf()
                for k in range(8):
                    c.dma("pool", lambda e, k=k: e.dma_start(out=win[:, k, :], in_=w_in[k * 128:(k + 1) * 128, :]), writes=[Bwin])
                if fwd:
                    wo = sb("wo", [128, 8, D], BF16); Bwo = Buf()
                    for k in range(8):
                        c.dma("pool", lambda e, k=k: e.dma_start(out=wo[:, k, :], in_=w_out[k * 128:(k + 1) * 128, :]), writes=[Bwo])
                pv = sb("pv", [128, NPV]); Bpv = Buf()
                c.dma("sp", lambda e: e.dma_start(out=pv[:], in_=pvec), writes=[Bpv])
                p8t = sb("p8t", [8, 8]); Bp8 = Buf()
                c.dma("sp", lambda e: e.dma_start(out=p8t[:], in_=p8), writes=[Bp8])
                lrb = sb("lrb", [128, 3 * 512], BF16); Blr = Buf()
                w2o = 0 if fwd else 512
                c.dma("pool", lambda e: e.dma_start(out=lrb[:, 0:512], in_=lowr[:, w2o:w2o + 512]), writes=[Blr])
                c.dma("pool", lambda e: e.dma_start(out=lrb[:, 512:1536], in_=lowr[:, 1024:2048]), writes=[Blr])
                csb = sb("csb", [128, NCST * 128], BF16); Bcsb = Buf()
                c.dma("pool", lambda e: e.dma_start(out=csb[:], in_=cst), writes=[Bcsb])
                m8b = sb("m8b", [8, 1024], BF16); Bm8 = Buf()
                c.dma("pool", lambda e: e.dma_start(out=m8b[:], in_=m8), writes=[Bm8])
                Kb = lambda n: csb[:, CST[n] * 128:(CST[n] + 1) * 128]
                mask4 = sb("mask4", [128, 512], BF16); Bm4 = Buf()
                for q, n in enumerate(["msu", "mui", "msu", "mui"]):
                    c.op("dve", lambda e, q=q, n=n: e.tensor_copy(out=mask4[:, q * 128:(q + 1) * 128], in_=Kb(n)),
                         reads=[Bcsb], writes=[Bm4])
                P = lambda n, j=0, w=1: pv[:, PV[n][0] + j:PV[n][0] + j + w]
                der = sb("der", [128, 15 + 4]); Bder = Buf()
                c.op("dve", lambda e: e.tensor_tensor(out=der[:, 0:15], in0=P("mp", 0, 15), in1=P("mn", 0, 15), op=ALU.add),
                     reads=[Bpv], writes=[Bder])
                c.op("dve", lambda e: e.tensor_scalar(out=der[:, 0:15], in0=der[:, 0:15], scalar1=-1.0, scalar2=1.0,
                                                      op0=ALU.mult, op1=ALU.add), reads=[Bder], writes=[Bder])
                c.op("dve", lambda e: e.tensor_scalar(out=der[:, 15:19], in0=P("ka", 0, 4), scalar1=-1.0, scalar2=1.0,
                                                      op0=ALU.mult, op1=ALU.add), reads=[Bpv], writes=[Bder])
                a8 = sb("a8", [8, 1]); Ba8 = Buf()
                acol = 2 if fwd else 3
                c.op("act", lambda e: e.activation(out=a8[:], in_=p8t[:, acol:acol + 1], func=AF.Exp), reads=[Bp8], writes=[Ba8])
                c.op("dve", lambda e: e.tensor_scalar(out=a8[:], in0=a8[:], scalar1=-1.0, scalar2=None, op0=ALU.mult),
                     reads=[Ba8], writes=[Ba8])
                dtb = p8t[:, (0 if fwd else 1):(1 if fwd else 2)]
                w2 = lrb[:, 0:512]
                a2 = lrb[:, 512:1024]
                g2 = lrb[:, 1024:1536]
                w0n = "w0f" if fwd else "w0b"
                mpn, mnn = ("mp", "mn") if fwd else ("mn", "mp")

                Hb = sb("Hb", [128, 4, 128], BF16); BH = [Buf() for _ in range(4)]
                STf = sb("STf", [128, 512]); STb = sb("STb", [128, 512], BF16); BST = Buf()
                c.op("pool", lambda e: e.memset(Hb[:], 0.0), writes=BH)
                c.op("pool", lambda e: e.memset(STf[:], 0.0), writes=[BST])
                c.op("pool", lambda e: e.memset(STb[:], 0.0), writes=[BST])
                ARe_t = sb("ARe", [128, 4, 256], BF16)
                ARo_t = sb("ARo", [128, 4, 256], BF16)
                BAR_t = Buf()
                c.op("pool", lambda e: e.memset(ARe_t[:], 0.0), writes=[BAR_t])
                c.op("pool", lambda e: e.memset(ARo_t[:], 0.0), writes=[BAR_t])
                CSt = sb("CS", [128, 4, 129]); BCSt = Buf()
                c.op("pool", lambda e: e.memset(CSt[:], 0.0), writes=[BCSt])
                ptmp = sb("ptmp", [128, 128]); Bptmp = Buf()
                Bfence = Buf()
                ones_t = sb("ones_t", [128, 128]); Bones = Buf()
                c.op("pool", lambda e: e.memset(ones_t[:], 1.0), writes=[Bones])
                DT3 = sb("DT3", [128, 3, 2, 128], BF16); BDT3 = Buf()
                RB3 = sb("RB3", [128, 3, 8, 128], BF16); BRB3 = Buf()
                c.op("pool", lambda e: e.memset(DT3[:], 0.0), writes=[BDT3])
                c.op("pool", lambda e: e.memset(RB3[:], 0.0), writes=[BRB3])

                arena = sb("arena", [128, 11 * 512]); Bar = [Buf() for _ in range(11)]

                def AS(i, n=1, parts=128, nch=None):
                    nch = 4 * n if nch is None else nch
                    return (arena[0:parts, 512 * i:512 * i + 128 * nch].rearrange("p (a b) -> p a b", b=128), Bar[i:i + n])

                def AU(i, nchunks):
                    return (arena[:, 512 * i:512 * i + 132 * nchunks].rearrange("p (a b) -> p a b", b=132), Bar[i:i + 4])

                r_xb = rot("xb", [128, 8, 132], BF16, 2)
                r_zs = one("zs", [128, 4, 128])
                r_xsf = one("xsf", [128, 4, 128])
                r_xbc = one("xbcb", [128, 8, 128], BF16)
                r_dtok = one("dtok", [128, 16])
                r_gm = one("gm", [128, 2, 128])
                r_MT = one("MT", [128, 8, 128], BF16)
                r_Cd = one("Cd", [128, 8, 128], BF16)
                r_xdt = one("xdt", [128, 2, 512], BF16)
                r_btok = one("btok", [128, 2, 128], BF16)
                r_lrin = one("lrin", [128, 3, 128], BF16)
                r_sp = rot("sp", [128, 4, 128], BF16, 2)
                r_op = rot("opb", [128, 4, 128], BF16, 4)
                r_tok = rot("tok", [128, 4, 128], BF16, 2)
                r_p0 = rot("p0", [128, 128], BF16, 2)
                r_e4 = rot("e4", [128, 512], BF16, 2)
                r_pp = rot("ppq", [128, 256], BF16, 2)
                r_zf = rot("zf", [128, 128], F32, 2)
                r_zb = rot("zb", [128, 128], BF16, 2)
                r_wu = rot("wu", [128, 2, 128], BF16, 2)
                r_qt = rot("qt", [128, 128], BF16, 1)
                r_mtb = rot("mtb", [128, 128], BF16, 1)
                r_yhl = rot("yhl", [128, 2, 512], BF16, 2)
                if fwd:
                    r_ybl = rot("ybl", [128, 2, 512], BF16, 2)
                    r_bg = rot("bg", [128, 4, 128], F32, 2)
                    r_mix = one("mix", [128, 8, 128], BF16)
                    r_st = rot("stat", [128, 128], F32, 4)
                    r_sps = rot("sps", [128, 128], BF16, 2)

                evac_rr = [0]

                def evac(out, in_, reads, writes, engs=("act", "dve")):
                    e_ = engs[evac_rr[0] % len(engs)]
                    evac_rr[0] += 1
                    if e_ == "act":
                        c.op("act", lambda e: e.activation(out=out, in_=in_, func=AF.Copy), reads=reads, writes=writes)
                    else:
                        c.op(e_, lambda e: e.tensor_copy(out=out, in_=in_), reads=reads, writes=writes)

                def blocksum4(src, Bsrc, cname):
                    hi, Bhi = r_sp.next()
                    lo, Blo = r_sp.next()
                    split2(src, hi[:], lo[:], [Bsrc], Bhi, Blo)
                    ps, pb = pp.next()
                    for cp in range(4):
                        o_ = ps[:, cp * 128:(cp + 1) * 128]
                        mm(o_, Kb(cname), hi[:, cp, :], True, False, [Bcsb, Bhi], [pb[cp]])
                        mm(o_, Kb(cname), lo[:, cp, :], False, True, [Bcsb, Blo], [pb[cp]])
                    return ps[:].rearrange("p (a b) -> p a b", b=128), pb

                def link_state():
                    lk = P("link")
                    for cpr in range(4):
                        c.op("dve", lambda e, cpr=cpr: e.tensor_scalar(out=Hb[:, cpr, :], in0=Hb[:, cpr, :], scalar1=lk,
                                                                       scalar2=None, op0=ALU.mult),
                             reads=[BH[cpr], Bpv], writes=[BH[cpr]])
                    c.op("dve", lambda e: e.tensor_scalar(out=STf[:], in0=STf[:], scalar1=lk, scalar2=None, op0=ALU.mult),
                         reads=[BST, Bpv], writes=[BST])
                    c.op("dve", lambda e: e.tensor_scalar(out=STb[:], in0=STb[:], scalar1=lk, scalar2=None, op0=ALU.mult),
                         reads=[BST, Bpv], writes=[BST])

                for ti in range(min(NT, NTL)):
                    u_i, lt = divmod(ti, NTU)
                    if ti == NTU:
                        link_state()
                    col0 = u_i * (TU + 4) + lt * C
                    xb, Bxb = r_xb.next()
                    c.dma("pool", lambda e, xb=xb, col0=col0: e.dma_start(
                        out=xb[:], in_=xsrc[:, col0:col0 + 132].rearrange("(k p) t -> p k t", p=128)), writes=[Bxb])
                    if SUB < 1:
                        continue

                    def inproj(U, BU, chunks):
                        for g0 in range(0, len(chunks), 3):
                            grp = chunks[g0:g0 + 3]
                            n_g = len(grp)
                            ps, pb = pp.next()
                            for j, cc in enumerate(grp):
                                for k in range(8):
                                    mm(ps[:, j * 132:(j + 1) * 132], win[:, k, cc * 128:(cc + 1) * 128], xb[:, k, :],
                                       k == 0, k == 7, [Bwin, Bxb], pb)
                            evac(U[:, g0:g0 + n_g, :], ps[:, 0:n_g * 132].rearrange("p (a b) -> p a b", b=132), pb, [BU])

                    U, BU = AU(6, 13)
                    inproj(U, BU, list(range(12)) + [CDT])
                    UDT = 12
                    if SUB < 2:
                        continue
                    zs, Bzs = r_zs.next()
                    c.op("act", lambda e, zs=zs, U=U: e.activation(out=zs[:], in_=U[:, CZ:CZ + 4, 2:130], func=AF.Silu),
                         reads=[BU], writes=[Bzs])
                    cv, Bcv = AS(0, 2)
                    for j in range(8):
                        eng = ("dve", "pool")[j % 2]
                        for k5 in range(5):
                            kk_ = k5 if fwd else 4 - k5
                            wk = pv[:, PV["cw"][0] + j * 5 + kk_:PV["cw"][0] + j * 5 + kk_ + 1]
                            src = U[:, CXS + j, k5:k5 + 128]
                            if k5 == 0:
                                c.op(eng, lambda e, j=j, wk=wk, src=src, cv=cv: e.tensor_scalar(
                                    out=cv[:, j, :], in0=src, scalar1=wk, scalar2=P("cb", j), op0=ALU.mult, op1=ALU.add),
                                     reads=[BU, Bpv], writes=[Bcv])
                            elif eng == "dve":
                                c.op(eng, lambda e, j=j, wk=wk, src=src, cv=cv: e.scalar_tensor_tensor(
                                    out=cv[:, j, :], in0=src, scalar=wk, in1=cv[:, j, :], op0=ALU.mult, op1=ALU.add),
                                     reads=[BU, Bpv, Bcv], writes=[Bcv])
                            else:
                                c.op(eng, lambda e, wk=wk, src=src: e.tensor_scalar(
                                    out=ptmp[:], in0=src, scalar1=wk, scalar2=None, op0=ALU.mult), reads=[BU, Bpv], writes=[Bptmp])
                                c.op(eng, lambda e, j=j, cv=cv: e.tensor_tensor(out=cv[:, j, :], in0=cv[:, j, :], in1=ptmp[:], op=ALU.add),
                                     reads=[Bptmp, Bcv], writes=[Bcv])
                    xsf, Bxsf = r_xsf.next()
                    xbc, Bxbc = r_xbc.next()
                    c.op("act", lambda e, xsf=xsf, cv=cv: e.activation(out=xsf[:], in_=cv[:, 0:4, :], func=AF.Silu),
                         reads=[Bcv], writes=[Bxsf])
                    c.op("act", lambda e, xbc=xbc, cv=cv: e.activation(out=xbc[:, 4:8, :], in_=cv[:, 4:8, :], func=AF.Silu),
                         reads=[Bcv], writes=[Bxbc])
                    c.op("pool", lambda e, xbc=xbc, xsf=xsf: e.tensor_copy(out=xbc[:, 0:4, :], in_=xsf[:]),
                         reads=[Bxsf], writes=[Bxbc])
                    d8, Bd8 = AS(10, 1, parts=8)
                    c.op("act", lambda e, d8=d8, U=U: e.activation(out=d8[:, 0, :], in_=U[0:8, UDT, 2:130], func=AF.Exp,
                                                                   bias=dtb), reads=[BU, Bp8], writes=[Bd8])
                    c.op("act", lambda e, d8=d8: e.activation(out=d8[:, 1, :], in_=d8[:, 0, :], func=AF.Ln, bias=1.0),
                         reads=[Bd8], writes=[Bd8])
                    c.op("dve", lambda e, d8=d8: e.tensor_scalar(out=d8[:, 2, :], in0=d8[:, 1, :], scalar1=a8[:, 0:1],
                                                                 scalar2=None, op0=ALU.mult), reads=[Bd8, Ba8], writes=[Bd8])
                    c.op("dve", lambda e, d8=d8: e.tensor_tensor_scan(out=d8[:, 3, :], data0=ones_t[0:8, :], data1=d8[:, 2, :],
                                                                      initial=0.0, op0=ALU.mult, op1=ALU.add),
                         reads=[Bd8, Bones], writes=[Bd8])
                    for q, (src_i, r_i) in enumerate(((1, 0), (3, 2))):
                        x_ = d8[:, src_i, :]
                        r_ = d8[:, r_i, :]
                        c.op("dve", lambda e, q=q, x_=x_: e.tensor_copy(out=DT3[0:8, 0, q, :], in_=x_), reads=[Bd8], writes=[BDT3])
                        c.op("dve", lambda e, q=q, x_=x_, r_=r_: e.tensor_tensor(out=r_, in0=x_, in1=DT3[0:8, 0, q, :], op=ALU.subtract),
                             reads=[Bd8, BDT3], writes=[Bd8])
                        c.op("dve", lambda e, q=q, r_=r_: e.tensor_copy(out=DT3[0:8, 1, q, :], in_=r_), reads=[Bd8], writes=[BDT3])
                        c.op("dve", lambda e, q=q, r_=r_: e.tensor_tensor(out=r_, in0=r_, in1=DT3[0:8, 1, q, :], op=ALU.subtract),
                             reads=[Bd8, BDT3], writes=[Bd8])
                        c.op("dve", lambda e, q=q, r_=r_: e.tensor_copy(out=DT3[0:8, 2, q, :], in_=r_), reads=[Bd8], writes=[BDT3])
                    ps, pb = pp.next()
                    for q in range(2):
                        for i3 in range(3):
                            mm(ps[:, q * 128:(q + 1) * 128], DT3[:, i3, q, :], Kb("ident"), i3 == 0, i3 == 2, [BDT3, Bcsb], [pb[q]])
                    dtok, Bdtok = r_dtok.next()
                    c.op("dve", lambda e, dtok=dtok, ps=ps: e.tensor_copy(out=dtok[:, 0:8], in_=ps[:, 0:8]), reads=[pb[0], pb[1]], writes=[Bdtok])
                    c.op("dve", lambda e, dtok=dtok, ps=ps: e.tensor_copy(out=dtok[:, 8:16], in_=ps[:, 128:136]), reads=[pb[1]], writes=[Bdtok])
                    for i3 in range(3):
                        for h in range(8):
                            c.op("pool", lambda e, i3=i3, h=h: e.tensor_tensor(
                                out=RB3[0:8, i3, h, :], in0=DT3[0:8, i3, 1, :], in1=m8b[:, h * 128:(h + 1) * 128], op=ALU.mult),
                                 reads=[BDT3, Bm8], writes=[BRB3])
                    psR0, pbR0 = pp.next()
                    psR1, pbR1 = pp.next()
                    for (psR, pbR, h0) in ((psR0, pbR0, 0), (psR1, pbR1, 4)):
                        if os.environ.get("MK_R", "") == "1":
                            for hh_ in range(4):
                                for i3 in range(3):
                                    mm(psR[:, hh_ * 128:(hh_ + 1) * 128], Kb("ones"), RB3[:, i3, h0 + hh_, :], i3 == 0, i3 == 2, [Bcsb, BRB3], pbR)
                            continue
                        for i3 in range(3):
                            mm(psR[:], Kb("ones"), RB3[:, i3, h0:h0 + 4, :].rearrange("p a b -> p (a b)"), i3 == 0, i3 == 2, [Bcsb, BRB3],
                               (pbR + [Bfence]) if (h0 == 4 and i3 == int(os.environ.get("MK_FENCE", "-1"))) else pbR)
                    seg, Bseg = AS(2, 2)
                    E, BE = AS(4, 2)
                    if os.environ.get("MK_FENCE", "-1") != "-1":
                        pbR0 = pbR0 + [Bfence]
                    for _d in range(int(os.environ.get("MK_DELAY", "0"))):
                        c.op("dve", lambda e: e.tensor_copy(out=ptmp[:], in_=ones_t[:]), reads=[pbR0, Bones], writes=[Bptmp])
                    if os.environ.get("MK_SEGV", "") == "1":
                        evac(seg[:, 0:4, :], psR0[:].rearrange("p (a b) -> p a b", b=128), pbR0, [Bseg])
                        evac(seg[:, 4:8, :], psR1[:].rearrange("p (a b) -> p a b", b=128), pbR1, [Bseg])
                    for h in range(8):
                        psR, pbR = (psR0, pbR0) if h < 4 else (psR1, pbR1)
                        hh = h % 4
                        if os.environ.get("MK_SEGV", "") == "1":
                            c.op("dve", lambda e, h=h, seg=seg, dtok=dtok: e.tensor_scalar(
                                out=seg[:, h, :], in0=seg[:, h, :], scalar1=dtok[:, 8 + h:9 + h], scalar2=0.0,
                                op0=ALU.subtract, op1=ALU.min), reads=[Bseg, Bdtok], writes=[Bseg])
                            continue
                        SV = os.environ.get("MK_SEGV", "")
                        if SV == "a":
                            c.op("dve", lambda e, h=h, hh=hh, psR=psR, seg=seg, dtok=dtok: e.tensor_copy(
                                out=seg[:, h, :], in_=psR[:, hh * 128:(hh + 1) * 128]), reads=[pbR[hh], Bdtok], writes=[Bseg])
                            continue
                        if SV == "e":
                            if hh == 0:
                                c.op("dve", lambda e, h=h, psR=psR, seg=seg: e.tensor_copy(
                                    out=seg[:, h:h + 4, :], in_=psR[:].rearrange("p (a b) -> p a b", b=128)), reads=[pbR, Bdtok], writes=[Bseg])
                            continue
                        if SV == "f":
                            c.op("dve", lambda e, h=h, hh=hh, psR=psR, seg=seg, dtok=dtok: e.tensor_copy(
                                out=ptmp[:], in_=psR[:, hh * 128:(hh + 1) * 128]), reads=[pbR[hh], Bdtok], writes=[Bptmp])
                            continue
                        if SV == "b":
                            c.op("dve", lambda e, h=h, hh=hh, psR=psR, seg=seg, dtok=dtok: e.tensor_scalar(
                                out=seg[:, h, :], in0=psR[:, hh * 128:(hh + 1) * 128], scalar1=1.5, scalar2=0.0,
                                op0=ALU.subtract, op1=ALU.min), reads=[pbR[hh], Bdtok], writes=[Bseg])
                            continue
                        if SV == "d":
                            c.op("act", lambda e, h=h, hh=hh, psR=psR, seg=seg, dtok=dtok: e.activation(
                                out=seg[:, h, :], in_=psR[:, hh * 128:(hh + 1) * 128], func=AF.Copy, scale=dtok[:, 8 + h:9 + h]),
                                 reads=[pbR[hh], Bdtok], writes=[Bseg])
                            continue
                        c.op("dve", lambda e, h=h, hh=hh, psR=psR, seg=seg, dtok=dtok: e.tensor_scalar(
                            out=seg[:, h, :], in0=psR[:, hh * 128:(hh + 1) * 128], scalar1=dtok[:, 8 + h:9 + h], scalar2=0.0,
                            op0=ALU.subtract, op1=ALU.min), reads=[pbR[hh], Bdtok], writes=[Bseg])
                    c.op("act", lambda e, E=E, psR0=psR0: e.activation(out=E[:, 0:4, :], in_=psR0[:].rearrange("p (a b) -> p a b", b=128),
                                                                     func=AF.Exp), reads=pbR0, writes=[BE])
                    c.op("act", lambda e, E=E, psR1=psR1: e.activation(out=E[:, 4:8, :], in_=psR1[:].rearrange("p (a b) -> p a b", b=128),
                                                                     func=AF.Exp), reads=pbR1, writes=[BE])
                    c.op("act", lambda e, seg=seg: e.activation(out=seg[:], in_=seg[:], func=AF.Exp), reads=[Bseg], writes=[Bseg])
                    ps, pb = pp.next()
                    for g in range(2):
                        mm(ps[:, g * 128:(g + 1) * 128], xbc[:, 4 + g, :], xbc[:, 6 + g, :], True, True, [Bxbc], [pb[g]])
                    gm, Bgm = r_gm.next()
                    for g in range(2):
                        c.op("dve", lambda e, g=g, gm=gm, ps=ps: e.tensor_tensor(out=gm[:, g, :], in0=ps[:, g * 128:(g + 1) * 128],
                                                                                 in1=Kb("mui"), op=ALU.mult),
                             reads=[pb[g], Bcsb], writes=[Bgm])
                    MT, BMT = r_MT.next()
                    Cd, BCd = r_Cd.next()
                    for h in range(8):
                        c.ew(lambda e, h=h, MT=MT, seg=seg, gm=gm: e.tensor_tensor(out=MT[:, h, :], in0=seg[:, h, :],
                                                                                   in1=gm[:, h // 4, :], op=ALU.mult),
                             reads=[Bseg, Bgm], writes=[BMT])
                        c.ew(lambda e, h=h, Cd=Cd, E=E, xbc=xbc: e.tensor_tensor(out=Cd[:, h, :], in0=E[:, h, :],
                                                                                 in1=xbc[:, 6 + h // 4, :], op=ALU.mult),
                             reads=[BE, Bxbc], writes=[BCd])
                    ps, pb = pp.next()
                    for j in range(4):
                        mm(ps[:, j * 128:(j + 1) * 128], xbc[:, j, :], Kb("ident"), True, True, [Bxbc, Bcsb], [pb[j]])
                    xdt, Bxdt = r_xdt.next()
                    for h in range(8):
                        c.op("dve", lambda e, h=h, xdt=xdt, ps=ps, dtok=dtok: e.tensor_scalar(
                            out=xdt[:, 0, h * 64:(h + 1) * 64], in0=ps[:, h * 64:(h + 1) * 64], scalar1=dtok[:, h:h + 1],
                            scalar2=None, op0=ALU.mult), reads=[pb[h // 2], Bdtok], writes=[Bxdt])
                    for h in range(8):
                        c.ew(lambda e, h=h, xdt=xdt, seg=seg: e.tensor_scalar(
                            out=xdt[:, 1, h * 64:(h + 1) * 64], in0=xdt[:, 0, h * 64:(h + 1) * 64], scalar1=seg[:, h, 127:128],
                            scalar2=None, op0=ALU.mult), reads=[Bxdt, Bseg], writes=[Bxdt])
                    ps, pb = pp.next()
                    for g in range(2):
                        mm(ps[:, g * 128:(g + 1) * 128], xbc[:, 4 + g, :], Kb("ident"), True, True, [Bxbc, Bcsb], [pb[g]])
                    btok, Bbtok = r_btok.next()
                    evac(btok[:], ps[:, 0:256].rearrange("p (a b) -> p a b", b=128), pb[0:2], [Bbtok])
                    for h in range(8):
                        o_ = YS_t[:, h * 64:(h + 1) * 64]
                        mm(o_, MT[:, h, :], xdt[:, 0, h * 64:(h + 1) * 64], True, False, [BMT, Bxdt], [YS_b])
                        mm(o_, Cd[:, h, :], STb[:, h * 64:(h + 1) * 64], False, True, [BCd, BST], [YS_b])
                    ysh, Bysh = r_yhl.next()
                    split2(YS_t[:], ysh[:, 0, :], ysh[:, 1, :], [YS_b], Bysh, Bysh, psum=True)
                    ps, pb = pp.next()
                    for g in range(2):
                        mm(ps[:, g * 256:(g + 1) * 256], btok[:, g, :], xdt[:, 1, g * 256:(g + 1) * 256], True, True,
                           [Bbtok, Bxdt], pb[2 * g:2 * g + 2])
                    for h in range(8):
                        c.op("dve", lambda e, h=h, ps=ps, E=E: e.scalar_tensor_tensor(
                            out=STf[:, h * 64:(h + 1) * 64], in0=STf[:, h * 64:(h + 1) * 64], scalar=E[:, h, 127:128],
                            in1=ps[:, h * 64:(h + 1) * 64], op0=ALU.mult, op1=ALU.add), reads=[BST, BE, pb[h // 2]], writes=[BST])
                    c.op("act", lambda e: e.activation(out=STb[:], in_=STf[:], func=AF.Copy), reads=[BST], writes=[BST])
                    if not fwd:
                        for q, key in enumerate(("ybs_h", "ybs_l")):
                            c.dma("sp", lambda e, ysh=ysh, q=q, key=key, ti=ti: e.dma_start(out=ysc[key][ti * 128:(ti + 1) * 128, :], in_=ysh[:, q, :]),
                                  reads=[Bysh], writes=[B_ysc[key][ti]])
                    if SUB < 3:
                        continue

                    U, BU = AU(4, 15)
                    inproj(U, BU, list(range(CR, CR + 15)))
                    us, Bus = AS(0, 4, nch=15)
                    for j in range(15):
                        eng = ("pool", "dve")[j % 2]
                        cc = j
                        c.op(eng, lambda e, j=j, cc=cc, us=us, U=U: e.tensor_scalar(
                            out=us[:, j, :], in0=U[:, cc, 2:130], scalar1=der[:, j:j + 1], scalar2=None, op0=ALU.mult),
                             reads=[BU, Bder], writes=[Bus])
                        for (sl_, mun) in (((1, 129), mpn), ((3, 131), mnn)):
                            if eng == "dve":
                                c.op(eng, lambda e, j=j, cc=cc, us=us, U=U, sl_=sl_, mun=mun: e.scalar_tensor_tensor(
                                    out=us[:, j, :], in0=U[:, cc, sl_[0]:sl_[1]], scalar=P(mun, j), in1=us[:, j, :], op0=ALU.mult, op1=ALU.add),
                                     reads=[BU, Bpv, Bus], writes=[Bus])
                            else:
                                c.op(eng, lambda e, j=j, cc=cc, U=U, sl_=sl_, mun=mun: e.tensor_scalar(
                                    out=ptmp[:], in0=U[:, cc, sl_[0]:sl_[1]], scalar1=P(mun, j), scalar2=None, op0=ALU.mult),
                                     reads=[BU, Bpv], writes=[Bptmp])
                                c.op(eng, lambda e, j=j, us=us: e.tensor_tensor(out=us[:, j, :], in0=us[:, j, :], in1=ptmp[:], op=ALU.add),
                                     reads=[Bptmp, Bus], writes=[Bus])
                    lrin, Blrin = r_lrin.next()
                    c.op("act", lambda e, us=us, lrin=lrin: e.activation(out=lrin[:, 0, :], in_=us[:, 12, :], func=AF.Tanh), reads=[Bus], writes=[Blrin])
                    c.op("act", lambda e, us=us, lrin=lrin: e.activation(out=lrin[:, 1, :], in_=us[:, 13, :], func=AF.Copy), reads=[Bus], writes=[Blrin])
                    c.op("act", lambda e, us=us, lrin=lrin: e.activation(out=lrin[:, 2, :], in_=us[:, 14, :], func=AF.Sigmoid), reads=[Bus], writes=[Blrin])
                    psL, pbL = pp.next()
                    psI, pbI = pp.next()
                    for cp in range(4):
                        mm(psL[:, cp * 128:(cp + 1) * 128], w2[:, cp * 128:(cp + 1) * 128], lrin[:, 0, :], True, True, [Blr, Blrin], [pbL[cp]])
                        mm(psI[:, cp * 128:(cp + 1) * 128], a2[:, cp * 128:(cp + 1) * 128], lrin[:, 1, :], True, True, [Blr, Blrin], [pbI[cp]])
                    sg, Bsg = AS(4)
                    icl, Bicl = AS(5)
                    for cp in range(4):
                        c.op("act", lambda e, cp=cp, sg=sg, psL=psL: e.activation(out=sg[:, cp, :], in_=psL[:, cp * 128:(cp + 1) * 128],
                                                                              func=AF.Sigmoid, bias=P(w0n, cp)),
                             reads=[pbL[cp], Bpv], writes=[Bsg])
                        c.op("act", lambda e, cp=cp, icl=icl, psI=psI: e.activation(out=icl[:, cp, :], in_=psI[:, cp * 128:(cp + 1) * 128],
                                                                                func=AF.Sigmoid, bias=P("a0", cp)),
                             reads=[pbI[cp], Bpv], writes=[Bicl])
                    for cp in range(4):
                        c.op("dve", lambda e, cp=cp, sg=sg: e.tensor_tensor_scan(
                            out=CSt[:, cp, 1:129], data0=ones_t[:], data1=sg[:, cp, :], initial=0.0, op0=ALU.mult, op1=ALU.add),
                             reads=[Bsg, Bones, BCSt], writes=[BCSt])
                    gam, Bgam = AS(6)
                    igam, Bigam = AS(7)
                    gprev, Bgprev = AS(8)
                    c.op("act", lambda e, gam=gam: e.activation(out=gam[:], in_=CSt[:, :, 1:129], func=AF.Exp, scale=-CDEC),
                         reads=[BCSt], writes=[Bgam])
                    c.op("act", lambda e, igam=igam: e.activation(out=igam[:], in_=CSt[:, :, 1:129], func=AF.Exp, scale=CDEC),
                         reads=[BCSt], writes=[Bigam])
                    c.op("act", lambda e, gprev=gprev: e.activation(out=gprev[:], in_=CSt[:, :, 0:128], func=AF.Exp, scale=-CDEC),
                         reads=[BCSt], writes=[Bgprev])
                    kkt, Bkk = AS(9)
                    for cp in range(4):
                        c.ew(lambda e, cp=cp, kkt=kkt, us=us: e.tensor_scalar(out=kkt[:, cp, :], in0=us[:, 4 + cp, :], scalar1=P("kk", cp),
                                                                              scalar2=None, op0=ALU.mult), reads=[Bus, Bpv], writes=[Bkk])
                    sq, Bsq = sg, Bsg
                    c.ew(lambda e, sq=sq, kkt=kkt: e.tensor_tensor(out=sq[:], in0=kkt[:], in1=kkt[:], op=ALU.mult), reads=[Bkk], writes=[Bsq])
                    psS, pbS = blocksum4(sq[:], Bsq, "bo64")
                    rn, Brn = AS(10)
                    c.op("dve", lambda e, rn=rn, psS=psS: e.tensor_scalar(out=rn[:], in0=psS, scalar1=1e-24, scalar2=None, op0=ALU.max),
                         reads=pbS, writes=[Brn])
                    c.op("act", lambda e, rn=rn: e.activation(out=rn[:], in_=rn[:], func=AF.Sqrt), reads=[Brn], writes=[Brn])
                    c.op("dve", lambda e, rn=rn: e.reciprocal(out=rn[:], in_=rn[:]), reads=[Brn], writes=[Brn])
                    c.ew(lambda e, kkt=kkt, rn=rn: e.tensor_tensor(out=kkt[:], in0=kkt[:], in1=rn[:], op=ALU.mult), reads=[Bkk, Brn], writes=[Bkk])
                    km, Bkm = rn, Brn
                    for cp in range(4):
                        c.ew(lambda e, cp=cp, km=km, icl=icl: e.tensor_scalar(out=km[:, cp, :], in0=icl[:, cp, :], scalar1=P("ka", cp),
                                                                              scalar2=der[:, 15 + cp:16 + cp], op0=ALU.mult, op1=ALU.add),
                             reads=[Bicl, Bpv, Bder], writes=[Bkm])
                    c.ew(lambda e, km=km, us=us: e.tensor_tensor(out=km[:], in0=km[:], in1=us[:, 4:8, :], op=ALU.mult), reads=[Bkm, Bus], writes=[Bkm])
                    if fwd:
                        rk_, Brk = sq, Bsq
                        for cp in range(4):
                            c.op("dve", lambda e, cp=cp, rk_=rk_, us=us, km=km: e.scalar_tensor_tensor(
                                out=rk_[:, cp, :], in0=us[:, cp, :], scalar=P("rk", cp), in1=km[:, cp, :], op0=ALU.mult, op1=ALU.mult),
                                 reads=[Bus, Bpv, Bkm], writes=[Brk])
                        psB, pbB = blocksum4(rk_[:], Brk, "bo64")
                        bonv, Bbonv = r_bg.next()
                        c.op("dve", lambda e, bonv=bonv, psB=psB, us=us: e.tensor_tensor(out=bonv[:], in0=psB, in1=us[:, 8:12, :], op=ALU.mult),
                             reads=pbB + [Bus], writes=[Bbonv])
                        psG, pbG = pp.next()
                        for cp in range(4):
                            mm(psG[:, cp * 128:(cp + 1) * 128], g2[:, cp * 128:(cp + 1) * 128], lrin[:, 2, :], True, True, [Blr, Blrin], [pbG[cp]])
                        gT, BgT = r_bg.next()
                        c.op("act", lambda e, gT=gT, psG=psG: e.activation(out=gT[:], in_=psG[:].rearrange("p (a b) -> p a b", b=128), func=AF.Copy),
                             reads=pbG, writes=[BgT])
                    KT, BKT = r_op.next()
                    BT, BBT = r_op.next()
                    AT, BAT = r_op.next()
                    vb, Bvb = r_op.next()
                    c.ew(lambda e, KT=KT, km=km, igam=igam: e.tensor_tensor(out=KT[:], in0=km[:], in1=igam[:], op=ALU.mult),
                         reads=[Bkm, Bigam], writes=[BKT])
                    c.ew(lambda e, icl=icl, kkt=kkt: e.tensor_tensor(out=icl[:], in0=icl[:], in1=kkt[:], op=ALU.mult), reads=[Bicl, Bkk], writes=[Bicl])
                    c.ew(lambda e, BT=BT, icl=icl, igam=igam: e.tensor_tensor(out=BT[:], in0=icl[:], in1=igam[:], op=ALU.mult),
                         reads=[Bicl, Bigam], writes=[BBT])
                    c.op("dve", lambda e, AT=AT, kkt=kkt, gprev=gprev: e.scalar_tensor_tensor(out=AT[:], in0=kkt[:], scalar=-1.0, in1=gprev[:],
                                                                                             op0=ALU.mult, op1=ALU.mult),
                         reads=[Bkk, Bgprev], writes=[BAT])
                    c.ew(lambda e, vb=vb, us=us: e.tensor_copy(out=vb[:], in_=us[:, 8:12, :]), reads=[Bus], writes=[Bvb])
                    c.ew(lambda e, AT=AT: e.tensor_copy(out=ARe_t[0:64, :, 0:128], in_=AT[0:64, :, :]), reads=[BAT], writes=[BAR_t])
                    c.ew(lambda e, AT=AT: e.tensor_copy(out=ARo_t[64:128, :, 0:128], in_=AT[64:128, :, :]), reads=[BAT], writes=[BAR_t])
                    c.ew(lambda e, us=us, gam=gam: e.tensor_tensor(out=ARe_t[0:64, :, 128:256], in0=us[0:64, 0:4, :], in1=gam[0:64, :, :],
                                                                   op=ALU.mult), reads=[Bus, Bgam], writes=[BAR_t])
                    c.ew(lambda e, us=us, gam=gam: e.tensor_tensor(out=ARo_t[64:128, :, 128:256], in0=us[64:128, 0:4, :], in1=gam[64:128, :, :],
                                                                   op=ALU.mult), reads=[Bus, Bgam], writes=[BAR_t])
                    if SUB < 4:
                        continue
                    if fwd:
                        src_t = NT - 1 - ti
                        ybS, BybS = r_ybl.next()
                        ybR, BybR = r_ybl.next()
                        for (dst_, Bdst_, kh, kl) in ((ybS, BybS, "ybs_h", "ybs_l"), (ybR, BybR, "ybr_h", "ybr_l")):
                            for q, key in enumerate((kh, kl)):
                                c.dma("sp", lambda e, dst_=dst_, q=q, key=key, src_t=src_t: e.dma_start(
                                    out=dst_[:, q, :], in_=ysc[key][src_t * 128:(src_t + 1) * 128, :]),
                                      reads=[B_ysc[key][src_t]], writes=[Bdst_])
                    for cp in range(4):
                        ps, pb = pp.next()
                        for q, src in enumerate((AT, BT, KT, vb)):
                            mm(ps[:, q * 128:(q + 1) * 128], src[:, cp, :], Kb("ident"), True, True, [BAT, BBT, BKT, Bvb, Bcsb], [pb[q]])
                        tok, Btok = r_tok.next()
                        evac(tok[:], ps[:].rearrange("p (a b) -> p a b", b=128), pb, [Btok])
                        wu, Bwu = r_wu.next()
                        e4s = []
                        for x in range(2):
                            AR_x = (ARe_t, ARo_t)[x]
                            hs = slice(x * 64, (x + 1) * 64)
                            psA, pbA = pp.next()
                            psB4, pbB4 = pp.next()
                            mm(psA[:, 0:128], AR_x[:, cp, 0:128], BT[:, cp, :], True, True, [BAR_t, BBT], [pbA[0]])
                            mm(psB4[:, 0:256], BT[:, cp, :], AR_x[:, cp, :], True, True, [BAR_t, BBT], pbB4[0:2])
                            mm(psB4[:, 256:512], KT[:, cp, :], AR_x[:, cp, :], True, True, [BAR_t, BKT], pbB4[2:4])
                            p0, Bp0 = r_p0.next()
                            c.op("dve", lambda e, p0=p0, psA=psA: e.tensor_tensor(out=p0[:], in0=psA[:, 0:128], in1=Kb("msl"), op=ALU.mult),
                                 reads=[pbA[0], Bcsb], writes=[Bp0])
                            e4, Be4 = r_e4.next()
                            c.op("dve", lambda e, e4=e4, psB4=psB4: e.tensor_tensor(out=e4[:], in0=psB4[:], in1=mask4[:], op=ALU.mult),
                                 reads=pbB4 + [Bm4], writes=[Be4])
                            e4s.append((e4, Be4))
                            mm(psA[:, 128:192], e4[:, 256:384], tok[:, 3, hs], True, True, [Be4, Btok], [pbA[1]])
                            zf, Bzf = r_zf.next()
                            zb, Bzb = r_zb.next()
                            c.op("act", lambda e, zf=zf, tok=tok, hs=hs: e.activation(out=zf[:, 0:64], in_=tok[:, 0, hs], func=AF.Copy),
                                 reads=[Btok], writes=[Bzf])
                            c.op("act", lambda e, zf=zf, psA=psA: e.activation(out=zf[:, 64:128], in_=psA[:, 128:192], func=AF.Copy),
                                 reads=[pbA[1]], writes=[Bzf])
                            c.op("dve", lambda e, zf=zf, zb=zb: e.tensor_copy(out=zb[:], in_=zf[:]), reads=[Bzf], writes=[Bzb])
                            Pn, PnT = p0[:], e4[:, 0:128]
                            BPn = [Bp0, Be4]
                            for it in range(7):
                                psZ, pbZ = pp.next()
                                mm(psZ[:, 0:128], PnT, zb[:], True, True, BPn + [Bzb], [pbZ[0]])
                                c.op("dve", lambda e, zf=zf, psZ=psZ: e.tensor_tensor(out=zf[:], in0=psZ[:, 0:128], in1=zf[:], op=ALU.add),
                                     reads=[pbZ[0], Bzf], writes=[Bzf])
                                if it < 6:
                                    zb, Bzb = r_zb.next()
                                    c.op("act", lambda e, zf=zf, zb=zb: e.activation(out=zb[:], in_=zf[:], func=AF.Copy), reads=[Bzf], writes=[Bzb])
                                    mm(psZ[:, 128:256], PnT, Pn, True, True, BPn, [pbZ[1]])
                                    mm(psZ[:, 256:384], Pn, PnT, True, True, BPn, [pbZ[2]])
                                    ppq, Bppq = r_pp.next()
                                    evac(ppq[:], psZ[:, 128:384], pbZ[1:3], [Bppq])
                                    Pn, PnT = ppq[:, 0:128], ppq[:, 128:256]
                                    BPn = [Bppq]
                            c.op("act", lambda e, wu=wu, zf=zf, hs=hs: e.activation(out=wu[:, 0, hs], in_=zf[:, 0:64], func=AF.Copy),
                                 reads=[Bzf], writes=[Bwu])
                            c.op("dve", lambda e, wu=wu, zf=zf, hs=hs: e.tensor_copy(out=wu[:, 1, hs], in_=zf[:, 64:128]),
                                 reads=[Bzf], writes=[Bwu])
                        qt, Bqt = r_qt.next()
                        psQ, pbQ = pp.next()
                        for x in range(2):
                            e4, Be4 = e4s[x]
                            AR_x = (ARe_t, ARo_t)[x]
                            hs = slice(x * 64, (x + 1) * 64)
                            mm(psQ[:, x * 128:(x + 1) * 128], wu[:, 0, :], e4[:, 128:256], True, True, [Bwu, Be4], [pbQ[x]])
                            c.op("dve", lambda e, qt=qt, psQ=psQ, x=x, hs=hs, AR_x=AR_x, cp=cp: e.tensor_tensor(
                                out=qt[hs, :], in0=psQ[hs, x * 128:(x + 1) * 128], in1=AR_x[hs, cp, 128:256], op=ALU.add),
                                 reads=[pbQ[x], BAR_t], writes=[Bqt])
                        yo = YR_t[:, cp * 128:(cp + 1) * 128]
                        mm(yo, qt[:], Hb[:, cp, :], True, False, [Bqt, BH[cp]], [YR_b])
                        for x in range(2):
                            e4, Be4 = e4s[x]
                            hs = slice(x * 64, (x + 1) * 64)
                            yx = YR_t[:, cp * 128 + x * 64:cp * 128 + (x + 1) * 64]
                            mm(yx, e4[:, 128:256], wu[:, 1, hs], False, False, [Be4, Bwu], [YR_b])
                            mm(yx, e4[:, 384:512], tok[:, 3, hs], False, x == 1, [Be4, Btok], [YR_b])
                        psM, pbM = pp.next()
                        mm(psM[:, 0:128], wu[:, 0, :], tok[:, 1, :], True, True, [Bwu, Btok], [pbM[0]])
                        mtb, Bmtb = r_mtb.next()
                        c.op("dve", lambda e, mtb=mtb, psM=psM: e.tensor_tensor(out=mtb[:], in0=psM[:, 0:128], in1=Kb("bmask"), op=ALU.mult),
                             reads=[pbM[0], Bcsb], writes=[Bmtb])
                        hsl = psM[:, 128:256]
                        mm(hsl, tok[:, 1, :], wu[:, 1, :], True, False, [Btok, Bwu], [pbM[1]])
                        mm(hsl, tok[:, 2, :], tok[:, 3, :], False, False, [Btok], [pbM[1]])
                        mm(hsl, Kb("ident"), Hb[:, cp, :], False, False, [Bcsb, BH[cp]], [pbM[1]])
                        mm(hsl, mtb[:], Hb[:, cp, :], False, True, [Bmtb, BH[cp]], [pbM[1]])
                        c.op("dve", lambda e, cp=cp, psM=psM, gam=gam: e.scalar_tensor_tensor(
                            out=Hb[:, cp, :], in0=psM[:, 128:256], scalar=gam[:, cp, 127:128], in1=Kb("bmask"), op0=ALU.mult, op1=ALU.mult),
                             reads=[pbM[1], Bgam, Bcsb], writes=[BH[cp]])
                    yrh, Byrh = r_yhl.next()
                    split2(YR_t[:], yrh[:, 0, :], yrh[:, 1, :], [YR_b], Byrh, Byrh, psum=True)
                    if not fwd:
                        for q, key in enumerate(("ybr_h", "ybr_l")):
                            c.dma("sp", lambda e, yrh=yrh, q=q, key=key, ti=ti: e.dma_start(out=ysc[key][ti * 128:(ti + 1) * 128, :], in_=yrh[:, q, :]),
                                  reads=[Byrh], writes=[B_ysc[key][ti]])
                        continue
                    if SUB < 5:
                        continue

                    mix, Bmix = r_mix.next()

                    def ytrans(yh, Byh, yb, Byb):
                        ps, pb = pp.next()
                        for j in range(4):
                            o_ = ps[:, j * 128:(j + 1) * 128]
                            cs_ = slice(j * 128, (j + 1) * 128)
                            mm(o_, yh[:, 0, cs_], Kb("ident"), True, False, [Byh, Bcsb], [pb[j]])
                            mm(o_, yh[:, 1, cs_], Kb("ident"), False, False, [Byh, Bcsb], [pb[j]])
                            mm(o_, yb[:, 0, cs_], Kb("J"), False, False, [Byb, Bcsb], [pb[j]])
                            mm(o_, yb[:, 1, cs_], Kb("J"), False, True, [Byb, Bcsb], [pb[j]])
                        return ps, pb
                    ps, pb = ytrans(ysh, Bysh, ybS, BybS)
                    yg, Byg = AS(0)
                    for j in range(4):
                        c.op("dve", lambda e, j=j, yg=yg, xsf=xsf, ps=ps: e.scalar_tensor_tensor(
                            out=yg[:, j, :], in0=xsf[:, j, :], scalar=P("dch", j), in1=ps[:, j * 128:(j + 1) * 128], op0=ALU.mult, op1=ALU.add),
                             reads=[Bxsf, Bpv, pb[j]], writes=[Byg])
                    c.ew(lambda e, yg=yg, zs=zs: e.tensor_tensor(out=yg[:], in0=yg[:], in1=zs[:], op=ALU.mult), reads=[Byg, Bzs], writes=[Byg])
                    sq2, Bsq2 = AS(2)
                    c.ew(lambda e, sq2=sq2, yg=yg: e.tensor_tensor(out=sq2[:], in0=yg[:], in1=yg[:], op=ALU.mult), reads=[Byg], writes=[Bsq2])
                    hi, Bhi = r_sp.next()
                    lo, Blo = r_sp.next()
                    split2(sq2[:], hi[:], lo[:], [Bsq2], Bhi, Blo)
                    ps, pb = pp.next()
                    for g in range(2):
                        o_ = ps[:, g * 128:(g + 1) * 128]
                        for j in range(2):
                            mm(o_, Kb("o256"), hi[:, 2 * g + j, :], j == 0, False, [Bcsb, Bhi], [pb[g]])
                            mm(o_, Kb("o256"), lo[:, 2 * g + j, :], False, j == 1, [Bcsb, Blo], [pb[g]])
                    rs, Brs = r_st.next()
                    rs2, Brs2 = r_st.next()
                    rsqrt_eps(rs[:], ps[:, 0:128], RMS_EPS, [pb[0]], [Brs])
                    rsqrt_eps(rs2[:], ps[:, 128:256], RMS_EPS, [pb[1]], [Brs2])
                    for j in range(4):
                        rr_, Brr_ = (rs, Brs) if j < 2 else (rs2, Brs2)
                        c.op("dve", lambda e, j=j, mix=mix, yg=yg, rr_=rr_: e.scalar_tensor_tensor(
                            out=mix[:, j, :], in0=yg[:, j, :], scalar=P("nw", j), in1=rr_[:], op0=ALU.mult, op1=ALU.mult),
                             reads=[Byg, Bpv, Brr_], writes=[Bmix])
                    ps, pb = ytrans(yrh, Byrh, ybR, BybR)
                    yf, Byf = AS(1)
                    evac(yf[:], ps[:].rearrange("p (a b) -> p a b", b=128), pb, [Byf])
                    psm, pbm = blocksum4(yf[:], Byf, "bo64m")
                    c.op("dve", lambda e, yf=yf, psm=psm: e.tensor_tensor(out=yf[:], in0=yf[:], in1=psm, op=ALU.subtract), reads=[Byf] + pbm, writes=[Byf])
                    c.ew(lambda e, sq2=sq2, yf=yf: e.tensor_tensor(out=sq2[:], in0=yf[:], in1=yf[:], op=ALU.mult), reads=[Byf], writes=[Bsq2])
                    psv, pbv = blocksum4(sq2[:], Bsq2, "bo64m")
                    rsqrt_eps(sq2[:], psv, GN_EPS, pbv, [Bsq2])
                    c.ew(lambda e, yf=yf, sq2=sq2: e.tensor_tensor(out=yf[:], in0=yf[:], in1=sq2[:], op=ALU.mult), reads=[Byf, Bsq2], writes=[Byf])
                    for j in range(4):
                        c.ew(lambda e, j=j, yf=yf: e.tensor_scalar(out=yf[:, j, :], in0=yf[:, j, :], scalar1=P("lw", j), scalar2=P("lb", j),
                                                                   op0=ALU.mult, op1=ALU.add), reads=[Byf, Bpv], writes=[Byf])
                    c.ew(lambda e, yf=yf, bonv=bonv: e.tensor_tensor(out=yf[:], in0=yf[:], in1=bonv[:], op=ALU.add), reads=[Byf, Bbonv], writes=[Byf])
                    c.ew(lambda e, mix=mix, yf=yf, gT=gT: e.tensor_tensor(out=mix[:, 4:8, :], in0=yf[:], in1=gT[:], op=ALU.mult),
                         reads=[Byf, BgT], writes=[Bmix])
                    xf, Bxf = AS(9, 2)
                    c.dma("sp", lambda e, xf=xf, col0=col0: e.dma_start(
                        out=xf, in_=xT[:, col0 + 2:col0 + 130].rearrange("(k p) t -> p k t", p=128)), writes=[Bxf])
                    pre, Bpre = AS(3, 2)
                    for half in range(2):
                        ps, pb = pp.next()
                        for j in range(4):
                            dc = half * 4 + j
                            for k in range(8):
                                mm(ps[:, j * 128:(j + 1) * 128], wo[:, k, dc * 128:(dc + 1) * 128], mix[:, k, :], k == 0, k == 7,
                                   [Bwo, Bmix], [pb[j]])
                        c.op("dve", lambda e, half=half, pre=pre, xf=xf, ps=ps: e.scalar_tensor_tensor(
                            out=pre[:, half * 4:(half + 1) * 4, :], in0=xf[:, half * 4:(half + 1) * 4, :], scalar=ALPHA,
                            in1=ps[:].rearrange("p (a b) -> p a b", b=128), op0=ALU.mult, op1=ALU.add), reads=[Bxf] + pb, writes=[Bpre])
                    hout, Bhout = AS(7, 2)
                    sqL, BsqL = AS(5, 2)
                    layer_norm(pp, Kb, pre, Bpre, hout, Bhout, sqL, BsqL, r_st, r_sps, P, "l1w", "l1b", Bcsb, Bpv, 128)
                    t0 = ti * C
                    c.dma("sp", lambda e, hout=hout, t0=t0: e.dma_start(
                        out=hT[:, t0:t0 + 128].rearrange("(k p) t -> p k t", p=128), in_=hout), reads=[Bhout], writes=[B_hT[ti]])
                c.full_barrier()
                c.run_block()

        def ffn_phase():
            NF = 128
            with ExitStack() as es:
                sb = lambda n, s, dt=F32: es.enter_context(nc.sbuf_tensor(n, s, dt))
                rot = lambda n, s, dt=F32, k=2: Rot(nc, es, n, s, dt, k)
                pp = PsumPool(nc, es, ["pf%d" % i for i in range(8)])
                wfi = sb("wfi", [128, 8, 2 * DFF], BF16); Bwfi = Buf()
                wfo = sb("wfo", [128, NFC, D], BF16); Bwfo = Buf()
                for k in range(8):
                    c.dma("pool", lambda e, k=k: e.dma_start(out=wfi[:, k, :], in_=w_fi[k * 128:(k + 1) * 128, :]), writes=[Bwfi])
                for k in range(NFC):
                    c.dma("pool", lambda e, k=k: e.dma_start(out=wfo[:, k, :], in_=w_fo[k * 128:(k + 1) * 128, :]), writes=[Bwfo])
                pv = sb("pv2", [128, NPV]); Bpv = Buf()
                c.dma("sp", lambda e: e.dma_start(out=pv[:], in_=pvec), writes=[Bpv])
                csb = sb("csb2", [128, NCST * 128], BF16); Bcsb = Buf()
                c.dma("pool", lambda e: e.dma_start(out=csb[:], in_=cst), writes=[Bcsb])
                Kb = lambda n: csb[:, CST[n] * 128:(CST[n] + 1) * 128]
                P = lambda n, j=0, w=1: pv[:, PV[n][0] + j:PV[n][0] + j + w]
                r_hf = rot("hf", [128, 8, NF], F32, 2)
                r_hb = rot("hb", [128, 8, NF], BF16, 2)
                r_act = rot("actt", [128, NFC, NF], BF16, 1)
                r_sl = rot("sl", [128, NF], F32, 3)
                r_sq = rot("sq2", [128, 8, NF], F32, 1)
                r_st = rot("st2", [128, NF], F32, 4)
                r_sps = rot("sps2", [128, NF], BF16, 2)
                r_out = rot("outt", [128, 8, NF], F32, 2)
                for fi in range(min(TT // NF, NTL)):
                    t0 = fi * NF
                    deps = [B_hT[t] for t in range(t0 // C, (t0 + NF) // C)]
                    hf, Bhf = r_hf.next()
                    hb, Bhb = r_hb.next()
                    c.dma("sp", lambda e, hf=hf, t0=t0: e.dma_start(out=hf[:], in_=hT[:, t0:t0 + NF].rearrange("(k p) t -> p k t", p=128)),
                          reads=deps, writes=[Bhf])
                    c.op("act", lambda e, hf=hf, hb=hb: e.activation(out=hb[:], in_=hf[:], func=AF.Copy), reads=[Bhf], writes=[Bhb])
                    actt, Bact = r_act.next()
                    for fc in range(NFC):
                        ps, pb = pp.next()
                        for k in range(8):
                            mm(ps[:, 0:NF], wfi[:, k, fc * 128:(fc + 1) * 128], hb[:, k, :], k == 0, k == 7, [Bwfi, Bhb], pb[0:2])
                        for k in range(8):
                            mm(ps[:, 256:256 + NF], wfi[:, k, DFF + fc * 128:DFF + (fc + 1) * 128], hb[:, k, :], k == 0, k == 7,
                               [Bwfi, Bhb], pb[2:4])
                        sl, Bsl = r_sl.next()
                        c.op("act", lambda e, sl=sl, ps=ps: e.activation(out=sl[:], in_=ps[:, 0:NF], func=AF.Silu), reads=pb[0:2], writes=[Bsl])
                        c.op("dve", lambda e, fc=fc, sl=sl, ps=ps, actt=actt: e.tensor_tensor(
                            out=actt[:, fc, :], in0=ps[:, 256:256 + NF], in1=sl[:], op=ALU.mult), reads=pb[2:4] + [Bsl], writes=[Bact])
                    for half in range(4):
                        ps, pb = pp.next()
                        for j in range(2):
                            dc = half * 2 + j
                            for k in range(NFC):
                                mm(ps[:, j * 256:j * 256 + NF], wfo[:, k, dc * 128:(dc + 1) * 128], actt[:, k, :], k == 0, k == NFC - 1,
                                   [Bwfo, Bact], pb[2 * j:2 * j + 2])
                        for j in range(2):
                            dc = half * 2 + j
                            c.op("dve", lambda e, dc=dc, j=j, hf=hf, ps=ps: e.scalar_tensor_tensor(
                                out=hf[:, dc, :], in0=hf[:, dc, :], scalar=ALPHA, in1=ps[:, j * 256:j * 256 + NF], op0=ALU.mult, op1=ALU.add),
                                 reads=[Bhf] + pb[2 * j:2 * j + 2], writes=[Bhf])
                    out, Bout = r_out.next()
                    sq, Bsq = r_sq.next()
                    layer_norm(pp, Kb, hf[:], Bhf, out[:], Bout, sq[:], Bsq, r_st, r_sps, P, "l2w", "l2b", Bcsb, Bpv, NF)
                    Bo = Buf()
                    B_out.append(Bo)
                    c.dma("sp", lambda e, out=out, t0=t0: e.dma_start(out=yT[:, t0:t0 + NF].rearrange("(k p) t -> p k t", p=128), in_=out[:]),
                          reads=[Bout], writes=[Bo])
                c.wait_all("sp", B_out)
                c.run_block()

        mixer_phase(False)
        if stage >= 2:
            mixer_phase(True)
        if stage >= 3:
            ffn_phase()
    return nc


def _consts():
    i = np.arange(128)
    p, f = i[:, None], i[None, :]
    blocks = {
        "ident": (p == f), "J": (p + f == 127), "msl": (f < p), "msu": (p < f), "mui": (p <= f),
        "bmask": ((p // 64) == (f // 64)), "bo64": ((p // 64) == (f // 64)),
    }
    out = np.zeros((128, NCST * 128), np.float32)
    for n, m in blocks.items():
        out[:, CST[n] * 128:(CST[n] + 1) * 128] = m.astype(np.float32)
    out[:, CST["bo64m"] * 128:(CST["bo64m"] + 1) * 128] = blocks["bo64"].astype(np.float32) / 64.0
    out[:, CST["o1024"] * 128:(CST["o1024"] + 1) * 128] = 1.0 / 1024.0
    out[:, CST["o256"] * 128:(CST["o256"] + 1) * 128] = 1.0 / 256.0
    out[:, CST["ones"] * 128:(CST["ones"] + 1) * 128] = 1.0
    m8 = np.zeros((8, 1024), np.float32)
    for h in range(8):
        m8[h, h * 128:(h + 1) * 128] = 1.0
    return out, m8


def _chunkvec(v, n):
    return np.ascontiguousarray(np.asarray(v, np.float32).reshape(n, 128).T)


def prep_params(inp):
    g = lambda n: np.asarray(inp[n], np.float32)[0]
    w_in = g("w_in")
    M_W, CONV = 512, 1024
    o_z, o_xbc, o_dt = 0, 512, 1536
    o_r = 1544
    W = np.zeros((D, NCH_IN * 128), np.float32)
    W[:, 0:512] = w_in[:, o_z:o_z + 512]
    W[:, 512:1536] = w_in[:, o_xbc:o_xbc + 1024]
    W[:, 1536:3072] = w_in[:, o_r:o_r + 1536]
    W[:, CWL * 128:CWL * 128 + 64] = w_in[:, o_r + 1536:o_r + 1600]
    W[:, CAL * 128:CAL * 128 + 64] = w_in[:, o_r + 1600:o_r + 1664]
    W[:, CGL * 128:CGL * 128 + 128] = w_in[:, o_r + 1664:o_r + 1792]
    W[:, CDT * 128:CDT * 128 + 8] = w_in[:, o_dt:o_dt + 8]
    pv = np.zeros((128, NPV), np.float32)
    def put(name, arr):
        o, w = PV[name]
        pv[:, o:o + w] = arr
    cw = g("conv_w")
    cwp = np.zeros((128, 8, 5), np.float32)
    for j in range(8):
        cwp[:, j, :] = cw[:, j * 128:(j + 1) * 128].T
    put("cw", cwp.reshape(128, 40))
    put("cb", _chunkvec(g("conv_b"), 8))
    put("dch", _chunkvec(np.repeat(g("m_d"), 64), 4))
    put("nw", _chunkvec(g("m_norm_w"), 4))
    def mu15(v):
        o = np.zeros((128, 15), np.float32)
        o[:, 0:12] = _chunkvec(v[0:1536], 12)
        o[0:64, 12] = v[1536:1600]
        o[0:64, 13] = v[1600:1664]
        o[:, 14] = v[1664:1792]
        return o
    put("mp", mu15(g("r_mu_prev")))
    put("mn", mu15(g("r_mu_next")))
    put("w0f", _chunkvec(g("r_w0_f"), 4))
    put("w0b", _chunkvec(g("r_w0_b"), 4))
    put("a0", _chunkvec(g("r_a0"), 4))
    put("kk", _chunkvec(g("r_k_k"), 4))
    put("ka", _chunkvec(g("r_k_a"), 4))
    put("rk", _chunkvec(g("r_r_k").reshape(-1), 4))
    put("lw", _chunkvec(g("r_lnx_w"), 4))
    put("lb", _chunkvec(g("r_lnx_b"), 4))
    put("l1w", _chunkvec(g("ln1_w"), 8))
    put("l1b", _chunkvec(g("ln1_b"), 8))
    put("l2w", _chunkvec(g("ln2_w"), 8))
    put("l2b", _chunkvec(g("ln2_b"), 8))
    p8 = np.zeros((8, 8), np.float32)
    p8[:, 0] = g("m_dt_bias_f"); p8[:, 1] = g("m_dt_bias_b")
    p8[:, 2] = g("m_a_log_f"); p8[:, 3] = g("m_a_log_b")
    lowr = np.zeros((128, 4 * 512), np.float32)
    lowr[0:64, 0:512] = g("r_w2_f")
    lowr[0:64, 512:1024] = g("r_w2_b")
    lowr[0:64, 1024:1536] = g("r_a2")
    lowr[:, 1536:2048] = g("r_g2")
    cst, m8 = _consts()
    return dict(w_in=W, w_out=np.ascontiguousarray(g("w_out")), w_fi=np.ascontiguousarray(g("w_ffn_in")),
                w_fo=np.ascontiguousarray(g("w_ffn_out")), p8=p8, lowr=lowr, cst=cst, m8=m8), pv


def layout_core(units, link, TU):
    xT = np.zeros((D, 2 * (TU + 4)), np.float32)
    xTr = np.zeros((D, 2 * (TU + 4)), np.float32)
    for u, (seq, s) in enumerate(units):
        if seq is None:
            continue
        T = seq.shape[0]
        lo, hi = s - 2, s + TU + 2
        blk = np.zeros((TU + 4, D), np.float32)
        a, b = max(lo, 0), min(hi, T)
        blk[a - lo:b - lo] = seq[a:b]
        xT[:, u * (TU + 4):(u + 1) * (TU + 4)] = blk.T
        ur = 1 - u
        xTr[:, ur * (TU + 4):(ur + 1) * (TU + 4)] = blk[::-1].T
    return xT, xTr


_NC_CACHE = {}


def run_cores(assign, params, pv, TU, n_cores):
    if TU not in _NC_CACHE:
        import os
        _NC_CACHE[TU] = build(TU, stage=int(os.environ.get("MK_STAGE", "3")))
    nc = _NC_CACHE[TU]
    in_maps = []
    for units, link in assign:
        xT, xTr = layout_core(units, link, TU)
        pvc = pv.copy()
        pvc[:, PV["link"][0]] = float(link)
        m = dict(params)
        m.update(xT=xT, xTr=xTr, pvec=pvc)
        in_maps.append(m)
    res = run_bass_kernel_spmd(nc, in_maps, core_ids=list(range(n_cores)))
    return [r["yT"] for r in res.results]


def kernel(**inputs):
    xp = np.asarray(inputs["x_prompt"], np.float32)
    xs = np.asarray(inputs["x_sample"], np.float32)
    params, pv = prep_params(inputs)
    TU = xs.shape[1]
    assert xp.shape[1] == 2 * TU
    assign = []
    for b in range(xp.shape[0]):
        assign.append(([(xp[b], 0), (xp[b], TU)], 1))
    nb = xs.shape[0]
    slots = [[(xs[i], 0)] for i in range(nb)]
    n_rest = 8 - len(assign)
    per = [[] for _ in range(n_rest)]
    for i in range(nb):
        per[i % n_rest].append(i)
    for lst in per:
        u = [(xs[i], 0) for i in lst]
        while len(u) < 2:
            u.append((None, 0))
        assign.append((u, 0))
    outs = run_cores(assign, params, pv, TU, 8)
    y_p = np.zeros_like(xp)
    y_s = np.zeros_like(xs)
    for b in range(xp.shape[0]):
        y_p[b] = outs[b].T
    for ci, lst in enumerate(per):
        o = outs[xp.shape[0] + ci]
        for u, i in enumerate(lst):
            y_s[i] = o[:, u * TU:(u + 1) * TU].T
    return (y_p, y_s)
```

```python
import numpy as np
from contextlib import ExitStack
import concourse.bass as bass
import concourse.mybir as mybir
from concourse.bass_utils import run_bass_kernel_spmd

F32 = mybir.dt.float32
BF16 = mybir.dt.bfloat16
AF = mybir.ActivationFunctionType
ALU = mybir.AluOpType

D = 1024
C = 128
NCH_IN = 28
DFF = 2816
NFC = DFF // 128
ALPHA = 2.0 ** 0.25
LN_EPS = 1e-5
RMS_EPS = 1e-5
GN_EPS = 64e-5
CDEC = float(np.exp(-0.5))

CZ, CXS, CB, CC_, CR, CK, CV, CWL, CAL, CGL, CDT = 0, 4, 8, 10, 12, 16, 20, 24, 25, 26, 27

PV = {}
_o = 0
for _n, _w in [("cw", 40), ("cb", 8), ("dch", 4), ("nw", 4), ("mp", 15), ("mn", 15), ("w0f", 4), ("w0b", 4),
               ("a0", 4), ("kk", 4), ("ka", 4), ("rk", 4), ("lw", 4), ("lb", 4), ("l1w", 8), ("l1b", 8),
               ("l2w", 8), ("l2b", 8), ("link", 1)]:
    PV[_n] = (_o, _w)
    _o += _w
NPV = _o
CST = {n: i for i, n in enumerate(["ident", "J", "msl", "msu", "mui", "bmask", "bo64", "bo64m", "o1024", "o256", "ones"])}
NCST = len(CST)


class Buf:
    __slots__ = ("name", "w", "r", "psum")

    def __init__(self, name="", psum=False):
        self.name = name
        self.w = None
        self.r = []
        self.psum = psum


class Ctx:
    ENGS = ("pe", "act", "dve", "pool", "sp")

    def __init__(self, nc, es, n_dma_slots=6):
        self.nc = nc
        self.ops = {e: [] for e in self.ENGS}
        self.sems = {}
        self.count = {}
        for e in self.ENGS:
            self.sems[e] = es.enter_context(nc.semaphore("s_" + e))
            self.count[e] = 0
        self.waited = {e: {} for e in self.ENGS}
        self.slots = {}
        for q in ("sp", "pool"):
            self.slots[q] = []
            for i in range(n_dma_slots):
                key = "d_%s_%d" % (q, i)
                self.sems[key] = es.enter_context(nc.semaphore(key))
                self.count[key] = 0
                self.slots[q].append(key)
        self.slot_rr = {"sp": 0, "pool": 0}
        self.rr = 0
        import os
        self.PEX = Buf("PEX")
        self.pex_on = os.environ.get("MK_PEX", "0") == "1"
        self.cut = int(os.environ.get("MK_CUT", "0"))
        self.nops = 0
        self.last_lines = None

    def _cut(self):
        self.nops += 1
        if self.cut and self.nops >= self.cut:
            if self.nops == self.cut:
                import sys
                f = sys._getframe(2)
                lines = []
                while f is not None and len(lines) < 5:
                    lines.append(f.f_lineno)
                    f = f.f_back
                print("MK_CUT: first dropped op #%d at lines %s" % (self.nops, lines), flush=True)
            return True
        return False

    def _deps(self, eng, reads, writes):
        need = {}
        for b in reads:
            ev = b.w
            if ev is not None and need.get(ev[0], 0) < ev[1]:
                need[ev[0]] = ev[1]
        for b in writes:
            ev = b.w
            if ev is not None and need.get(ev[0], 0) < ev[1]:
                need[ev[0]] = ev[1]
            for ev in b.r:
                if need.get(ev[0], 0) < ev[1]:
                    need[ev[0]] = ev[1]
        waits = []
        wd = self.waited[eng]
        for k, v in need.items():
            if k == eng and eng == "pe":
                continue
            if wd.get(k, 0) >= v:
                continue
            wd[k] = v
            waits.append((k, v))
        return waits

    def _mark(self, ev, reads, writes):
        for b in reads:
            b.r.append(ev)
            if len(b.r) > 24:
                m = {}
                for k, v in b.r:
                    if m.get(k, 0) < v:
                        m[k] = v
                b.r = list(m.items())
        for b in writes:
            b.w = ev
            b.r = []

    @staticmethod
    def _flat(bs):
        out = []
        for b in bs:
            if isinstance(b, (list, tuple)):
                out.extend(Ctx._flat(b))
            else:
                out.append(b)
        return out

    def op(self, eng, fn, reads=(), writes=()):
        if self._cut():
            return
        reads, writes = self._flat(reads), self._flat(writes)
        pr = [b for b in reads if b.psum]
        if pr:
            reads = [b for b in reads if not b.psum]
            writes = list(writes) + pr
        if self.pex_on:
            if eng == "pe":
                reads = list(reads) + [self.PEX]
            elif eng == "dve" and any(b.psum for b in writes):
                writes = list(writes) + [self.PEX]
        waits = self._deps(eng, reads, writes)
        self.count[eng] += 1
        ev = (eng, self.count[eng])
        self.ops[eng].append((waits, fn, eng, 1))
        self._mark(ev, reads, writes)

    def dma(self, q, fn, reads=(), writes=()):
        if self._cut():
            return
        slot = self.slots[q][self.slot_rr[q] % len(self.slots[q])]
        self.slot_rr[q] += 1
        reads, writes = self._flat(reads), self._flat(writes)
        waits = self._deps(q, reads, writes)
        prev = self.count[slot]
        if prev > 0 and self.waited[q].get(slot, 0) < prev:
            self.waited[q][slot] = prev
            waits.append((slot, prev))
        self.count[slot] += 16
        ev = (slot, self.count[slot])
        self.ops[q].append((waits, fn, slot, 16))
        self._mark(ev, reads, writes)

    def wait_all(self, eng, bufs):
        if self.cut and self.nops >= self.cut:
            return
        waits = self._deps(eng, self._flat(bufs), ())
        self.ops[eng].append((waits, None, None, 0))

    def ew(self, fn, reads=(), writes=(), engs=("dve", "pool")):
        e = engs[self.rr % len(engs)]
        self.rr += 1
        self.op(e, fn, reads, writes)

    def full_barrier(self):
        for eng in self.ENGS:
            waits = []
            for k, v in self.count.items():
                if v > 0 and self.waited[eng].get(k, 0) < v and not (k == eng):
                    self.waited[eng][k] = v
                    waits.append((k, v))
            self.ops[eng].append((waits, None, None, 0))

    def replay(self, engname, e):
        sems = self.sems
        for waits, fn, semkey, inc in self.ops[engname]:
            for k, v in waits:
                e.wait_ge(sems[k], v)
            if fn is not None:
                fn(e).then_inc(sems[semkey], inc)

    def run_block(self):
        nc = self.nc
        with nc.Block() as block:
            @block.tensor
            def _(e):
                self.replay("pe", e)

            @block.scalar
            def _(e):
                self.replay("act", e)

            @block.vector
            def _(e):
                self.replay("dve", e)

            @block.gpsimd
            def _(e):
                self.replay("pool", e)

            @block.sync
            def _(e):
                self.replay("sp", e)
        self.ops = {e: [] for e in self.ENGS}


class Rot:
    def __init__(self, nc, es, name, shape, dt, n=2):
        self.t = [es.enter_context(nc.sbuf_tensor("%s_%d" % (name, i), shape, dt)) for i in range(n)]
        self.b = [Buf(name) for _ in range(n)]
        self.i = 0

    def next(self):
        k = self.i % len(self.t)
        self.i += 1
        return self.t[k], self.b[k]


class PsumPool:
    def __init__(self, nc, es, names):
        self.t = [es.enter_context(nc.psum_tensor(n, [128, 512], F32)) for n in names]
        self.b = [[Buf(n, psum=True)] * 4 for n in names]
        self.i = 0

    def next(self):
        k = self.i % len(self.t)
        self.i += 1
        return self.t[k], self.b[k]


def build(TU, stage=3):
    import os
    SUB = float(os.environ.get('MK_SUB', '9'))
    NTL = int(os.environ.get('MK_NT', '999'))
    NTU = TU // C
    NT = 2 * NTU
    TT = 2 * TU
    XW = 2 * (TU + 4)
    nc = bass.Bass("TRN2", target_bir_lowering=False)
    dram = lambda n, s, dt=F32, kind="ExternalInput": nc.dram_tensor(n, s, dt, kind=kind).ap()
    xT = dram("xT", [D, XW])
    xTr = dram("xTr", [D, XW])
    w_in = dram("w_in", [D, NCH_IN * 128])
    w_out = dram("w_out", [D, D])
    w_fi = dram("w_fi", [D, 2 * DFF])
    w_fo = dram("w_fo", [DFF, D])
    pvec = dram("pvec", [128, NPV])
    p8 = dram("p8", [8, 8])
    lowr = dram("lowr", [128, 4 * 512])
    cst = dram("cst", [128, NCST * 128])
    m8 = dram("m8", [8, 1024])
    yT = dram("yT", [D, TT], kind="ExternalOutput")
    ysc = {k: dram(k, [NT * 128, 512], BF16, kind="Internal") for k in ("ybr_h", "ybr_l", "ybs_h", "ybs_l")}
    hT = dram("hT", [D, TT], kind="Internal")

    with ExitStack() as es0:
        c = Ctx(nc, es0)
        B_ysc = {k: [Buf() for _ in range(NT)] for k in ysc}
        B_hT = [Buf() for _ in range(NT)]
        B_out = []

        def mk_mm(c):
            def mm(out, lhsT, rhs, start, stop, reads, writes):
                c.op("pe", lambda e: e.matmul(out, lhsT=lhsT, rhs=rhs, start=start, stop=stop), reads=reads, writes=writes)
            return mm
        mm = mk_mm(c)

        def split2(src, hi, lo, reads, Bhi, Blo, psum=False):
            c.op("act", lambda e: e.activation(out=hi, in_=src, func=AF.Copy), reads=reads, writes=[Bhi])
            if psum:
                c.op("dve", lambda e: e.tensor_tensor(out=lo, in0=src, in1=hi, op=ALU.subtract), reads=list(reads) + [Bhi], writes=[Blo])
            else:
                c.ew(lambda e: e.tensor_tensor(out=lo, in0=src, in1=hi, op=ALU.subtract), reads=list(reads) + [Bhi], writes=[Blo])

        def rsqrt_eps(out, in_, eps, reads, writes):
            c.op("dve", lambda e: e.tensor_scalar(out=out, in0=in_, scalar1=float(eps), scalar2=None, op0=ALU.add), reads=reads, writes=writes)
            c.op("act", lambda e: e.activation(out=out, in_=out, func=AF.Sqrt), reads=writes, writes=writes)
            c.op("dve", lambda e: e.reciprocal(out=out, in_=out), reads=writes, writes=writes)

        def layer_norm(pp, Kb, pre, Bpre, out, Bout, sq, Bsq, r_st, r_sp, P, wn, bn, Bcsb, Bpv, N):
            def stat(src, Bsrc):
                red, Bred = r_st.next()
                c.op("dve", lambda e: e.tensor_reduce(out=red[:, 0:N], in_=src.rearrange("p a n -> p n a"), axis=mybir.AxisListType.X, op=ALU.add),
                     reads=[Bsrc], writes=[Bred])
                hi, Bhi = r_sp.next()
                lo, Blo = r_sp.next()
                split2(red[:, 0:N], hi[:, 0:N], lo[:, 0:N], [Bred], Bhi, Blo)
                ps, pb = pp.next()
                mm(ps[:, 0:N], Kb("o1024"), hi[:, 0:N], True, False, [Bcsb, Bhi], pb)
                mm(ps[:, 0:N], Kb("o1024"), lo[:, 0:N], False, True, [Bcsb, Blo], pb)
                return ps, pb
            ps, pb = stat(pre, Bpre)
            mu, Bmu = r_st.next()
            c.op("act", lambda e: e.activation(out=mu[:, 0:N], in_=ps[:, 0:N], func=AF.Copy), reads=pb, writes=[Bmu])
            bcN = lambda t: t[:, 0:N].unsqueeze(1).to_broadcast([128, 8, N])
            bcP = lambda n: P(n, 0, 8).unsqueeze(2).to_broadcast([128, 8, N])
            c.ew(lambda e: e.tensor_tensor(out=pre, in0=pre, in1=bcN(mu), op=ALU.subtract), reads=[Bpre, Bmu], writes=[Bpre])
            c.ew(lambda e: e.tensor_tensor(out=sq, in0=pre, in1=pre, op=ALU.mult), reads=[Bpre], writes=[Bsq])
            ps2, pb2 = stat(sq, Bsq)
            rs, Brs = r_st.next()
            rsqrt_eps(rs[:, 0:N], ps2[:, 0:N], LN_EPS, pb2, [Brs])
            c.ew(lambda e: e.tensor_tensor(out=pre, in0=pre, in1=bcN(rs), op=ALU.mult), reads=[Bpre, Brs], writes=[Bpre])
            c.ew(lambda e: e.tensor_tensor(out=pre, in0=pre, in1=bcP(wn), op=ALU.mult), reads=[Bpre, Bpv], writes=[Bpre])
            c.ew(lambda e: e.tensor_tensor(out=out, in0=pre, in1=bcP(bn), op=ALU.add), reads=[Bpre, Bpv], writes=[Bout])

        def mixer_phase(fwd):
            with ExitStack() as es:
                sfx = "F" if fwd else "B"
                sb = lambda n, s, dt=F32: es.enter_context(nc.sbuf_tensor(n + sfx, s, dt))
                rot = lambda n, s, dt=F32, k=2: Rot(nc, es, n + sfx, s, dt, k)
                pp = PsumPool(nc, es, ["pp%d%s" % (i, sfx) for i in range(6)])
                YR_t = es.enter_context(nc.psum_tensor("YR" + sfx, [128, 512], F32)); YR_b = Buf("YR", psum=True)
                YS_t = es.enter_context(nc.psum_tensor("YS" + sfx, [128, 512], F32)); YS_b = Buf("YS", psum=True)
                xsrc = xT if fwd else xTr

                class Fix:
                    def __init__(self, t, b):
                        self.t, self.b = t, b

                    def next(self):
                        return self.t, self.b
                one = lambda n, s, dt=F32: Fix(sb(n, s, dt), Buf(n))

                win = sb("win", [128, 8, NCH_IN * 128], BF16); Bwin = Buf()
                for k in range(8):
                    c.dma("pool", lambda e, k=k: e.dma_start(out=win[:, k, :], in_=w_in[k * 128:(k + 1) * 128, :]), writes=[Bwin])
                if fwd:
                    wo = sb("wo", [128, 8, D], BF16); Bwo = Buf()
                    for k in range(8):
                        c.dma("pool", lambda e, k=k: e.dma_start(out=wo[:, k, :], in_=w_out[k * 128:(k + 1) * 128, :]), writes=[Bwo])
                pv = sb("pv", [128, NPV]); Bpv = Buf()
                c.dma("sp", lambda e: e.dma_start(out=pv[:], in_=pvec), writes=[Bpv])
                p8t = sb("p8t", [8, 8]); Bp8 = Buf()
                c.dma("sp", lambda e: e.dma_start(out=p8t[:], in_=p8), writes=[Bp8])
                lrb = sb("lrb", [128, 3 * 512], BF16); Blr = Buf()
                w2o = 0 if fwd else 512
                c.dma("pool", lambda e: e.dma_start(out=lrb[:, 0:512], in_=lowr[:, w2o:w2o + 512]), writes=[Blr])
                c.dma("pool", lambda e: e.dma_start(out=lrb[:, 512:1536], in_=lowr[:, 1024:2048]), writes=[Blr])
                csb = sb("csb", [128, NCST * 128], BF16); Bcsb = Buf()
                c.dma("pool", lambda e: e.dma_start(out=csb[:], in_=cst), writes=[Bcsb])
                m8b = sb("m8b", [8, 1024], BF16); Bm8 = Buf()
                c.dma("pool", lambda e: e.dma_start(out=m8b[:], in_=m8), writes=[Bm8])
                Kb = lambda n: csb[:, CST[n] * 128:(CST[n] + 1) * 128]
                mask4 = sb("mask4", [128, 512], BF16); Bm4 = Buf()
                for q, n in enumerate(["msu", "mui", "msu", "mui"]):
                    c.op("dve", lambda e, q=q, n=n: e.tensor_copy(out=mask4[:, q * 128:(q + 1) * 128], in_=Kb(n)),
                         reads=[Bcsb], writes=[Bm4])
                P = lambda n, j=0, w=1: pv[:, PV[n][0] + j:PV[n][0] + j + w]
                der = sb("der", [128, 15 + 4]); Bder = Buf()
                c.op("dve", lambda e: e.tensor_tensor(out=der[:, 0:15], in0=P("mp", 0, 15), in1=P("mn", 0, 15), op=ALU.add),
                     reads=[Bpv], writes=[Bder])
                c.op("dve", lambda e: e.tensor_scalar(out=der[:, 0:15], in0=der[:, 0:15], scalar1=-1.0, scalar2=1.0,
                                                      op0=ALU.mult, op1=ALU.add), reads=[Bder], writes=[Bder])
                c.op("dve", lambda e: e.tensor_scalar(out=der[:, 15:19], in0=P("ka", 0, 4), scalar1=-1.0, scalar2=1.0,
                                                      op0=ALU.mult, op1=ALU.add), reads=[Bpv], writes=[Bder])
                a8 = sb("a8", [8, 1]); Ba8 = Buf()
                acol = 2 if fwd else 3
                c.op("act", lambda e: e.activation(out=a8[:], in_=p8t[:, acol:acol + 1], func=AF.Exp), reads=[Bp8], writes=[Ba8])
                c.op("dve", lambda e: e.tensor_scalar(out=a8[:], in0=a8[:], scalar1=-1.0, scalar2=None, op0=ALU.mult),
                     reads=[Ba8], writes=[Ba8])
                dtb = p8t[:, (0 if fwd else 1):(1 if fwd else 2)]
                w2 = lrb[:, 0:512]
                a2 = lrb[:, 512:1024]
                g2 = lrb[:, 1024:1536]
                w0n = "w0f" if fwd else "w0b"
                mpn, mnn = ("mp", "mn") if fwd else ("mn", "mp")

                Hb = sb("Hb", [128, 4, 128], BF16); BH = [Buf() for _ in range(4)]
                STf = sb("STf", [128, 512]); STb = sb("STb", [128, 512], BF16); BST = Buf()
                c.op("pool", lambda e: e.memset(Hb[:], 0.0), writes=BH)
                c.op("pool", lambda e: e.memset(STf[:], 0.0), writes=[BST])
                c.op("pool", lambda e: e.memset(STb[:], 0.0), writes=[BST])
                ARe_t = sb("ARe", [128, 4, 256], BF16)
                ARo_t = sb("ARo", [128, 4, 256], BF16)
                BAR_t = Buf()
                c.op("pool", lambda e: e.memset(ARe_t[:], 0.0), writes=[BAR_t])
                c.op("pool", lambda e: e.memset(ARo_t[:], 0.0), writes=[BAR_t])
                CSt = sb("CS", [128, 4, 129]); BCSt = Buf()
                c.op("pool", lambda e: e.memset(CSt[:], 0.0), writes=[BCSt])
                ptmp = sb("ptmp", [128, 128]); Bptmp = Buf()
                Bfence = Buf()
                ones_t = sb("ones_t", [128, 128]); Bones = Buf()
                c.op("pool", lambda e: e.memset(ones_t[:], 1.0), writes=[Bones])
                DT3 = sb("DT3", [128, 3, 2, 128], BF16); BDT3 = Buf()
                RB3 = sb("RB3", [128, 3, 8, 128], BF16); BRB3 = Buf()
                c.op("pool", lambda e: e.memset(DT3[:], 0.0), writes=[BDT3])
                c.op("pool", lambda e: e.memset(RB3[:], 0.0), writes=[BRB3])

                arena = sb("arena", [128, 11 * 512]); Bar = [Buf() for _ in range(11)]

                def AS(i, n=1, parts=128, nch=None):
                    nch = 4 * n if nch is None else nch
                    return (arena[0:parts, 512 * i:512 * i + 128 * nch].rearrange("p (a b) -> p a b", b=128), Bar[i:i + n])

                def AU(i, nchunks):
                    return (arena[:, 512 * i:512 * i + 132 * nchunks].rearrange("p (a b) -> p a b", b=132), Bar[i:i + 4])

                r_xb = rot("xb", [128, 8, 132], BF16, 2)
                r_zs = one("zs", [128, 4, 128])
                r_xsf = one("xsf", [128, 4, 128])
                r_xbc = one("xbcb", [128, 8, 128], BF16)
                r_dtok = one("dtok", [128, 16])
                r_gm = one("gm", [128, 2, 128])
                r_MT = one("MT", [128, 8, 128], BF16)
                r_Cd = one("Cd", [128, 8, 128], BF16)
                r_xdt = one("xdt", [128, 2, 512], BF16)
                r_btok = one("btok", [128, 2, 128], BF16)
                r_lrin = one("lrin", [128, 3, 128], BF16)
                r_sp = rot("sp", [128, 4, 128], BF16, 2)
                r_op = rot("opb", [128, 4, 128], BF16, 4)
                r_tok = rot("tok", [128, 4, 128], BF16, 2)
                r_p0 = rot("p0", [128, 128], BF16, 2)
                r_e4 = rot("e4", [128, 512], BF16, 2)
                r_pp = rot("ppq", [128, 256], BF16, 2)
                r_zf = rot("zf", [128, 128], F32, 2)
                r_zb = rot("zb", [128, 128], BF16, 2)
                r_wu = rot("wu", [128, 2, 128], BF16, 2)
                r_qt = rot("qt", [128, 128], BF16, 1)
                r_mtb = rot("mtb", [128, 128], BF16, 1)
                r_yhl = rot("yhl", [128, 2, 512], BF16, 2)
                if fwd:
                    r_ybl = rot("ybl", [128, 2, 512], BF16, 2)
                    r_bg = rot("bg", [128, 4, 128], F32, 2)
                    r_mix = one("mix", [128, 8, 128], BF16)
                    r_st = rot("stat", [128, 128], F32, 4)
                    r_sps = rot("sps", [128, 128], BF16, 2)

                evac_rr = [0]

                def evac(out, in_, reads, writes, engs=("act", "dve")):
                    e_ = engs[evac_rr[0] % len(engs)]
                    evac_rr[0] += 1
                    if e_ == "act":
                        c.op("act", lambda e: e.activation(out=out, in_=in_, func=AF.Copy), reads=reads, writes=writes)
                    else:
                        c.op(e_, lambda e: e.tensor_copy(out=out, in_=in_), reads=reads, writes=writes)

                def blocksum4(src, Bsrc, cname):
                    hi, Bhi = r_sp.next()
                    lo, Blo = r_sp.next()
                    split2(src, hi[:], lo[:], [Bsrc], Bhi, Blo)
                    ps, pb = pp.next()
                    for cp in range(4):
                        o_ = ps[:, cp * 128:(cp + 1) * 128]
                        mm(o_, Kb(cname), hi[:, cp, :], True, False, [Bcsb, Bhi], [pb[cp]])
                        mm(o_, Kb(cname), lo[:, cp, :], False, True, [Bcsb, Blo], [pb[cp]])
                    return ps[:].rearrange("p (a b) -> p a b", b=128), pb

                def link_state():
                    lk = P("link")
                    for cpr in range(4):
                        c.op("dve", lambda e, cpr=cpr: e.tensor_scalar(out=Hb[:, cpr, :], in0=Hb[:, cpr, :], scalar1=lk,
                                                                       scalar2=None, op0=ALU.mult),
                             reads=[BH[cpr], Bpv], writes=[BH[cpr]])
                    c.op("dve", lambda e: e.tensor_scalar(out=STf[:], in0=STf[:], scalar1=lk, scalar2=None, op0=ALU.mult),
                         reads=[BST, Bpv], writes=[BST])
                    c.op("dve", lambda e: e.tensor_scalar(out=STb[:], in0=STb[:], scalar1=lk, scalar2=None, op0=ALU.mult),
                         reads=[BST, Bpv], writes=[BST])

                for ti in range(min(NT, NTL)):
                    u_i, lt = divmod(ti, NTU)
                    if ti == NTU:
                        link_state()
                    col0 = u_i * (TU + 4) + lt * C
                    xb, Bxb = r_xb.next()
                    c.dma("pool", lambda e, xb=xb, col0=col0: e.dma_start(
                        out=xb[:], in_=xsrc[:, col0:col0 + 132].rearrange("(k p) t -> p k t", p=128)), writes=[Bxb])
                    if SUB < 1:
                        continue

                    def inproj(U, BU, chunks):
                        for g0 in range(0, len(chunks), 3):
                            grp = chunks[g0:g0 + 3]
                            n_g = len(grp)
                            ps, pb = pp.next()
                            for j, cc in enumerate(grp):
                                for k in range(8):
                                    mm(ps[:, j * 132:(j + 1) * 132], win[:, k, cc * 128:(cc + 1) * 128], xb[:, k, :],
                                       k == 0, k == 7, [Bwin, Bxb], pb)
                            evac(U[:, g0:g0 + n_g, :], ps[:, 0:n_g * 132].rearrange("p (a b) -> p a b", b=132), pb, [BU])

                    U, BU = AU(6, 13)
                    inproj(U, BU, list(range(12)) + [CDT])
                    UDT = 12
                    if SUB < 2:
                        continue
                    zs, Bzs = r_zs.next()
                    c.op("act", lambda e, zs=zs, U=U: e.activation(out=zs[:], in_=U[:, CZ:CZ + 4, 2:130], func=AF.Silu),
                         reads=[BU], writes=[Bzs])
                    cv, Bcv = AS(0, 2)
                    for j in range(8):
                        eng = ("dve", "pool")[j % 2]
                        for k5 in range(5):
                            kk_ = k5 if fwd else 4 - k5
                            wk = pv[:, PV["cw"][0] + j * 5 + kk_:PV["cw"][0] + j * 5 + kk_ + 1]
                            src = U[:, CXS + j, k5:k5 + 128]
                            if k5 == 0:
                                c.op(eng, lambda e, j=j, wk=wk, src=src, cv=cv: e.tensor_scalar(
                                    out=cv[:, j, :], in0=src, scalar1=wk, scalar2=P("cb", j), op0=ALU.mult, op1=ALU.add),
                                     reads=[BU, Bpv], writes=[Bcv])
                            elif eng == "dve":
                                c.op(eng, lambda e, j=j, wk=wk, src=src, cv=cv: e.scalar_tensor_tensor(
                                    out=cv[:, j, :], in0=src, scalar=wk, in1=cv[:, j, :], op0=ALU.mult, op1=ALU.add),
                                     reads=[BU, Bpv, Bcv], writes=[Bcv])
                            else:
                                c.op(eng, lambda e, wk=wk, src=src: e.tensor_scalar(
                                    out=ptmp[:], in0=src, scalar1=wk, scalar2=None, op0=ALU.mult), reads=[BU, Bpv], writes=[Bptmp])
                                c.op(eng, lambda e, j=j, cv=cv: e.tensor_tensor(out=cv[:, j, :], in0=cv[:, j, :], in1=ptmp[:], op=ALU.add),
                                     reads=[Bptmp, Bcv], writes=[Bcv])
                    xsf, Bxsf = r_xsf.next()
                    xbc, Bxbc = r_xbc.next()
                    c.op("act", lambda e, xsf=xsf, cv=cv: e.activation(out=xsf[:], in_=cv[:, 0:4, :], func=AF.Silu),
                         reads=[Bcv], writes=[Bxsf])
                    c.op("act", lambda e, xbc=xbc, cv=cv: e.activation(out=xbc[:, 4:8, :], in_=cv[:, 4:8, :], func=AF.Silu),
                         reads=[Bcv], writes=[Bxbc])
                    c.op("pool", lambda e, xbc=xbc, xsf=xsf: e.tensor_copy(out=xbc[:, 0:4, :], in_=xsf[:]),
                         reads=[Bxsf], writes=[Bxbc])
                    d8, Bd8 = AS(10, 1, parts=8)
                    c.op("act", lambda e, d8=d8, U=U: e.activation(out=d8[:, 0, :], in_=U[0:8, UDT, 2:130], func=AF.Exp,
                                                                   bias=dtb), reads=[BU, Bp8], writes=[Bd8])
                    c.op("act", lambda e, d8=d8: e.activation(out=d8[:, 1, :], in_=d8[:, 0, :], func=AF.Ln, bias=1.0),
                         reads=[Bd8], writes=[Bd8])
                    c.op("dve", lambda e, d8=d8: e.tensor_scalar(out=d8[:, 2, :], in0=d8[:, 1, :], scalar1=a8[:, 0:1],
                                                                 scalar2=None, op0=ALU.mult), reads=[Bd8, Ba8], writes=[Bd8])
                    c.op("dve", lambda e, d8=d8: e.tensor_tensor_scan(out=d8[:, 3, :], data0=ones_t[0:8, :], data1=d8[:, 2, :],
                                                                      initial=0.0, op0=ALU.mult, op1=ALU.add),
                         reads=[Bd8, Bones], writes=[Bd8])
                    for q, (src_i, r_i) in enumerate(((1, 0), (3, 2))):
                        x_ = d8[:, src_i, :]
                        r_ = d8[:, r_i, :]
                        c.op("dve", lambda e, q=q, x_=x_: e.tensor_copy(out=DT3[0:8, 0, q, :], in_=x_), reads=[Bd8], writes=[BDT3])
                        c.op("dve", lambda e, q=q, x_=x_, r_=r_: e.tensor_tensor(out=r_, in0=x_, in1=DT3[0:8, 0, q, :], op=ALU.subtract),
                             reads=[Bd8, BDT3], writes=[Bd8])
                        c.op("dve", lambda e, q=q, r_=r_: e.tensor_copy(out=DT3[0:8, 1, q, :], in_=r_), reads=[Bd8], writes=[BDT3])
                        c.op("dve", lambda e, q=q, r_=r_: e.tensor_tensor(out=r_, in0=r_, in1=DT3[0:8, 1, q, :], op=ALU.subtract),
                             reads=[Bd8, BDT3], writes=[Bd8])
                        c.op("dve", lambda e, q=q, r_=r_: e.tensor_copy(out=DT3[0:8, 2, q, :], in_=r_), reads=[Bd8], writes=[BDT3])
                    ps, pb = pp.next()
                    for q in range(2):
                        for i3 in range(3):
                            mm(ps[:, q * 128:(q + 1) * 128], DT3[:, i3, q, :], Kb("ident"), i3 == 0, i3 == 2, [BDT3, Bcsb], [pb[q]])
                    dtok, Bdtok = r_dtok.next()
                    c.op("dve", lambda e, dtok=dtok, ps=ps: e.tensor_copy(out=dtok[:, 0:8], in_=ps[:, 0:8]), reads=[pb[0], pb[1]], writes=[Bdtok])
                    c.op("dve", lambda e, dtok=dtok, ps=ps: e.tensor_copy(out=dtok[:, 8:16], in_=ps[:, 128:136]), reads=[pb[1]], writes=[Bdtok])
                    for i3 in range(3):
                        c.ew(lambda e, i3=i3: e.tensor_tensor(
                            out=RB3[0:8, i3, :, :], in0=m8b[:].rearrange("p (h l) -> p h l", l=128),
                            in1=DT3[0:8, i3, 1:2, :].to_broadcast([8, 8, 128]), op=ALU.mult), reads=[BDT3, Bm8], writes=[BRB3])
                    psR0, pbR0 = pp.next()
                    if os.environ.get("MK_SKIPB", "") == "1":
                        pp.next()
                    psR1, pbR1 = pp.next()
                    for (psR, pbR, h0) in ((psR0, pbR0, 0), (psR1, pbR1, 4)):
                        if os.environ.get("MK_R", "") == "1":
                            for hh_ in range(4):
                                for i3 in range(3):
                                    mm(psR[:, hh_ * 128:(hh_ + 1) * 128], Kb("ones"), RB3[:, i3, h0 + hh_, :], i3 == 0, i3 == 2, [Bcsb, BRB3], pbR)
                            continue
                        for i3 in range(3):
                            mm(psR[:], Kb("ones"), RB3[:, i3, h0:h0 + 4, :].rearrange("p a b -> p (a b)"), i3 == 0, i3 == 2, [Bcsb, BRB3],
                               (pbR + [Bfence]) if (h0 == 4 and i3 == int(os.environ.get("MK_FENCE", "-1"))) else pbR)
                    seg, Bseg = AS(2, 2)
                    E, BE = AS(4, 2)
                    if os.environ.get("MK_FENCE", "-1") != "-1":
                        pbR0 = pbR0 + [Bfence]
                    for _d in range(int(os.environ.get("MK_DELAY", "0"))):
                        c.op("dve", lambda e: e.tensor_copy(out=ptmp[:], in_=ones_t[:]), reads=[pbR0, Bones], writes=[Bptmp])
                    for (psR, pbR, h0) in ((psR0, pbR0, 0), (psR1, pbR1, 4)):
                        c.op("dve", lambda e, psR=psR, h0=h0, seg=seg, dtok=dtok: e.tensor_tensor(
                            out=seg[:, h0:h0 + 4, :], in0=psR[:].rearrange("p (a b) -> p a b", b=128),
                            in1=dtok[:, 8 + h0:12 + h0].unsqueeze(2).to_broadcast([128, 4, 128]), op=ALU.subtract),
                             reads=[pbR, Bdtok], writes=[Bseg])
                    c.ew(lambda e, seg=seg: e.tensor_scalar(out=seg[:], in0=seg[:], scalar1=0.0, scalar2=None, op0=ALU.min), reads=[Bseg], writes=[Bseg])
                    c.op("act", lambda e, E=E, psR0=psR0: e.activation(out=E[:, 0:4, :], in_=psR0[:].rearrange("p (a b) -> p a b", b=128),
                                                                     func=AF.Exp), reads=pbR0, writes=[BE])
                    c.op("act", lambda e, E=E, psR1=psR1: e.activation(out=E[:, 4:8, :], in_=psR1[:].rearrange("p (a b) -> p a b", b=128),
                                                                     func=AF.Exp), reads=pbR1, writes=[BE])
                    c.op("act", lambda e, seg=seg: e.activation(out=seg[:], in_=seg[:], func=AF.Exp), reads=[Bseg], writes=[Bseg])
                    ps, pb = pp.next()
                    for g in range(2):
                        mm(ps[:, g * 128:(g + 1) * 128], xbc[:, 4 + g, :], xbc[:, 6 + g, :], True, True, [Bxbc], [pb[g]])
                    gm, Bgm = r_gm.next()
                    for g in range(2):
                        c.op("dve", lambda e, g=g, gm=gm, ps=ps: e.tensor_tensor(out=gm[:, g, :], in0=ps[:, g * 128:(g + 1) * 128],
                                                                                 in1=Kb("mui"), op=ALU.mult),
                             reads=[pb[g], Bcsb], writes=[Bgm])
                    MT, BMT = r_MT.next()
                    Cd, BCd = r_Cd.next()
                    for g in range(2):
                        c.ew(lambda e, g=g, MT=MT, seg=seg, gm=gm: e.tensor_tensor(
                            out=MT[:, 4 * g:4 * g + 4, :], in0=seg[:, 4 * g:4 * g + 4, :], in1=gm[:, g:g + 1, :].to_broadcast([128, 4, 128]), op=ALU.mult),
                             reads=[Bseg, Bgm], writes=[BMT])
                        c.ew(lambda e, g=g, Cd=Cd, E=E, xbc=xbc: e.tensor_tensor(
                            out=Cd[:, 4 * g:4 * g + 4, :], in0=E[:, 4 * g:4 * g + 4, :], in1=xbc[:, 6 + g:7 + g, :].to_broadcast([128, 4, 128]), op=ALU.mult),
                             reads=[BE, Bxbc], writes=[BCd])
                    ps, pb = pp.next()
                    for j in range(4):
                        mm(ps[:, j * 128:(j + 1) * 128], xbc[:, j, :], Kb("ident"), True, True, [Bxbc, Bcsb], [pb[j]])
                    xdt, Bxdt = r_xdt.next()
                    v864 = lambda ap: ap.rearrange("p (h q) -> p h q", q=64)
                    c.op("dve", lambda e, xdt=xdt, ps=ps, dtok=dtok: e.tensor_tensor(
                        out=v864(xdt[:, 0, :]), in0=v864(ps[:]), in1=dtok[:, 0:8].unsqueeze(2).to_broadcast([128, 8, 64]), op=ALU.mult),
                         reads=[pb, Bdtok], writes=[Bxdt])
                    c.ew(lambda e, xdt=xdt, seg=seg: e.tensor_tensor(
                        out=v864(xdt[:, 1, :]), in0=v864(xdt[:, 0, :]), in1=seg[:, :, 127:128].to_broadcast([128, 8, 64]), op=ALU.mult),
                         reads=[Bxdt, Bseg], writes=[Bxdt])
                    ps, pb = pp.next()
                    for g in range(2):
                        mm(ps[:, g * 128:(g + 1) * 128], xbc[:, 4 + g, :], Kb("ident"), True, True, [Bxbc, Bcsb], [pb[g]])
                    btok, Bbtok = r_btok.next()
                    evac(btok[:], ps[:, 0:256].rearrange("p (a b) -> p a b", b=128), pb[0:2], [Bbtok])
                    for h in range(8):
                        o_ = YS_t[:, h * 64:(h + 1) * 64]
                        mm(o_, MT[:, h, :], xdt[:, 0, h * 64:(h + 1) * 64], True, False, [BMT, Bxdt], [YS_b])
                        mm(o_, Cd[:, h, :], STb[:, h * 64:(h + 1) * 64], False, True, [BCd, BST], [YS_b])
                    ysh, Bysh = r_yhl.next()
                    split2(YS_t[:], ysh[:, 0, :], ysh[:, 1, :], [YS_b], Bysh, Bysh, psum=True)
                    ps, pb = pp.next()
                    for g in range(2):
                        mm(ps[:, g * 256:(g + 1) * 256], btok[:, g, :], xdt[:, 1, g * 256:(g + 1) * 256], True, True,
                           [Bbtok, Bxdt], pb[2 * g:2 * g + 2])
                    c.ew(lambda e, E=E: e.tensor_tensor(out=v864(STf[:]), in0=v864(STf[:]), in1=E[:, :, 127:128].to_broadcast([128, 8, 64]), op=ALU.mult),
                         reads=[BST, BE], writes=[BST])
                    c.op("dve", lambda e, ps=ps: e.tensor_tensor(out=STf[:], in0=STf[:], in1=ps[:], op=ALU.add), reads=[BST, pb], writes=[BST])
                    c.op("act", lambda e: e.activation(out=STb[:], in_=STf[:], func=AF.Copy), reads=[BST], writes=[BST])
                    if not fwd:
                        for q, key in enumerate(("ybs_h", "ybs_l")):
                            c.dma("sp", lambda e, ysh=ysh, q=q, key=key, ti=ti: e.dma_start(out=ysc[key][ti * 128:(ti + 1) * 128, :], in_=ysh[:, q, :]),
                                  reads=[Bysh], writes=[B_ysc[key][ti]])
                    if SUB < 3:
                        continue

                    U, BU = AU(4, 15)
                    inproj(U, BU, list(range(CR, CR + 15)))
                    us, Bus = AS(0, 4, nch=15)
                    tsh, Btsh = AS(8, 2)
                    bc15 = lambda ap, j0, n: ap[:, j0:j0 + n].unsqueeze(2).to_broadcast([128, n, 128])
                    c.ew(lambda e, us=us, U=U: e.tensor_tensor(out=us, in0=U[:, 0:15, 2:130], in1=bc15(der, 0, 15), op=ALU.mult),
                         reads=[BU, Bder], writes=[Bus])
                    for (j0, n) in ((0, 8), (8, 7)):
                        for (sl_, mun) in (((1, 129), mpn), ((3, 131), mnn)):
                            mu_ap = pv[:, PV[mun][0]:PV[mun][0] + 15]
                            c.ew(lambda e, j0=j0, n=n, sl_=sl_, mu_ap=mu_ap, U=U, tsh=tsh: e.tensor_tensor(
                                out=tsh[:, 0:n, :], in0=U[:, j0:j0 + n, sl_[0]:sl_[1]], in1=bc15(mu_ap, j0, n), op=ALU.mult),
                                 reads=[BU, Bpv], writes=[Btsh])
                            c.ew(lambda e, j0=j0, n=n, us=us, tsh=tsh: e.tensor_tensor(
                                out=us[:, j0:j0 + n, :], in0=us[:, j0:j0 + n, :], in1=tsh[:, 0:n, :], op=ALU.add),
                                 reads=[Bus, Btsh], writes=[Bus])
                    lrin, Blrin = r_lrin.next()
                    c.op("act", lambda e, us=us, lrin=lrin: e.activation(out=lrin[:, 0, :], in_=us[:, 12, :], func=AF.Tanh), reads=[Bus], writes=[Blrin])
                    c.op("act", lambda e, us=us, lrin=lrin: e.activation(out=lrin[:, 1, :], in_=us[:, 13, :], func=AF.Copy), reads=[Bus], writes=[Blrin])
                    c.op("act", lambda e, us=us, lrin=lrin: e.activation(out=lrin[:, 2, :], in_=us[:, 14, :], func=AF.Sigmoid), reads=[Bus], writes=[Blrin])
                    psL, pbL = pp.next()
                    psI, pbI = pp.next()
                    for cp in range(4):
                        mm(psL[:, cp * 128:(cp + 1) * 128], w2[:, cp * 128:(cp + 1) * 128], lrin[:, 0, :], True, True, [Blr, Blrin], [pbL[cp]])
                        mm(psI[:, cp * 128:(cp + 1) * 128], a2[:, cp * 128:(cp + 1) * 128], lrin[:, 1, :], True, True, [Blr, Blrin], [pbI[cp]])
                    sg, Bsg = AS(4)
                    icl, Bicl = AS(5)
                    bc4 = lambda n: P(n, 0, 4).unsqueeze(2).to_broadcast([128, 4, 128])
                    v4 = lambda ps_: ps_[:].rearrange("p (a b) -> p a b", b=128)
                    c.op("dve", lambda e, sg=sg, psL=psL: e.tensor_tensor(out=sg[:], in0=v4(psL), in1=bc4(w0n), op=ALU.add), reads=[pbL, Bpv], writes=[Bsg])
                    c.op("act", lambda e, sg=sg: e.activation(out=sg[:], in_=sg[:], func=AF.Sigmoid), reads=[Bsg], writes=[Bsg])
                    c.op("dve", lambda e, icl=icl, psI=psI: e.tensor_tensor(out=icl[:], in0=v4(psI), in1=bc4("a0"), op=ALU.add), reads=[pbI, Bpv], writes=[Bicl])
                    c.op("act", lambda e, icl=icl: e.activation(out=icl[:], in_=icl[:], func=AF.Sigmoid), reads=[Bicl], writes=[Bicl])
                    for cp in range(4):
                        c.op("dve", lambda e, cp=cp, sg=sg: e.tensor_tensor_scan(
                            out=CSt[:, cp, 1:129], data0=ones_t[:], data1=sg[:, cp, :], initial=0.0, op0=ALU.mult, op1=ALU.add),
                             reads=[Bsg, Bones, BCSt], writes=[BCSt])
                    gam, Bgam = AS(6)
                    igam, Bigam = AS(7)
                    gprev, Bgprev = AS(8)
                    c.op("act", lambda e, gam=gam: e.activation(out=gam[:], in_=CSt[:, :, 1:129], func=AF.Exp, scale=-CDEC),
                         reads=[BCSt], writes=[Bgam])
                    c.op("act", lambda e, igam=igam: e.activation(out=igam[:], in_=CSt[:, :, 1:129], func=AF.Exp, scale=CDEC),
                         reads=[BCSt], writes=[Bigam])
                    c.op("act", lambda e, gprev=gprev: e.activation(out=gprev[:], in_=CSt[:, :, 0:128], func=AF.Exp, scale=-CDEC),
                         reads=[BCSt], writes=[Bgprev])
                    kkt, Bkk = AS(9)
                    c.ew(lambda e, kkt=kkt, us=us: e.tensor_tensor(out=kkt[:], in0=us[:, 4:8, :], in1=bc4("kk"), op=ALU.mult), reads=[Bus, Bpv], writes=[Bkk])
                    sq, Bsq = sg, Bsg
                    c.ew(lambda e, sq=sq, kkt=kkt: e.tensor_tensor(out=sq[:], in0=kkt[:], in1=kkt[:], op=ALU.mult), reads=[Bkk], writes=[Bsq])
                    psS, pbS = blocksum4(sq[:], Bsq, "bo64")
                    rn, Brn = AS(10)
                    c.op("dve", lambda e, rn=rn, psS=psS: e.tensor_scalar(out=rn[:], in0=psS, scalar1=1e-24, scalar2=None, op0=ALU.max),
                         reads=pbS, writes=[Brn])
                    c.op("act", lambda e, rn=rn: e.activation(out=rn[:], in_=rn[:], func=AF.Sqrt), reads=[Brn], writes=[Brn])
                    c.op("dve", lambda e, rn=rn: e.reciprocal(out=rn[:], in_=rn[:]), reads=[Brn], writes=[Brn])
                    c.ew(lambda e, kkt=kkt, rn=rn: e.tensor_tensor(out=kkt[:], in0=kkt[:], in1=rn[:], op=ALU.mult), reads=[Bkk, Brn], writes=[Bkk])
                    km, Bkm = rn, Brn
                    c.ew(lambda e, km=km, icl=icl: e.tensor_tensor(out=km[:], in0=icl[:], in1=bc4("ka"), op=ALU.mult), reads=[Bicl, Bpv], writes=[Bkm])
                    c.ew(lambda e, km=km: e.tensor_tensor(out=km[:], in0=km[:], in1=der[:, 15:19].unsqueeze(2).to_broadcast([128, 4, 128]), op=ALU.add),
                         reads=[Bkm, Bder], writes=[Bkm])
                    c.ew(lambda e, km=km, us=us: e.tensor_tensor(out=km[:], in0=km[:], in1=us[:, 4:8, :], op=ALU.mult), reads=[Bkm, Bus], writes=[Bkm])
                    if fwd:
                        rk_, Brk = sq, Bsq
                        c.ew(lambda e, rk_=rk_, us=us: e.tensor_tensor(out=rk_[:], in0=us[:, 0:4, :], in1=bc4("rk"), op=ALU.mult), reads=[Bus, Bpv], writes=[Brk])
                        c.ew(lambda e, rk_=rk_, km=km: e.tensor_tensor(out=rk_[:], in0=rk_[:], in1=km[:], op=ALU.mult), reads=[Brk, Bkm], writes=[Brk])
                        psB, pbB = blocksum4(rk_[:], Brk, "bo64")
                        bonv, Bbonv = r_bg.next()
                        c.op("dve", lambda e, bonv=bonv, psB=psB, us=us: e.tensor_tensor(out=bonv[:], in0=psB, in1=us[:, 8:12, :], op=ALU.mult),
                             reads=pbB + [Bus], writes=[Bbonv])
                        psG, pbG = pp.next()
                        for cp in range(4):
                            mm(psG[:, cp * 128:(cp + 1) * 128], g2[:, cp * 128:(cp + 1) * 128], lrin[:, 2, :], True, True, [Blr, Blrin], [pbG[cp]])
                        gT, BgT = r_bg.next()
                        c.op("act", lambda e, gT=gT, psG=psG: e.activation(out=gT[:], in_=psG[:].rearrange("p (a b) -> p a b", b=128), func=AF.Copy),
                             reads=pbG, writes=[BgT])
                    KT, BKT = r_op.next()
                    BT, BBT = r_op.next()
                    AT, BAT = r_op.next()
                    vb, Bvb = r_op.next()
                    c.ew(lambda e, KT=KT, km=km, igam=igam: e.tensor_tensor(out=KT[:], in0=km[:], in1=igam[:], op=ALU.mult),
                         reads=[Bkm, Bigam], writes=[BKT])
                    c.ew(lambda e, icl=icl, kkt=kkt: e.tensor_tensor(out=icl[:], in0=icl[:], in1=kkt[:], op=ALU.mult), reads=[Bicl, Bkk], writes=[Bicl])
                    c.ew(lambda e, BT=BT, icl=icl, igam=igam: e.tensor_tensor(out=BT[:], in0=icl[:], in1=igam[:], op=ALU.mult),
                         reads=[Bicl, Bigam], writes=[BBT])
                    c.op("dve", lambda e, AT=AT, kkt=kkt, gprev=gprev: e.scalar_tensor_tensor(out=AT[:], in0=kkt[:], scalar=-1.0, in1=gprev[:],
                                                                                             op0=ALU.mult, op1=ALU.mult),
                         reads=[Bkk, Bgprev], writes=[BAT])
                    c.ew(lambda e, vb=vb, us=us: e.tensor_copy(out=vb[:], in_=us[:, 8:12, :]), reads=[Bus], writes=[Bvb])
                    c.ew(lambda e, AT=AT: e.tensor_copy(out=ARe_t[0:64, :, 0:128], in_=AT[0:64, :, :]), reads=[BAT], writes=[BAR_t])
                    c.ew(lambda e, AT=AT: e.tensor_copy(out=ARo_t[64:128, :, 0:128], in_=AT[64:128, :, :]), reads=[BAT], writes=[BAR_t])
                    c.ew(lambda e, us=us, gam=gam: e.tensor_tensor(out=ARe_t[0:64, :, 128:256], in0=us[0:64, 0:4, :], in1=gam[0:64, :, :],
                                                                   op=ALU.mult), reads=[Bus, Bgam], writes=[BAR_t])
                    c.ew(lambda e, us=us, gam=gam: e.tensor_tensor(out=ARo_t[64:128, :, 128:256], in0=us[64:128, 0:4, :], in1=gam[64:128, :, :],
                                                                   op=ALU.mult), reads=[Bus, Bgam], writes=[BAR_t])
                    if SUB < 4:
                        continue
                    if fwd:
                        src_t = NT - 1 - ti
                        ybS, BybS = r_ybl.next()
                        ybR, BybR = r_ybl.next()
                        for (dst_, Bdst_, kh, kl) in ((ybS, BybS, "ybs_h", "ybs_l"), (ybR, BybR, "ybr_h", "ybr_l")):
                            for q, key in enumerate((kh, kl)):
                                c.dma("sp", lambda e, dst_=dst_, q=q, key=key, src_t=src_t: e.dma_start(
                                    out=dst_[:, q, :], in_=ysc[key][src_t * 128:(src_t + 1) * 128, :]),
                                      reads=[B_ysc[key][src_t]], writes=[Bdst_])
                    for cp in range(4):
                        ps, pb = pp.next()
                        for q, src in enumerate((AT, BT, KT, vb)):
                            mm(ps[:, q * 128:(q + 1) * 128], src[:, cp, :], Kb("ident"), True, True, [BAT, BBT, BKT, Bvb, Bcsb], [pb[q]])
                        tok, Btok = r_tok.next()
                        evac(tok[:], ps[:].rearrange("p (a b) -> p a b", b=128), pb, [Btok])
                        wu, Bwu = r_wu.next()
                        e4s = []
                        for x in range(2):
                            AR_x = (ARe_t, ARo_t)[x]
                            hs = slice(x * 64, (x + 1) * 64)
                            psA, pbA = pp.next()
                            psB4, pbB4 = pp.next()
                            mm(psA[:, 0:128], AR_x[:, cp, 0:128], BT[:, cp, :], True, True, [BAR_t, BBT], [pbA[0]])
                            mm(psB4[:, 0:256], BT[:, cp, :], AR_x[:, cp, :], True, True, [BAR_t, BBT], pbB4[0:2])
                            mm(psB4[:, 256:512], KT[:, cp, :], AR_x[:, cp, :], True, True, [BAR_t, BKT], pbB4[2:4])
                            p0, Bp0 = r_p0.next()
                            c.op("dve", lambda e, p0=p0, psA=psA: e.tensor_tensor(out=p0[:], in0=psA[:, 0:128], in1=Kb("msl"), op=ALU.mult),
                                 reads=[pbA[0], Bcsb], writes=[Bp0])
                            e4, Be4 = r_e4.next()
                            c.op("dve", lambda e, e4=e4, psB4=psB4: e.tensor_tensor(out=e4[:], in0=psB4[:], in1=mask4[:], op=ALU.mult),
                                 reads=pbB4 + [Bm4], writes=[Be4])
                            e4s.append((e4, Be4))
                            mm(psA[:, 128:192], e4[:, 256:384], tok[:, 3, hs], True, True, [Be4, Btok], [pbA[1]])
                            zf, Bzf = r_zf.next()
                            zb, Bzb = r_zb.next()
                            c.op("act", lambda e, zf=zf, tok=tok, hs=hs: e.activation(out=zf[:, 0:64], in_=tok[:, 0, hs], func=AF.Copy),
                                 reads=[Btok], writes=[Bzf])
                            c.op("act", lambda e, zf=zf, psA=psA: e.activation(out=zf[:, 64:128], in_=psA[:, 128:192], func=AF.Copy),
                                 reads=[pbA[1]], writes=[Bzf])
                            c.op("dve", lambda e, zf=zf, zb=zb: e.tensor_copy(out=zb[:], in_=zf[:]), reads=[Bzf], writes=[Bzb])
                            Pn, PnT = p0[:], e4[:, 0:128]
                            BPn = [Bp0, Be4]
                            for it in range(7):
                                psZ, pbZ = pp.next()
                                mm(psZ[:, 0:128], PnT, zb[:], True, True, BPn + [Bzb], [pbZ[0]])
                                c.op("dve", lambda e, zf=zf, psZ=psZ: e.tensor_tensor(out=zf[:], in0=psZ[:, 0:128], in1=zf[:], op=ALU.add),
                                     reads=[pbZ[0], Bzf], writes=[Bzf])
                                if it < 6:
                                    zb, Bzb = r_zb.next()
                                    c.op("act", lambda e, zf=zf, zb=zb: e.activation(out=zb[:], in_=zf[:], func=AF.Copy), reads=[Bzf], writes=[Bzb])
                                    mm(psZ[:, 128:256], PnT, Pn, True, True, BPn, [pbZ[1]])
                                    mm(psZ[:, 256:384], Pn, PnT, True, True, BPn, [pbZ[2]])
                                    ppq, Bppq = r_pp.next()
                                    evac(ppq[:], psZ[:, 128:384], pbZ[1:3], [Bppq])
                                    Pn, PnT = ppq[:, 0:128], ppq[:, 128:256]
                                    BPn = [Bppq]
                            c.op("act", lambda e, wu=wu, zf=zf, hs=hs: e.activation(out=wu[:, 0, hs], in_=zf[:, 0:64], func=AF.Copy),
                                 reads=[Bzf], writes=[Bwu])
                            c.op("dve", lambda e, wu=wu, zf=zf, hs=hs: e.tensor_copy(out=wu[:, 1, hs], in_=zf[:, 64:128]),
                                 reads=[Bzf], writes=[Bwu])
                        qt, Bqt = r_qt.next()
                        psQ, pbQ = pp.next()
                        for x in range(2):
                            e4, Be4 = e4s[x]
                            AR_x = (ARe_t, ARo_t)[x]
                            hs = slice(x * 64, (x + 1) * 64)
                            mm(psQ[:, x * 128:(x + 1) * 128], wu[:, 0, :], e4[:, 128:256], True, True, [Bwu, Be4], [pbQ[x]])
                            c.op("dve", lambda e, qt=qt, psQ=psQ, x=x, hs=hs, AR_x=AR_x, cp=cp: e.tensor_tensor(
                                out=qt[hs, :], in0=psQ[hs, x * 128:(x + 1) * 128], in1=AR_x[hs, cp, 128:256], op=ALU.add),
                                 reads=[pbQ[x], BAR_t], writes=[Bqt])
                        yo = YR_t[:, cp * 128:(cp + 1) * 128]
                        mm(yo, qt[:], Hb[:, cp, :], True, False, [Bqt, BH[cp]], [YR_b])
                        for x in range(2):
                            e4, Be4 = e4s[x]
                            hs = slice(x * 64, (x + 1) * 64)
                            yx = YR_t[:, cp * 128 + x * 64:cp * 128 + (x + 1) * 64]
                            mm(yx, e4[:, 128:256], wu[:, 1, hs], False, False, [Be4, Bwu], [YR_b])
                            mm(yx, e4[:, 384:512], tok[:, 3, hs], False, x == 1, [Be4, Btok], [YR_b])
                        psM, pbM = pp.next()
                        mm(psM[:, 0:128], wu[:, 0, :], tok[:, 1, :], True, True, [Bwu, Btok], [pbM[0]])
                        mtb, Bmtb = r_mtb.next()
                        c.op("dve", lambda e, mtb=mtb, psM=psM: e.tensor_tensor(out=mtb[:], in0=psM[:, 0:128], in1=Kb("bmask"), op=ALU.mult),
                             reads=[pbM[0], Bcsb], writes=[Bmtb])
                        hsl = psM[:, 128:256]
                        mm(hsl, tok[:, 1, :], wu[:, 1, :], True, False, [Btok, Bwu], [pbM[1]])
                        mm(hsl, tok[:, 2, :], tok[:, 3, :], False, False, [Btok], [pbM[1]])
                        mm(hsl, Kb("ident"), Hb[:, cp, :], False, False, [Bcsb, BH[cp]], [pbM[1]])
                        mm(hsl, mtb[:], Hb[:, cp, :], False, True, [Bmtb, BH[cp]], [pbM[1]])
                        c.op("dve", lambda e, cp=cp, psM=psM, gam=gam: e.scalar_tensor_tensor(
                            out=Hb[:, cp, :], in0=psM[:, 128:256], scalar=gam[:, cp, 127:128], in1=Kb("bmask"), op0=ALU.mult, op1=ALU.mult),
                             reads=[pbM[1], Bgam, Bcsb], writes=[BH[cp]])
                    yrh, Byrh = r_yhl.next()
                    split2(YR_t[:], yrh[:, 0, :], yrh[:, 1, :], [YR_b], Byrh, Byrh, psum=True)
                    if not fwd:
                        for q, key in enumerate(("ybr_h", "ybr_l")):
                            c.dma("sp", lambda e, yrh=yrh, q=q, key=key, ti=ti: e.dma_start(out=ysc[key][ti * 128:(ti + 1) * 128, :], in_=yrh[:, q, :]),
                                  reads=[Byrh], writes=[B_ysc[key][ti]])
                        continue
                    if SUB < 5:
                        continue

                    mix, Bmix = r_mix.next()

                    def ytrans(yh, Byh, yb, Byb):
                        ps, pb = pp.next()
                        for j in range(4):
                            o_ = ps[:, j * 128:(j + 1) * 128]
                            cs_ = slice(j * 128, (j + 1) * 128)
                            mm(o_, yh[:, 0, cs_], Kb("ident"), True, False, [Byh, Bcsb], [pb[j]])
                            mm(o_, yh[:, 1, cs_], Kb("ident"), False, False, [Byh, Bcsb], [pb[j]])
                            mm(o_, yb[:, 0, cs_], Kb("J"), False, False, [Byb, Bcsb], [pb[j]])
                            mm(o_, yb[:, 1, cs_], Kb("J"), False, True, [Byb, Bcsb], [pb[j]])
                        return ps, pb
                    ps, pb = ytrans(ysh, Bysh, ybS, BybS)
                    yg, Byg = AS(0)
                    for j in range(4):
                        c.op("dve", lambda e, j=j, yg=yg, xsf=xsf, ps=ps: e.scalar_tensor_tensor(
                            out=yg[:, j, :], in0=xsf[:, j, :], scalar=P("dch", j), in1=ps[:, j * 128:(j + 1) * 128], op0=ALU.mult, op1=ALU.add),
                             reads=[Bxsf, Bpv, pb[j]], writes=[Byg])
                    c.ew(lambda e, yg=yg, zs=zs: e.tensor_tensor(out=yg[:], in0=yg[:], in1=zs[:], op=ALU.mult), reads=[Byg, Bzs], writes=[Byg])
                    sq2, Bsq2 = AS(2)
                    c.ew(lambda e, sq2=sq2, yg=yg: e.tensor_tensor(out=sq2[:], in0=yg[:], in1=yg[:], op=ALU.mult), reads=[Byg], writes=[Bsq2])
                    hi, Bhi = r_sp.next()
                    lo, Blo = r_sp.next()
                    split2(sq2[:], hi[:], lo[:], [Bsq2], Bhi, Blo)
                    ps, pb = pp.next()
                    for g in range(2):
                        o_ = ps[:, g * 128:(g + 1) * 128]
                        for j in range(2):
                            mm(o_, Kb("o256"), hi[:, 2 * g + j, :], j == 0, False, [Bcsb, Bhi], [pb[g]])
                            mm(o_, Kb("o256"), lo[:, 2 * g + j, :], False, j == 1, [Bcsb, Blo], [pb[g]])
                    rs, Brs = r_st.next()
                    rs2, Brs2 = r_st.next()
                    rsqrt_eps(rs[:], ps[:, 0:128], RMS_EPS, [pb[0]], [Brs])
                    rsqrt_eps(rs2[:], ps[:, 128:256], RMS_EPS, [pb[1]], [Brs2])
                    for j in range(4):
                        rr_, Brr_ = (rs, Brs) if j < 2 else (rs2, Brs2)
                        c.op("dve", lambda e, j=j, mix=mix, yg=yg, rr_=rr_: e.scalar_tensor_tensor(
                            out=mix[:, j, :], in0=yg[:, j, :], scalar=P("nw", j), in1=rr_[:], op0=ALU.mult, op1=ALU.mult),
                             reads=[Byg, Bpv, Brr_], writes=[Bmix])
                    ps, pb = ytrans(yrh, Byrh, ybR, BybR)
                    yf, Byf = AS(1)
                    evac(yf[:], ps[:].rearrange("p (a b) -> p a b", b=128), pb, [Byf])
                    psm, pbm = blocksum4(yf[:], Byf, "bo64m")
                    c.op("dve", lambda e, yf=yf, psm=psm: e.tensor_tensor(out=yf[:], in0=yf[:], in1=psm, op=ALU.subtract), reads=[Byf] + pbm, writes=[Byf])
                    c.ew(lambda e, sq2=sq2, yf=yf: e.tensor_tensor(out=sq2[:], in0=yf[:], in1=yf[:], op=ALU.mult), reads=[Byf], writes=[Bsq2])
                    psv, pbv = blocksum4(sq2[:], Bsq2, "bo64m")
                    rsqrt_eps(sq2[:], psv, GN_EPS, pbv, [Bsq2])
                    c.ew(lambda e, yf=yf, sq2=sq2: e.tensor_tensor(out=yf[:], in0=yf[:], in1=sq2[:], op=ALU.mult), reads=[Byf, Bsq2], writes=[Byf])
                    c.ew(lambda e, yf=yf: e.tensor_tensor(out=yf[:], in0=yf[:], in1=bc4("lw"), op=ALU.mult), reads=[Byf, Bpv], writes=[Byf])
                    c.ew(lambda e, yf=yf: e.tensor_tensor(out=yf[:], in0=yf[:], in1=bc4("lb"), op=ALU.add), reads=[Byf, Bpv], writes=[Byf])
                    c.ew(lambda e, yf=yf, bonv=bonv: e.tensor_tensor(out=yf[:], in0=yf[:], in1=bonv[:], op=ALU.add), reads=[Byf, Bbonv], writes=[Byf])
                    c.ew(lambda e, mix=mix, yf=yf, gT=gT: e.tensor_tensor(out=mix[:, 4:8, :], in0=yf[:], in1=gT[:], op=ALU.mult),
                         reads=[Byf, BgT], writes=[Bmix])
                    xf, Bxf = AS(9, 2)
                    c.dma("sp", lambda e, xf=xf, col0=col0: e.dma_start(
                        out=xf, in_=xT[:, col0 + 2:col0 + 130].rearrange("(k p) t -> p k t", p=128)), writes=[Bxf])
                    pre, Bpre = AS(3, 2)
                    for half in range(2):
                        ps, pb = pp.next()
                        for j in range(4):
                            dc = half * 4 + j
                            for k in range(8):
                                mm(ps[:, j * 128:(j + 1) * 128], wo[:, k, dc * 128:(dc + 1) * 128], mix[:, k, :], k == 0, k == 7,
                                   [Bwo, Bmix], [pb[j]])
                        c.op("dve", lambda e, half=half, pre=pre, xf=xf, ps=ps: e.scalar_tensor_tensor(
                            out=pre[:, half * 4:(half + 1) * 4, :], in0=xf[:, half * 4:(half + 1) * 4, :], scalar=ALPHA,
                            in1=ps[:].rearrange("p (a b) -> p a b", b=128), op0=ALU.mult, op1=ALU.add), reads=[Bxf] + pb, writes=[Bpre])
                    hout, Bhout = AS(7, 2)
                    sqL, BsqL = AS(5, 2)
                    layer_norm(pp, Kb, pre, Bpre, hout, Bhout, sqL, BsqL, r_st, r_sps, P, "l1w", "l1b", Bcsb, Bpv, 128)
                    t0 = ti * C
                    c.dma("sp", lambda e, hout=hout, t0=t0: e.dma_start(
                        out=hT[:, t0:t0 + 128].rearrange("(k p) t -> p k t", p=128), in_=hout), reads=[Bhout], writes=[B_hT[ti]])
                c.full_barrier()
                c.run_block()

        def ffn_phase():
            NF = 128
            with ExitStack() as es:
                sb = lambda n, s, dt=F32: es.enter_context(nc.sbuf_tensor(n, s, dt))
                rot = lambda n, s, dt=F32, k=2: Rot(nc, es, n, s, dt, k)
                pp = PsumPool(nc, es, ["pf%d" % i for i in range(8)])
                wfi = sb("wfi", [128, 8, 2 * DFF], BF16); Bwfi = Buf()
                wfo = sb("wfo", [128, NFC, D], BF16); Bwfo = Buf()
                for k in range(8):
                    c.dma("pool", lambda e, k=k: e.dma_start(out=wfi[:, k, :], in_=w_fi[k * 128:(k + 1) * 128, :]), writes=[Bwfi])
                for k in range(NFC):
                    c.dma("pool", lambda e, k=k: e.dma_start(out=wfo[:, k, :], in_=w_fo[k * 128:(k + 1) * 128, :]), writes=[Bwfo])
                pv = sb("pv2", [128, NPV]); Bpv = Buf()
                c.dma("sp", lambda e: e.dma_start(out=pv[:], in_=pvec), writes=[Bpv])
                csb = sb("csb2", [128, NCST * 128], BF16); Bcsb = Buf()
                c.dma("pool", lambda e: e.dma_start(out=csb[:], in_=cst), writes=[Bcsb])
                Kb = lambda n: csb[:, CST[n] * 128:(CST[n] + 1) * 128]
                P = lambda n, j=0, w=1: pv[:, PV[n][0] + j:PV[n][0] + j + w]
                r_hf = rot("hf", [128, 8, NF], F32, 2)
                r_hb = rot("hb", [128, 8, NF], BF16, 2)
                r_act = rot("actt", [128, NFC, NF], BF16, 1)
                r_sl = rot("sl", [128, NF], F32, 3)
                r_sq = rot("sq2", [128, 8, NF], F32, 1)
                r_st = rot("st2", [128, NF], F32, 4)
                r_sps = rot("sps2", [128, NF], BF16, 2)
                r_out = rot("outt", [128, 8, NF], F32, 2)
                for fi in range(min(TT // NF, NTL)):
                    t0 = fi * NF
                    deps = [B_hT[t] for t in range(t0 // C, (t0 + NF) // C)]
                    hf, Bhf = r_hf.next()
                    hb, Bhb = r_hb.next()
                    c.dma("sp", lambda e, hf=hf, t0=t0: e.dma_start(out=hf[:], in_=hT[:, t0:t0 + NF].rearrange("(k p) t -> p k t", p=128)),
                          reads=deps, writes=[Bhf])
                    c.op("act", lambda e, hf=hf, hb=hb: e.activation(out=hb[:], in_=hf[:], func=AF.Copy), reads=[Bhf], writes=[Bhb])
                    actt, Bact = r_act.next()
                    for fc in range(NFC):
                        ps, pb = pp.next()
                        for k in range(8):
                            mm(ps[:, 0:NF], wfi[:, k, fc * 128:(fc + 1) * 128], hb[:, k, :], k == 0, k == 7, [Bwfi, Bhb], pb[0:2])
                        for k in range(8):
                            mm(ps[:, 256:256 + NF], wfi[:, k, DFF + fc * 128:DFF + (fc + 1) * 128], hb[:, k, :], k == 0, k == 7,
                               [Bwfi, Bhb], pb[2:4])
                        sl, Bsl = r_sl.next()
                        c.op("act", lambda e, sl=sl, ps=ps: e.activation(out=sl[:], in_=ps[:, 0:NF], func=AF.Silu), reads=pb[0:2], writes=[Bsl])
                        c.op("dve", lambda e, fc=fc, sl=sl, ps=ps, actt=actt: e.tensor_tensor(
                            out=actt[:, fc, :], in0=ps[:, 256:256 + NF], in1=sl[:], op=ALU.mult), reads=pb[2:4] + [Bsl], writes=[Bact])
                    for half in range(4):
                        ps, pb = pp.next()
                        for j in range(2):
                            dc = half * 2 + j
                            for k in range(NFC):
                                mm(ps[:, j * 256:j * 256 + NF], wfo[:, k, dc * 128:(dc + 1) * 128], actt[:, k, :], k == 0, k == NFC - 1,
                                   [Bwfo, Bact], pb[2 * j:2 * j + 2])
                        for j in range(2):
                            dc = half * 2 + j
                            c.op("dve", lambda e, dc=dc, j=j, hf=hf, ps=ps: e.scalar_tensor_tensor(
                                out=hf[:, dc, :], in0=hf[:, dc, :], scalar=ALPHA, in1=ps[:, j * 256:j * 256 + NF], op0=ALU.mult, op1=ALU.add),
                                 reads=[Bhf] + pb[2 * j:2 * j + 2], writes=[Bhf])
                    out, Bout = r_out.next()
                    sq, Bsq = r_sq.next()
                    layer_norm(pp, Kb, hf[:], Bhf, out[:], Bout, sq[:], Bsq, r_st, r_sps, P, "l2w", "l2b", Bcsb, Bpv, NF)
                    Bo = Buf()
                    B_out.append(Bo)
                    c.dma("sp", lambda e, out=out, t0=t0: e.dma_start(out=yT[:, t0:t0 + NF].rearrange("(k p) t -> p k t", p=128), in_=out[:]),
                          reads=[Bout], writes=[Bo])
                c.wait_all("sp", B_out)
                c.run_block()

        mixer_phase(False)
        if stage >= 2:
            mixer_phase(True)
        if stage >= 3:
            ffn_phase()
    return nc


def _consts():
    i = np.arange(128)
    p, f = i[:, None], i[None, :]
    blocks = {
        "ident": (p == f), "J": (p + f == 127), "msl": (f < p), "msu": (p < f), "mui": (p <= f),
        "bmask": ((p // 64) == (f // 64)), "bo64": ((p // 64) == (f // 64)),
    }
    out = np.zeros((128, NCST * 128), np.float32)
    for n, m in blocks.items():
        out[:, CST[n] * 128:(CST[n] + 1) * 128] = m.astype(np.float32)
    out[:, CST["bo64m"] * 128:(CST["bo64m"] + 1) * 128] = blocks["bo64"].astype(np.float32) / 64.0
    out[:, CST["o1024"] * 128:(CST["o1024"] + 1) * 128] = 1.0 / 1024.0
    out[:, CST["o256"] * 128:(CST["o256"] + 1) * 128] = 1.0 / 256.0
    out[:, CST["ones"] * 128:(CST["ones"] + 1) * 128] = 1.0
    m8 = np.zeros((8, 1024), np.float32)
    for h in range(8):
        m8[h, h * 128:(h + 1) * 128] = 1.0
    return out, m8


def _chunkvec(v, n):
    return np.ascontiguousarray(np.asarray(v, np.float32).reshape(n, 128).T)


def prep_params(inp):
    g = lambda n: np.asarray(inp[n], np.float32)[0]
    w_in = g("w_in")
    M_W, CONV = 512, 1024
    o_z, o_xbc, o_dt = 0, 512, 1536
    o_r = 1544
    W = np.zeros((D, NCH_IN * 128), np.float32)
    W[:, 0:512] = w_in[:, o_z:o_z + 512]
    W[:, 512:1536] = w_in[:, o_xbc:o_xbc + 1024]
    W[:, 1536:3072] = w_in[:, o_r:o_r + 1536]
    W[:, CWL * 128:CWL * 128 + 64] = w_in[:, o_r + 1536:o_r + 1600]
    W[:, CAL * 128:CAL * 128 + 64] = w_in[:, o_r + 1600:o_r + 1664]
    W[:, CGL * 128:CGL * 128 + 128] = w_in[:, o_r + 1664:o_r + 1792]
    W[:, CDT * 128:CDT * 128 + 8] = w_in[:, o_dt:o_dt + 8]
    pv = np.zeros((128, NPV), np.float32)
    def put(name, arr):
        o, w = PV[name]
        pv[:, o:o + w] = arr
    cw = g("conv_w")
    cwp = np.zeros((128, 8, 5), np.float32)
    for j in range(8):
        cwp[:, j, :] = cw[:, j * 128:(j + 1) * 128].T
    put("cw", cwp.reshape(128, 40))
    put("cb", _chunkvec(g("conv_b"), 8))
    put("dch", _chunkvec(np.repeat(g("m_d"), 64), 4))
    put("nw", _chunkvec(g("m_norm_w"), 4))
    def mu15(v):
        o = np.zeros((128, 15), np.float32)
        o[:, 0:12] = _chunkvec(v[0:1536], 12)
        o[0:64, 12] = v[1536:1600]
        o[0:64, 13] = v[1600:1664]
        o[:, 14] = v[1664:1792]
        return o
    put("mp", mu15(g("r_mu_prev")))
    put("mn", mu15(g("r_mu_next")))
    put("w0f", _chunkvec(g("r_w0_f"), 4))
    put("w0b", _chunkvec(g("r_w0_b"), 4))
    put("a0", _chunkvec(g("r_a0"), 4))
    put("kk", _chunkvec(g("r_k_k"), 4))
    put("ka", _chunkvec(g("r_k_a"), 4))
    put("rk", _chunkvec(g("r_r_k").reshape(-1), 4))
    put("lw", _chunkvec(g("r_lnx_w"), 4))
    put("lb", _chunkvec(g("r_lnx_b"), 4))
    put("l1w", _chunkvec(g("ln1_w"), 8))
    put("l1b", _chunkvec(g("ln1_b"), 8))
    put("l2w", _chunkvec(g("ln2_w"), 8))
    put("l2b", _chunkvec(g("ln2_b"), 8))
    p8 = np.zeros((8, 8), np.float32)
    p8[:, 0] = g("m_dt_bias_f"); p8[:, 1] = g("m_dt_bias_b")
    p8[:, 2] = g("m_a_log_f"); p8[:, 3] = g("m_a_log_b")
    lowr = np.zeros((128, 4 * 512), np.float32)
    lowr[0:64, 0:512] = g("r_w2_f")
    lowr[0:64, 512:1024] = g("r_w2_b")
    lowr[0:64, 1024:1536] = g("r_a2")
    lowr[:, 1536:2048] = g("r_g2")
    cst, m8 = _consts()
    return dict(w_in=W, w_out=np.ascontiguousarray(g("w_out")), w_fi=np.ascontiguousarray(g("w_ffn_in")),
                w_fo=np.ascontiguousarray(g("w_ffn_out")), p8=p8, lowr=lowr, cst=cst, m8=m8), pv


def layout_core(units, link, TU):
    xT = np.zeros((D, 2 * (TU + 4)), np.float32)
    xTr = np.zeros((D, 2 * (TU + 4)), np.float32)
    for u, (seq, s) in enumerate(units):
        if seq is None:
            continue
        T = seq.shape[0]
        lo, hi = s - 2, s + TU + 2
        blk = np.zeros((TU + 4, D), np.float32)
        a, b = max(lo, 0), min(hi, T)
        blk[a - lo:b - lo] = seq[a:b]
        xT[:, u * (TU + 4):(u + 1) * (TU + 4)] = blk.T
        ur = 1 - u
        xTr[:, ur * (TU + 4):(ur + 1) * (TU + 4)] = blk[::-1].T
    return xT, xTr


_NC_CACHE = {}


def run_cores(assign, params, pv, TU, n_cores):
    if TU not in _NC_CACHE:
        import os
        _NC_CACHE[TU] = build(TU, stage=int(os.environ.get("MK_STAGE", "3")))
    nc = _NC_CACHE[TU]
    in_maps = []
    for units, link in assign:
        xT, xTr = layout_core(units, link, TU)
        pvc = pv.copy()
        pvc[:, PV["link"][0]] = float(link)
        m = dict(params)
        m.update(xT=xT, xTr=xTr, pvec=pvc)
        in_maps.append(m)
    res = run_bass_kernel_spmd(nc, in_maps, core_ids=list(range(n_cores)))
    return [r["yT"] for r in res.results]


def kernel(**inputs):
    xp = np.asarray(inputs["x_prompt"], np.float32)
    xs = np.asarray(inputs["x_sample"], np.float32)
    params, pv = prep_params(inputs)
    TU = xs.shape[1]
    assert xp.shape[1] == 2 * TU
    assign = []
    for b in range(xp.shape[0]):
        assign.append(([(xp[b], 0), (xp[b], TU)], 1))
    nb = xs.shape[0]
    slots = [[(xs[i], 0)] for i in range(nb)]
    n_rest = 8 - len(assign)
    per = [[] for _ in range(n_rest)]
    for i in range(nb):
        per[i % n_rest].append(i)
    for lst in per:
        u = [(xs[i], 0) for i in lst]
        while len(u) < 2:
            u.append((None, 0))
        assign.append((u, 0))
    outs = run_cores(assign, params, pv, TU, 8)
    y_p = np.zeros_like(xp)
    y_s = np.zeros_like(xs)
    for b in range(xp.shape[0]):
        y_p[b] = outs[b].T
    for ci, lst in enumerate(per):
        o = outs[xp.shape[0] + ci]
        for u, i in enumerate(lst):
            y_s[i] = o[:, u * TU:(u + 1) * TU].T
    return (y_p, y_s)
```

```python
import numpy as np
from contextlib import ExitStack
import concourse.bass as bass
import concourse.mybir as mybir
from concourse.bass_utils import run_bass_kernel_spmd

F32 = mybir.dt.float32
BF16 = mybir.dt.bfloat16
AF = mybir.ActivationFunctionType
ALU = mybir.AluOpType

D = 1024
C = 128
NCH_IN = 28
DFF = 2816
NFC = DFF // 128
ALPHA = 2.0 ** 0.25
LN_EPS = 1e-5
RMS_EPS = 1e-5
GN_EPS = 64e-5
CDEC = float(np.exp(-0.5))

CZ, CXS, CB, CC_, CR, CK, CV, CWL, CAL, CGL, CDT = 0, 4, 8, 10, 12, 16, 20, 24, 25, 26, 27

PV = {}
_o = 0
for _n, _w in [("cw", 40), ("cb", 8), ("dch", 4), ("nw", 4), ("mp", 15), ("mn", 15), ("w0f", 4), ("w0b", 4),
               ("a0", 4), ("kk", 4), ("ka", 4), ("rk", 4), ("lw", 4), ("lb", 4), ("l1w", 8), ("l1b", 8),
               ("l2w", 8), ("l2b", 8), ("link", 1)]:
    PV[_n] = (_o, _w)
    _o += _w
NPV = _o
CST = {n: i for i, n in enumerate(["ident", "J", "msl", "msu", "mui", "bmask", "bo64", "bo64m", "o1024", "o256", "ones"])}
NCST = len(CST)


class Buf:
    __slots__ = ("name", "w", "r", "psum")

    def __init__(self, name="", psum=False):
        self.name = name
        self.w = None
        self.r = []
        self.psum = psum


class Ctx:
    ENGS = ("pe", "act", "dve", "pool", "sp")

    def __init__(self, nc, es, n_dma_slots=6):
        self.nc = nc
        self.ops = {e: [] for e in self.ENGS}
        self.sems = {}
        self.count = {}
        for e in self.ENGS:
            self.sems[e] = es.enter_context(nc.semaphore("s_" + e))
            self.count[e] = 0
        self.waited = {e: {} for e in self.ENGS}
        self.slots = {}
        for q in ("sp", "pool"):
            self.slots[q] = []
            for i in range(n_dma_slots):
                key = "d_%s_%d" % (q, i)
                self.sems[key] = es.enter_context(nc.semaphore(key))
                self.count[key] = 0
                self.slots[q].append(key)
        self.slot_rr = {"sp": 0, "pool": 0}
        self.rr = 0
        import os
        self.PEX = Buf("PEX")
        self.pex_on = os.environ.get("MK_PEX", "0") == "1"
        self.cut = int(os.environ.get("MK_CUT", "0"))
        self.nops = 0
        self.last_lines = None

    def _cut(self):
        self.nops += 1
        if self.cut and self.nops >= self.cut:
            if self.nops == self.cut:
                import sys
                f = sys._getframe(2)
                lines = []
                while f is not None and len(lines) < 5:
                    lines.append(f.f_lineno)
                    f = f.f_back
                print("MK_CUT: first dropped op #%d at lines %s" % (self.nops, lines), flush=True)
            return True
        return False

    def _deps(self, eng, reads, writes):
        need = {}
        for b in reads:
            ev = b.w
            if ev is not None and need.get(ev[0], 0) < ev[1]:
                need[ev[0]] = ev[1]
        for b in writes:
            ev = b.w
            if ev is not None and need.get(ev[0], 0) < ev[1]:
                need[ev[0]] = ev[1]
            for ev in b.r:
                if need.get(ev[0], 0) < ev[1]:
                    need[ev[0]] = ev[1]
        waits = []
        wd = self.waited[eng]
        for k, v in need.items():
            if k == eng and eng == "pe":
                continue
            if wd.get(k, 0) >= v:
                continue
            wd[k] = v
            waits.append((k, v))
        return waits

    def _mark(self, ev, reads, writes):
        for b in reads:
            b.r.append(ev)
            if len(b.r) > 24:
                m = {}
                for k, v in b.r:
                    if m.get(k, 0) < v:
                        m[k] = v
                b.r = list(m.items())
        for b in writes:
            b.w = ev
            b.r = []

    @staticmethod
    def _flat(bs):
        out = []
        for b in bs:
            if isinstance(b, (list, tuple)):
                out.extend(Ctx._flat(b))
            else:
                out.append(b)
        return out

    def op(self, eng, fn, reads=(), writes=()):
        if self._cut():
            return
        reads, writes = self._flat(reads), self._flat(writes)
        pr = [b for b in reads if b.psum]
        if pr:
            reads = [b for b in reads if not b.psum]
            writes = list(writes) + pr
        if self.pex_on:
            if eng == "pe":
                reads = list(reads) + [self.PEX]
            elif eng == "dve" and any(b.psum for b in writes):
                writes = list(writes) + [self.PEX]
        waits = self._deps(eng, reads, writes)
        self.count[eng] += 1
        ev = (eng, self.count[eng])
        self.ops[eng].append((waits, fn, eng, 1))
        self._mark(ev, reads, writes)

    def dma(self, q, fn, reads=(), writes=()):
        if self._cut():
            return
        slot = self.slots[q][self.slot_rr[q] % len(self.slots[q])]
        self.slot_rr[q] += 1
        reads, writes = self._flat(reads), self._flat(writes)
        waits = self._deps(q, reads, writes)
        prev = self.count[slot]
        if prev > 0 and self.waited[q].get(slot, 0) < prev:
            self.waited[q][slot] = prev
            waits.append((slot, prev))
        self.count[slot] += 16
        ev = (slot, self.count[slot])
        self.ops[q].append((waits, fn, slot, 16))
        self._mark(ev, reads, writes)

    def wait_all(self, eng, bufs):
        if self.cut and self.nops >= self.cut:
            return
        waits = self._deps(eng, self._flat(bufs), ())
        self.ops[eng].append((waits, None, None, 0))

    def ew(self, fn, reads=(), writes=(), engs=("dve", "pool")):
        e = engs[self.rr % len(engs)]
        self.rr += 1
        self.op(e, fn, reads, writes)

    def full_barrier(self):
        for eng in self.ENGS:
            waits = []
            for k, v in self.count.items():
                if v > 0 and self.waited[eng].get(k, 0) < v and not (k == eng):
                    self.waited[eng][k] = v
                    waits.append((k, v))
            self.ops[eng].append((waits, None, None, 0))

    def replay(self, engname, e):
        sems = self.sems
        for waits, fn, semkey, inc in self.ops[engname]:
            for k, v in waits:
                e.wait_ge(sems[k], v)
            if fn is not None:
                fn(e).then_inc(sems[semkey], inc)

    def run_block(self):
        nc = self.nc
        with nc.Block() as block:
            @block.tensor
            def _(e):
                self.replay("pe", e)

            @block.scalar
            def _(e):
                self.replay("act", e)

            @block.vector
            def _(e):
                self.replay("dve", e)

            @block.gpsimd
            def _(e):
                self.replay("pool", e)

            @block.sync
            def _(e):
                self.replay("sp", e)
        self.ops = {e: [] for e in self.ENGS}


class Rot:
    def __init__(self, nc, es, name, shape, dt, n=2):
        self.t = [es.enter_context(nc.sbuf_tensor("%s_%d" % (name, i), shape, dt)) for i in range(n)]
        self.b = [Buf(name) for _ in range(n)]
        self.i = 0

    def next(self):
        k = self.i % len(self.t)
        self.i += 1
        return self.t[k], self.b[k]


class PsumPool:
    def __init__(self, nc, es, names):
        self.t = [es.enter_context(nc.psum_tensor(n, [128, 512], F32)) for n in names]
        self.b = [[Buf(n, psum=True)] * 4 for n in names]
        self.i = 0

    def next(self):
        k = self.i % len(self.t)
        self.i += 1
        return self.t[k], self.b[k]


def build(TU, stage=3):
    import os
    SUB = float(os.environ.get('MK_SUB', '9'))
    NTL = int(os.environ.get('MK_NT', '999'))
    NTU = TU // C
    NT = 2 * NTU
    TT = 2 * TU
    XW = 2 * (TU + 4)
    nc = bass.Bass("TRN2", target_bir_lowering=False)
    dram = lambda n, s, dt=F32, kind="ExternalInput": nc.dram_tensor(n, s, dt, kind=kind).ap()
    xT = dram("xT", [D, XW])
    xTr = dram("xTr", [D, XW])
    w_in = dram("w_in", [D, NCH_IN * 128])
    w_out = dram("w_out", [D, D])
    w_fi = dram("w_fi", [D, 2 * DFF])
    w_fo = dram("w_fo", [DFF, D])
    pvec = dram("pvec", [128, NPV])
    p8 = dram("p8", [8, 8])
    lowr = dram("lowr", [128, 4 * 512])
    cst = dram("cst", [128, NCST * 128])
    m8 = dram("m8", [8, 1024])
    yT = dram("yT", [D, TT], kind="ExternalOutput")
    ysc = {k: dram(k, [NT * 128, 512], BF16, kind="Internal") for k in ("ybr_h", "ybr_l", "ybs_h", "ybs_l")}
    hT = dram("hT", [D, TT], kind="Internal")

    with ExitStack() as es0:
        c = Ctx(nc, es0)
        B_ysc = {k: [Buf() for _ in range(NT)] for k in ysc}
        B_hT = [Buf() for _ in range(NT)]
        B_out = []

        def mk_mm(c):
            def mm(out, lhsT, rhs, start, stop, reads, writes):
                c.op("pe", lambda e: e.matmul(out, lhsT=lhsT, rhs=rhs, start=start, stop=stop), reads=reads, writes=writes)
            return mm
        mm = mk_mm(c)

        def split2(src, hi, lo, reads, Bhi, Blo, psum=False):
            c.op("act", lambda e: e.activation(out=hi, in_=src, func=AF.Copy), reads=reads, writes=[Bhi])
            if psum:
                c.op("dve", lambda e: e.tensor_tensor(out=lo, in0=src, in1=hi, op=ALU.subtract), reads=list(reads) + [Bhi], writes=[Blo])
            else:
                c.ew(lambda e: e.tensor_tensor(out=lo, in0=src, in1=hi, op=ALU.subtract), reads=list(reads) + [Bhi], writes=[Blo])

        def rsqrt_eps(out, in_, eps, reads, writes):
            c.op("dve", lambda e: e.tensor_scalar(out=out, in0=in_, scalar1=float(eps), scalar2=None, op0=ALU.add), reads=reads, writes=writes)
            c.op("act", lambda e: e.activation(out=out, in_=out, func=AF.Sqrt), reads=writes, writes=writes)
            c.op("dve", lambda e: e.reciprocal(out=out, in_=out), reads=writes, writes=writes)

        def layer_norm(pp, Kb, pre, Bpre, out, Bout, sq, Bsq, r_st, r_sp, P, wn, bn, Bcsb, Bpv, N):
            def stat(src, Bsrc):
                red, Bred = r_st.next()
                c.op("dve", lambda e: e.tensor_reduce(out=red[:, 0:N], in_=src.rearrange("p a n -> p n a"), axis=mybir.AxisListType.X, op=ALU.add),
                     reads=[Bsrc], writes=[Bred])
                hi, Bhi = r_sp.next()
                lo, Blo = r_sp.next()
                split2(red[:, 0:N], hi[:, 0:N], lo[:, 0:N], [Bred], Bhi, Blo)
                ps, pb = pp.next()
                mm(ps[:, 0:N], Kb("o1024"), hi[:, 0:N], True, False, [Bcsb, Bhi], pb)
                mm(ps[:, 0:N], Kb("o1024"), lo[:, 0:N], False, True, [Bcsb, Blo], pb)
                return ps, pb
            ps, pb = stat(pre, Bpre)
            mu, Bmu = r_st.next()
            c.op("act", lambda e: e.activation(out=mu[:, 0:N], in_=ps[:, 0:N], func=AF.Copy), reads=pb, writes=[Bmu])
            bcN = lambda t: t[:, 0:N].unsqueeze(1).to_broadcast([128, 8, N])
            bcP = lambda n: P(n, 0, 8).unsqueeze(2).to_broadcast([128, 8, N])
            c.ew(lambda e: e.tensor_tensor(out=pre, in0=pre, in1=bcN(mu), op=ALU.subtract), reads=[Bpre, Bmu], writes=[Bpre])
            c.ew(lambda e: e.tensor_tensor(out=sq, in0=pre, in1=pre, op=ALU.mult), reads=[Bpre], writes=[Bsq])
            ps2, pb2 = stat(sq, Bsq)
            rs, Brs = r_st.next()
            rsqrt_eps(rs[:, 0:N], ps2[:, 0:N], LN_EPS, pb2, [Brs])
            c.ew(lambda e: e.tensor_tensor(out=pre, in0=pre, in1=bcN(rs), op=ALU.mult), reads=[Bpre, Brs], writes=[Bpre])
            c.ew(lambda e: e.tensor_tensor(out=pre, in0=pre, in1=bcP(wn), op=ALU.mult), reads=[Bpre, Bpv], writes=[Bpre])
            c.ew(lambda e: e.tensor_tensor(out=out, in0=pre, in1=bcP(bn), op=ALU.add), reads=[Bpre, Bpv], writes=[Bout])

        def mixer_phase(fwd):
            with ExitStack() as es:
                sfx = "F" if fwd else "B"
                sb = lambda n, s, dt=F32: es.enter_context(nc.sbuf_tensor(n + sfx, s, dt))
                rot = lambda n, s, dt=F32, k=2: Rot(nc, es, n + sfx, s, dt, k)
                pp = PsumPool(nc, es, ["pp%d%s" % (i, sfx) for i in range(6)])
                YR_t = es.enter_context(nc.psum_tensor("YR" + sfx, [128, 512], F32)); YR_b = Buf("YR", psum=True)
                YS_t = es.enter_context(nc.psum_tensor("YS" + sfx, [128, 512], F32)); YS_b = Buf("YS", psum=True)
                xsrc = xT if fwd else xTr

                class Fix:
                    def __init__(self, t, b):
                        self.t, self.b = t, b

                    def next(self):
                        return self.t, self.b
                one = lambda n, s, dt=F32: Fix(sb(n, s, dt), Buf(n))

                win = sb("win", [128, 8, NCH_IN * 128], BF16); Bwin = Buf()
                for k in range(8):
                    c.dma("pool", lambda e, k=k: e.dma_start(out=win[:, k, :], in_=w_in[k * 128:(k + 1) * 128, :]), writes=[Bwin])
                if fwd:
                    wo = sb("wo", [128, 8, D], BF16); Bwo = Buf()
                    for k in range(8):
                        c.dma("pool", lambda e, k=k: e.dma_start(out=wo[:, k, :], in_=w_out[k * 128:(k + 1) * 128, :]), writes=[Bwo])
                pv = sb("pv", [128, NPV]); Bpv = Buf()
                c.dma("sp", lambda e: e.dma_start(out=pv[:], in_=pvec), writes=[Bpv])
                p8t = sb("p8t", [8, 8]); Bp8 = Buf()
                c.dma("sp", lambda e: e.dma_start(out=p8t[:], in_=p8), writes=[Bp8])
                lrb = sb("lrb", [128, 3 * 512], BF16); Blr = Buf()
                w2o = 0 if fwd else 512
                c.dma("pool", lambda e: e.dma_start(out=lrb[:, 0:512], in_=lowr[:, w2o:w2o + 512]), writes=[Blr])
                c.dma("pool", lambda e: e.dma_start(out=lrb[:, 512:1536], in_=lowr[:, 1024:2048]), writes=[Blr])
                csb = sb("csb", [128, NCST * 128], BF16); Bcsb = Buf()
                c.dma("pool", lambda e: e.dma_start(out=csb[:], in_=cst), writes=[Bcsb])
                m8b = sb("m8b", [8, 1024], BF16); Bm8 = Buf()
                c.dma("pool", lambda e: e.dma_start(out=m8b[:], in_=m8), writes=[Bm8])
                Kb = lambda n: csb[:, CST[n] * 128:(CST[n] + 1) * 128]
                mask4 = sb("mask4", [128, 512], BF16); Bm4 = Buf()
                for q, n in enumerate(["msu", "mui", "msu", "mui"]):
                    c.op("dve", lambda e, q=q, n=n: e.tensor_copy(out=mask4[:, q * 128:(q + 1) * 128], in_=Kb(n)),
                         reads=[Bcsb], writes=[Bm4])
                P = lambda n, j=0, w=1: pv[:, PV[n][0] + j:PV[n][0] + j + w]
                der = sb("der", [128, 15 + 4]); Bder = Buf()
                c.op("dve", lambda e: e.tensor_tensor(out=der[:, 0:15], in0=P("mp", 0, 15), in1=P("mn", 0, 15), op=ALU.add),
                     reads=[Bpv], writes=[Bder])
                c.op("dve", lambda e: e.tensor_scalar(out=der[:, 0:15], in0=der[:, 0:15], scalar1=-1.0, scalar2=1.0,
                                                      op0=ALU.mult, op1=ALU.add), reads=[Bder], writes=[Bder])
                c.op("dve", lambda e: e.tensor_scalar(out=der[:, 15:19], in0=P("ka", 0, 4), scalar1=-1.0, scalar2=1.0,
                                                      op0=ALU.mult, op1=ALU.add), reads=[Bpv], writes=[Bder])
                a8 = sb("a8", [8, 1]); Ba8 = Buf()
                acol = 2 if fwd else 3
                c.op("act", lambda e: e.activation(out=a8[:], in_=p8t[:, acol:acol + 1], func=AF.Exp), reads=[Bp8], writes=[Ba8])
                c.op("dve", lambda e: e.tensor_scalar(out=a8[:], in0=a8[:], scalar1=-1.0, scalar2=None, op0=ALU.mult),
                     reads=[Ba8], writes=[Ba8])
                dtb = p8t[:, (0 if fwd else 1):(1 if fwd else 2)]
                w2 = lrb[:, 0:512]
                a2 = lrb[:, 512:1024]
                g2 = lrb[:, 1024:1536]
                w0n = "w0f" if fwd else "w0b"
                mpn, mnn = ("mp", "mn") if fwd else ("mn", "mp")

                Hb = sb("Hb", [128, 4, 128], BF16); BH = [Buf() for _ in range(4)]
                STf = sb("STf", [128, 512]); STb = sb("STb", [128, 512], BF16); BST = Buf()
                c.op("pool", lambda e: e.memset(Hb[:], 0.0), writes=BH)
                c.op("pool", lambda e: e.memset(STf[:], 0.0), writes=[BST])
                c.op("pool", lambda e: e.memset(STb[:], 0.0), writes=[BST])
                ARe_t = sb("ARe", [128, 4, 256], BF16)
                ARo_t = sb("ARo", [128, 4, 256], BF16)
                BAR_t = Buf()
                c.op("pool", lambda e: e.memset(ARe_t[:], 0.0), writes=[BAR_t])
                c.op("pool", lambda e: e.memset(ARo_t[:], 0.0), writes=[BAR_t])
                CSt = sb("CS", [128, 4, 129]); BCSt = Buf()
                c.op("pool", lambda e: e.memset(CSt[:], 0.0), writes=[BCSt])
                ptmp = sb("ptmp", [128, 128]); Bptmp = Buf()
                Bfence = Buf()
                ones_t = sb("ones_t", [128, 128]); Bones = Buf()
                c.op("pool", lambda e: e.memset(ones_t[:], 1.0), writes=[Bones])
                DT3 = sb("DT3", [128, 3, 2, 128], BF16); BDT3 = Buf()
                RB3 = sb("RB3", [128, 2, 8, 128], BF16); BRB3 = [Buf(), Buf()]
                c.op("pool", lambda e: e.memset(DT3[:], 0.0), writes=[BDT3])
                c.op("pool", lambda e: e.memset(RB3[:], 0.0), writes=BRB3)

                arena = sb("arena", [128, 11 * 512]); Bar = [Buf() for _ in range(11)]

                def AS(i, n=1, parts=128, nch=None):
                    nch = 4 * n if nch is None else nch
                    return (arena[0:parts, 512 * i:512 * i + 128 * nch].rearrange("p (a b) -> p a b", b=128), Bar[i:i + n])

                def AU(i, nchunks):
                    return (arena[:, 512 * i:512 * i + 132 * nchunks].rearrange("p (a b) -> p a b", b=132), Bar[i:i + 4])

                r_xb = rot("xb", [128, 8, 132], BF16, 1 if fwd else 2)
                r_zs = one("zs", [128, 4, 128])
                r_xsf = one("xsf", [128, 4, 128])
                r_xbc = one("xbcb", [128, 8, 128], BF16)
                r_dtok = one("dtok", [128, 16])
                r_gm = one("gm", [128, 2, 128])
                r_MT = one("MT", [128, 8, 128], BF16)
                r_Cd = one("Cd", [128, 8, 128], BF16)
                r_xdt = one("xdt", [128, 2, 512], BF16)
                r_btok = one("btok", [128, 2, 128], BF16)
                r_lrin = one("lrin", [128, 3, 128], BF16)
                r_sp = rot("sp", [128, 4, 128], BF16, 2)
                r_op = rot("opb", [128, 4, 128], BF16, 4)
                LS = int(os.environ.get("MK_LS_F" if fwd else "MK_LS_B", "4" if fwd else "8"))
                NPR = max(1, LS // 2)
                r_tok = rot("tok", [128, 4, 128], BF16, max(2, NPR))
                r_p0 = rot("p0", [128, 128], BF16, LS)
                r_e4 = rot("e4", [128, 512], BF16, LS)
                r_pp = rot("ppq", [128, 256], BF16, 2 * LS)
                r_zf = rot("zf", [128, 128], F32, LS)
                r_zb = rot("zb", [128, 128], BF16, LS)
                r_wu = rot("wu", [128, 2, 128], BF16, max(2, NPR))
                r_qt = rot("qt", [128, 128], BF16, 1)
                r_mtb = rot("mtb", [128, 128], BF16, 1)
                r_yhl = rot("yhl", [128, 2, 512], BF16, 2)
                if fwd:
                    r_ybl = rot("ybl", [128, 2, 512], BF16, 2)
                    r_bg = rot("bg", [128, 4, 128], F32, 2)
                    r_mix = one("mix", [128, 8, 128], BF16)
                    r_st = rot("stat", [128, 128], F32, 4)
                    r_sps = rot("sps", [128, 128], BF16, 2)

                evac_rr = [0]

                def evac(out, in_, reads, writes, engs=("act", "dve")):
                    e_ = engs[evac_rr[0] % len(engs)]
                    evac_rr[0] += 1
                    if e_ == "act":
                        c.op("act", lambda e: e.activation(out=out, in_=in_, func=AF.Copy), reads=reads, writes=writes)
                    else:
                        c.op(e_, lambda e: e.tensor_copy(out=out, in_=in_), reads=reads, writes=writes)

                def blocksum4(src, Bsrc, cname):
                    hi, Bhi = r_sp.next()
                    lo, Blo = r_sp.next()
                    split2(src, hi[:], lo[:], [Bsrc], Bhi, Blo)
                    ps, pb = pp.next()
                    for cp in range(4):
                        o_ = ps[:, cp * 128:(cp + 1) * 128]
                        mm(o_, Kb(cname), hi[:, cp, :], True, False, [Bcsb, Bhi], [pb[cp]])
                        mm(o_, Kb(cname), lo[:, cp, :], False, True, [Bcsb, Blo], [pb[cp]])
                    return ps[:].rearrange("p (a b) -> p a b", b=128), pb

                def link_state():
                    lk = P("link")
                    for cpr in range(4):
                        c.op("dve", lambda e, cpr=cpr: e.tensor_scalar(out=Hb[:, cpr, :], in0=Hb[:, cpr, :], scalar1=lk,
                                                                       scalar2=None, op0=ALU.mult),
                             reads=[BH[cpr], Bpv], writes=[BH[cpr]])
                    c.op("dve", lambda e: e.tensor_scalar(out=STf[:], in0=STf[:], scalar1=lk, scalar2=None, op0=ALU.mult),
                         reads=[BST, Bpv], writes=[BST])
                    c.op("dve", lambda e: e.tensor_scalar(out=STb[:], in0=STb[:], scalar1=lk, scalar2=None, op0=ALU.mult),
                         reads=[BST, Bpv], writes=[BST])

                for ti in range(min(NT, NTL)):
                    u_i, lt = divmod(ti, NTU)
                    if ti == NTU:
                        link_state()
                    col0 = u_i * (TU + 4) + lt * C
                    xb, Bxb = r_xb.next()
                    c.dma("pool", lambda e, xb=xb, col0=col0: e.dma_start(
                        out=xb[:], in_=xsrc[:, col0:col0 + 132].rearrange("(k p) t -> p k t", p=128)), writes=[Bxb])
                    if SUB < 1:
                        continue

                    def inproj(U, BU, chunks):
                        for g0 in range(0, len(chunks), 3):
                            grp = chunks[g0:g0 + 3]
                            n_g = len(grp)
                            ps, pb = pp.next()
                            for j, cc in enumerate(grp):
                                for k in range(8):
                                    mm(ps[:, j * 132:(j + 1) * 132], win[:, k, cc * 128:(cc + 1) * 128], xb[:, k, :],
                                       k == 0, k == 7, [Bwin, Bxb], pb)
                            evac(U[:, g0:g0 + n_g, :], ps[:, 0:n_g * 132].rearrange("p (a b) -> p a b", b=132), pb, [BU])

                    U, BU = AU(6, 13)
                    inproj(U, BU, list(range(12)) + [CDT])
                    UDT = 12
                    if SUB < 2:
                        continue
                    zs, Bzs = r_zs.next()
                    c.op("act", lambda e, zs=zs, U=U: e.activation(out=zs[:], in_=U[:, CZ:CZ + 4, 2:130], func=AF.Silu),
                         reads=[BU], writes=[Bzs])
                    cv, Bcv = AS(0, 2)
                    for j in range(8):
                        eng = ("dve", "pool")[j % 2]
                        for k5 in range(5):
                            kk_ = k5 if fwd else 4 - k5
                            wk = pv[:, PV["cw"][0] + j * 5 + kk_:PV["cw"][0] + j * 5 + kk_ + 1]
                            src = U[:, CXS + j, k5:k5 + 128]
                            if k5 == 0:
                                c.op(eng, lambda e, j=j, wk=wk, src=src, cv=cv: e.tensor_scalar(
                                    out=cv[:, j, :], in0=src, scalar1=wk, scalar2=P("cb", j), op0=ALU.mult, op1=ALU.add),
                                     reads=[BU, Bpv], writes=[Bcv])
                            elif eng == "dve":
                                c.op(eng, lambda e, j=j, wk=wk, src=src, cv=cv: e.scalar_tensor_tensor(
                                    out=cv[:, j, :], in0=src, scalar=wk, in1=cv[:, j, :], op0=ALU.mult, op1=ALU.add),
                                     reads=[BU, Bpv, Bcv], writes=[Bcv])
                            else:
                                c.op(eng, lambda e, wk=wk, src=src: e.tensor_scalar(
                                    out=ptmp[:], in0=src, scalar1=wk, scalar2=None, op0=ALU.mult), reads=[BU, Bpv], writes=[Bptmp])
                                c.op(eng, lambda e, j=j, cv=cv: e.tensor_tensor(out=cv[:, j, :], in0=cv[:, j, :], in1=ptmp[:], op=ALU.add),
                                     reads=[Bptmp, Bcv], writes=[Bcv])
                    xsf, Bxsf = r_xsf.next()
                    xbc, Bxbc = r_xbc.next()
                    c.op("act", lambda e, xsf=xsf, cv=cv: e.activation(out=xsf[:], in_=cv[:, 0:4, :], func=AF.Silu),
                         reads=[Bcv], writes=[Bxsf])
                    c.op("act", lambda e, xbc=xbc, cv=cv: e.activation(out=xbc[:, 4:8, :], in_=cv[:, 4:8, :], func=AF.Silu),
                         reads=[Bcv], writes=[Bxbc])
                    c.op("pool", lambda e, xbc=xbc, xsf=xsf: e.tensor_copy(out=xbc[:, 0:4, :], in_=xsf[:]),
                         reads=[Bxsf], writes=[Bxbc])
                    d8, Bd8 = AS(10, 1, parts=8)
                    c.op("act", lambda e, d8=d8, U=U: e.activation(out=d8[:, 0, :], in_=U[0:8, UDT, 2:130], func=AF.Exp,
                                                                   bias=dtb), reads=[BU, Bp8], writes=[Bd8])
                    c.op("act", lambda e, d8=d8: e.activation(out=d8[:, 1, :], in_=d8[:, 0, :], func=AF.Ln, bias=1.0),
                         reads=[Bd8], writes=[Bd8])
                    c.op("dve", lambda e, d8=d8: e.tensor_scalar(out=d8[:, 2, :], in0=d8[:, 1, :], scalar1=a8[:, 0:1],
                                                                 scalar2=None, op0=ALU.mult), reads=[Bd8, Ba8], writes=[Bd8])
                    c.op("dve", lambda e, d8=d8: e.tensor_tensor_scan(out=d8[:, 3, :], data0=ones_t[0:8, :], data1=d8[:, 2, :],
                                                                      initial=0.0, op0=ALU.mult, op1=ALU.add),
                         reads=[Bd8, Bones], writes=[Bd8])
                    for q, (src_i, r_i) in enumerate(((1, 0), (3, 2))):
                        x_ = d8[:, src_i, :]
                        r_ = d8[:, r_i, :]
                        c.op("dve", lambda e, q=q, x_=x_: e.tensor_copy(out=DT3[0:8, 0, q, :], in_=x_), reads=[Bd8], writes=[BDT3])
                        c.op("dve", lambda e, q=q, x_=x_, r_=r_: e.tensor_tensor(out=r_, in0=x_, in1=DT3[0:8, 0, q, :], op=ALU.subtract),
                             reads=[Bd8, BDT3], writes=[Bd8])
                        c.op("dve", lambda e, q=q, r_=r_: e.tensor_copy(out=DT3[0:8, 1, q, :], in_=r_), reads=[Bd8], writes=[BDT3])
                        c.op("dve", lambda e, q=q, r_=r_: e.tensor_tensor(out=r_, in0=r_, in1=DT3[0:8, 1, q, :], op=ALU.subtract),
                             reads=[Bd8, BDT3], writes=[Bd8])
                        c.op("dve", lambda e, q=q, r_=r_: e.tensor_copy(out=DT3[0:8, 2, q, :], in_=r_), reads=[Bd8], writes=[BDT3])
                    ps, pb = pp.next()
                    for q in range(2):
                        for i3 in range(3):
                            mm(ps[:, q * 128:(q + 1) * 128], DT3[:, i3, q, :], Kb("ident"), i3 == 0, i3 == 2, [BDT3, Bcsb], [pb[q]])
                    dtok, Bdtok = r_dtok.next()
                    c.op("dve", lambda e, dtok=dtok, ps=ps: e.tensor_copy(out=dtok[:, 0:8], in_=ps[:, 0:8]), reads=[pb[0], pb[1]], writes=[Bdtok])
                    c.op("dve", lambda e, dtok=dtok, ps=ps: e.tensor_copy(out=dtok[:, 8:16], in_=ps[:, 128:136]), reads=[pb[1]], writes=[Bdtok])
                    psR0, pbR0 = pp.next()
                    if os.environ.get("MK_SKIPB", "") == "1":
                        pp.next()
                    psR1, pbR1 = pp.next()
                    for i3 in range(3):
                        rb = i3 % 2
                        c.ew(lambda e, i3=i3, rb=rb: e.tensor_tensor(
                            out=RB3[0:8, rb, :, :], in0=m8b[:].rearrange("p (h l) -> p h l", l=128),
                            in1=DT3[0:8, i3, 1:2, :].to_broadcast([8, 8, 128]), op=ALU.mult), reads=[BDT3, Bm8], writes=[BRB3[rb]])
                        for (psR, pbR, h0) in ((psR0, pbR0, 0), (psR1, pbR1, 4)):
                            mm(psR[:], Kb("ones"), RB3[:, rb, h0:h0 + 4, :].rearrange("p a b -> p (a b)"), i3 == 0, i3 == 2, [Bcsb, BRB3[rb]], pbR)
                    seg, Bseg = AS(2, 2)
                    E, BE = AS(4, 2)
                    if os.environ.get("MK_FENCE", "-1") != "-1":
                        pbR0 = pbR0 + [Bfence]
                    for _d in range(int(os.environ.get("MK_DELAY", "0"))):
                        c.op("dve", lambda e: e.tensor_copy(out=ptmp[:], in_=ones_t[:]), reads=[pbR0, Bones], writes=[Bptmp])
                    for (psR, pbR, h0) in ((psR0, pbR0, 0), (psR1, pbR1, 4)):
                        c.op("dve", lambda e, psR=psR, h0=h0, seg=seg, dtok=dtok: e.tensor_tensor(
                            out=seg[:, h0:h0 + 4, :], in0=psR[:].rearrange("p (a b) -> p a b", b=128),
                            in1=dtok[:, 8 + h0:12 + h0].unsqueeze(2).to_broadcast([128, 4, 128]), op=ALU.subtract),
                             reads=[pbR, Bdtok], writes=[Bseg])
                    c.ew(lambda e, seg=seg: e.tensor_scalar(out=seg[:], in0=seg[:], scalar1=0.0, scalar2=None, op0=ALU.min), reads=[Bseg], writes=[Bseg])
                    c.op("act", lambda e, E=E, psR0=psR0: e.activation(out=E[:, 0:4, :], in_=psR0[:].rearrange("p (a b) -> p a b", b=128),
                                                                     func=AF.Exp), reads=pbR0, writes=[BE])
                    c.op("act", lambda e, E=E, psR1=psR1: e.activation(out=E[:, 4:8, :], in_=psR1[:].rearrange("p (a b) -> p a b", b=128),
                                                                     func=AF.Exp), reads=pbR1, writes=[BE])
                    c.op("act", lambda e, seg=seg: e.activation(out=seg[:], in_=seg[:], func=AF.Exp), reads=[Bseg], writes=[Bseg])
                    ps, pb = pp.next()
                    for g in range(2):
                        mm(ps[:, g * 128:(g + 1) * 128], xbc[:, 4 + g, :], xbc[:, 6 + g, :], True, True, [Bxbc], [pb[g]])
                    gm, Bgm = r_gm.next()
                    for g in range(2):
                        c.op("dve", lambda e, g=g, gm=gm, ps=ps: e.tensor_tensor(out=gm[:, g, :], in0=ps[:, g * 128:(g + 1) * 128],
                                                                                 in1=Kb("mui"), op=ALU.mult),
                             reads=[pb[g], Bcsb], writes=[Bgm])
                    MT, BMT = r_MT.next()
                    Cd, BCd = r_Cd.next()
                    for g in range(2):
                        c.ew(lambda e, g=g, MT=MT, seg=seg, gm=gm: e.tensor_tensor(
                            out=MT[:, 4 * g:4 * g + 4, :], in0=seg[:, 4 * g:4 * g + 4, :], in1=gm[:, g:g + 1, :].to_broadcast([128, 4, 128]), op=ALU.mult),
                             reads=[Bseg, Bgm], writes=[BMT])
                        c.ew(lambda e, g=g, Cd=Cd, E=E, xbc=xbc: e.tensor_tensor(
                            out=Cd[:, 4 * g:4 * g + 4, :], in0=E[:, 4 * g:4 * g + 4, :], in1=xbc[:, 6 + g:7 + g, :].to_broadcast([128, 4, 128]), op=ALU.mult),
                             reads=[BE, Bxbc], writes=[BCd])
                    ps, pb = pp.next()
                    for j in range(4):
                        mm(ps[:, j * 128:(j + 1) * 128], xbc[:, j, :], Kb("ident"), True, True, [Bxbc, Bcsb], [pb[j]])
                    xdt, Bxdt = r_xdt.next()
                    v864 = lambda ap: ap.rearrange("p (h q) -> p h q", q=64)
                    c.op("dve", lambda e, xdt=xdt, ps=ps, dtok=dtok: e.tensor_tensor(
                        out=v864(xdt[:, 0, :]), in0=v864(ps[:]), in1=dtok[:, 0:8].unsqueeze(2).to_broadcast([128, 8, 64]), op=ALU.mult),
                         reads=[pb, Bdtok], writes=[Bxdt])
                    c.ew(lambda e, xdt=xdt, seg=seg: e.tensor_tensor(
                        out=v864(xdt[:, 1, :]), in0=v864(xdt[:, 0, :]), in1=seg[:, :, 127:128].to_broadcast([128, 8, 64]), op=ALU.mult),
                         reads=[Bxdt, Bseg], writes=[Bxdt])
                    ps, pb = pp.next()
                    for g in range(2):
                        mm(ps[:, g * 128:(g + 1) * 128], xbc[:, 4 + g, :], Kb("ident"), True, True, [Bxbc, Bcsb], [pb[g]])
                    btok, Bbtok = r_btok.next()
                    evac(btok[:], ps[:, 0:256].rearrange("p (a b) -> p a b", b=128), pb[0:2], [Bbtok])
                    for h in range(8):
                        o_ = YS_t[:, h * 64:(h + 1) * 64]
                        mm(o_, MT[:, h, :], xdt[:, 0, h * 64:(h + 1) * 64], True, False, [BMT, Bxdt], [YS_b])
                        mm(o_, Cd[:, h, :], STb[:, h * 64:(h + 1) * 64], False, True, [BCd, BST], [YS_b])
                    ysh, Bysh = r_yhl.next()
                    split2(YS_t[:], ysh[:, 0, :], ysh[:, 1, :], [YS_b], Bysh, Bysh, psum=True)
                    ps, pb = pp.next()
                    for g in range(2):
                        mm(ps[:, g * 256:(g + 1) * 256], btok[:, g, :], xdt[:, 1, g * 256:(g + 1) * 256], True, True,
                           [Bbtok, Bxdt], pb[2 * g:2 * g + 2])
                    c.ew(lambda e, E=E: e.tensor_tensor(out=v864(STf[:]), in0=v864(STf[:]), in1=E[:, :, 127:128].to_broadcast([128, 8, 64]), op=ALU.mult),
                         reads=[BST, BE], writes=[BST])
                    c.op("dve", lambda e, ps=ps: e.tensor_tensor(out=STf[:], in0=STf[:], in1=ps[:], op=ALU.add), reads=[BST, pb], writes=[BST])
                    c.op("act", lambda e: e.activation(out=STb[:], in_=STf[:], func=AF.Copy), reads=[BST], writes=[BST])
                    if not fwd:
                        for q, key in enumerate(("ybs_h", "ybs_l")):
                            c.dma("sp", lambda e, ysh=ysh, q=q, key=key, ti=ti: e.dma_start(out=ysc[key][ti * 128:(ti + 1) * 128, :], in_=ysh[:, q, :]),
                                  reads=[Bysh], writes=[B_ysc[key][ti]])
                    if SUB < 3:
                        continue

                    U, BU = AU(4, 15)
                    inproj(U, BU, list(range(CR, CR + 15)))
                    us, Bus = AS(0, 4, nch=15)
                    tsh, Btsh = AS(8, 2)
                    bc15 = lambda ap, j0, n: ap[:, j0:j0 + n].unsqueeze(2).to_broadcast([128, n, 128])
                    c.ew(lambda e, us=us, U=U: e.tensor_tensor(out=us, in0=U[:, 0:15, 2:130], in1=bc15(der, 0, 15), op=ALU.mult),
                         reads=[BU, Bder], writes=[Bus])
                    for (j0, n) in ((0, 8), (8, 7)):
                        for (sl_, mun) in (((1, 129), mpn), ((3, 131), mnn)):
                            mu_ap = pv[:, PV[mun][0]:PV[mun][0] + 15]
                            c.ew(lambda e, j0=j0, n=n, sl_=sl_, mu_ap=mu_ap, U=U, tsh=tsh: e.tensor_tensor(
                                out=tsh[:, 0:n, :], in0=U[:, j0:j0 + n, sl_[0]:sl_[1]], in1=bc15(mu_ap, j0, n), op=ALU.mult),
                                 reads=[BU, Bpv], writes=[Btsh])
                            c.ew(lambda e, j0=j0, n=n, us=us, tsh=tsh: e.tensor_tensor(
                                out=us[:, j0:j0 + n, :], in0=us[:, j0:j0 + n, :], in1=tsh[:, 0:n, :], op=ALU.add),
                                 reads=[Bus, Btsh], writes=[Bus])
                    lrin, Blrin = r_lrin.next()
                    c.op("act", lambda e, us=us, lrin=lrin: e.activation(out=lrin[:, 0, :], in_=us[:, 12, :], func=AF.Tanh), reads=[Bus], writes=[Blrin])
                    c.op("act", lambda e, us=us, lrin=lrin: e.activation(out=lrin[:, 1, :], in_=us[:, 13, :], func=AF.Copy), reads=[Bus], writes=[Blrin])
                    c.op("act", lambda e, us=us, lrin=lrin: e.activation(out=lrin[:, 2, :], in_=us[:, 14, :], func=AF.Sigmoid), reads=[Bus], writes=[Blrin])
                    psL, pbL = pp.next()
                    psI, pbI = pp.next()
                    for cp in range(4):
                        mm(psL[:, cp * 128:(cp + 1) * 128], w2[:, cp * 128:(cp + 1) * 128], lrin[:, 0, :], True, True, [Blr, Blrin], [pbL[cp]])
                        mm(psI[:, cp * 128:(cp + 1) * 128], a2[:, cp * 128:(cp + 1) * 128], lrin[:, 1, :], True, True, [Blr, Blrin], [pbI[cp]])
                    sg, Bsg = AS(4)
                    icl, Bicl = AS(5)
                    bc4 = lambda n: P(n, 0, 4).unsqueeze(2).to_broadcast([128, 4, 128])
                    v4 = lambda ps_: ps_[:].rearrange("p (a b) -> p a b", b=128)
                    c.op("dve", lambda e, sg=sg, psL=psL: e.tensor_tensor(out=sg[:], in0=v4(psL), in1=bc4(w0n), op=ALU.add), reads=[pbL, Bpv], writes=[Bsg])
                    c.op("act", lambda e, sg=sg: e.activation(out=sg[:], in_=sg[:], func=AF.Sigmoid), reads=[Bsg], writes=[Bsg])
                    c.op("dve", lambda e, icl=icl, psI=psI: e.tensor_tensor(out=icl[:], in0=v4(psI), in1=bc4("a0"), op=ALU.add), reads=[pbI, Bpv], writes=[Bicl])
                    c.op("act", lambda e, icl=icl: e.activation(out=icl[:], in_=icl[:], func=AF.Sigmoid), reads=[Bicl], writes=[Bicl])
                    for cp in range(4):
                        c.op("dve", lambda e, cp=cp, sg=sg: e.tensor_tensor_scan(
                            out=CSt[:, cp, 1:129], data0=ones_t[:], data1=sg[:, cp, :], initial=0.0, op0=ALU.mult, op1=ALU.add),
                             reads=[Bsg, Bones, BCSt], writes=[BCSt])
                    gam, Bgam = AS(6)
                    igam, Bigam = AS(7)
                    gprev, Bgprev = AS(8)
                    c.op("act", lambda e, gam=gam: e.activation(out=gam[:], in_=CSt[:, :, 1:129], func=AF.Exp, scale=-CDEC),
                         reads=[BCSt], writes=[Bgam])
                    c.op("act", lambda e, igam=igam: e.activation(out=igam[:], in_=CSt[:, :, 1:129], func=AF.Exp, scale=CDEC),
                         reads=[BCSt], writes=[Bigam])
                    c.op("act", lambda e, gprev=gprev: e.activation(out=gprev[:], in_=CSt[:, :, 0:128], func=AF.Exp, scale=-CDEC),
                         reads=[BCSt], writes=[Bgprev])
                    kkt, Bkk = AS(9)
                    c.ew(lambda e, kkt=kkt, us=us: e.tensor_tensor(out=kkt[:], in0=us[:, 4:8, :], in1=bc4("kk"), op=ALU.mult), reads=[Bus, Bpv], writes=[Bkk])
                    sq, Bsq = sg, Bsg
                    c.ew(lambda e, sq=sq, kkt=kkt: e.tensor_tensor(out=sq[:], in0=kkt[:], in1=kkt[:], op=ALU.mult), reads=[Bkk], writes=[Bsq])
                    psS, pbS = blocksum4(sq[:], Bsq, "bo64")
                    rn, Brn = AS(10)
                    c.op("dve", lambda e, rn=rn, psS=psS: e.tensor_scalar(out=rn[:], in0=psS, scalar1=1e-24, scalar2=None, op0=ALU.max),
                         reads=pbS, writes=[Brn])
                    c.op("act", lambda e, rn=rn: e.activation(out=rn[:], in_=rn[:], func=AF.Sqrt), reads=[Brn], writes=[Brn])
                    c.op("dve", lambda e, rn=rn: e.reciprocal(out=rn[:], in_=rn[:]), reads=[Brn], writes=[Brn])
                    c.ew(lambda e, kkt=kkt, rn=rn: e.tensor_tensor(out=kkt[:], in0=kkt[:], in1=rn[:], op=ALU.mult), reads=[Bkk, Brn], writes=[Bkk])
                    km, Bkm = rn, Brn
                    c.ew(lambda e, km=km, icl=icl: e.tensor_tensor(out=km[:], in0=icl[:], in1=bc4("ka"), op=ALU.mult), reads=[Bicl, Bpv], writes=[Bkm])
                    c.ew(lambda e, km=km: e.tensor_tensor(out=km[:], in0=km[:], in1=der[:, 15:19].unsqueeze(2).to_broadcast([128, 4, 128]), op=ALU.add),
                         reads=[Bkm, Bder], writes=[Bkm])
                    c.ew(lambda e, km=km, us=us: e.tensor_tensor(out=km[:], in0=km[:], in1=us[:, 4:8, :], op=ALU.mult), reads=[Bkm, Bus], writes=[Bkm])
                    if fwd:
                        rk_, Brk = sq, Bsq
                        c.ew(lambda e, rk_=rk_, us=us: e.tensor_tensor(out=rk_[:], in0=us[:, 0:4, :], in1=bc4("rk"), op=ALU.mult), reads=[Bus, Bpv], writes=[Brk])
                        c.ew(lambda e, rk_=rk_, km=km: e.tensor_tensor(out=rk_[:], in0=rk_[:], in1=km[:], op=ALU.mult), reads=[Brk, Bkm], writes=[Brk])
                        psB, pbB = blocksum4(rk_[:], Brk, "bo64")
                        bonv, Bbonv = r_bg.next()
                        c.op("dve", lambda e, bonv=bonv, psB=psB, us=us: e.tensor_tensor(out=bonv[:], in0=psB, in1=us[:, 8:12, :], op=ALU.mult),
                             reads=pbB + [Bus], writes=[Bbonv])
                        psG, pbG = pp.next()
                        for cp in range(4):
                            mm(psG[:, cp * 128:(cp + 1) * 128], g2[:, cp * 128:(cp + 1) * 128], lrin[:, 2, :], True, True, [Blr, Blrin], [pbG[cp]])
                        gT, BgT = r_bg.next()
                        c.op("act", lambda e, gT=gT, psG=psG: e.activation(out=gT[:], in_=psG[:].rearrange("p (a b) -> p a b", b=128), func=AF.Copy),
                             reads=pbG, writes=[BgT])
                    KT, BKT = r_op.next()
                    BT, BBT = r_op.next()
                    AT, BAT = r_op.next()
                    vb, Bvb = r_op.next()
                    c.ew(lambda e, KT=KT, km=km, igam=igam: e.tensor_tensor(out=KT[:], in0=km[:], in1=igam[:], op=ALU.mult),
                         reads=[Bkm, Bigam], writes=[BKT])
                    c.ew(lambda e, icl=icl, kkt=kkt: e.tensor_tensor(out=icl[:], in0=icl[:], in1=kkt[:], op=ALU.mult), reads=[Bicl, Bkk], writes=[Bicl])
                    c.ew(lambda e, BT=BT, icl=icl, igam=igam: e.tensor_tensor(out=BT[:], in0=icl[:], in1=igam[:], op=ALU.mult),
                         reads=[Bicl, Bigam], writes=[BBT])
                    c.op("dve", lambda e, AT=AT, kkt=kkt, gprev=gprev: e.scalar_tensor_tensor(out=AT[:], in0=kkt[:], scalar=-1.0, in1=gprev[:],
                                                                                             op0=ALU.mult, op1=ALU.mult),
                         reads=[Bkk, Bgprev], writes=[BAT])
                    c.ew(lambda e, vb=vb, us=us: e.tensor_copy(out=vb[:], in_=us[:, 8:12, :]), reads=[Bus], writes=[Bvb])
                    c.ew(lambda e, AT=AT: e.tensor_copy(out=ARe_t[0:64, :, 0:128], in_=AT[0:64, :, :]), reads=[BAT], writes=[BAR_t])
                    c.ew(lambda e, AT=AT: e.tensor_copy(out=ARo_t[64:128, :, 0:128], in_=AT[64:128, :, :]), reads=[BAT], writes=[BAR_t])
                    c.ew(lambda e, us=us, gam=gam: e.tensor_tensor(out=ARe_t[0:64, :, 128:256], in0=us[0:64, 0:4, :], in1=gam[0:64, :, :],
                                                                   op=ALU.mult), reads=[Bus, Bgam], writes=[BAR_t])
                    c.ew(lambda e, us=us, gam=gam: e.tensor_tensor(out=ARo_t[64:128, :, 128:256], in0=us[64:128, 0:4, :], in1=gam[64:128, :, :],
                                                                   op=ALU.mult), reads=[Bus, Bgam], writes=[BAR_t])
                    if SUB < 4:
                        continue
                    if fwd:
                        src_t = NT - 1 - ti
                        ybS, BybS = r_ybl.next()
                        ybR, BybR = r_ybl.next()
                        for (dst_, Bdst_, kh, kl) in ((ybS, BybS, "ybs_h", "ybs_l"), (ybR, BybR, "ybr_h", "ybr_l")):
                            for q, key in enumerate((kh, kl)):
                                c.dma("sp", lambda e, dst_=dst_, q=q, key=key, src_t=src_t: e.dma_start(
                                    out=dst_[:, q, :], in_=ysc[key][src_t * 128:(src_t + 1) * 128, :]),
                                      reads=[B_ysc[key][src_t]], writes=[Bdst_])
                    heads_all = [(cp, x) for cp in range(4) for x in range(2)]
                    for g0 in range(0, 8, LS):
                        grp = heads_all[g0:g0 + LS]
                        cps = sorted(set(cp for cp, _ in grp))
                        toks, wus, st = {}, {}, {}
                        for cp in cps:
                            ps, pb = pp.next()
                            for q, src in enumerate((AT, BT, KT, vb)):
                                mm(ps[:, q * 128:(q + 1) * 128], src[:, cp, :], Kb("ident"), True, True, [BAT, BBT, BKT, Bvb, Bcsb], [pb[q]])
                            tok, Btok = r_tok.next()
                            evac(tok[:], ps[:].rearrange("p (a b) -> p a b", b=128), pb, [Btok])
                            toks[cp] = (tok, Btok)
                            wus[cp] = r_wu.next()
                        for (cp, x) in grp:
                            tok, Btok = toks[cp]
                            AR_x = (ARe_t, ARo_t)[x]
                            hs = slice(x * 64, (x + 1) * 64)
                            psA, pbA = pp.next()
                            psB4, pbB4 = pp.next()
                            mm(psA[:, 0:128], AR_x[:, cp, 0:128], BT[:, cp, :], True, True, [BAR_t, BBT], [pbA[0]])
                            mm(psB4[:, 0:256], BT[:, cp, :], AR_x[:, cp, :], True, True, [BAR_t, BBT], pbB4[0:2])
                            mm(psB4[:, 256:512], KT[:, cp, :], AR_x[:, cp, :], True, True, [BAR_t, BKT], pbB4[2:4])
                            p0, Bp0 = r_p0.next()
                            c.op("dve", lambda e, p0=p0, psA=psA: e.tensor_tensor(out=p0[:], in0=psA[:, 0:128], in1=Kb("msl"), op=ALU.mult),
                                 reads=[pbA[0], Bcsb], writes=[Bp0])
                            e4, Be4 = r_e4.next()
                            c.op("dve", lambda e, e4=e4, psB4=psB4: e.tensor_tensor(out=e4[:], in0=psB4[:], in1=mask4[:], op=ALU.mult),
                                 reads=pbB4 + [Bm4], writes=[Be4])
                            mm(psA[:, 128:192], e4[:, 256:384], tok[:, 3, hs], True, True, [Be4, Btok], [pbA[1]])
                            zf, Bzf = r_zf.next()
                            zb, Bzb = r_zb.next()
                            c.op("act", lambda e, zf=zf, tok=tok, hs=hs: e.activation(out=zf[:, 0:64], in_=tok[:, 0, hs], func=AF.Copy),
                                 reads=[Btok], writes=[Bzf])
                            c.op("act", lambda e, zf=zf, psA=psA: e.activation(out=zf[:, 64:128], in_=psA[:, 128:192], func=AF.Copy),
                                 reads=[pbA[1]], writes=[Bzf])
                            c.ew(lambda e, zf=zf, zb=zb: e.tensor_copy(out=zb[:], in_=zf[:]), reads=[Bzf], writes=[Bzb])
                            st[(cp, x)] = dict(e4=e4, Be4=Be4, zf=zf, Bzf=Bzf, zb=zb, Bzb=Bzb, Pn=p0[:], PnT=e4[:, 0:128], BPn=[Bp0, Be4])
                        for it in range(7):
                            for (cp, x) in grp:
                                d_ = st[(cp, x)]
                                zf, Bzf, zb, Bzb = d_["zf"], d_["Bzf"], d_["zb"], d_["Bzb"]
                                Pn, PnT, BPn = d_["Pn"], d_["PnT"], d_["BPn"]
                                psZ, pbZ = pp.next()
                                mm(psZ[:, 0:128], PnT, zb[:], True, True, BPn + [Bzb], [pbZ[0]])
                                if it < 6:
                                    mm(psZ[:, 128:256], PnT, Pn, True, True, BPn, [pbZ[1]])
                                    mm(psZ[:, 256:384], Pn, PnT, True, True, BPn, [pbZ[2]])
                                c.op("dve", lambda e, zf=zf, psZ=psZ: e.tensor_tensor(out=zf[:], in0=psZ[:, 0:128], in1=zf[:], op=ALU.add),
                                     reads=[pbZ[0], Bzf], writes=[Bzf])
                                if it < 6:
                                    ppq, Bppq = r_pp.next()
                                    c.op("act", lambda e, ppq=ppq, psZ=psZ: e.activation(out=ppq[:], in_=psZ[:, 128:384], func=AF.Copy),
                                         reads=pbZ[1:3], writes=[Bppq])
                                    c.ew(lambda e, zf=zf, zb=zb: e.tensor_copy(out=zb[:], in_=zf[:]), reads=[Bzf], writes=[Bzb])
                                    d_["Pn"], d_["PnT"], d_["BPn"] = ppq[:, 0:128], ppq[:, 128:256], [Bppq]
                        for (cp, x) in grp:
                            d_ = st[(cp, x)]
                            wu, Bwu = wus[cp]
                            hs = slice(x * 64, (x + 1) * 64)
                            zf, Bzf = d_["zf"], d_["Bzf"]
                            c.op("act", lambda e, wu=wu, zf=zf, hs=hs: e.activation(out=wu[:, 0, hs], in_=zf[:, 0:64], func=AF.Copy),
                                 reads=[Bzf], writes=[Bwu])
                            c.ew(lambda e, wu=wu, zf=zf, hs=hs: e.tensor_copy(out=wu[:, 1, hs], in_=zf[:, 64:128]),
                                 reads=[Bzf], writes=[Bwu])
                        for cp in cps:
                            if (cp, 0) not in st or (cp, 1) not in st:
                                continue
                            tok, Btok = toks[cp]
                            wu, Bwu = wus[cp]
                            qt, Bqt = r_qt.next()
                            psQ, pbQ = pp.next()
                            for x in range(2):
                                e4, Be4 = st[(cp, x)]["e4"], st[(cp, x)]["Be4"]
                                AR_x = (ARe_t, ARo_t)[x]
                                hs = slice(x * 64, (x + 1) * 64)
                                mm(psQ[:, x * 128:(x + 1) * 128], wu[:, 0, :], e4[:, 128:256], True, True, [Bwu, Be4], [pbQ[x]])
                            for x in range(2):
                                AR_x = (ARe_t, ARo_t)[x]
                                hs = slice(x * 64, (x + 1) * 64)
                                c.op("dve", lambda e, qt=qt, psQ=psQ, x=x, hs=hs, AR_x=AR_x, cp=cp: e.tensor_tensor(
                                    out=qt[hs, :], in0=psQ[hs, x * 128:(x + 1) * 128], in1=AR_x[hs, cp, 128:256], op=ALU.add),
                                     reads=[pbQ[x], BAR_t], writes=[Bqt])
                            yo = YR_t[:, cp * 128:(cp + 1) * 128]
                            mm(yo, qt[:], Hb[:, cp, :], True, False, [Bqt, BH[cp]], [YR_b])
                            for x in range(2):
                                e4, Be4 = st[(cp, x)]["e4"], st[(cp, x)]["Be4"]
                                hs = slice(x * 64, (x + 1) * 64)
                                yx = YR_t[:, cp * 128 + x * 64:cp * 128 + (x + 1) * 64]
                                mm(yx, e4[:, 128:256], wu[:, 1, hs], False, False, [Be4, Bwu], [YR_b])
                                mm(yx, e4[:, 384:512], tok[:, 3, hs], False, x == 1, [Be4, Btok], [YR_b])
                            psM, pbM = pp.next()
                            mm(psM[:, 0:128], wu[:, 0, :], tok[:, 1, :], True, True, [Bwu, Btok], [pbM[0]])
                            mtb, Bmtb = r_mtb.next()
                            c.op("dve", lambda e, mtb=mtb, psM=psM: e.tensor_tensor(out=mtb[:], in0=psM[:, 0:128], in1=Kb("bmask"), op=ALU.mult),
                                 reads=[pbM[0], Bcsb], writes=[Bmtb])
                            psH, pbH = pp.next()
                            hsl = psH[:, 0:128]
                            mm(hsl, tok[:, 1, :], wu[:, 1, :], True, False, [Btok, Bwu], [pbH[0]])
                            mm(hsl, tok[:, 2, :], tok[:, 3, :], False, False, [Btok], [pbH[0]])
                            mm(hsl, Kb("ident"), Hb[:, cp, :], False, False, [Bcsb, BH[cp]], [pbH[0]])
                            mm(hsl, mtb[:], Hb[:, cp, :], False, True, [Bmtb, BH[cp]], [pbH[0]])
                            c.op("dve", lambda e, cp=cp, psH=psH, gam=gam: e.scalar_tensor_tensor(
                                out=Hb[:, cp, :], in0=psH[:, 0:128], scalar=gam[:, cp, 127:128], in1=Kb("bmask"), op0=ALU.mult, op1=ALU.mult),
                                 reads=[pbH[0], Bgam, Bcsb], writes=[BH[cp]])
                    yrh, Byrh = r_yhl.next()
                    split2(YR_t[:], yrh[:, 0, :], yrh[:, 1, :], [YR_b], Byrh, Byrh, psum=True)
                    if not fwd:
                        for q, key in enumerate(("ybr_h", "ybr_l")):
                            c.dma("sp", lambda e, yrh=yrh, q=q, key=key, ti=ti: e.dma_start(out=ysc[key][ti * 128:(ti + 1) * 128, :], in_=yrh[:, q, :]),
                                  reads=[Byrh], writes=[B_ysc[key][ti]])
                        continue
                    if SUB < 5:
                        continue

                    mix, Bmix = r_mix.next()

                    def ytrans(yh, Byh, yb, Byb):
                        ps, pb = pp.next()
                        for j in range(4):
                            o_ = ps[:, j * 128:(j + 1) * 128]
                            cs_ = slice(j * 128, (j + 1) * 128)
                            mm(o_, yh[:, 0, cs_], Kb("ident"), True, False, [Byh, Bcsb], [pb[j]])
                            mm(o_, yh[:, 1, cs_], Kb("ident"), False, False, [Byh, Bcsb], [pb[j]])
                            mm(o_, yb[:, 0, cs_], Kb("J"), False, False, [Byb, Bcsb], [pb[j]])
                            mm(o_, yb[:, 1, cs_], Kb("J"), False, True, [Byb, Bcsb], [pb[j]])
                        return ps, pb
                    ps, pb = ytrans(ysh, Bysh, ybS, BybS)
                    yg, Byg = AS(0)
                    for j in range(4):
                        c.op("dve", lambda e, j=j, yg=yg, xsf=xsf, ps=ps: e.scalar_tensor_tensor(
                            out=yg[:, j, :], in0=xsf[:, j, :], scalar=P("dch", j), in1=ps[:, j * 128:(j + 1) * 128], op0=ALU.mult, op1=ALU.add),
                             reads=[Bxsf, Bpv, pb[j]], writes=[Byg])
                    c.ew(lambda e, yg=yg, zs=zs: e.tensor_tensor(out=yg[:], in0=yg[:], in1=zs[:], op=ALU.mult), reads=[Byg, Bzs], writes=[Byg])
                    sq2, Bsq2 = AS(2)
                    c.ew(lambda e, sq2=sq2, yg=yg: e.tensor_tensor(out=sq2[:], in0=yg[:], in1=yg[:], op=ALU.mult), reads=[Byg], writes=[Bsq2])
                    hi, Bhi = r_sp.next()
                    lo, Blo = r_sp.next()
                    split2(sq2[:], hi[:], lo[:], [Bsq2], Bhi, Blo)
                    ps, pb = pp.next()
                    for g in range(2):
                        o_ = ps[:, g * 128:(g + 1) * 128]
                        for j in range(2):
                            mm(o_, Kb("o256"), hi[:, 2 * g + j, :], j == 0, False, [Bcsb, Bhi], [pb[g]])
                            mm(o_, Kb("o256"), lo[:, 2 * g + j, :], False, j == 1, [Bcsb, Blo], [pb[g]])
                    rs, Brs = r_st.next()
                    rs2, Brs2 = r_st.next()
                    rsqrt_eps(rs[:], ps[:, 0:128], RMS_EPS, [pb[0]], [Brs])
                    rsqrt_eps(rs2[:], ps[:, 128:256], RMS_EPS, [pb[1]], [Brs2])
                    for j in range(4):
                        rr_, Brr_ = (rs, Brs) if j < 2 else (rs2, Brs2)
                        c.op("dve", lambda e, j=j, mix=mix, yg=yg, rr_=rr_: e.scalar_tensor_tensor(
                            out=mix[:, j, :], in0=yg[:, j, :], scalar=P("nw", j), in1=rr_[:], op0=ALU.mult, op1=ALU.mult),
                             reads=[Byg, Bpv, Brr_], writes=[Bmix])
                    ps, pb = ytrans(yrh, Byrh, ybR, BybR)
                    yf, Byf = AS(1)
                    evac(yf[:], ps[:].rearrange("p (a b) -> p a b", b=128), pb, [Byf])
                    psm, pbm = blocksum4(yf[:], Byf, "bo64m")
                    c.op("dve", lambda e, yf=yf, psm=psm: e.tensor_tensor(out=yf[:], in0=yf[:], in1=psm, op=ALU.subtract), reads=[Byf] + pbm, writes=[Byf])
                    c.ew(lambda e, sq2=sq2, yf=yf: e.tensor_tensor(out=sq2[:], in0=yf[:], in1=yf[:], op=ALU.mult), reads=[Byf], writes=[Bsq2])
                    psv, pbv = blocksum4(sq2[:], Bsq2, "bo64m")
                    rsqrt_eps(sq2[:], psv, GN_EPS, pbv, [Bsq2])
                    c.ew(lambda e, yf=yf, sq2=sq2: e.tensor_tensor(out=yf[:], in0=yf[:], in1=sq2[:], op=ALU.mult), reads=[Byf, Bsq2], writes=[Byf])
                    c.ew(lambda e, yf=yf: e.tensor_tensor(out=yf[:], in0=yf[:], in1=bc4("lw"), op=ALU.mult), reads=[Byf, Bpv], writes=[Byf])
                    c.ew(lambda e, yf=yf: e.tensor_tensor(out=yf[:], in0=yf[:], in1=bc4("lb"), op=ALU.add), reads=[Byf, Bpv], writes=[Byf])
                    c.ew(lambda e, yf=yf, bonv=bonv: e.tensor_tensor(out=yf[:], in0=yf[:], in1=bonv[:], op=ALU.add), reads=[Byf, Bbonv], writes=[Byf])
                    c.ew(lambda e, mix=mix, yf=yf, gT=gT: e.tensor_tensor(out=mix[:, 4:8, :], in0=yf[:], in1=gT[:], op=ALU.mult),
                         reads=[Byf, BgT], writes=[Bmix])
                    xf, Bxf = AS(9, 2)
                    c.dma("sp", lambda e, xf=xf, col0=col0: e.dma_start(
                        out=xf, in_=xT[:, col0 + 2:col0 + 130].rearrange("(k p) t -> p k t", p=128)), writes=[Bxf])
                    pre, Bpre = AS(3, 2)
                    for half in range(2):
                        ps, pb = pp.next()
                        for j in range(4):
                            dc = half * 4 + j
                            for k in range(8):
                                mm(ps[:, j * 128:(j + 1) * 128], wo[:, k, dc * 128:(dc + 1) * 128], mix[:, k, :], k == 0, k == 7,
                                   [Bwo, Bmix], [pb[j]])
                        c.op("dve", lambda e, half=half, pre=pre, xf=xf, ps=ps: e.scalar_tensor_tensor(
                            out=pre[:, half * 4:(half + 1) * 4, :], in0=xf[:, half * 4:(half + 1) * 4, :], scalar=ALPHA,
                            in1=ps[:].rearrange("p (a b) -> p a b", b=128), op0=ALU.mult, op1=ALU.add), reads=[Bxf] + pb, writes=[Bpre])
                    hout, Bhout = AS(7, 2)
                    sqL, BsqL = AS(5, 2)
                    layer_norm(pp, Kb, pre, Bpre, hout, Bhout, sqL, BsqL, r_st, r_sps, P, "l1w", "l1b", Bcsb, Bpv, 128)
                    t0 = ti * C
                    c.dma("sp", lambda e, hout=hout, t0=t0: e.dma_start(
                        out=hT[:, t0:t0 + 128].rearrange("(k p) t -> p k t", p=128), in_=hout), reads=[Bhout], writes=[B_hT[ti]])
                c.full_barrier()
                c.run_block()

        def ffn_phase():
            NF = 128
            with ExitStack() as es:
                sb = lambda n, s, dt=F32: es.enter_context(nc.sbuf_tensor(n, s, dt))
                rot = lambda n, s, dt=F32, k=2: Rot(nc, es, n, s, dt, k)
                pp = PsumPool(nc, es, ["pf%d" % i for i in range(8)])
                wfi = sb("wfi", [128, 8, 2 * DFF], BF16); Bwfi = Buf()
                wfo = sb("wfo", [128, NFC, D], BF16); Bwfo = Buf()
                for k in range(8):
                    c.dma("pool", lambda e, k=k: e.dma_start(out=wfi[:, k, :], in_=w_fi[k * 128:(k + 1) * 128, :]), writes=[Bwfi])
                for k in range(NFC):
                    c.dma("pool", lambda e, k=k: e.dma_start(out=wfo[:, k, :], in_=w_fo[k * 128:(k + 1) * 128, :]), writes=[Bwfo])
                pv = sb("pv2", [128, NPV]); Bpv = Buf()
                c.dma("sp", lambda e: e.dma_start(out=pv[:], in_=pvec), writes=[Bpv])
                csb = sb("csb2", [128, NCST * 128], BF16); Bcsb = Buf()
                c.dma("pool", lambda e: e.dma_start(out=csb[:], in_=cst), writes=[Bcsb])
                Kb = lambda n: csb[:, CST[n] * 128:(CST[n] + 1) * 128]
                P = lambda n, j=0, w=1: pv[:, PV[n][0] + j:PV[n][0] + j + w]
                r_hf = rot("hf", [128, 8, NF], F32, 2)
                r_hb = rot("hb", [128, 8, NF], BF16, 2)
                r_act = rot("actt", [128, NFC, NF], BF16, 1)
                r_sl = rot("sl", [128, NF], F32, 3)
                r_sq = rot("sq2", [128, 8, NF], F32, 1)
                r_st = rot("st2", [128, NF], F32, 4)
                r_sps = rot("sps2", [128, NF], BF16, 2)
                r_out = rot("outt", [128, 8, NF], F32, 2)
                for fi in range(min(TT // NF, NTL)):
                    t0 = fi * NF
                    deps = [B_hT[t] for t in range(t0 // C, (t0 + NF) // C)]
                    hf, Bhf = r_hf.next()
                    hb, Bhb = r_hb.next()
                    c.dma("sp", lambda e, hf=hf, t0=t0: e.dma_start(out=hf[:], in_=hT[:, t0:t0 + NF].rearrange("(k p) t -> p k t", p=128)),
                          reads=deps, writes=[Bhf])
                    c.op("act", lambda e, hf=hf, hb=hb: e.activation(out=hb[:], in_=hf[:], func=AF.Copy), reads=[Bhf], writes=[Bhb])
                    actt, Bact = r_act.next()
                    for fc in range(NFC):
                        ps, pb = pp.next()
                        for k in range(8):
                            mm(ps[:, 0:NF], wfi[:, k, fc * 128:(fc + 1) * 128], hb[:, k, :], k == 0, k == 7, [Bwfi, Bhb], pb[0:2])
                        for k in range(8):
                            mm(ps[:, 256:256 + NF], wfi[:, k, DFF + fc * 128:DFF + (fc + 1) * 128], hb[:, k, :], k == 0, k == 7,
                               [Bwfi, Bhb], pb[2:4])
                        sl, Bsl = r_sl.next()
                        c.op("act", lambda e, sl=sl, ps=ps: e.activation(out=sl[:], in_=ps[:, 0:NF], func=AF.Silu), reads=pb[0:2], writes=[Bsl])
                        c.op("dve", lambda e, fc=fc, sl=sl, ps=ps, actt=actt: e.tensor_tensor(
                            out=actt[:, fc, :], in0=ps[:, 256:256 + NF], in1=sl[:], op=ALU.mult), reads=pb[2:4] + [Bsl], writes=[Bact])
                    for half in range(4):
                        ps, pb = pp.next()
                        for j in range(2):
                            dc = half * 2 + j
                            for k in range(NFC):
                                mm(ps[:, j * 256:j * 256 + NF], wfo[:, k, dc * 128:(dc + 1) * 128], actt[:, k, :], k == 0, k == NFC - 1,
                                   [Bwfo, Bact], pb[2 * j:2 * j + 2])
                        for j in range(2):
                            dc = half * 2 + j
                            c.op("dve", lambda e, dc=dc, j=j, hf=hf, ps=ps: e.scalar_tensor_tensor(
                                out=hf[:, dc, :], in0=hf[:, dc, :], scalar=ALPHA, in1=ps[:, j * 256:j * 256 + NF], op0=ALU.mult, op1=ALU.add),
                                 reads=[Bhf] + pb[2 * j:2 * j + 2], writes=[Bhf])
                    out, Bout = r_out.next()
                    sq, Bsq = r_sq.next()
                    layer_norm(pp, Kb, hf[:], Bhf, out[:], Bout, sq[:], Bsq, r_st, r_sps, P, "l2w", "l2b", Bcsb, Bpv, NF)
                    Bo = Buf()
                    B_out.append(Bo)
                    c.dma("sp", lambda e, out=out, t0=t0: e.dma_start(out=yT[:, t0:t0 + NF].rearrange("(k p) t -> p k t", p=128), in_=out[:]),
                          reads=[Bout], writes=[Bo])
                c.wait_all("sp", B_out)
                c.run_block()

        mixer_phase(False)
        if stage >= 2:
            mixer_phase(True)
        if stage >= 3:
            ffn_phase()
    return nc


def _consts():
    i = np.arange(128)
    p, f = i[:, None], i[None, :]
    blocks = {
        "ident": (p == f), "J": (p + f == 127), "msl": (f < p), "msu": (p < f), "mui": (p <= f),
        "bmask": ((p // 64) == (f // 64)), "bo64": ((p // 64) == (f // 64)),
    }
    out = np.zeros((128, NCST * 128), np.float32)
    for n, m in blocks.items():
        out[:, CST[n] * 128:(CST[n] + 1) * 128] = m.astype(np.float32)
    out[:, CST["bo64m"] * 128:(CST["bo64m"] + 1) * 128] = blocks["bo64"].astype(np.float32) / 64.0
    out[:, CST["o1024"] * 128:(CST["o1024"] + 1) * 128] = 1.0 / 1024.0
    out[:, CST["o256"] * 128:(CST["o256"] + 1) * 128] = 1.0 / 256.0
    out[:, CST["ones"] * 128:(CST["ones"] + 1) * 128] = 1.0
    m8 = np.zeros((8, 1024), np.float32)
    for h in range(8):
        m8[h, h * 128:(h + 1) * 128] = 1.0
    return out, m8


def _chunkvec(v, n):
    return np.ascontiguousarray(np.asarray(v, np.float32).reshape(n, 128).T)


def prep_params(inp):
    g = lambda n: np.asarray(inp[n], np.float32)[0]
    w_in = g("w_in")
    M_W, CONV = 512, 1024
    o_z, o_xbc, o_dt = 0, 512, 1536
    o_r = 1544
    W = np.zeros((D, NCH_IN * 128), np.float32)
    W[:, 0:512] = w_in[:, o_z:o_z + 512]
    W[:, 512:1536] = w_in[:, o_xbc:o_xbc + 1024]
    W[:, 1536:3072] = w_in[:, o_r:o_r + 1536]
    W[:, CWL * 128:CWL * 128 + 64] = w_in[:, o_r + 1536:o_r + 1600]
    W[:, CAL * 128:CAL * 128 + 64] = w_in[:, o_r + 1600:o_r + 1664]
    W[:, CGL * 128:CGL * 128 + 128] = w_in[:, o_r + 1664:o_r + 1792]
    W[:, CDT * 128:CDT * 128 + 8] = w_in[:, o_dt:o_dt + 8]
    pv = np.zeros((128, NPV), np.float32)
    def put(name, arr):
        o, w = PV[name]
        pv[:, o:o + w] = arr
    cw = g("conv_w")
    cwp = np.zeros((128, 8, 5), np.float32)
    for j in range(8):
        cwp[:, j, :] = cw[:, j * 128:(j + 1) * 128].T
    put("cw", cwp.reshape(128, 40))
    put("cb", _chunkvec(g("conv_b"), 8))
    put("dch", _chunkvec(np.repeat(g("m_d"), 64), 4))
    put("nw", _chunkvec(g("m_norm_w"), 4))
    def mu15(v):
        o = np.zeros((128, 15), np.float32)
        o[:, 0:12] = _chunkvec(v[0:1536], 12)
        o[0:64, 12] = v[1536:1600]
        o[0:64, 13] = v[1600:1664]
        o[:, 14] = v[1664:1792]
        return o
    put("mp", mu15(g("r_mu_prev")))
    put("mn", mu15(g("r_mu_next")))
    put("w0f", _chunkvec(g("r_w0_f"), 4))
    put("w0b", _chunkvec(g("r_w0_b"), 4))
    put("a0", _chunkvec(g("r_a0"), 4))
    put("kk", _chunkvec(g("r_k_k"), 4))
    put("ka", _chunkvec(g("r_k_a"), 4))
    put("rk", _chunkvec(g("r_r_k").reshape(-1), 4))
    put("lw", _chunkvec(g("r_lnx_w"), 4))
    put("lb", _chunkvec(g("r_lnx_b"), 4))
    put("l1w", _chunkvec(g("ln1_w"), 8))
    put("l1b", _chunkvec(g("ln1_b"), 8))
    put("l2w", _chunkvec(g("ln2_w"), 8))
    put("l2b", _chunkvec(g("ln2_b"), 8))
    p8 = np.zeros((8, 8), np.float32)
    p8[:, 0] = g("m_dt_bias_f"); p8[:, 1] = g("m_dt_bias_b")
    p8[:, 2] = g("m_a_log_f"); p8[:, 3] = g("m_a_log_b")
    lowr = np.zeros((128, 4 * 512), np.float32)
    lowr[0:64, 0:512] = g("r_w2_f")
    lowr[0:64, 512:1024] = g("r_w2_b")
    lowr[0:64, 1024:1536] = g("r_a2")
    lowr[:, 1536:2048] = g("r_g2")
    cst, m8 = _consts()
    return dict(w_in=W, w_out=np.ascontiguousarray(g("w_out")), w_fi=np.ascontiguousarray(g("w_ffn_in")),
                w_fo=np.ascontiguousarray(g("w_ffn_out")), p8=p8, lowr=lowr, cst=cst, m8=m8), pv


def layout_core(units, link, TU):
    xT = np.zeros((D, 2 * (TU + 4)), np.float32)
    xTr = np.zeros((D, 2 * (TU + 4)), np.float32)
    for u, (seq, s) in enumerate(units):
        if seq is None:
            continue
        T = seq.shape[0]
        lo, hi = s - 2, s + TU + 2
        blk = np.zeros((TU + 4, D), np.float32)
        a, b = max(lo, 0), min(hi, T)
        blk[a - lo:b - lo] = seq[a:b]
        xT[:, u * (TU + 4):(u + 1) * (TU + 4)] = blk.T
        ur = 1 - u
        xTr[:, ur * (TU + 4):(ur + 1) * (TU + 4)] = blk[::-1].T
    return xT, xTr


_NC_CACHE = {}


def run_cores(assign, params, pv, TU, n_cores):
    if TU not in _NC_CACHE:
        import os
        _NC_CACHE[TU] = build(TU, stage=int(os.environ.get("MK_STAGE", "3")))
    nc = _NC_CACHE[TU]
    in_maps = []
    for units, link in assign:
        xT, xTr = layout_core(units, link, TU)
        pvc = pv.copy()
        pvc[:, PV["link"][0]] = float(link)
        m = dict(params)
        m.update(xT=xT, xTr=xTr, pvec=pvc)
        in_maps.append(m)
    res = run_bass_kernel_spmd(nc, in_maps, core_ids=list(range(n_cores)))
    return [r["yT"] for r in res.results]


def kernel(**inputs):
    xp = np.asarray(inputs["x_prompt"], np.float32)
    xs = np.asarray(inputs["x_sample"], np.float32)
    params, pv = prep_params(inputs)
    TU = xs.shape[1]
    assert xp.shape[1] == 2 * TU
    assign = []
    for b in range(xp.shape[0]):
        assign.append(([(xp[b], 0), (xp[b], TU)], 1))
    nb = xs.shape[0]
    slots = [[(xs[i], 0)] for i in range(nb)]
    n_rest = 8 - len(assign)
    per = [[] for _ in range(n_rest)]
    for i in range(nb):
        per[i % n_rest].append(i)
    for lst in per:
        u = [(xs[i], 0) for i in lst]
        while len(u) < 2:
            u.append((None, 0))
        assign.append((u, 0))
    outs = run_cores(assign, params, pv, TU, 8)
    y_p = np.zeros_like(xp)
    y_s = np.zeros_like(xs)
    for b in range(xp.shape[0]):
        y_p[b] = outs[b].T
    for ci, lst in enumerate(per):
        o = outs[xp.shape[0] + ci]
        for u, i in enumerate(lst):
            y_s[i] = o[:, u * TU:(u + 1) * TU].T
    return (y_p, y_s)
```

```python
import numpy as np
from contextlib import ExitStack
import concourse.bass as bass
import concourse.mybir as mybir
from concourse.bass_utils import run_bass_kernel_spmd

F32 = mybir.dt.float32
BF16 = mybir.dt.bfloat16
AF = mybir.ActivationFunctionType
ALU = mybir.AluOpType

D = 1024
C = 128
NCH_IN = 28
DFF = 2816
NFC = DFF // 128
ALPHA = 2.0 ** 0.25
LN_EPS = 1e-5
RMS_EPS = 1e-5
GN_EPS = 64e-5
CDEC = float(np.exp(-0.5))

CZ, CXS, CB, CC_, CR, CK, CV, CWL, CAL, CGL, CDT = 0, 4, 8, 10, 12, 16, 20, 24, 25, 26, 27

PV = {}
_o = 0
for _n, _w in [("cw", 40), ("cb", 8), ("dch", 4), ("nw", 4), ("mp", 15), ("mn", 15), ("w0f", 4), ("w0b", 4),
               ("a0", 4), ("kk", 4), ("ka", 4), ("rk", 4), ("lw", 4), ("lb", 4), ("l1w", 8), ("l1b", 8),
               ("l2w", 8), ("l2b", 8), ("link", 1)]:
    PV[_n] = (_o, _w)
    _o += _w
NPV = _o
CST = {n: i for i, n in enumerate(["ident", "J", "msl", "msu", "mui", "bmask", "bo64", "bo64m", "o1024", "o256", "ones"])}
NCST = len(CST)


class Buf:
    __slots__ = ("name", "w", "r", "psum")

    def __init__(self, name="", psum=False):
        self.name = name
        self.w = None
        self.r = []
        self.psum = psum


class Ctx:
    ENGS = ("pe", "act", "dve", "pool", "sp")

    def __init__(self, nc, es, n_dma_slots=6):
        self.nc = nc
        self.ops = {e: [] for e in self.ENGS}
        self.sems = {}
        self.count = {}
        for e in self.ENGS:
            self.sems[e] = es.enter_context(nc.semaphore("s_" + e))
            self.count[e] = 0
        self.waited = {e: {} for e in self.ENGS}
        self.slots = {}
        for q in ("sp", "pool"):
            self.slots[q] = []
            for i in range(n_dma_slots):
                key = "d_%s_%d" % (q, i)
                self.sems[key] = es.enter_context(nc.semaphore(key))
                self.count[key] = 0
                self.slots[q].append(key)
        self.slot_rr = {"sp": 0, "pool": 0}
        self.rr = 0
        import os
        self.ew_engs = tuple(os.environ.get("MK_EW", "dve").split(","))
        self.PEX = Buf("PEX")
        self.pex_on = os.environ.get("MK_PEX", "0") == "1"
        self.cut = int(os.environ.get("MK_CUT", "0"))
        self.nops = 0
        self.last_lines = None

    def _cut(self):
        self.nops += 1
        if self.cut and self.nops >= self.cut:
            if self.nops == self.cut:
                import sys
                f = sys._getframe(2)
                lines = []
                while f is not None and len(lines) < 5:
                    lines.append(f.f_lineno)
                    f = f.f_back
                print("MK_CUT: first dropped op #%d at lines %s" % (self.nops, lines), flush=True)
            return True
        return False

    def _deps(self, eng, reads, writes):
        need = {}
        for b in reads:
            ev = b.w
            if ev is not None and need.get(ev[0], 0) < ev[1]:
                need[ev[0]] = ev[1]
        for b in writes:
            ev = b.w
            if ev is not None and need.get(ev[0], 0) < ev[1]:
                need[ev[0]] = ev[1]
            for ev in b.r:
                if need.get(ev[0], 0) < ev[1]:
                    need[ev[0]] = ev[1]
        waits = []
        wd = self.waited[eng]
        for k, v in need.items():
            if k == eng and eng == "pe":
                continue
            if wd.get(k, 0) >= v:
                continue
            wd[k] = v
            waits.append((k, v))
        return waits

    def _mark(self, ev, reads, writes):
        for b in reads:
            b.r.append(ev)
            if len(b.r) > 24:
                m = {}
                for k, v in b.r:
                    if m.get(k, 0) < v:
                        m[k] = v
                b.r = list(m.items())
        for b in writes:
            b.w = ev
            b.r = []

    @staticmethod
    def _flat(bs):
        out = []
        for b in bs:
            if isinstance(b, (list, tuple)):
                out.extend(Ctx._flat(b))
            else:
                out.append(b)
        return out

    def op(self, eng, fn, reads=(), writes=()):
        if self._cut():
            return
        reads, writes = self._flat(reads), self._flat(writes)
        pr = [b for b in reads if b.psum]
        if pr:
            reads = [b for b in reads if not b.psum]
            writes = list(writes) + pr
        if self.pex_on:
            if eng == "pe":
                reads = list(reads) + [self.PEX]
            elif eng == "dve" and any(b.psum for b in writes):
                writes = list(writes) + [self.PEX]
        waits = self._deps(eng, reads, writes)
        self.count[eng] += 1
        ev = (eng, self.count[eng])
        self.ops[eng].append((waits, fn, eng, 1))
        self._mark(ev, reads, writes)

    def dma(self, q, fn, reads=(), writes=()):
        if self._cut():
            return
        slot = self.slots[q][self.slot_rr[q] % len(self.slots[q])]
        self.slot_rr[q] += 1
        reads, writes = self._flat(reads), self._flat(writes)
        waits = self._deps(q, reads, writes)
        prev = self.count[slot]
        if prev > 0 and self.waited[q].get(slot, 0) < prev:
            self.waited[q][slot] = prev
            waits.append((slot, prev))
        self.count[slot] += 16
        ev = (slot, self.count[slot])
        self.ops[q].append((waits, fn, slot, 16))
        self._mark(ev, reads, writes)

    def wait_all(self, eng, bufs):
        if self.cut and self.nops >= self.cut:
            return
        waits = self._deps(eng, self._flat(bufs), ())
        self.ops[eng].append((waits, None, None, 0))

    def ew(self, fn, reads=(), writes=(), engs=None):
        if engs is None:
            engs = self.ew_engs
        e = engs[self.rr % len(engs)]
        self.rr += 1
        self.op(e, fn, reads, writes)

    def full_barrier(self):
        for eng in self.ENGS:
            waits = []
            for k, v in self.count.items():
                if v > 0 and self.waited[eng].get(k, 0) < v and not (k == eng):
                    self.waited[eng][k] = v
                    waits.append((k, v))
            self.ops[eng].append((waits, None, None, 0))

    def replay(self, engname, e):
        sems = self.sems
        for waits, fn, semkey, inc in self.ops[engname]:
            for k, v in waits:
                e.wait_ge(sems[k], v)
            if fn is not None:
                fn(e).then_inc(sems[semkey], inc)

    def run_block(self):
        nc = self.nc
        with nc.Block() as block:
            @block.tensor
            def _(e):
                self.replay("pe", e)

            @block.scalar
            def _(e):
                self.replay("act", e)

            @block.vector
            def _(e):
                self.replay("dve", e)

            @block.gpsimd
            def _(e):
                self.replay("pool", e)

            @block.sync
            def _(e):
                self.replay("sp", e)
        self.ops = {e: [] for e in self.ENGS}


class Rot:
    def __init__(self, nc, es, name, shape, dt, n=2):
        self.t = [es.enter_context(nc.sbuf_tensor("%s_%d" % (name, i), shape, dt)) for i in range(n)]
        self.b = [Buf(name) for _ in range(n)]
        self.i = 0

    def next(self):
        k = self.i % len(self.t)
        self.i += 1
        return self.t[k], self.b[k]


class PsumPool:
    def __init__(self, nc, es, names):
        self.t = [es.enter_context(nc.psum_tensor(n, [128, 512], F32)) for n in names]
        self.b = [[Buf(n, psum=True)] * 4 for n in names]
        self.i = 0

    def next(self):
        k = self.i % len(self.t)
        self.i += 1
        return self.t[k], self.b[k]


def build(TU, stage=3):
    import os
    SUB = float(os.environ.get('MK_SUB', '9'))
    NTL = int(os.environ.get('MK_NT', '999'))
    NTU = TU // C
    NT = 2 * NTU
    TT = 2 * TU
    XW = 2 * (TU + 4)
    nc = bass.Bass("TRN2", target_bir_lowering=False)
    dram = lambda n, s, dt=F32, kind="ExternalInput": nc.dram_tensor(n, s, dt, kind=kind).ap()
    xT = dram("xT", [D, XW])
    xTr = dram("xTr", [D, XW])
    w_in = dram("w_in", [D, NCH_IN * 128])
    w_out = dram("w_out", [D, D])
    w_fi = dram("w_fi", [D, 2 * DFF])
    w_fo = dram("w_fo", [DFF, D])
    pvec = dram("pvec", [128, NPV])
    p8 = dram("p8", [8, 8])
    lowr = dram("lowr", [128, 4 * 512])
    cst = dram("cst", [128, NCST * 128])
    m8 = dram("m8", [8, 1024])
    yT = dram("yT", [D, TT], kind="ExternalOutput")
    ysc = {k: dram(k, [NT * 128, 512], BF16, kind="Internal") for k in ("ybr_h", "ybr_l", "ybs_h", "ybs_l")}
    hT = dram("hT", [D, TT], kind="Internal")

    with ExitStack() as es0:
        c = Ctx(nc, es0)
        B_ysc = {k: [Buf() for _ in range(NT)] for k in ysc}
        B_hT = [Buf() for _ in range(NT)]
        B_out = []

        def mk_mm(c):
            def mm(out, lhsT, rhs, start, stop, reads, writes):
                c.op("pe", lambda e: e.matmul(out, lhsT=lhsT, rhs=rhs, start=start, stop=stop), reads=reads, writes=writes)
            return mm
        mm = mk_mm(c)

        def split2(src, hi, lo, reads, Bhi, Blo, psum=False):
            c.op("act", lambda e: e.activation(out=hi, in_=src, func=AF.Copy), reads=reads, writes=[Bhi])
            if psum:
                c.op("dve", lambda e: e.tensor_tensor(out=lo, in0=src, in1=hi, op=ALU.subtract), reads=list(reads) + [Bhi], writes=[Blo])
            else:
                c.ew(lambda e: e.tensor_tensor(out=lo, in0=src, in1=hi, op=ALU.subtract), reads=list(reads) + [Bhi], writes=[Blo])

        def rsqrt_eps(out, in_, eps, reads, writes):
            c.op("dve", lambda e: e.tensor_scalar(out=out, in0=in_, scalar1=float(eps), scalar2=None, op0=ALU.add), reads=reads, writes=writes)
            c.op("act", lambda e: e.activation(out=out, in_=out, func=AF.Sqrt), reads=writes, writes=writes)
            c.op("dve", lambda e: e.reciprocal(out=out, in_=out), reads=writes, writes=writes)

        def layer_norm(pp, Kb, pre, Bpre, out, Bout, sq, Bsq, r_st, r_sp, P, wn, bn, Bcsb, Bpv, N):
            def stat(src, Bsrc):
                red, Bred = r_st.next()
                c.op("dve", lambda e: e.tensor_reduce(out=red[:, 0:N], in_=src.rearrange("p a n -> p n a"), axis=mybir.AxisListType.X, op=ALU.add),
                     reads=[Bsrc], writes=[Bred])
                hi, Bhi = r_sp.next()
                lo, Blo = r_sp.next()
                split2(red[:, 0:N], hi[:, 0:N], lo[:, 0:N], [Bred], Bhi, Blo)
                ps, pb = pp.next()
                mm(ps[:, 0:N], Kb("o1024"), hi[:, 0:N], True, False, [Bcsb, Bhi], pb)
                mm(ps[:, 0:N], Kb("o1024"), lo[:, 0:N], False, True, [Bcsb, Blo], pb)
                return ps, pb
            ps, pb = stat(pre, Bpre)
            mu, Bmu = r_st.next()
            c.op("act", lambda e: e.activation(out=mu[:, 0:N], in_=ps[:, 0:N], func=AF.Copy), reads=pb, writes=[Bmu])
            bcN = lambda t: t[:, 0:N].unsqueeze(1).to_broadcast([128, 8, N])
            bcP = lambda n: P(n, 0, 8).unsqueeze(2).to_broadcast([128, 8, N])
            c.ew(lambda e: e.tensor_tensor(out=pre, in0=pre, in1=bcN(mu), op=ALU.subtract), reads=[Bpre, Bmu], writes=[Bpre])
            c.ew(lambda e: e.tensor_tensor(out=sq, in0=pre, in1=pre, op=ALU.mult), reads=[Bpre], writes=[Bsq])
            ps2, pb2 = stat(sq, Bsq)
            rs, Brs = r_st.next()
            rsqrt_eps(rs[:, 0:N], ps2[:, 0:N], LN_EPS, pb2, [Brs])
            c.ew(lambda e: e.tensor_tensor(out=pre, in0=pre, in1=bcN(rs), op=ALU.mult), reads=[Bpre, Brs], writes=[Bpre])
            c.ew(lambda e: e.tensor_tensor(out=pre, in0=pre, in1=bcP(wn), op=ALU.mult), reads=[Bpre, Bpv], writes=[Bpre])
            c.ew(lambda e: e.tensor_tensor(out=out, in0=pre, in1=bcP(bn), op=ALU.add), reads=[Bpre, Bpv], writes=[Bout])

        def mixer_phase(fwd):
            with ExitStack() as es:
                sfx = "F" if fwd else "B"
                sb = lambda n, s, dt=F32: es.enter_context(nc.sbuf_tensor(n + sfx, s, dt))
                rot = lambda n, s, dt=F32, k=2: Rot(nc, es, n + sfx, s, dt, k)
                pp = PsumPool(nc, es, ["pp%d%s" % (i, sfx) for i in range(6)])
                YR_t = es.enter_context(nc.psum_tensor("YR" + sfx, [128, 512], F32)); YR_b = Buf("YR", psum=True)
                YS_t = es.enter_context(nc.psum_tensor("YS" + sfx, [128, 512], F32)); YS_b = Buf("YS", psum=True)
                xsrc = xT if fwd else xTr

                class Fix:
                    def __init__(self, t, b):
                        self.t, self.b = t, b

                    def next(self):
                        return self.t, self.b
                one = lambda n, s, dt=F32: Fix(sb(n, s, dt), Buf(n))

                win = sb("win", [128, 8, NCH_IN * 128], BF16); Bwin = Buf()
                for k in range(8):
                    c.dma("pool", lambda e, k=k: e.dma_start(out=win[:, k, :], in_=w_in[k * 128:(k + 1) * 128, :]), writes=[Bwin])
                if fwd:
                    wo = sb("wo", [128, 8, D], BF16); Bwo = Buf()
                    for k in range(8):
                        c.dma("pool", lambda e, k=k: e.dma_start(out=wo[:, k, :], in_=w_out[k * 128:(k + 1) * 128, :]), writes=[Bwo])
                pv = sb("pv", [128, NPV]); Bpv = Buf()
                c.dma("sp", lambda e: e.dma_start(out=pv[:], in_=pvec), writes=[Bpv])
                p8t = sb("p8t", [8, 8]); Bp8 = Buf()
                c.dma("sp", lambda e: e.dma_start(out=p8t[:], in_=p8), writes=[Bp8])
                lrb = sb("lrb", [128, 3 * 512], BF16); Blr = Buf()
                w2o = 0 if fwd else 512
                c.dma("pool", lambda e: e.dma_start(out=lrb[:, 0:512], in_=lowr[:, w2o:w2o + 512]), writes=[Blr])
                c.dma("pool", lambda e: e.dma_start(out=lrb[:, 512:1536], in_=lowr[:, 1024:2048]), writes=[Blr])
                csb = sb("csb", [128, NCST * 128], BF16); Bcsb = Buf()
                c.dma("pool", lambda e: e.dma_start(out=csb[:], in_=cst), writes=[Bcsb])
                m8b = sb("m8b", [8, 1024], BF16); Bm8 = Buf()
                c.dma("pool", lambda e: e.dma_start(out=m8b[:], in_=m8), writes=[Bm8])
                Kb = lambda n: csb[:, CST[n] * 128:(CST[n] + 1) * 128]
                mask4 = sb("mask4", [128, 512], BF16); Bm4 = Buf()
                for q, n in enumerate(["msu", "mui", "msu", "mui"]):
                    c.op("dve", lambda e, q=q, n=n: e.tensor_copy(out=mask4[:, q * 128:(q + 1) * 128], in_=Kb(n)),
                         reads=[Bcsb], writes=[Bm4])
                P = lambda n, j=0, w=1: pv[:, PV[n][0] + j:PV[n][0] + j + w]
                der = sb("der", [128, 15 + 4]); Bder = Buf()
                c.op("dve", lambda e: e.tensor_tensor(out=der[:, 0:15], in0=P("mp", 0, 15), in1=P("mn", 0, 15), op=ALU.add),
                     reads=[Bpv], writes=[Bder])
                c.op("dve", lambda e: e.tensor_scalar(out=der[:, 0:15], in0=der[:, 0:15], scalar1=-1.0, scalar2=1.0,
                                                      op0=ALU.mult, op1=ALU.add), reads=[Bder], writes=[Bder])
                c.op("dve", lambda e: e.tensor_scalar(out=der[:, 15:19], in0=P("ka", 0, 4), scalar1=-1.0, scalar2=1.0,
                                                      op0=ALU.mult, op1=ALU.add), reads=[Bpv], writes=[Bder])
                a8 = sb("a8", [8, 1]); Ba8 = Buf()
                acol = 2 if fwd else 3
                c.op("act", lambda e: e.activation(out=a8[:], in_=p8t[:, acol:acol + 1], func=AF.Exp), reads=[Bp8], writes=[Ba8])
                c.op("dve", lambda e: e.tensor_scalar(out=a8[:], in0=a8[:], scalar1=-1.0, scalar2=None, op0=ALU.mult),
                     reads=[Ba8], writes=[Ba8])
                dtb = p8t[:, (0 if fwd else 1):(1 if fwd else 2)]
                w2 = lrb[:, 0:512]
                a2 = lrb[:, 512:1024]
                g2 = lrb[:, 1024:1536]
                w0n = "w0f" if fwd else "w0b"
                mpn, mnn = ("mp", "mn") if fwd else ("mn", "mp")

                Hb = sb("Hb", [128, 4, 128], BF16); BH = [Buf() for _ in range(4)]
                STf = sb("STf", [128, 512]); STb = sb("STb", [128, 512], BF16); BST = Buf()
                c.op("pool", lambda e: e.memset(Hb[:], 0.0), writes=BH)
                c.op("pool", lambda e: e.memset(STf[:], 0.0), writes=[BST])
                c.op("pool", lambda e: e.memset(STb[:], 0.0), writes=[BST])
                ARe_t = sb("ARe", [128, 4, 256], BF16)
                ARo_t = sb("ARo", [128, 4, 256], BF16)
                BAR_t = Buf()
                c.op("pool", lambda e: e.memset(ARe_t[:], 0.0), writes=[BAR_t])
                c.op("pool", lambda e: e.memset(ARo_t[:], 0.0), writes=[BAR_t])
                CSt = sb("CS", [128, 4, 129]); BCSt = Buf()
                c.op("pool", lambda e: e.memset(CSt[:], 0.0), writes=[BCSt])
                ptmp = sb("ptmp", [128, 128]); Bptmp = Buf()
                Bfence = Buf()
                ones_t = sb("ones_t", [128, 128]); Bones = Buf()
                c.op("pool", lambda e: e.memset(ones_t[:], 1.0), writes=[Bones])
                DT3 = sb("DT3", [128, 3, 2, 128], BF16); BDT3 = Buf()
                RB3 = sb("RB3", [128, 2, 8, 128], BF16); BRB3 = [Buf(), Buf()]
                c.op("pool", lambda e: e.memset(DT3[:], 0.0), writes=[BDT3])
                c.op("pool", lambda e: e.memset(RB3[:], 0.0), writes=BRB3)

                arena = sb("arena", [128, 11 * 512]); Bar = [Buf() for _ in range(11)]

                def AS(i, n=1, parts=128, nch=None):
                    nch = 4 * n if nch is None else nch
                    return (arena[0:parts, 512 * i:512 * i + 128 * nch].rearrange("p (a b) -> p a b", b=128), Bar[i:i + n])

                def AU(i, nchunks):
                    return (arena[:, 512 * i:512 * i + 132 * nchunks].rearrange("p (a b) -> p a b", b=132), Bar[i:i + 4])

                r_xb = rot("xb", [128, 8, 132], BF16, 1 if fwd else 2)
                r_zs = one("zs", [128, 4, 128])
                r_xsf = one("xsf", [128, 4, 128])
                r_xbc = one("xbcb", [128, 8, 128], BF16)
                r_dtok = one("dtok", [128, 16])
                r_gm = one("gm", [128, 2, 128])
                r_MT = one("MT", [128, 8, 128], BF16)
                r_Cd = one("Cd", [128, 8, 128], BF16)
                r_xdt = one("xdt", [128, 2, 512], BF16)
                r_btok = one("btok", [128, 2, 128], BF16)
                r_lrin = one("lrin", [128, 3, 128], BF16)
                r_sp = rot("sp", [128, 4, 128], BF16, 2)
                r_op = rot("opb", [128, 4, 128], BF16, 4)
                LS = int(os.environ.get("MK_LS_F" if fwd else "MK_LS_B", "4" if fwd else "8"))
                NPR = max(1, LS // 2)
                r_tok = rot("tok", [128, 4, 128], BF16, max(2, NPR))
                r_p0 = rot("p0", [128, 128], BF16, LS)
                r_e4 = rot("e4", [128, 512], BF16, LS)
                r_pp = rot("ppq", [128, 256], BF16, 2 * LS)
                r_zf = rot("zf", [128, 128], F32, LS)
                r_zb = rot("zb", [128, 128], BF16, LS)
                r_wu = rot("wu", [128, 2, 128], BF16, max(2, NPR))
                r_qt = rot("qt", [128, 128], BF16, 1)
                r_mtb = rot("mtb", [128, 128], BF16, 1)
                r_yhl = rot("yhl", [128, 2, 512], BF16, 2)
                if fwd:
                    r_ybl = rot("ybl", [128, 2, 512], BF16, 2)
                    r_bg = rot("bg", [128, 4, 128], F32, 2)
                    r_mix = one("mix", [128, 8, 128], BF16)
                    r_st = rot("stat", [128, 128], F32, 4)
                    r_sps = rot("sps", [128, 128], BF16, 2)

                evac_rr = [0]

                def evac(out, in_, reads, writes, engs=("act", "dve")):
                    e_ = engs[evac_rr[0] % len(engs)]
                    evac_rr[0] += 1
                    if e_ == "act":
                        c.op("act", lambda e: e.activation(out=out, in_=in_, func=AF.Copy), reads=reads, writes=writes)
                    else:
                        c.op(e_, lambda e: e.tensor_copy(out=out, in_=in_), reads=reads, writes=writes)

                def blocksum4(src, Bsrc, cname):
                    hi, Bhi = r_sp.next()
                    lo, Blo = r_sp.next()
                    split2(src, hi[:], lo[:], [Bsrc], Bhi, Blo)
                    ps, pb = pp.next()
                    for cp in range(4):
                        o_ = ps[:, cp * 128:(cp + 1) * 128]
                        mm(o_, Kb(cname), hi[:, cp, :], True, False, [Bcsb, Bhi], [pb[cp]])
                        mm(o_, Kb(cname), lo[:, cp, :], False, True, [Bcsb, Blo], [pb[cp]])
                    return ps[:].rearrange("p (a b) -> p a b", b=128), pb

                def link_state():
                    lk = P("link")
                    for cpr in range(4):
                        c.op("dve", lambda e, cpr=cpr: e.tensor_scalar(out=Hb[:, cpr, :], in0=Hb[:, cpr, :], scalar1=lk,
                                                                       scalar2=None, op0=ALU.mult),
                             reads=[BH[cpr], Bpv], writes=[BH[cpr]])
                    c.op("dve", lambda e: e.tensor_scalar(out=STf[:], in0=STf[:], scalar1=lk, scalar2=None, op0=ALU.mult),
                         reads=[BST, Bpv], writes=[BST])
                    c.op("dve", lambda e: e.tensor_scalar(out=STb[:], in0=STb[:], scalar1=lk, scalar2=None, op0=ALU.mult),
                         reads=[BST, Bpv], writes=[BST])

                for ti in range(min(NT, NTL)):
                    u_i, lt = divmod(ti, NTU)
                    if ti == NTU:
                        link_state()
                    col0 = u_i * (TU + 4) + lt * C
                    xb, Bxb = r_xb.next()
                    c.dma("pool", lambda e, xb=xb, col0=col0: e.dma_start(
                        out=xb[:], in_=xsrc[:, col0:col0 + 132].rearrange("(k p) t -> p k t", p=128)), writes=[Bxb])
                    if SUB < 1:
                        continue

                    def inproj(U, BU, chunks):
                        for g0 in range(0, len(chunks), 3):
                            grp = chunks[g0:g0 + 3]
                            n_g = len(grp)
                            ps, pb = pp.next()
                            for j, cc in enumerate(grp):
                                for k in range(8):
                                    mm(ps[:, j * 132:(j + 1) * 132], win[:, k, cc * 128:(cc + 1) * 128], xb[:, k, :],
                                       k == 0, k == 7, [Bwin, Bxb], pb)
                            evac(U[:, g0:g0 + n_g, :], ps[:, 0:n_g * 132].rearrange("p (a b) -> p a b", b=132), pb, [BU])

                    U, BU = AU(6, 13)
                    inproj(U, BU, list(range(12)) + [CDT])
                    UDT = 12
                    if SUB < 2:
                        continue
                    zs, Bzs = r_zs.next()
                    c.op("act", lambda e, zs=zs, U=U: e.activation(out=zs[:], in_=U[:, CZ:CZ + 4, 2:130], func=AF.Silu),
                         reads=[BU], writes=[Bzs])
                    cv, Bcv = AS(0, 2)
                    for j in range(8):
                        eng = ("dve", "pool")[j % 2] if os.environ.get("MK_CONVP", "0") == "1" else "dve"
                        for k5 in range(5):
                            kk_ = k5 if fwd else 4 - k5
                            wk = pv[:, PV["cw"][0] + j * 5 + kk_:PV["cw"][0] + j * 5 + kk_ + 1]
                            src = U[:, CXS + j, k5:k5 + 128]
                            if k5 == 0:
                                c.op(eng, lambda e, j=j, wk=wk, src=src, cv=cv: e.tensor_scalar(
                                    out=cv[:, j, :], in0=src, scalar1=wk, scalar2=P("cb", j), op0=ALU.mult, op1=ALU.add),
                                     reads=[BU, Bpv], writes=[Bcv])
                            elif eng == "dve":
                                c.op(eng, lambda e, j=j, wk=wk, src=src, cv=cv: e.scalar_tensor_tensor(
                                    out=cv[:, j, :], in0=src, scalar=wk, in1=cv[:, j, :], op0=ALU.mult, op1=ALU.add),
                                     reads=[BU, Bpv, Bcv], writes=[Bcv])
                            else:
                                c.op(eng, lambda e, wk=wk, src=src: e.tensor_scalar(
                                    out=ptmp[:], in0=src, scalar1=wk, scalar2=None, op0=ALU.mult), reads=[BU, Bpv], writes=[Bptmp])
                                c.op(eng, lambda e, j=j, cv=cv: e.tensor_tensor(out=cv[:, j, :], in0=cv[:, j, :], in1=ptmp[:], op=ALU.add),
                                     reads=[Bptmp, Bcv], writes=[Bcv])
                    xsf, Bxsf = r_xsf.next()
                    xbc, Bxbc = r_xbc.next()
                    c.op("act", lambda e, xsf=xsf, cv=cv: e.activation(out=xsf[:], in_=cv[:, 0:4, :], func=AF.Silu),
                         reads=[Bcv], writes=[Bxsf])
                    c.op("act", lambda e, xbc=xbc, cv=cv: e.activation(out=xbc[:, 4:8, :], in_=cv[:, 4:8, :], func=AF.Silu),
                         reads=[Bcv], writes=[Bxbc])
                    c.op("act", lambda e, xbc=xbc, xsf=xsf: e.activation(out=xbc[:, 0:4, :], in_=xsf[:], func=AF.Copy),
                         reads=[Bxsf], writes=[Bxbc])
                    d8, Bd8 = AS(10, 1, parts=8)
                    c.op("act", lambda e, d8=d8, U=U: e.activation(out=d8[:, 0, :], in_=U[0:8, UDT, 2:130], func=AF.Exp,
                                                                   bias=dtb), reads=[BU, Bp8], writes=[Bd8])
                    c.op("act", lambda e, d8=d8: e.activation(out=d8[:, 1, :], in_=d8[:, 0, :], func=AF.Ln, bias=1.0),
                         reads=[Bd8], writes=[Bd8])
                    c.op("dve", lambda e, d8=d8: e.tensor_scalar(out=d8[:, 2, :], in0=d8[:, 1, :], scalar1=a8[:, 0:1],
                                                                 scalar2=None, op0=ALU.mult), reads=[Bd8, Ba8], writes=[Bd8])
                    c.op("dve", lambda e, d8=d8: e.tensor_tensor_scan(out=d8[:, 3, :], data0=ones_t[0:8, :], data1=d8[:, 2, :],
                                                                      initial=0.0, op0=ALU.mult, op1=ALU.add),
                         reads=[Bd8, Bones], writes=[Bd8])
                    for q, (src_i, r_i) in enumerate(((1, 0), (3, 2))):
                        x_ = d8[:, src_i, :]
                        r_ = d8[:, r_i, :]
                        c.op("dve", lambda e, q=q, x_=x_: e.tensor_copy(out=DT3[0:8, 0, q, :], in_=x_), reads=[Bd8], writes=[BDT3])
                        c.op("dve", lambda e, q=q, x_=x_, r_=r_: e.tensor_tensor(out=r_, in0=x_, in1=DT3[0:8, 0, q, :], op=ALU.subtract),
                             reads=[Bd8, BDT3], writes=[Bd8])
                        c.op("dve", lambda e, q=q, r_=r_: e.tensor_copy(out=DT3[0:8, 1, q, :], in_=r_), reads=[Bd8], writes=[BDT3])
                        c.op("dve", lambda e, q=q, r_=r_: e.tensor_tensor(out=r_, in0=r_, in1=DT3[0:8, 1, q, :], op=ALU.subtract),
                             reads=[Bd8, BDT3], writes=[Bd8])
                        c.op("dve", lambda e, q=q, r_=r_: e.tensor_copy(out=DT3[0:8, 2, q, :], in_=r_), reads=[Bd8], writes=[BDT3])
                    ps, pb = pp.next()
                    for q in range(2):
                        for i3 in range(3):
                            mm(ps[:, q * 128:(q + 1) * 128], DT3[:, i3, q, :], Kb("ident"), i3 == 0, i3 == 2, [BDT3, Bcsb], [pb[q]])
                    dtok, Bdtok = r_dtok.next()
                    c.op("dve", lambda e, dtok=dtok, ps=ps: e.tensor_copy(out=dtok[:, 0:8], in_=ps[:, 0:8]), reads=[pb[0], pb[1]], writes=[Bdtok])
                    c.op("dve", lambda e, dtok=dtok, ps=ps: e.tensor_copy(out=dtok[:, 8:16], in_=ps[:, 128:136]), reads=[pb[1]], writes=[Bdtok])
                    psR0, pbR0 = pp.next()
                    if os.environ.get("MK_SKIPB", "") == "1":
                        pp.next()
                    psR1, pbR1 = pp.next()
                    for i3 in range(3):
                        rb = i3 % 2
                        c.ew(lambda e, i3=i3, rb=rb: e.tensor_tensor(
                            out=RB3[0:8, rb, :, :], in0=m8b[:].rearrange("p (h l) -> p h l", l=128),
                            in1=DT3[0:8, i3, 1:2, :].to_broadcast([8, 8, 128]), op=ALU.mult), reads=[BDT3, Bm8], writes=[BRB3[rb]])
                        for (psR, pbR, h0) in ((psR0, pbR0, 0), (psR1, pbR1, 4)):
                            mm(psR[:], Kb("ones"), RB3[:, rb, h0:h0 + 4, :].rearrange("p a b -> p (a b)"), i3 == 0, i3 == 2, [Bcsb, BRB3[rb]], pbR)
                    seg, Bseg = AS(2, 2)
                    E, BE = AS(4, 2)
                    if os.environ.get("MK_FENCE", "-1") != "-1":
                        pbR0 = pbR0 + [Bfence]
                    for _d in range(int(os.environ.get("MK_DELAY", "0"))):
                        c.op("dve", lambda e: e.tensor_copy(out=ptmp[:], in_=ones_t[:]), reads=[pbR0, Bones], writes=[Bptmp])
                    for (psR, pbR, h0) in ((psR0, pbR0, 0), (psR1, pbR1, 4)):
                        c.op("dve", lambda e, psR=psR, h0=h0, seg=seg, dtok=dtok: e.tensor_tensor(
                            out=seg[:, h0:h0 + 4, :], in0=psR[:].rearrange("p (a b) -> p a b", b=128),
                            in1=dtok[:, 8 + h0:12 + h0].unsqueeze(2).to_broadcast([128, 4, 128]), op=ALU.subtract),
                             reads=[pbR, Bdtok], writes=[Bseg])
                    c.ew(lambda e, seg=seg: e.tensor_scalar(out=seg[:], in0=seg[:], scalar1=0.0, scalar2=None, op0=ALU.min), reads=[Bseg], writes=[Bseg])
                    c.op("act", lambda e, E=E, psR0=psR0: e.activation(out=E[:, 0:4, :], in_=psR0[:].rearrange("p (a b) -> p a b", b=128),
                                                                     func=AF.Exp), reads=pbR0, writes=[BE])
                    c.op("act", lambda e, E=E, psR1=psR1: e.activation(out=E[:, 4:8, :], in_=psR1[:].rearrange("p (a b) -> p a b", b=128),
                                                                     func=AF.Exp), reads=pbR1, writes=[BE])
                    c.op("act", lambda e, seg=seg: e.activation(out=seg[:], in_=seg[:], func=AF.Exp), reads=[Bseg], writes=[Bseg])
                    ps, pb = pp.next()
                    for g in range(2):
                        mm(ps[:, g * 128:(g + 1) * 128], xbc[:, 4 + g, :], xbc[:, 6 + g, :], True, True, [Bxbc], [pb[g]])
                    gm, Bgm = r_gm.next()
                    for g in range(2):
                        c.op("dve", lambda e, g=g, gm=gm, ps=ps: e.tensor_tensor(out=gm[:, g, :], in0=ps[:, g * 128:(g + 1) * 128],
                                                                                 in1=Kb("mui"), op=ALU.mult),
                             reads=[pb[g], Bcsb], writes=[Bgm])
                    MT, BMT = r_MT.next()
                    Cd, BCd = r_Cd.next()
                    for g in range(2):
                        c.ew(lambda e, g=g, MT=MT, seg=seg, gm=gm: e.tensor_tensor(
                            out=MT[:, 4 * g:4 * g + 4, :], in0=seg[:, 4 * g:4 * g + 4, :], in1=gm[:, g:g + 1, :].to_broadcast([128, 4, 128]), op=ALU.mult),
                             reads=[Bseg, Bgm], writes=[BMT])
                        c.ew(lambda e, g=g, Cd=Cd, E=E, xbc=xbc: e.tensor_tensor(
                            out=Cd[:, 4 * g:4 * g + 4, :], in0=E[:, 4 * g:4 * g + 4, :], in1=xbc[:, 6 + g:7 + g, :].to_broadcast([128, 4, 128]), op=ALU.mult),
                             reads=[BE, Bxbc], writes=[BCd])
                    ps, pb = pp.next()
                    for j in range(4):
                        mm(ps[:, j * 128:(j + 1) * 128], xbc[:, j, :], Kb("ident"), True, True, [Bxbc, Bcsb], [pb[j]])
                    xdt, Bxdt = r_xdt.next()
                    v864 = lambda ap: ap.rearrange("p (h q) -> p h q", q=64)
                    c.op("dve", lambda e, xdt=xdt, ps=ps, dtok=dtok: e.tensor_tensor(
                        out=v864(xdt[:, 0, :]), in0=v864(ps[:]), in1=dtok[:, 0:8].unsqueeze(2).to_broadcast([128, 8, 64]), op=ALU.mult),
                         reads=[pb, Bdtok], writes=[Bxdt])
                    c.ew(lambda e, xdt=xdt, seg=seg: e.tensor_tensor(
                        out=v864(xdt[:, 1, :]), in0=v864(xdt[:, 0, :]), in1=seg[:, :, 127:128].to_broadcast([128, 8, 64]), op=ALU.mult),
                         reads=[Bxdt, Bseg], writes=[Bxdt])
                    ps, pb = pp.next()
                    for g in range(2):
                        mm(ps[:, g * 128:(g + 1) * 128], xbc[:, 4 + g, :], Kb("ident"), True, True, [Bxbc, Bcsb], [pb[g]])
                    btok, Bbtok = r_btok.next()
                    evac(btok[:], ps[:, 0:256].rearrange("p (a b) -> p a b", b=128), pb[0:2], [Bbtok])
                    for h in range(8):
                        o_ = YS_t[:, h * 64:(h + 1) * 64]
                        mm(o_, MT[:, h, :], xdt[:, 0, h * 64:(h + 1) * 64], True, False, [BMT, Bxdt], [YS_b])
                        mm(o_, Cd[:, h, :], STb[:, h * 64:(h + 1) * 64], False, True, [BCd, BST], [YS_b])
                    ysh, Bysh = r_yhl.next()
                    split2(YS_t[:], ysh[:, 0, :], ysh[:, 1, :], [YS_b], Bysh, Bysh, psum=True)
                    ps, pb = pp.next()
                    for g in range(2):
                        mm(ps[:, g * 256:(g + 1) * 256], btok[:, g, :], xdt[:, 1, g * 256:(g + 1) * 256], True, True,
                           [Bbtok, Bxdt], pb[2 * g:2 * g + 2])
                    c.ew(lambda e, E=E: e.tensor_tensor(out=v864(STf[:]), in0=v864(STf[:]), in1=E[:, :, 127:128].to_broadcast([128, 8, 64]), op=ALU.mult),
                         reads=[BST, BE], writes=[BST])
                    c.op("dve", lambda e, ps=ps: e.tensor_tensor(out=STf[:], in0=STf[:], in1=ps[:], op=ALU.add), reads=[BST, pb], writes=[BST])
                    c.op("act", lambda e: e.activation(out=STb[:], in_=STf[:], func=AF.Copy), reads=[BST], writes=[BST])
                    if not fwd:
                        for q, key in enumerate(("ybs_h", "ybs_l")):
                            c.dma("sp", lambda e, ysh=ysh, q=q, key=key, ti=ti: e.dma_start(out=ysc[key][ti * 128:(ti + 1) * 128, :], in_=ysh[:, q, :]),
                                  reads=[Bysh], writes=[B_ysc[key][ti]])
                    if SUB < 3:
                        continue

                    U, BU = AU(4, 15)
                    inproj(U, BU, list(range(CR, CR + 15)))
                    us, Bus = AS(0, 4, nch=15)
                    tsh, Btsh = AS(8, 2)
                    bc15 = lambda ap, j0, n: ap[:, j0:j0 + n].unsqueeze(2).to_broadcast([128, n, 128])
                    c.ew(lambda e, us=us, U=U: e.tensor_tensor(out=us, in0=U[:, 0:15, 2:130], in1=bc15(der, 0, 15), op=ALU.mult),
                         reads=[BU, Bder], writes=[Bus])
                    for (j0, n) in ((0, 8), (8, 7)):
                        for (sl_, mun) in (((1, 129), mpn), ((3, 131), mnn)):
                            mu_ap = pv[:, PV[mun][0]:PV[mun][0] + 15]
                            c.ew(lambda e, j0=j0, n=n, sl_=sl_, mu_ap=mu_ap, U=U, tsh=tsh: e.tensor_tensor(
                                out=tsh[:, 0:n, :], in0=U[:, j0:j0 + n, sl_[0]:sl_[1]], in1=bc15(mu_ap, j0, n), op=ALU.mult),
                                 reads=[BU, Bpv], writes=[Btsh])
                            c.ew(lambda e, j0=j0, n=n, us=us, tsh=tsh: e.tensor_tensor(
                                out=us[:, j0:j0 + n, :], in0=us[:, j0:j0 + n, :], in1=tsh[:, 0:n, :], op=ALU.add),
                                 reads=[Bus, Btsh], writes=[Bus])
                    lrin, Blrin = r_lrin.next()
                    c.op("act", lambda e, us=us, lrin=lrin: e.activation(out=lrin[:, 0, :], in_=us[:, 12, :], func=AF.Tanh), reads=[Bus], writes=[Blrin])
                    c.op("act", lambda e, us=us, lrin=lrin: e.activation(out=lrin[:, 1, :], in_=us[:, 13, :], func=AF.Copy), reads=[Bus], writes=[Blrin])
                    c.op("act", lambda e, us=us, lrin=lrin: e.activation(out=lrin[:, 2, :], in_=us[:, 14, :], func=AF.Sigmoid), reads=[Bus], writes=[Blrin])
                    psL, pbL = pp.next()
                    psI, pbI = pp.next()
                    for cp in range(4):
                        mm(psL[:, cp * 128:(cp + 1) * 128], w2[:, cp * 128:(cp + 1) * 128], lrin[:, 0, :], True, True, [Blr, Blrin], [pbL[cp]])
                        mm(psI[:, cp * 128:(cp + 1) * 128], a2[:, cp * 128:(cp + 1) * 128], lrin[:, 1, :], True, True, [Blr, Blrin], [pbI[cp]])
                    sg, Bsg = AS(4)
                    icl, Bicl = AS(5)
                    bc4 = lambda n: P(n, 0, 4).unsqueeze(2).to_broadcast([128, 4, 128])
                    v4 = lambda ps_: ps_[:].rearrange("p (a b) -> p a b", b=128)
                    c.op("dve", lambda e, sg=sg, psL=psL: e.tensor_tensor(out=sg[:], in0=v4(psL), in1=bc4(w0n), op=ALU.add), reads=[pbL, Bpv], writes=[Bsg])
                    c.op("act", lambda e, sg=sg: e.activation(out=sg[:], in_=sg[:], func=AF.Sigmoid), reads=[Bsg], writes=[Bsg])
                    c.op("dve", lambda e, icl=icl, psI=psI: e.tensor_tensor(out=icl[:], in0=v4(psI), in1=bc4("a0"), op=ALU.add), reads=[pbI, Bpv], writes=[Bicl])
                    c.op("act", lambda e, icl=icl: e.activation(out=icl[:], in_=icl[:], func=AF.Sigmoid), reads=[Bicl], writes=[Bicl])
                    for cp in range(4):
                        c.op("dve", lambda e, cp=cp, sg=sg: e.tensor_tensor_scan(
                            out=CSt[:, cp, 1:129], data0=ones_t[:], data1=sg[:, cp, :], initial=0.0, op0=ALU.mult, op1=ALU.add),
                             reads=[Bsg, Bones, BCSt], writes=[BCSt])
                    gam, Bgam = AS(6)
                    igam, Bigam = AS(7)
                    gprev, Bgprev = AS(8)
                    c.op("act", lambda e, gam=gam: e.activation(out=gam[:], in_=CSt[:, :, 1:129], func=AF.Exp, scale=-CDEC),
                         reads=[BCSt], writes=[Bgam])
                    c.op("act", lambda e, igam=igam: e.activation(out=igam[:], in_=CSt[:, :, 1:129], func=AF.Exp, scale=CDEC),
                         reads=[BCSt], writes=[Bigam])
                    c.op("act", lambda e, gprev=gprev: e.activation(out=gprev[:], in_=CSt[:, :, 0:128], func=AF.Exp, scale=-CDEC),
                         reads=[BCSt], writes=[Bgprev])
                    kkt, Bkk = AS(9)
                    c.ew(lambda e, kkt=kkt, us=us: e.tensor_tensor(out=kkt[:], in0=us[:, 4:8, :], in1=bc4("kk"), op=ALU.mult), reads=[Bus, Bpv], writes=[Bkk])
                    sq, Bsq = sg, Bsg
                    c.ew(lambda e, sq=sq, kkt=kkt: e.tensor_tensor(out=sq[:], in0=kkt[:], in1=kkt[:], op=ALU.mult), reads=[Bkk], writes=[Bsq])
                    psS, pbS = blocksum4(sq[:], Bsq, "bo64")
                    rn, Brn = AS(10)
                    c.op("dve", lambda e, rn=rn, psS=psS: e.tensor_scalar(out=rn[:], in0=psS, scalar1=1e-24, scalar2=None, op0=ALU.max),
                         reads=pbS, writes=[Brn])
                    c.op("act", lambda e, rn=rn: e.activation(out=rn[:], in_=rn[:], func=AF.Sqrt), reads=[Brn], writes=[Brn])
                    c.op("dve", lambda e, rn=rn: e.reciprocal(out=rn[:], in_=rn[:]), reads=[Brn], writes=[Brn])
                    c.ew(lambda e, kkt=kkt, rn=rn: e.tensor_tensor(out=kkt[:], in0=kkt[:], in1=rn[:], op=ALU.mult), reads=[Bkk, Brn], writes=[Bkk])
                    km, Bkm = rn, Brn
                    c.ew(lambda e, km=km, icl=icl: e.tensor_tensor(out=km[:], in0=icl[:], in1=bc4("ka"), op=ALU.mult), reads=[Bicl, Bpv], writes=[Bkm])
                    c.ew(lambda e, km=km: e.tensor_tensor(out=km[:], in0=km[:], in1=der[:, 15:19].unsqueeze(2).to_broadcast([128, 4, 128]), op=ALU.add),
                         reads=[Bkm, Bder], writes=[Bkm])
                    c.ew(lambda e, km=km, us=us: e.tensor_tensor(out=km[:], in0=km[:], in1=us[:, 4:8, :], op=ALU.mult), reads=[Bkm, Bus], writes=[Bkm])
                    if fwd:
                        rk_, Brk = sq, Bsq
                        c.ew(lambda e, rk_=rk_, us=us: e.tensor_tensor(out=rk_[:], in0=us[:, 0:4, :], in1=bc4("rk"), op=ALU.mult), reads=[Bus, Bpv], writes=[Brk])
                        c.ew(lambda e, rk_=rk_, km=km: e.tensor_tensor(out=rk_[:], in0=rk_[:], in1=km[:], op=ALU.mult), reads=[Brk, Bkm], writes=[Brk])
                        psB, pbB = blocksum4(rk_[:], Brk, "bo64")
                        bonv, Bbonv = r_bg.next()
                        c.op("dve", lambda e, bonv=bonv, psB=psB, us=us: e.tensor_tensor(out=bonv[:], in0=psB, in1=us[:, 8:12, :], op=ALU.mult),
                             reads=pbB + [Bus], writes=[Bbonv])
                        psG, pbG = pp.next()
                        for cp in range(4):
                            mm(psG[:, cp * 128:(cp + 1) * 128], g2[:, cp * 128:(cp + 1) * 128], lrin[:, 2, :], True, True, [Blr, Blrin], [pbG[cp]])
                        gT, BgT = r_bg.next()
                        c.op("act", lambda e, gT=gT, psG=psG: e.activation(out=gT[:], in_=psG[:].rearrange("p (a b) -> p a b", b=128), func=AF.Copy),
                             reads=pbG, writes=[BgT])
                    KT, BKT = r_op.next()
                    BT, BBT = r_op.next()
                    AT, BAT = r_op.next()
                    vb, Bvb = r_op.next()
                    c.ew(lambda e, KT=KT, km=km, igam=igam: e.tensor_tensor(out=KT[:], in0=km[:], in1=igam[:], op=ALU.mult),
                         reads=[Bkm, Bigam], writes=[BKT])
                    c.ew(lambda e, icl=icl, kkt=kkt: e.tensor_tensor(out=icl[:], in0=icl[:], in1=kkt[:], op=ALU.mult), reads=[Bicl, Bkk], writes=[Bicl])
                    c.ew(lambda e, BT=BT, icl=icl, igam=igam: e.tensor_tensor(out=BT[:], in0=icl[:], in1=igam[:], op=ALU.mult),
                         reads=[Bicl, Bigam], writes=[BBT])
                    c.op("dve", lambda e, AT=AT, kkt=kkt, gprev=gprev: e.scalar_tensor_tensor(out=AT[:], in0=kkt[:], scalar=-1.0, in1=gprev[:],
                                                                                             op0=ALU.mult, op1=ALU.mult),
                         reads=[Bkk, Bgprev], writes=[BAT])
                    c.ew(lambda e, vb=vb, us=us: e.tensor_copy(out=vb[:], in_=us[:, 8:12, :]), reads=[Bus], writes=[Bvb])
                    c.ew(lambda e, AT=AT: e.tensor_copy(out=ARe_t[0:64, :, 0:128], in_=AT[0:64, :, :]), reads=[BAT], writes=[BAR_t])
                    c.ew(lambda e, AT=AT: e.tensor_copy(out=ARo_t[64:128, :, 0:128], in_=AT[64:128, :, :]), reads=[BAT], writes=[BAR_t])
                    c.ew(lambda e, us=us, gam=gam: e.tensor_tensor(out=ARe_t[0:64, :, 128:256], in0=us[0:64, 0:4, :], in1=gam[0:64, :, :],
                                                                   op=ALU.mult), reads=[Bus, Bgam], writes=[BAR_t])
                    c.ew(lambda e, us=us, gam=gam: e.tensor_tensor(out=ARo_t[64:128, :, 128:256], in0=us[64:128, 0:4, :], in1=gam[64:128, :, :],
                                                                   op=ALU.mult), reads=[Bus, Bgam], writes=[BAR_t])
                    if SUB < 4:
                        continue
                    if fwd:
                        src_t = NT - 1 - ti
                        ybS, BybS = r_ybl.next()
                        ybR, BybR = r_ybl.next()
                        for (dst_, Bdst_, kh, kl) in ((ybS, BybS, "ybs_h", "ybs_l"), (ybR, BybR, "ybr_h", "ybr_l")):
                            for q, key in enumerate((kh, kl)):
                                c.dma("sp", lambda e, dst_=dst_, q=q, key=key, src_t=src_t: e.dma_start(
                                    out=dst_[:, q, :], in_=ysc[key][src_t * 128:(src_t + 1) * 128, :]),
                                      reads=[B_ysc[key][src_t]], writes=[Bdst_])
                    heads_all = [(cp, x) for cp in range(4) for x in range(2)]
                    for g0 in range(0, 8, LS):
                        grp = heads_all[g0:g0 + LS]
                        cps = sorted(set(cp for cp, _ in grp))
                        toks, wus, st = {}, {}, {}
                        for cp in cps:
                            ps, pb = pp.next()
                            for q, src in enumerate((AT, BT, KT, vb)):
                                mm(ps[:, q * 128:(q + 1) * 128], src[:, cp, :], Kb("ident"), True, True, [BAT, BBT, BKT, Bvb, Bcsb], [pb[q]])
                            tok, Btok = r_tok.next()
                            evac(tok[:], ps[:].rearrange("p (a b) -> p a b", b=128), pb, [Btok])
                            toks[cp] = (tok, Btok)
                            wus[cp] = r_wu.next()
                        for (cp, x) in grp:
                            tok, Btok = toks[cp]
                            AR_x = (ARe_t, ARo_t)[x]
                            hs = slice(x * 64, (x + 1) * 64)
                            psA, pbA = pp.next()
                            psB4, pbB4 = pp.next()
                            mm(psA[:, 0:128], AR_x[:, cp, 0:128], BT[:, cp, :], True, True, [BAR_t, BBT], [pbA[0]])
                            mm(psB4[:, 0:256], BT[:, cp, :], AR_x[:, cp, :], True, True, [BAR_t, BBT], pbB4[0:2])
                            mm(psB4[:, 256:512], KT[:, cp, :], AR_x[:, cp, :], True, True, [BAR_t, BKT], pbB4[2:4])
                            p0, Bp0 = r_p0.next()
                            c.op("dve", lambda e, p0=p0, psA=psA: e.tensor_tensor(out=p0[:], in0=psA[:, 0:128], in1=Kb("msl"), op=ALU.mult),
                                 reads=[pbA[0], Bcsb], writes=[Bp0])
                            e4, Be4 = r_e4.next()
                            c.op("dve", lambda e, e4=e4, psB4=psB4: e.tensor_tensor(out=e4[:], in0=psB4[:], in1=mask4[:], op=ALU.mult),
                                 reads=pbB4 + [Bm4], writes=[Be4])
                            mm(psA[:, 128:192], e4[:, 256:384], tok[:, 3, hs], True, True, [Be4, Btok], [pbA[1]])
                            zf, Bzf = r_zf.next()
                            zb, Bzb = r_zb.next()
                            c.op("act", lambda e, zf=zf, tok=tok, hs=hs: e.activation(out=zf[:, 0:64], in_=tok[:, 0, hs], func=AF.Copy),
                                 reads=[Btok], writes=[Bzf])
                            c.op("act", lambda e, zf=zf, psA=psA: e.activation(out=zf[:, 64:128], in_=psA[:, 128:192], func=AF.Copy),
                                 reads=[pbA[1]], writes=[Bzf])
                            c.ew(lambda e, zf=zf, zb=zb: e.tensor_copy(out=zb[:], in_=zf[:]), reads=[Bzf], writes=[Bzb])
                            st[(cp, x)] = dict(e4=e4, Be4=Be4, zf=zf, Bzf=Bzf, zb=zb, Bzb=Bzb, Pn=p0[:], PnT=e4[:, 0:128], BPn=[Bp0, Be4])
                        for it in range(7):
                            for (cp, x) in grp:
                                d_ = st[(cp, x)]
                                zf, Bzf, zb, Bzb = d_["zf"], d_["Bzf"], d_["zb"], d_["Bzb"]
                                Pn, PnT, BPn = d_["Pn"], d_["PnT"], d_["BPn"]
                                psZ, pbZ = pp.next()
                                mm(psZ[:, 0:128], PnT, zb[:], True, True, BPn + [Bzb], [pbZ[0]])
                                if it < 6:
                                    mm(psZ[:, 128:256], PnT, Pn, True, True, BPn, [pbZ[1]])
                                    mm(psZ[:, 256:384], Pn, PnT, True, True, BPn, [pbZ[2]])
                                c.op("dve", lambda e, zf=zf, psZ=psZ: e.tensor_tensor(out=zf[:], in0=psZ[:, 0:128], in1=zf[:], op=ALU.add),
                                     reads=[pbZ[0], Bzf], writes=[Bzf])
                                if it < 6:
                                    ppq, Bppq = r_pp.next()
                                    c.op("act", lambda e, ppq=ppq, psZ=psZ: e.activation(out=ppq[:], in_=psZ[:, 128:384], func=AF.Copy),
                                         reads=pbZ[1:3], writes=[Bppq])
                                    c.ew(lambda e, zf=zf, zb=zb: e.tensor_copy(out=zb[:], in_=zf[:]), reads=[Bzf], writes=[Bzb])
                                    d_["Pn"], d_["PnT"], d_["BPn"] = ppq[:, 0:128], ppq[:, 128:256], [Bppq]
                        for (cp, x) in grp:
                            d_ = st[(cp, x)]
                            wu, Bwu = wus[cp]
                            hs = slice(x * 64, (x + 1) * 64)
                            zf, Bzf = d_["zf"], d_["Bzf"]
                            c.op("act", lambda e, wu=wu, zf=zf, hs=hs: e.activation(out=wu[:, 0, hs], in_=zf[:, 0:64], func=AF.Copy),
                                 reads=[Bzf], writes=[Bwu])
                            c.ew(lambda e, wu=wu, zf=zf, hs=hs: e.tensor_copy(out=wu[:, 1, hs], in_=zf[:, 64:128]),
                                 reads=[Bzf], writes=[Bwu])
                        for cp in cps:
                            if (cp, 0) not in st or (cp, 1) not in st:
                                continue
                            tok, Btok = toks[cp]
                            wu, Bwu = wus[cp]
                            qt, Bqt = r_qt.next()
                            psQ, pbQ = pp.next()
                            for x in range(2):
                                e4, Be4 = st[(cp, x)]["e4"], st[(cp, x)]["Be4"]
                                AR_x = (ARe_t, ARo_t)[x]
                                hs = slice(x * 64, (x + 1) * 64)
                                mm(psQ[:, x * 128:(x + 1) * 128], wu[:, 0, :], e4[:, 128:256], True, True, [Bwu, Be4], [pbQ[x]])
                            for x in range(2):
                                AR_x = (ARe_t, ARo_t)[x]
                                hs = slice(x * 64, (x + 1) * 64)
                                c.op("dve", lambda e, qt=qt, psQ=psQ, x=x, hs=hs, AR_x=AR_x, cp=cp: e.tensor_tensor(
                                    out=qt[hs, :], in0=psQ[hs, x * 128:(x + 1) * 128], in1=AR_x[hs, cp, 128:256], op=ALU.add),
                                     reads=[pbQ[x], BAR_t], writes=[Bqt])
                            yo = YR_t[:, cp * 128:(cp + 1) * 128]
                            mm(yo, qt[:], Hb[:, cp, :], True, False, [Bqt, BH[cp]], [YR_b])
                            for x in range(2):
                                e4, Be4 = st[(cp, x)]["e4"], st[(cp, x)]["Be4"]
                                hs = slice(x * 64, (x + 1) * 64)
                                yx = YR_t[:, cp * 128 + x * 64:cp * 128 + (x + 1) * 64]
                                mm(yx, e4[:, 128:256], wu[:, 1, hs], False, False, [Be4, Bwu], [YR_b])
                                mm(yx, e4[:, 384:512], tok[:, 3, hs], False, x == 1, [Be4, Btok], [YR_b])
                            psM, pbM = pp.next()
                            mm(psM[:, 0:128], wu[:, 0, :], tok[:, 1, :], True, True, [Bwu, Btok], [pbM[0]])
                            mtb, Bmtb = r_mtb.next()
                            c.op("dve", lambda e, mtb=mtb, psM=psM: e.tensor_tensor(out=mtb[:], in0=psM[:, 0:128], in1=Kb("bmask"), op=ALU.mult),
                                 reads=[pbM[0], Bcsb], writes=[Bmtb])
                            psH, pbH = pp.next()
                            hsl = psH[:, 0:128]
                            mm(hsl, tok[:, 1, :], wu[:, 1, :], True, False, [Btok, Bwu], [pbH[0]])
                            mm(hsl, tok[:, 2, :], tok[:, 3, :], False, False, [Btok], [pbH[0]])
                            mm(hsl, Kb("ident"), Hb[:, cp, :], False, False, [Bcsb, BH[cp]], [pbH[0]])
                            mm(hsl, mtb[:], Hb[:, cp, :], False, True, [Bmtb, BH[cp]], [pbH[0]])
                            c.op("dve", lambda e, cp=cp, psH=psH, gam=gam: e.scalar_tensor_tensor(
                                out=Hb[:, cp, :], in0=psH[:, 0:128], scalar=gam[:, cp, 127:128], in1=Kb("bmask"), op0=ALU.mult, op1=ALU.mult),
                                 reads=[pbH[0], Bgam, Bcsb], writes=[BH[cp]])
                    yrh, Byrh = r_yhl.next()
                    split2(YR_t[:], yrh[:, 0, :], yrh[:, 1, :], [YR_b], Byrh, Byrh, psum=True)
                    if not fwd:
                        for q, key in enumerate(("ybr_h", "ybr_l")):
                            c.dma("sp", lambda e, yrh=yrh, q=q, key=key, ti=ti: e.dma_start(out=ysc[key][ti * 128:(ti + 1) * 128, :], in_=yrh[:, q, :]),
                                  reads=[Byrh], writes=[B_ysc[key][ti]])
                        continue
                    if SUB < 5:
                        continue

                    mix, Bmix = r_mix.next()

                    def ytrans(yh, Byh, yb, Byb):
                        ps, pb = pp.next()
                        for j in range(4):
                            o_ = ps[:, j * 128:(j + 1) * 128]
                            cs_ = slice(j * 128, (j + 1) * 128)
                            mm(o_, yh[:, 0, cs_], Kb("ident"), True, False, [Byh, Bcsb], [pb[j]])
                            mm(o_, yh[:, 1, cs_], Kb("ident"), False, False, [Byh, Bcsb], [pb[j]])
                            mm(o_, yb[:, 0, cs_], Kb("J"), False, False, [Byb, Bcsb], [pb[j]])
                            mm(o_, yb[:, 1, cs_], Kb("J"), False, True, [Byb, Bcsb], [pb[j]])
                        return ps, pb
                    ps, pb = ytrans(ysh, Bysh, ybS, BybS)
                    yg, Byg = AS(0)
                    for j in range(4):
                        c.op("dve", lambda e, j=j, yg=yg, xsf=xsf, ps=ps: e.scalar_tensor_tensor(
                            out=yg[:, j, :], in0=xsf[:, j, :], scalar=P("dch", j), in1=ps[:, j * 128:(j + 1) * 128], op0=ALU.mult, op1=ALU.add),
                             reads=[Bxsf, Bpv, pb[j]], writes=[Byg])
                    c.ew(lambda e, yg=yg, zs=zs: e.tensor_tensor(out=yg[:], in0=yg[:], in1=zs[:], op=ALU.mult), reads=[Byg, Bzs], writes=[Byg])
                    sq2, Bsq2 = AS(2)
                    c.ew(lambda e, sq2=sq2, yg=yg: e.tensor_tensor(out=sq2[:], in0=yg[:], in1=yg[:], op=ALU.mult), reads=[Byg], writes=[Bsq2])
                    hi, Bhi = r_sp.next()
                    lo, Blo = r_sp.next()
                    split2(sq2[:], hi[:], lo[:], [Bsq2], Bhi, Blo)
                    ps, pb = pp.next()
                    for g in range(2):
                        o_ = ps[:, g * 128:(g + 1) * 128]
                        for j in range(2):
                            mm(o_, Kb("o256"), hi[:, 2 * g + j, :], j == 0, False, [Bcsb, Bhi], [pb[g]])
                            mm(o_, Kb("o256"), lo[:, 2 * g + j, :], False, j == 1, [Bcsb, Blo], [pb[g]])
                    rs, Brs = r_st.next()
                    rs2, Brs2 = r_st.next()
                    rsqrt_eps(rs[:], ps[:, 0:128], RMS_EPS, [pb[0]], [Brs])
                    rsqrt_eps(rs2[:], ps[:, 128:256], RMS_EPS, [pb[1]], [Brs2])
                    for j in range(4):
                        rr_, Brr_ = (rs, Brs) if j < 2 else (rs2, Brs2)
                        c.op("dve", lambda e, j=j, mix=mix, yg=yg, rr_=rr_: e.scalar_tensor_tensor(
                            out=mix[:, j, :], in0=yg[:, j, :], scalar=P("nw", j), in1=rr_[:], op0=ALU.mult, op1=ALU.mult),
                             reads=[Byg, Bpv, Brr_], writes=[Bmix])
                    ps, pb = ytrans(yrh, Byrh, ybR, BybR)
                    yf, Byf = AS(1)
                    evac(yf[:], ps[:].rearrange("p (a b) -> p a b", b=128), pb, [Byf])
                    psm, pbm = blocksum4(yf[:], Byf, "bo64m")
                    c.op("dve", lambda e, yf=yf, psm=psm: e.tensor_tensor(out=yf[:], in0=yf[:], in1=psm, op=ALU.subtract), reads=[Byf] + pbm, writes=[Byf])
                    c.ew(lambda e, sq2=sq2, yf=yf: e.tensor_tensor(out=sq2[:], in0=yf[:], in1=yf[:], op=ALU.mult), reads=[Byf], writes=[Bsq2])
                    psv, pbv = blocksum4(sq2[:], Bsq2, "bo64m")
                    rsqrt_eps(sq2[:], psv, GN_EPS, pbv, [Bsq2])
                    c.ew(lambda e, yf=yf, sq2=sq2: e.tensor_tensor(out=yf[:], in0=yf[:], in1=sq2[:], op=ALU.mult), reads=[Byf, Bsq2], writes=[Byf])
                    c.ew(lambda e, yf=yf: e.tensor_tensor(out=yf[:], in0=yf[:], in1=bc4("lw"), op=ALU.mult), reads=[Byf, Bpv], writes=[Byf])
                    c.ew(lambda e, yf=yf: e.tensor_tensor(out=yf[:], in0=yf[:], in1=bc4("lb"), op=ALU.add), reads=[Byf, Bpv], writes=[Byf])
                    c.ew(lambda e, yf=yf, bonv=bonv: e.tensor_tensor(out=yf[:], in0=yf[:], in1=bonv[:], op=ALU.add), reads=[Byf, Bbonv], writes=[Byf])
                    c.ew(lambda e, mix=mix, yf=yf, gT=gT: e.tensor_tensor(out=mix[:, 4:8, :], in0=yf[:], in1=gT[:], op=ALU.mult),
                         reads=[Byf, BgT], writes=[Bmix])
                    xf, Bxf = AS(9, 2)
                    c.dma("sp", lambda e, xf=xf, col0=col0: e.dma_start(
                        out=xf, in_=xT[:, col0 + 2:col0 + 130].rearrange("(k p) t -> p k t", p=128)), writes=[Bxf])
                    pre, Bpre = AS(3, 2)
                    for half in range(2):
                        ps, pb = pp.next()
                        for j in range(4):
                            dc = half * 4 + j
                            for k in range(8):
                                mm(ps[:, j * 128:(j + 1) * 128], wo[:, k, dc * 128:(dc + 1) * 128], mix[:, k, :], k == 0, k == 7,
                                   [Bwo, Bmix], [pb[j]])
                        c.op("dve", lambda e, half=half, pre=pre, xf=xf, ps=ps: e.scalar_tensor_tensor(
                            out=pre[:, half * 4:(half + 1) * 4, :], in0=xf[:, half * 4:(half + 1) * 4, :], scalar=ALPHA,
                            in1=ps[:].rearrange("p (a b) -> p a b", b=128), op0=ALU.mult, op1=ALU.add), reads=[Bxf] + pb, writes=[Bpre])
                    hout, Bhout = AS(7, 2)
                    sqL, BsqL = AS(5, 2)
                    layer_norm(pp, Kb, pre, Bpre, hout, Bhout, sqL, BsqL, r_st, r_sps, P, "l1w", "l1b", Bcsb, Bpv, 128)
                    t0 = ti * C
                    c.dma("sp", lambda e, hout=hout, t0=t0: e.dma_start(
                        out=hT[:, t0:t0 + 128].rearrange("(k p) t -> p k t", p=128), in_=hout), reads=[Bhout], writes=[B_hT[ti]])
                c.full_barrier()
                c.run_block()

        def ffn_phase():
            NF = 256 if TT % 256 == 0 else 128
            with ExitStack() as es:
                sb = lambda n, s, dt=F32: es.enter_context(nc.sbuf_tensor(n, s, dt))
                rot = lambda n, s, dt=F32, k=2: Rot(nc, es, n, s, dt, k)
                pp = PsumPool(nc, es, ["pf%d" % i for i in range(8)])
                wfi = sb("wfi", [128, 8, 2 * DFF], BF16); Bwfi = Buf()
                wfo = sb("wfo", [128, NFC, D], BF16); Bwfo = Buf()
                for k in range(8):
                    c.dma("pool", lambda e, k=k: e.dma_start(out=wfi[:, k, :], in_=w_fi[k * 128:(k + 1) * 128, :]), writes=[Bwfi])
                for k in range(NFC):
                    c.dma("pool", lambda e, k=k: e.dma_start(out=wfo[:, k, :], in_=w_fo[k * 128:(k + 1) * 128, :]), writes=[Bwfo])
                pv = sb("pv2", [128, NPV]); Bpv = Buf()
                c.dma("sp", lambda e: e.dma_start(out=pv[:], in_=pvec), writes=[Bpv])
                csb = sb("csb2", [128, NCST * 128], BF16); Bcsb = Buf()
                c.dma("pool", lambda e: e.dma_start(out=csb[:], in_=cst), writes=[Bcsb])
                Kb = lambda n: csb[:, CST[n] * 128:(CST[n] + 1) * 128]
                P = lambda n, j=0, w=1: pv[:, PV[n][0] + j:PV[n][0] + j + w]
                r_hf = rot("hf", [128, 8, NF], F32, 1)
                r_hb = rot("hb", [128, 8, NF], BF16, 1)
                r_act = rot("actt", [128, NFC, NF], BF16, 1)
                r_sl = rot("sl", [128, NF], F32, 2)
                r_sq = rot("sq2", [128, 8, NF], F32, 1)
                r_st = rot("st2", [128, NF], F32, 4)
                r_sps = rot("sps2", [128, NF], BF16, 2)
                for fi in range(min(TT // NF, NTL)):
                    t0 = fi * NF
                    deps = [B_hT[t] for t in range(t0 // C, (t0 + NF) // C)]
                    hf, Bhf = r_hf.next()
                    hb, Bhb = r_hb.next()
                    c.dma("sp", lambda e, hf=hf, t0=t0: e.dma_start(out=hf[:], in_=hT[:, t0:t0 + NF].rearrange("(k p) t -> p k t", p=128)),
                          reads=deps, writes=[Bhf])
                    c.op("act", lambda e, hf=hf, hb=hb: e.activation(out=hb[:], in_=hf[:], func=AF.Copy), reads=[Bhf], writes=[Bhb])
                    actt, Bact = r_act.next()
                    for fc in range(NFC):
                        ps, pb = pp.next()
                        for k in range(8):
                            mm(ps[:, 0:NF], wfi[:, k, fc * 128:(fc + 1) * 128], hb[:, k, :], k == 0, k == 7, [Bwfi, Bhb], pb[0:2])
                        for k in range(8):
                            mm(ps[:, 256:256 + NF], wfi[:, k, DFF + fc * 128:DFF + (fc + 1) * 128], hb[:, k, :], k == 0, k == 7,
                               [Bwfi, Bhb], pb[2:4])
                        sl, Bsl = r_sl.next()
                        c.op("act", lambda e, sl=sl, ps=ps: e.activation(out=sl[:], in_=ps[:, 0:NF], func=AF.Silu), reads=pb[0:2], writes=[Bsl])
                        c.op("dve", lambda e, fc=fc, sl=sl, ps=ps, actt=actt: e.tensor_tensor(
                            out=actt[:, fc, :], in0=ps[:, 256:256 + NF], in1=sl[:], op=ALU.mult), reads=pb[2:4] + [Bsl], writes=[Bact])
                    for half in range(4):
                        ps, pb = pp.next()
                        for j in range(2):
                            dc = half * 2 + j
                            for k in range(NFC):
                                mm(ps[:, j * 256:j * 256 + NF], wfo[:, k, dc * 128:(dc + 1) * 128], actt[:, k, :], k == 0, k == NFC - 1,
                                   [Bwfo, Bact], pb[2 * j:2 * j + 2])
                        for j in range(2):
                            dc = half * 2 + j
                            c.op("dve", lambda e, dc=dc, j=j, hf=hf, ps=ps: e.scalar_tensor_tensor(
                                out=hf[:, dc, :], in0=hf[:, dc, :], scalar=ALPHA, in1=ps[:, j * 256:j * 256 + NF], op0=ALU.mult, op1=ALU.add),
                                 reads=[Bhf] + pb[2 * j:2 * j + 2], writes=[Bhf])
                    sq, Bsq = r_sq.next()
                    out, Bout = sq, Bsq
                    layer_norm(pp, Kb, hf[:], Bhf, out[:], Bout, sq[:], Bsq, r_st, r_sps, P, "l2w", "l2b", Bcsb, Bpv, NF)
                    Bo = Buf()
                    B_out.append(Bo)
                    c.dma("sp", lambda e, out=out, t0=t0: e.dma_start(out=yT[:, t0:t0 + NF].rearrange("(k p) t -> p k t", p=128), in_=out[:]),
                          reads=[Bout], writes=[Bo])
                c.wait_all("sp", B_out)
                c.run_block()

        mixer_phase(False)
        if stage >= 2:
            mixer_phase(True)
        if stage >= 3:
            ffn_phase()
    return nc


def _consts():
    i = np.arange(128)
    p, f = i[:, None], i[None, :]
    blocks = {
        "ident": (p == f), "J": (p + f == 127), "msl": (f < p), "msu": (p < f), "mui": (p <= f),
        "bmask": ((p // 64) == (f // 64)), "bo64": ((p // 64) == (f // 64)),
    }
    out = np.zeros((128, NCST * 128), np.float32)
    for n, m in blocks.items():
        out[:, CST[n] * 128:(CST[n] + 1) * 128] = m.astype(np.float32)
    out[:, CST["bo64m"] * 128:(CST["bo64m"] + 1) * 128] = blocks["bo64"].astype(np.float32) / 64.0
    out[:, CST["o1024"] * 128:(CST["o1024"] + 1) * 128] = 1.0 / 1024.0
    out[:, CST["o256"] * 128:(CST["o256"] + 1) * 128] = 1.0 / 256.0
    out[:, CST["ones"] * 128:(CST["ones"] + 1) * 128] = 1.0
    m8 = np.zeros((8, 1024), np.float32)
    for h in range(8):
        m8[h, h * 128:(h + 1) * 128] = 1.0
    return out, m8


def _chunkvec(v, n):
    return np.ascontiguousarray(np.asarray(v, np.float32).reshape(n, 128).T)


def prep_params(inp):
    g = lambda n: np.asarray(inp[n], np.float32)[0]
    w_in = g("w_in")
    M_W, CONV = 512, 1024
    o_z, o_xbc, o_dt = 0, 512, 1536
    o_r = 1544
    W = np.zeros((D, NCH_IN * 128), np.float32)
    W[:, 0:512] = w_in[:, o_z:o_z + 512]
    W[:, 512:1536] = w_in[:, o_xbc:o_xbc + 1024]
    W[:, 1536:3072] = w_in[:, o_r:o_r + 1536]
    W[:, CWL * 128:CWL * 128 + 64] = w_in[:, o_r + 1536:o_r + 1600]
    W[:, CAL * 128:CAL * 128 + 64] = w_in[:, o_r + 1600:o_r + 1664]
    W[:, CGL * 128:CGL * 128 + 128] = w_in[:, o_r + 1664:o_r + 1792]
    W[:, CDT * 128:CDT * 128 + 8] = w_in[:, o_dt:o_dt + 8]
    pv = np.zeros((128, NPV), np.float32)
    def put(name, arr):
        o, w = PV[name]
        pv[:, o:o + w] = arr
    cw = g("conv_w")
    cwp = np.zeros((128, 8, 5), np.float32)
    for j in range(8):
        cwp[:, j, :] = cw[:, j * 128:(j + 1) * 128].T
    put("cw", cwp.reshape(128, 40))
    put("cb", _chunkvec(g("conv_b"), 8))
    put("dch", _chunkvec(np.repeat(g("m_d"), 64), 4))
    put("nw", _chunkvec(g("m_norm_w"), 4))
    def mu15(v):
        o = np.zeros((128, 15), np.float32)
        o[:, 0:12] = _chunkvec(v[0:1536], 12)
        o[0:64, 12] = v[1536:1600]
        o[0:64, 13] = v[1600:1664]
        o[:, 14] = v[1664:1792]
        return o
    put("mp", mu15(g("r_mu_prev")))
    put("mn", mu15(g("r_mu_next")))
    put("w0f", _chunkvec(g("r_w0_f"), 4))
    put("w0b", _chunkvec(g("r_w0_b"), 4))
    put("a0", _chunkvec(g("r_a0"), 4))
    put("kk", _chunkvec(g("r_k_k"), 4))
    put("ka", _chunkvec(g("r_k_a"), 4))
    put("rk", _chunkvec(g("r_r_k").reshape(-1), 4))
    put("lw", _chunkvec(g("r_lnx_w"), 4))
    put("lb", _chunkvec(g("r_lnx_b"), 4))
    put("l1w", _chunkvec(g("ln1_w"), 8))
    put("l1b", _chunkvec(g("ln1_b"), 8))
    put("l2w", _chunkvec(g("ln2_w"), 8))
    put("l2b", _chunkvec(g("ln2_b"), 8))
    p8 = np.zeros((8, 8), np.float32)
    p8[:, 0] = g("m_dt_bias_f"); p8[:, 1] = g("m_dt_bias_b")
    p8[:, 2] = g("m_a_log_f"); p8[:, 3] = g("m_a_log_b")
    lowr = np.zeros((128, 4 * 512), np.float32)
    lowr[0:64, 0:512] = g("r_w2_f")
    lowr[0:64, 512:1024] = g("r_w2_b")
    lowr[0:64, 1024:1536] = g("r_a2")
    lowr[:, 1536:2048] = g("r_g2")
    cst, m8 = _consts()
    return dict(w_in=W, w_out=np.ascontiguousarray(g("w_out")), w_fi=np.ascontiguousarray(g("w_ffn_in")),
                w_fo=np.ascontiguousarray(g("w_ffn_out")), p8=p8, lowr=lowr, cst=cst, m8=m8), pv


def layout_core(units, link, TU):
    xT = np.zeros((D, 2 * (TU + 4)), np.float32)
    xTr = np.zeros((D, 2 * (TU + 4)), np.float32)
    for u, (seq, s) in enumerate(units):
        if seq is None:
            continue
        T = seq.shape[0]
        lo, hi = s - 2, s + TU + 2
        blk = np.zeros((TU + 4, D), np.float32)
        a, b = max(lo, 0), min(hi, T)
        blk[a - lo:b - lo] = seq[a:b]
        xT[:, u * (TU + 4):(u + 1) * (TU + 4)] = blk.T
        ur = 1 - u
        xTr[:, ur * (TU + 4):(ur + 1) * (TU + 4)] = blk[::-1].T
    return xT, xTr


_NC_CACHE = {}


def run_cores(assign, params, pv, TU, n_cores):
    if TU not in _NC_CACHE:
        import os
        _NC_CACHE[TU] = build(TU, stage=int(os.environ.get("MK_STAGE", "3")))
    nc = _NC_CACHE[TU]
    in_maps = []
    for units, link in assign:
        xT, xTr = layout_core(units, link, TU)
        pvc = pv.copy()
        pvc[:, PV["link"][0]] = float(link)
        m = dict(params)
        m.update(xT=xT, xTr=xTr, pvec=pvc)
        in_maps.append(m)
    res = run_bass_kernel_spmd(nc, in_maps, core_ids=list(range(n_cores)))
    return [r["yT"] for r in res.results]


def kernel(**inputs):
    xp = np.asarray(inputs["x_prompt"], np.float32)
    xs = np.asarray(inputs["x_sample"], np.float32)
    params, pv = prep_params(inputs)
    TU = xs.shape[1]
    assert xp.shape[1] == 2 * TU
    assign = []
    for b in range(xp.shape[0]):
        assign.append(([(xp[b], 0), (xp[b], TU)], 1))
    nb = xs.shape[0]
    slots = [[(xs[i], 0)] for i in range(nb)]
    n_rest = 8 - len(assign)
    per = [[] for _ in range(n_rest)]
    for i in range(nb):
        per[i % n_rest].append(i)
    for lst in per:
        u = [(xs[i], 0) for i in lst]
        while len(u) < 2:
            u.append((None, 0))
        assign.append((u, 0))
    outs = run_cores(assign, params, pv, TU, 8)
    y_p = np.zeros_like(xp)
    y_s = np.zeros_like(xs)
    for b in range(xp.shape[0]):
        y_p[b] = outs[b].T
    for ci, lst in enumerate(per):
        o = outs[xp.shape[0] + ci]
        for u, i in enumerate(lst):
            y_s[i] = o[:, u * TU:(u + 1) * TU].T
    return (y_p, y_s)
```

```python
import numpy as np
from contextlib import ExitStack
import concourse.bass as bass
import concourse.mybir as mybir
from concourse.bass_utils import run_bass_kernel_spmd

F32 = mybir.dt.float32
BF16 = mybir.dt.bfloat16
AF = mybir.ActivationFunctionType
ALU = mybir.AluOpType

D = 1024
C = 128
NCH_IN = 28
DFF = 2816
NFC = DFF // 128
ALPHA = 2.0 ** 0.25
LN_EPS = 1e-5
RMS_EPS = 1e-5
GN_EPS = 64e-5
CDEC = float(np.exp(-0.5))

CZ, CXS, CB, CC_, CR, CK, CV, CWL, CAL, CGL, CDT = 0, 4, 8, 10, 12, 16, 20, 24, 25, 26, 27

PV = {}
_o = 0
for _n, _w in [("cw", 40), ("cb", 8), ("dch", 4), ("nw", 4), ("mp", 15), ("mn", 15), ("w0f", 4), ("w0b", 4),
               ("a0", 4), ("kk", 4), ("ka", 4), ("rk", 4), ("lw", 4), ("lb", 4), ("l1w", 8), ("l1b", 8),
               ("l2w", 8), ("l2b", 8), ("link", 1)]:
    PV[_n] = (_o, _w)
    _o += _w
NPV = _o
CST = {n: i for i, n in enumerate(["ident", "J", "msl", "msu", "mui", "bmask", "bo64", "bo64m", "o1024", "o256", "ones"])}
NCST = len(CST)


class Buf:
    __slots__ = ("name", "w", "r", "psum")

    def __init__(self, name="", psum=False):
        self.name = name
        self.w = None
        self.r = []
        self.psum = psum


class Ctx:
    ENGS = ("pe", "act", "dve", "pool", "sp")

    def __init__(self, nc, es, n_dma_slots=6):
        self.nc = nc
        self.ops = {e: [] for e in self.ENGS}
        self.sems = {}
        self.count = {}
        for e in self.ENGS:
            self.sems[e] = es.enter_context(nc.semaphore("s_" + e))
            self.count[e] = 0
        self.waited = {e: {} for e in self.ENGS}
        self.slots = {}
        for q in ("sp", "pool"):
            self.slots[q] = []
            for i in range(n_dma_slots):
                key = "d_%s_%d" % (q, i)
                self.sems[key] = es.enter_context(nc.semaphore(key))
                self.count[key] = 0
                self.slots[q].append(key)
        self.slot_rr = {"sp": 0, "pool": 0}
        self.rr = 0
        import os
        self.ew_engs = tuple(os.environ.get("MK_EW", "dve").split(","))
        self.noself = tuple(x for x in os.environ.get("MK_NOSELF", "").split(",") if x)
        self.selfgap = int(os.environ.get("MK_SELFGAP", "0"))
        self.PEX = Buf("PEX")
        self.pex_on = os.environ.get("MK_PEX", "0") == "1"
        self.cut = int(os.environ.get("MK_CUT", "0"))
        self.nops = 0
        self.last_lines = None

    def _cut(self):
        self.nops += 1
        if self.cut and self.nops >= self.cut:
            if self.nops == self.cut:
                import sys
                f = sys._getframe(2)
                lines = []
                while f is not None and len(lines) < 5:
                    lines.append(f.f_lineno)
                    f = f.f_back
                print("MK_CUT: first dropped op #%d at lines %s" % (self.nops, lines), flush=True)
            return True
        return False

    def _deps(self, eng, reads, writes):
        need = {}
        for b in reads:
            ev = b.w
            if ev is not None and need.get(ev[0], 0) < ev[1]:
                need[ev[0]] = ev[1]
        for b in writes:
            ev = b.w
            if ev is not None and need.get(ev[0], 0) < ev[1]:
                need[ev[0]] = ev[1]
            for ev in b.r:
                if need.get(ev[0], 0) < ev[1]:
                    need[ev[0]] = ev[1]
        waits = []
        wd = self.waited[eng]
        for k, v in need.items():
            if k == eng and (eng == "pe" or eng in self.noself):
                continue
            if k == eng and self.selfgap and v <= self.count[eng] - self.selfgap:
                continue
            if wd.get(k, 0) >= v:
                continue
            wd[k] = v
            waits.append((k, v))
        return waits

    def _mark(self, ev, reads, writes):
        for b in reads:
            b.r.append(ev)
            if len(b.r) > 24:
                m = {}
                for k, v in b.r:
                    if m.get(k, 0) < v:
                        m[k] = v
                b.r = list(m.items())
        for b in writes:
            b.w = ev
            b.r = []

    @staticmethod
    def _flat(bs):
        out = []
        for b in bs:
            if isinstance(b, (list, tuple)):
                out.extend(Ctx._flat(b))
            else:
                out.append(b)
        return out

    def op(self, eng, fn, reads=(), writes=()):
        if self._cut():
            return
        reads, writes = self._flat(reads), self._flat(writes)
        pr = [b for b in reads if b.psum]
        if pr:
            reads = [b for b in reads if not b.psum]
            writes = list(writes) + pr
        if self.pex_on:
            if eng == "pe":
                reads = list(reads) + [self.PEX]
            elif eng == "dve" and any(b.psum for b in writes):
                writes = list(writes) + [self.PEX]
        waits = self._deps(eng, reads, writes)
        self.count[eng] += 1
        ev = (eng, self.count[eng])
        self.ops[eng].append((waits, fn, eng, 1))
        self._mark(ev, reads, writes)

    def dma(self, q, fn, reads=(), writes=()):
        if self._cut():
            return
        slot = self.slots[q][self.slot_rr[q] % len(self.slots[q])]
        self.slot_rr[q] += 1
        reads, writes = self._flat(reads), self._flat(writes)
        waits = self._deps(q, reads, writes)
        prev = self.count[slot]
        if prev > 0 and self.waited[q].get(slot, 0) < prev:
            self.waited[q][slot] = prev
            waits.append((slot, prev))
        self.count[slot] += 16
        ev = (slot, self.count[slot])
        self.ops[q].append((waits, fn, slot, 16))
        self._mark(ev, reads, writes)

    def wait_all(self, eng, bufs):
        if self.cut and self.nops >= self.cut:
            return
        waits = self._deps(eng, self._flat(bufs), ())
        self.ops[eng].append((waits, None, None, 0))

    def ew(self, fn, reads=(), writes=(), engs=None):
        if engs is None:
            engs = self.ew_engs
        e = engs[self.rr % len(engs)]
        self.rr += 1
        self.op(e, fn, reads, writes)

    def full_barrier(self):
        for eng in self.ENGS:
            waits = []
            for k, v in self.count.items():
                if v > 0 and self.waited[eng].get(k, 0) < v and not (k == eng):
                    self.waited[eng][k] = v
                    waits.append((k, v))
            self.ops[eng].append((waits, None, None, 0))

    def replay(self, engname, e):
        sems = self.sems
        for waits, fn, semkey, inc in self.ops[engname]:
            for k, v in waits:
                e.wait_ge(sems[k], v)
            if fn is not None:
                fn(e).then_inc(sems[semkey], inc)

    def run_block(self):
        nc = self.nc
        with nc.Block() as block:
            @block.tensor
            def _(e):
                self.replay("pe", e)

            @block.scalar
            def _(e):
                self.replay("act", e)

            @block.vector
            def _(e):
                self.replay("dve", e)

            @block.gpsimd
            def _(e):
                self.replay("pool", e)

            @block.sync
            def _(e):
                self.replay("sp", e)
        self.ops = {e: [] for e in self.ENGS}


class Rot:
    def __init__(self, nc, es, name, shape, dt, n=2):
        self.t = [es.enter_context(nc.sbuf_tensor("%s_%d" % (name, i), shape, dt)) for i in range(n)]
        self.b = [Buf(name) for _ in range(n)]
        self.i = 0

    def next(self):
        k = self.i % len(self.t)
        self.i += 1
        return self.t[k], self.b[k]


class PsumPool:
    def __init__(self, nc, es, names):
        self.t = [es.enter_context(nc.psum_tensor(n, [128, 512], F32)) for n in names]
        self.b = [[Buf(n, psum=True)] * 4 for n in names]
        self.i = 0

    def next(self):
        k = self.i % len(self.t)
        self.i += 1
        return self.t[k], self.b[k]


def build(TU, stage=3):
    import os
    SUB = float(os.environ.get('MK_SUB', '9'))
    NTL = int(os.environ.get('MK_NT', '999'))
    NTU = TU // C
    NT = 2 * NTU
    TT = 2 * TU
    XW = 2 * (TU + 4)
    nc = bass.Bass("TRN2", target_bir_lowering=False)
    dram = lambda n, s, dt=F32, kind="ExternalInput": nc.dram_tensor(n, s, dt, kind=kind).ap()
    xT = dram("xT", [D, XW])
    xTr = dram("xTr", [D, XW])
    w_in = dram("w_in", [D, NCH_IN * 128])
    w_out = dram("w_out", [D, D])
    w_fi = dram("w_fi", [D, 2 * DFF])
    w_fo = dram("w_fo", [DFF, D])
    pvec = dram("pvec", [128, NPV])
    p8 = dram("p8", [8, 8])
    lowr = dram("lowr", [128, 4 * 512])
    cst = dram("cst", [128, NCST * 128])
    m8 = dram("m8", [8, 1024])
    yT = dram("yT", [D, TT], kind="ExternalOutput")
    ysc = {k: dram(k, [NT * 128, 512], BF16, kind="Internal") for k in ("ybr_h", "ybr_l", "ybs_h", "ybs_l")}
    hT = dram("hT", [D, TT], kind="Internal")

    with ExitStack() as es0:
        c = Ctx(nc, es0)
        B_ysc = {k: [Buf() for _ in range(NT)] for k in ysc}
        B_hT = [Buf() for _ in range(NT)]
        B_out = []

        def mk_mm(c):
            def mm(out, lhsT, rhs, start, stop, reads, writes):
                c.op("pe", lambda e: e.matmul(out, lhsT=lhsT, rhs=rhs, start=start, stop=stop), reads=reads, writes=writes)
            return mm
        mm = mk_mm(c)

        def split2(src, hi, lo, reads, Bhi, Blo, psum=False):
            c.op("act", lambda e: e.activation(out=hi, in_=src, func=AF.Copy), reads=reads, writes=[Bhi])
            if psum:
                c.op("dve", lambda e: e.tensor_tensor(out=lo, in0=src, in1=hi, op=ALU.subtract), reads=list(reads) + [Bhi], writes=[Blo])
            else:
                c.ew(lambda e: e.tensor_tensor(out=lo, in0=src, in1=hi, op=ALU.subtract), reads=list(reads) + [Bhi], writes=[Blo])

        def rsqrt_eps(out, in_, eps, reads, writes):
            c.op("dve", lambda e: e.tensor_scalar(out=out, in0=in_, scalar1=float(eps), scalar2=None, op0=ALU.add), reads=reads, writes=writes)
            c.op("act", lambda e: e.activation(out=out, in_=out, func=AF.Sqrt), reads=writes, writes=writes)
            c.op("dve", lambda e: e.reciprocal(out=out, in_=out), reads=writes, writes=writes)

        def layer_norm(pp, Kb, pre, Bpre, out, Bout, sq, Bsq, r_st, r_sp, P, wn, bn, Bcsb, Bpv, N):
            def stat(src, Bsrc):
                red, Bred = r_st.next()
                c.op("dve", lambda e: e.tensor_reduce(out=red[:, 0:N], in_=src.rearrange("p a n -> p n a"), axis=mybir.AxisListType.X, op=ALU.add),
                     reads=[Bsrc], writes=[Bred])
                hi, Bhi = r_sp.next()
                lo, Blo = r_sp.next()
                split2(red[:, 0:N], hi[:, 0:N], lo[:, 0:N], [Bred], Bhi, Blo)
                ps, pb = pp.next()
                mm(ps[:, 0:N], Kb("o1024"), hi[:, 0:N], True, False, [Bcsb, Bhi], pb)
                mm(ps[:, 0:N], Kb("o1024"), lo[:, 0:N], False, True, [Bcsb, Blo], pb)
                return ps, pb
            ps, pb = stat(pre, Bpre)
            mu, Bmu = r_st.next()
            c.op("act", lambda e: e.activation(out=mu[:, 0:N], in_=ps[:, 0:N], func=AF.Copy), reads=pb, writes=[Bmu])
            bcN = lambda t: t[:, 0:N].unsqueeze(1).to_broadcast([128, 8, N])
            bcP = lambda n: P(n, 0, 8).unsqueeze(2).to_broadcast([128, 8, N])
            c.ew(lambda e: e.tensor_tensor(out=pre, in0=pre, in1=bcN(mu), op=ALU.subtract), reads=[Bpre, Bmu], writes=[Bpre])
            c.ew(lambda e: e.tensor_tensor(out=sq, in0=pre, in1=pre, op=ALU.mult), reads=[Bpre], writes=[Bsq])
            ps2, pb2 = stat(sq, Bsq)
            rs, Brs = r_st.next()
            rsqrt_eps(rs[:, 0:N], ps2[:, 0:N], LN_EPS, pb2, [Brs])
            c.ew(lambda e: e.tensor_tensor(out=pre, in0=pre, in1=bcN(rs), op=ALU.mult), reads=[Bpre, Brs], writes=[Bpre])
            c.ew(lambda e: e.tensor_tensor(out=pre, in0=pre, in1=bcP(wn), op=ALU.mult), reads=[Bpre, Bpv], writes=[Bpre])
            c.ew(lambda e: e.tensor_tensor(out=out, in0=pre, in1=bcP(bn), op=ALU.add), reads=[Bpre, Bpv], writes=[Bout])

        def mixer_phase(fwd):
            with ExitStack() as es:
                sfx = "F" if fwd else "B"
                sb = lambda n, s, dt=F32: es.enter_context(nc.sbuf_tensor(n + sfx, s, dt))
                rot = lambda n, s, dt=F32, k=2: Rot(nc, es, n + sfx, s, dt, k)
                pp = PsumPool(nc, es, ["pp%d%s" % (i, sfx) for i in range(6)])
                YR_t = es.enter_context(nc.psum_tensor("YR" + sfx, [128, 512], F32)); YR_b = Buf("YR", psum=True)
                YS_t = es.enter_context(nc.psum_tensor("YS" + sfx, [128, 512], F32)); YS_b = Buf("YS", psum=True)
                xsrc = xT if fwd else xTr

                class Fix:
                    def __init__(self, t, b):
                        self.t, self.b = t, b

                    def next(self):
                        return self.t, self.b
                one = lambda n, s, dt=F32: Fix(sb(n, s, dt), Buf(n))

                win = sb("win", [128, 8, NCH_IN * 128], BF16); Bwin = Buf()
                for k in range(8):
                    c.dma("pool", lambda e, k=k: e.dma_start(out=win[:, k, :], in_=w_in[k * 128:(k + 1) * 128, :]), writes=[Bwin])
                if fwd:
                    wo = sb("wo", [128, 8, D], BF16); Bwo = Buf()
                    for k in range(8):
                        c.dma("pool", lambda e, k=k: e.dma_start(out=wo[:, k, :], in_=w_out[k * 128:(k + 1) * 128, :]), writes=[Bwo])
                pv = sb("pv", [128, NPV]); Bpv = Buf()
                c.dma("sp", lambda e: e.dma_start(out=pv[:], in_=pvec), writes=[Bpv])
                p8t = sb("p8t", [8, 8]); Bp8 = Buf()
                c.dma("sp", lambda e: e.dma_start(out=p8t[:], in_=p8), writes=[Bp8])
                lrb = sb("lrb", [128, 3 * 512], BF16); Blr = Buf()
                w2o = 0 if fwd else 512
                c.dma("pool", lambda e: e.dma_start(out=lrb[:, 0:512], in_=lowr[:, w2o:w2o + 512]), writes=[Blr])
                c.dma("pool", lambda e: e.dma_start(out=lrb[:, 512:1536], in_=lowr[:, 1024:2048]), writes=[Blr])
                csb = sb("csb", [128, NCST * 128], BF16); Bcsb = Buf()
                c.dma("pool", lambda e: e.dma_start(out=csb[:], in_=cst), writes=[Bcsb])
                m8b = sb("m8b", [8, 1024], BF16); Bm8 = Buf()
                c.dma("pool", lambda e: e.dma_start(out=m8b[:], in_=m8), writes=[Bm8])
                Kb = lambda n: csb[:, CST[n] * 128:(CST[n] + 1) * 128]
                mask4 = sb("mask4", [128, 512], BF16); Bm4 = Buf()
                for q, n in enumerate(["msu", "mui", "msu", "mui"]):
                    c.op("dve", lambda e, q=q, n=n: e.tensor_copy(out=mask4[:, q * 128:(q + 1) * 128], in_=Kb(n)),
                         reads=[Bcsb], writes=[Bm4])
                P = lambda n, j=0, w=1: pv[:, PV[n][0] + j:PV[n][0] + j + w]
                der = sb("der", [128, 15 + 4]); Bder = Buf()
                c.op("dve", lambda e: e.tensor_tensor(out=der[:, 0:15], in0=P("mp", 0, 15), in1=P("mn", 0, 15), op=ALU.add),
                     reads=[Bpv], writes=[Bder])
                c.op("dve", lambda e: e.tensor_scalar(out=der[:, 0:15], in0=der[:, 0:15], scalar1=-1.0, scalar2=1.0,
                                                      op0=ALU.mult, op1=ALU.add), reads=[Bder], writes=[Bder])
                c.op("dve", lambda e: e.tensor_scalar(out=der[:, 15:19], in0=P("ka", 0, 4), scalar1=-1.0, scalar2=1.0,
                                                      op0=ALU.mult, op1=ALU.add), reads=[Bpv], writes=[Bder])
                a8 = sb("a8", [8, 1]); Ba8 = Buf()
                acol = 2 if fwd else 3
                c.op("act", lambda e: e.activation(out=a8[:], in_=p8t[:, acol:acol + 1], func=AF.Exp), reads=[Bp8], writes=[Ba8])
                c.op("dve", lambda e: e.tensor_scalar(out=a8[:], in0=a8[:], scalar1=-1.0, scalar2=None, op0=ALU.mult),
                     reads=[Ba8], writes=[Ba8])
                dtb = p8t[:, (0 if fwd else 1):(1 if fwd else 2)]
                w2 = lrb[:, 0:512]
                a2 = lrb[:, 512:1024]
                g2 = lrb[:, 1024:1536]
                w0n = "w0f" if fwd else "w0b"
                mpn, mnn = ("mp", "mn") if fwd else ("mn", "mp")

                Hb = sb("Hb", [128, 4, 128], BF16); BH = [Buf() for _ in range(4)]
                STf = sb("STf", [128, 512]); STb = sb("STb", [128, 512], BF16); BST = Buf()
                c.op("pool", lambda e: e.memset(Hb[:], 0.0), writes=BH)
                c.op("pool", lambda e: e.memset(STf[:], 0.0), writes=[BST])
                c.op("pool", lambda e: e.memset(STb[:], 0.0), writes=[BST])
                ARe_t = sb("ARe", [128, 4, 256], BF16)
                ARo_t = sb("ARo", [128, 4, 256], BF16)
                BAR_t = Buf()
                c.op("pool", lambda e: e.memset(ARe_t[:], 0.0), writes=[BAR_t])
                c.op("pool", lambda e: e.memset(ARo_t[:], 0.0), writes=[BAR_t])
                CSt = sb("CS", [128, 4, 129]); BCSt = Buf()
                c.op("pool", lambda e: e.memset(CSt[:], 0.0), writes=[BCSt])
                ptmp = sb("ptmp", [128, 128]); Bptmp = Buf()
                Bfence = Buf()
                ones_t = sb("ones_t", [128, 128]); Bones = Buf()
                c.op("pool", lambda e: e.memset(ones_t[:], 1.0), writes=[Bones])
                DT3 = sb("DT3", [128, 3, 2, 128], BF16); BDT3 = Buf()
                RB3 = sb("RB3", [128, 2, 8, 128], BF16); BRB3 = [Buf(), Buf()]
                c.op("pool", lambda e: e.memset(DT3[:], 0.0), writes=[BDT3])
                c.op("pool", lambda e: e.memset(RB3[:], 0.0), writes=BRB3)

                arena = sb("arena", [128, 11 * 512]); Bar = [Buf() for _ in range(11)]

                def AS(i, n=1, parts=128, nch=None):
                    nch = 4 * n if nch is None else nch
                    return (arena[0:parts, 512 * i:512 * i + 128 * nch].rearrange("p (a b) -> p a b", b=128), Bar[i:i + n])

                def AU(i, nchunks):
                    return (arena[:, 512 * i:512 * i + 132 * nchunks].rearrange("p (a b) -> p a b", b=132), Bar[i:i + 4])

                r_xb = rot("xb", [128, 8, 132], BF16, 1 if fwd else 2)
                if fwd:
                    U1t, BU1 = None, None
                else:
                    U1t = sb("U1", [128, 13, 132]); BU1 = Buf("U1")
                r_zs = one("zs", [128, 4, 128])
                r_xsf = one("xsf", [128, 4, 128])
                r_xbc = one("xbcb", [128, 8, 128], BF16)
                r_dtok = one("dtok", [128, 16])
                r_gm = one("gm", [128, 2, 128])
                r_MT = one("MT", [128, 8, 128], BF16)
                r_Cd = one("Cd", [128, 8, 128], BF16)
                r_xdt = one("xdt", [128, 2, 512], BF16)
                r_btok = one("btok", [128, 2, 128], BF16)
                r_lrin = one("lrin", [128, 3, 128], BF16)
                r_sp = rot("sp", [128, 4, 128], BF16, 2)
                r_op = rot("opb", [128, 4, 128], BF16, 4)
                LS = int(os.environ.get("MK_LS_F" if fwd else "MK_LS_B", "4" if fwd else "8"))
                NPR = max(1, LS // 2)
                r_tok = rot("tok", [128, 4, 128], BF16, max(2, NPR))
                r_p0 = rot("p0", [128, 128], BF16, LS)
                r_e4 = rot("e4", [128, 512], BF16, LS)
                NQ = LS // 4
                r_pp = rot("ppq", [128, 4, 256], BF16, 2 * NQ)
                r_zf = rot("zf", [128, 4, 128], F32, NQ)
                r_zb = rot("zb", [128, 4, 128], BF16, NQ)
                r_wu = rot("wu", [128, 2, 128], BF16, max(2, NPR))
                r_qt = rot("qt", [128, 128], BF16, 1)
                r_mtb = rot("mtb", [128, 128], BF16, 1)
                r_yhl = rot("yhl", [128, 2, 512], BF16, 2)
                if fwd:
                    r_ybl = rot("ybl", [128, 2, 512], BF16, 2)
                    r_bg = rot("bg", [128, 4, 128], F32, 2)
                    r_mix = one("mix", [128, 8, 128], BF16)
                    r_st = rot("stat", [128, 128], F32, 4)
                    r_sps = rot("sps", [128, 128], BF16, 2)

                evac_rr = [0]

                def evac(out, in_, reads, writes, engs=tuple(os.environ.get("MK_EVAC", "act,dve").split(","))):
                    e_ = engs[evac_rr[0] % len(engs)]
                    evac_rr[0] += 1
                    if e_ == "act":
                        c.op("act", lambda e: e.activation(out=out, in_=in_, func=AF.Copy), reads=reads, writes=writes)
                    else:
                        c.op(e_, lambda e: e.tensor_copy(out=out, in_=in_), reads=reads, writes=writes)

                def blocksum4(src, Bsrc, cname):
                    hi, Bhi = r_sp.next()
                    lo, Blo = r_sp.next()
                    split2(src, hi[:], lo[:], [Bsrc], Bhi, Blo)
                    ps, pb = pp.next()
                    for cp in range(4):
                        o_ = ps[:, cp * 128:(cp + 1) * 128]
                        mm(o_, Kb(cname), hi[:, cp, :], True, False, [Bcsb, Bhi], [pb[cp]])
                        mm(o_, Kb(cname), lo[:, cp, :], False, True, [Bcsb, Blo], [pb[cp]])
                    return ps[:].rearrange("p (a b) -> p a b", b=128), pb

                def link_state():
                    lk = P("link")
                    for cpr in range(4):
                        c.op("dve", lambda e, cpr=cpr: e.tensor_scalar(out=Hb[:, cpr, :], in0=Hb[:, cpr, :], scalar1=lk,
                                                                       scalar2=None, op0=ALU.mult),
                             reads=[BH[cpr], Bpv], writes=[BH[cpr]])
                    c.op("dve", lambda e: e.tensor_scalar(out=STf[:], in0=STf[:], scalar1=lk, scalar2=None, op0=ALU.mult),
                         reads=[BST, Bpv], writes=[BST])
                    c.op("dve", lambda e: e.tensor_scalar(out=STb[:], in0=STb[:], scalar1=lk, scalar2=None, op0=ALU.mult),
                         reads=[BST, Bpv], writes=[BST])

                for ti in range(min(NT, NTL)):
                    u_i, lt = divmod(ti, NTU)
                    if ti == NTU:
                        link_state()
                    col0 = u_i * (TU + 4) + lt * C
                    xb, Bxb = r_xb.next()
                    c.dma("pool", lambda e, xb=xb, col0=col0: e.dma_start(
                        out=xb[:], in_=xsrc[:, col0:col0 + 132].rearrange("(k p) t -> p k t", p=128)), writes=[Bxb])
                    if SUB < 1:
                        continue

                    def inproj(U, BU, chunks):
                        for g0 in range(0, len(chunks), 3):
                            grp = chunks[g0:g0 + 3]
                            n_g = len(grp)
                            ps, pb = pp.next()
                            for j, cc in enumerate(grp):
                                for k in range(8):
                                    mm(ps[:, j * 132:(j + 1) * 132], win[:, k, cc * 128:(cc + 1) * 128], xb[:, k, :],
                                       k == 0, k == 7, [Bwin, Bxb], pb)
                            evac(U[:, g0:g0 + n_g, :], ps[:, 0:n_g * 132].rearrange("p (a b) -> p a b", b=132), pb, [BU])

                    if U1t is not None:
                        U, BU = U1t[:], BU1
                    else:
                        U, BU = AU(6, 13)
                    inproj(U, BU, list(range(12)) + [CDT])
                    UDT = 12
                    if SUB < 2:
                        continue
                    zs, Bzs = r_zs.next()
                    c.op("act", lambda e, zs=zs, U=U: e.activation(out=zs[:], in_=U[:, CZ:CZ + 4, 2:130], func=AF.Silu),
                         reads=[BU], writes=[Bzs])
                    cv, Bcv = AS(0, 2)
                    for j in range(8):
                        eng = ("dve", "pool")[j % 2] if os.environ.get("MK_CONVP", "0") == "1" else "dve"
                        for k5 in range(5):
                            kk_ = k5 if fwd else 4 - k5
                            wk = pv[:, PV["cw"][0] + j * 5 + kk_:PV["cw"][0] + j * 5 + kk_ + 1]
                            src = U[:, CXS + j, k5:k5 + 128]
                            if k5 == 0:
                                c.op(eng, lambda e, j=j, wk=wk, src=src, cv=cv: e.tensor_scalar(
                                    out=cv[:, j, :], in0=src, scalar1=wk, scalar2=P("cb", j), op0=ALU.mult, op1=ALU.add),
                                     reads=[BU, Bpv], writes=[Bcv])
                            elif eng == "dve":
                                c.op(eng, lambda e, j=j, wk=wk, src=src, cv=cv: e.scalar_tensor_tensor(
                                    out=cv[:, j, :], in0=src, scalar=wk, in1=cv[:, j, :], op0=ALU.mult, op1=ALU.add),
                                     reads=[BU, Bpv, Bcv], writes=[Bcv])
                            else:
                                c.op(eng, lambda e, wk=wk, src=src: e.tensor_scalar(
                                    out=ptmp[:], in0=src, scalar1=wk, scalar2=None, op0=ALU.mult), reads=[BU, Bpv], writes=[Bptmp])
                                c.op(eng, lambda e, j=j, cv=cv: e.tensor_tensor(out=cv[:, j, :], in0=cv[:, j, :], in1=ptmp[:], op=ALU.add),
                                     reads=[Bptmp, Bcv], writes=[Bcv])
                    xsf, Bxsf = r_xsf.next()
                    xbc, Bxbc = r_xbc.next()
                    c.op("act", lambda e, xsf=xsf, cv=cv: e.activation(out=xsf[:], in_=cv[:, 0:4, :], func=AF.Silu),
                         reads=[Bcv], writes=[Bxsf])
                    c.op("act", lambda e, xbc=xbc, cv=cv: e.activation(out=xbc[:, 4:8, :], in_=cv[:, 4:8, :], func=AF.Silu),
                         reads=[Bcv], writes=[Bxbc])
                    c.op("act", lambda e, xbc=xbc, xsf=xsf: e.activation(out=xbc[:, 0:4, :], in_=xsf[:], func=AF.Copy),
                         reads=[Bxsf], writes=[Bxbc])
                    d8, Bd8 = AS(10, 1, parts=8)
                    c.op("act", lambda e, d8=d8, U=U: e.activation(out=d8[:, 0, :], in_=U[0:8, UDT, 2:130], func=AF.Exp,
                                                                   bias=dtb), reads=[BU, Bp8], writes=[Bd8])
                    c.op("act", lambda e, d8=d8: e.activation(out=d8[:, 1, :], in_=d8[:, 0, :], func=AF.Ln, bias=1.0),
                         reads=[Bd8], writes=[Bd8])
                    c.op("dve", lambda e, d8=d8: e.tensor_scalar(out=d8[:, 2, :], in0=d8[:, 1, :], scalar1=a8[:, 0:1],
                                                                 scalar2=None, op0=ALU.mult), reads=[Bd8, Ba8], writes=[Bd8])
                    c.op("dve", lambda e, d8=d8: e.tensor_tensor_scan(out=d8[:, 3, :], data0=ones_t[0:8, :], data1=d8[:, 2, :],
                                                                      initial=0.0, op0=ALU.mult, op1=ALU.add),
                         reads=[Bd8, Bones], writes=[Bd8])
                    for q, (src_i, r_i) in enumerate(((1, 0), (3, 2))):
                        x_ = d8[:, src_i, :]
                        r_ = d8[:, r_i, :]
                        c.op("dve", lambda e, q=q, x_=x_: e.tensor_copy(out=DT3[0:8, 0, q, :], in_=x_), reads=[Bd8], writes=[BDT3])
                        c.op("dve", lambda e, q=q, x_=x_, r_=r_: e.tensor_tensor(out=r_, in0=x_, in1=DT3[0:8, 0, q, :], op=ALU.subtract),
                             reads=[Bd8, BDT3], writes=[Bd8])
                        c.op("dve", lambda e, q=q, r_=r_: e.tensor_copy(out=DT3[0:8, 1, q, :], in_=r_), reads=[Bd8], writes=[BDT3])
                        c.op("dve", lambda e, q=q, r_=r_: e.tensor_tensor(out=r_, in0=r_, in1=DT3[0:8, 1, q, :], op=ALU.subtract),
                             reads=[Bd8, BDT3], writes=[Bd8])
                        c.op("dve", lambda e, q=q, r_=r_: e.tensor_copy(out=DT3[0:8, 2, q, :], in_=r_), reads=[Bd8], writes=[BDT3])
                    ps, pb = pp.next()
                    for q in range(2):
                        for i3 in range(3):
                            mm(ps[:, q * 128:(q + 1) * 128], DT3[:, i3, q, :], Kb("ident"), i3 == 0, i3 == 2, [BDT3, Bcsb], [pb[q]])
                    dtok, Bdtok = r_dtok.next()
                    c.op("dve", lambda e, dtok=dtok, ps=ps: e.tensor_copy(out=dtok[:, 0:8], in_=ps[:, 0:8]), reads=[pb[0], pb[1]], writes=[Bdtok])
                    c.op("dve", lambda e, dtok=dtok, ps=ps: e.tensor_copy(out=dtok[:, 8:16], in_=ps[:, 128:136]), reads=[pb[1]], writes=[Bdtok])
                    psR0, pbR0 = pp.next()
                    if os.environ.get("MK_SKIPB", "") == "1":
                        pp.next()
                    psR1, pbR1 = pp.next()
                    for i3 in range(3):
                        rb = i3 % 2
                        c.ew(lambda e, i3=i3, rb=rb: e.tensor_tensor(
                            out=RB3[0:8, rb, :, :], in0=m8b[:].rearrange("p (h l) -> p h l", l=128),
                            in1=DT3[0:8, i3, 1:2, :].to_broadcast([8, 8, 128]), op=ALU.mult), reads=[BDT3, Bm8], writes=[BRB3[rb]])
                        for (psR, pbR, h0) in ((psR0, pbR0, 0), (psR1, pbR1, 4)):
                            mm(psR[:], Kb("ones"), RB3[:, rb, h0:h0 + 4, :].rearrange("p a b -> p (a b)"), i3 == 0, i3 == 2, [Bcsb, BRB3[rb]], pbR)
                    seg, Bseg = AS(2, 2)
                    E, BE = AS(4, 2)
                    if os.environ.get("MK_FENCE", "-1") != "-1":
                        pbR0 = pbR0 + [Bfence]
                    for _d in range(int(os.environ.get("MK_DELAY", "0"))):
                        c.op("dve", lambda e: e.tensor_copy(out=ptmp[:], in_=ones_t[:]), reads=[pbR0, Bones], writes=[Bptmp])
                    for (psR, pbR, h0) in ((psR0, pbR0, 0), (psR1, pbR1, 4)):
                        c.op("dve", lambda e, psR=psR, h0=h0, seg=seg, dtok=dtok: e.tensor_tensor(
                            out=seg[:, h0:h0 + 4, :], in0=psR[:].rearrange("p (a b) -> p a b", b=128),
                            in1=dtok[:, 8 + h0:12 + h0].unsqueeze(2).to_broadcast([128, 4, 128]), op=ALU.subtract),
                             reads=[pbR, Bdtok], writes=[Bseg])
                    c.ew(lambda e, seg=seg: e.tensor_scalar(out=seg[:], in0=seg[:], scalar1=0.0, scalar2=None, op0=ALU.min), reads=[Bseg], writes=[Bseg])
                    c.op("act", lambda e, E=E, psR0=psR0: e.activation(out=E[:, 0:4, :], in_=psR0[:].rearrange("p (a b) -> p a b", b=128),
                                                                     func=AF.Exp), reads=pbR0, writes=[BE])
                    c.op("act", lambda e, E=E, psR1=psR1: e.activation(out=E[:, 4:8, :], in_=psR1[:].rearrange("p (a b) -> p a b", b=128),
                                                                     func=AF.Exp), reads=pbR1, writes=[BE])
                    c.op("act", lambda e, seg=seg: e.activation(out=seg[:], in_=seg[:], func=AF.Exp), reads=[Bseg], writes=[Bseg])
                    ps, pb = pp.next()
                    for g in range(2):
                        mm(ps[:, g * 128:(g + 1) * 128], xbc[:, 4 + g, :], xbc[:, 6 + g, :], True, True, [Bxbc], [pb[g]])
                    gm, Bgm = r_gm.next()
                    for g in range(2):
                        c.op("dve", lambda e, g=g, gm=gm, ps=ps: e.tensor_tensor(out=gm[:, g, :], in0=ps[:, g * 128:(g + 1) * 128],
                                                                                 in1=Kb("mui"), op=ALU.mult),
                             reads=[pb[g], Bcsb], writes=[Bgm])
                    MT, BMT = r_MT.next()
                    Cd, BCd = r_Cd.next()
                    for g in range(2):
                        c.ew(lambda e, g=g, MT=MT, seg=seg, gm=gm: e.tensor_tensor(
                            out=MT[:, 4 * g:4 * g + 4, :], in0=seg[:, 4 * g:4 * g + 4, :], in1=gm[:, g:g + 1, :].to_broadcast([128, 4, 128]), op=ALU.mult),
                             reads=[Bseg, Bgm], writes=[BMT])
                        c.ew(lambda e, g=g, Cd=Cd, E=E, xbc=xbc: e.tensor_tensor(
                            out=Cd[:, 4 * g:4 * g + 4, :], in0=E[:, 4 * g:4 * g + 4, :], in1=xbc[:, 6 + g:7 + g, :].to_broadcast([128, 4, 128]), op=ALU.mult),
                             reads=[BE, Bxbc], writes=[BCd])
                    ps, pb = pp.next()
                    for j in range(4):
                        mm(ps[:, j * 128:(j + 1) * 128], xbc[:, j, :], Kb("ident"), True, True, [Bxbc, Bcsb], [pb[j]])
                    xdt, Bxdt = r_xdt.next()
                    v864 = lambda ap: ap.rearrange("p (h q) -> p h q", q=64)
                    c.op("dve", lambda e, xdt=xdt, ps=ps, dtok=dtok: e.tensor_tensor(
                        out=v864(xdt[:, 0, :]), in0=v864(ps[:]), in1=dtok[:, 0:8].unsqueeze(2).to_broadcast([128, 8, 64]), op=ALU.mult),
                         reads=[pb, Bdtok], writes=[Bxdt])
                    c.ew(lambda e, xdt=xdt, seg=seg: e.tensor_tensor(
                        out=v864(xdt[:, 1, :]), in0=v864(xdt[:, 0, :]), in1=seg[:, :, 127:128].to_broadcast([128, 8, 64]), op=ALU.mult),
                         reads=[Bxdt, Bseg], writes=[Bxdt])
                    ps, pb = pp.next()
                    for g in range(2):
                        mm(ps[:, g * 128:(g + 1) * 128], xbc[:, 4 + g, :], Kb("ident"), True, True, [Bxbc, Bcsb], [pb[g]])
                    btok, Bbtok = r_btok.next()
                    evac(btok[:], ps[:, 0:256].rearrange("p (a b) -> p a b", b=128), pb[0:2], [Bbtok])
                    for h in range(8):
                        o_ = YS_t[:, h * 64:(h + 1) * 64]
                        mm(o_, MT[:, h, :], xdt[:, 0, h * 64:(h + 1) * 64], True, False, [BMT, Bxdt], [YS_b])
                        mm(o_, Cd[:, h, :], STb[:, h * 64:(h + 1) * 64], False, True, [BCd, BST], [YS_b])
                    ysh, Bysh = r_yhl.next()
                    split2(YS_t[:], ysh[:, 0, :], ysh[:, 1, :], [YS_b], Bysh, Bysh, psum=True)
                    ps, pb = pp.next()
                    for g in range(2):
                        mm(ps[:, g * 256:(g + 1) * 256], btok[:, g, :], xdt[:, 1, g * 256:(g + 1) * 256], True, True,
                           [Bbtok, Bxdt], pb[2 * g:2 * g + 2])
                    c.ew(lambda e, E=E: e.tensor_tensor(out=v864(STf[:]), in0=v864(STf[:]), in1=E[:, :, 127:128].to_broadcast([128, 8, 64]), op=ALU.mult),
                         reads=[BST, BE], writes=[BST])
                    c.op("dve", lambda e, ps=ps: e.tensor_tensor(out=STf[:], in0=STf[:], in1=ps[:], op=ALU.add), reads=[BST, pb], writes=[BST])
                    c.op("act", lambda e: e.activation(out=STb[:], in_=STf[:], func=AF.Copy), reads=[BST], writes=[BST])
                    if not fwd:
                        for q, key in enumerate(("ybs_h", "ybs_l")):
                            c.dma("sp", lambda e, ysh=ysh, q=q, key=key, ti=ti: e.dma_start(out=ysc[key][ti * 128:(ti + 1) * 128, :], in_=ysh[:, q, :]),
                                  reads=[Bysh], writes=[B_ysc[key][ti]])
                    if SUB < 3:
                        continue

                    U, BU = AU(4, 15)
                    inproj(U, BU, list(range(CR, CR + 15)))
                    us, Bus = AS(0, 4, nch=15)
                    tsh, Btsh = AS(8, 2)
                    bc15 = lambda ap, j0, n: ap[:, j0:j0 + n].unsqueeze(2).to_broadcast([128, n, 128])
                    c.ew(lambda e, us=us, U=U: e.tensor_tensor(out=us, in0=U[:, 0:15, 2:130], in1=bc15(der, 0, 15), op=ALU.mult),
                         reads=[BU, Bder], writes=[Bus])
                    for (j0, n) in ((0, 8), (8, 7)):
                        for (sl_, mun) in (((1, 129), mpn), ((3, 131), mnn)):
                            mu_ap = pv[:, PV[mun][0]:PV[mun][0] + 15]
                            c.ew(lambda e, j0=j0, n=n, sl_=sl_, mu_ap=mu_ap, U=U, tsh=tsh: e.tensor_tensor(
                                out=tsh[:, 0:n, :], in0=U[:, j0:j0 + n, sl_[0]:sl_[1]], in1=bc15(mu_ap, j0, n), op=ALU.mult),
                                 reads=[BU, Bpv], writes=[Btsh])
                            c.ew(lambda e, j0=j0, n=n, us=us, tsh=tsh: e.tensor_tensor(
                                out=us[:, j0:j0 + n, :], in0=us[:, j0:j0 + n, :], in1=tsh[:, 0:n, :], op=ALU.add),
                                 reads=[Bus, Btsh], writes=[Bus])
                    lrin, Blrin = r_lrin.next()
                    c.op("act", lambda e, us=us, lrin=lrin: e.activation(out=lrin[:, 0, :], in_=us[:, 12, :], func=AF.Tanh), reads=[Bus], writes=[Blrin])
                    c.op("act", lambda e, us=us, lrin=lrin: e.activation(out=lrin[:, 1, :], in_=us[:, 13, :], func=AF.Copy), reads=[Bus], writes=[Blrin])
                    c.op("act", lambda e, us=us, lrin=lrin: e.activation(out=lrin[:, 2, :], in_=us[:, 14, :], func=AF.Sigmoid), reads=[Bus], writes=[Blrin])
                    psL, pbL = pp.next()
                    psI, pbI = pp.next()
                    for cp in range(4):
                        mm(psL[:, cp * 128:(cp + 1) * 128], w2[:, cp * 128:(cp + 1) * 128], lrin[:, 0, :], True, True, [Blr, Blrin], [pbL[cp]])
                        mm(psI[:, cp * 128:(cp + 1) * 128], a2[:, cp * 128:(cp + 1) * 128], lrin[:, 1, :], True, True, [Blr, Blrin], [pbI[cp]])
                    sg, Bsg = AS(4)
                    icl, Bicl = AS(5)
                    bc4 = lambda n: P(n, 0, 4).unsqueeze(2).to_broadcast([128, 4, 128])
                    v4 = lambda ps_: ps_[:].rearrange("p (a b) -> p a b", b=128)
                    c.op("dve", lambda e, sg=sg, psL=psL: e.tensor_tensor(out=sg[:], in0=v4(psL), in1=bc4(w0n), op=ALU.add), reads=[pbL, Bpv], writes=[Bsg])
                    c.op("act", lambda e, sg=sg: e.activation(out=sg[:], in_=sg[:], func=AF.Sigmoid), reads=[Bsg], writes=[Bsg])
                    c.op("dve", lambda e, icl=icl, psI=psI: e.tensor_tensor(out=icl[:], in0=v4(psI), in1=bc4("a0"), op=ALU.add), reads=[pbI, Bpv], writes=[Bicl])
                    c.op("act", lambda e, icl=icl: e.activation(out=icl[:], in_=icl[:], func=AF.Sigmoid), reads=[Bicl], writes=[Bicl])
                    for cp in range(4):
                        c.op("dve", lambda e, cp=cp, sg=sg: e.tensor_tensor_scan(
                            out=CSt[:, cp, 1:129], data0=ones_t[:], data1=sg[:, cp, :], initial=0.0, op0=ALU.mult, op1=ALU.add),
                             reads=[Bsg, Bones, BCSt], writes=[BCSt])
                    gam, Bgam = AS(6)
                    igam, Bigam = AS(7)
                    gprev, Bgprev = AS(8)
                    c.op("act", lambda e, gam=gam: e.activation(out=gam[:], in_=CSt[:, :, 1:129], func=AF.Exp, scale=-CDEC),
                         reads=[BCSt], writes=[Bgam])
                    c.op("act", lambda e, igam=igam: e.activation(out=igam[:], in_=CSt[:, :, 1:129], func=AF.Exp, scale=CDEC),
                         reads=[BCSt], writes=[Bigam])
                    c.op("act", lambda e, gprev=gprev: e.activation(out=gprev[:], in_=CSt[:, :, 0:128], func=AF.Exp, scale=-CDEC),
                         reads=[BCSt], writes=[Bgprev])
                    kkt, Bkk = AS(9)
                    c.ew(lambda e, kkt=kkt, us=us: e.tensor_tensor(out=kkt[:], in0=us[:, 4:8, :], in1=bc4("kk"), op=ALU.mult), reads=[Bus, Bpv], writes=[Bkk])
                    sq, Bsq = sg, Bsg
                    c.ew(lambda e, sq=sq, kkt=kkt: e.tensor_tensor(out=sq[:], in0=kkt[:], in1=kkt[:], op=ALU.mult), reads=[Bkk], writes=[Bsq])
                    psS, pbS = blocksum4(sq[:], Bsq, "bo64")
                    rn, Brn = AS(10)
                    c.op("dve", lambda e, rn=rn, psS=psS: e.tensor_scalar(out=rn[:], in0=psS, scalar1=1e-24, scalar2=None, op0=ALU.max),
                         reads=pbS, writes=[Brn])
                    c.op("act", lambda e, rn=rn: e.activation(out=rn[:], in_=rn[:], func=AF.Sqrt), reads=[Brn], writes=[Brn])
                    c.op("dve", lambda e, rn=rn: e.reciprocal(out=rn[:], in_=rn[:]), reads=[Brn], writes=[Brn])
                    c.ew(lambda e, kkt=kkt, rn=rn: e.tensor_tensor(out=kkt[:], in0=kkt[:], in1=rn[:], op=ALU.mult), reads=[Bkk, Brn], writes=[Bkk])
                    km, Bkm = rn, Brn
                    c.ew(lambda e, km=km, icl=icl: e.tensor_tensor(out=km[:], in0=icl[:], in1=bc4("ka"), op=ALU.mult), reads=[Bicl, Bpv], writes=[Bkm])
                    c.ew(lambda e, km=km: e.tensor_tensor(out=km[:], in0=km[:], in1=der[:, 15:19].unsqueeze(2).to_broadcast([128, 4, 128]), op=ALU.add),
                         reads=[Bkm, Bder], writes=[Bkm])
                    c.ew(lambda e, km=km, us=us: e.tensor_tensor(out=km[:], in0=km[:], in1=us[:, 4:8, :], op=ALU.mult), reads=[Bkm, Bus], writes=[Bkm])
                    if fwd:
                        rk_, Brk = sq, Bsq
                        c.ew(lambda e, rk_=rk_, us=us: e.tensor_tensor(out=rk_[:], in0=us[:, 0:4, :], in1=bc4("rk"), op=ALU.mult), reads=[Bus, Bpv], writes=[Brk])
                        c.ew(lambda e, rk_=rk_, km=km: e.tensor_tensor(out=rk_[:], in0=rk_[:], in1=km[:], op=ALU.mult), reads=[Brk, Bkm], writes=[Brk])
                        psB, pbB = blocksum4(rk_[:], Brk, "bo64")
                        bonv, Bbonv = r_bg.next()
                        c.op("dve", lambda e, bonv=bonv, psB=psB, us=us: e.tensor_tensor(out=bonv[:], in0=psB, in1=us[:, 8:12, :], op=ALU.mult),
                             reads=pbB + [Bus], writes=[Bbonv])
                        psG, pbG = pp.next()
                        for cp in range(4):
                            mm(psG[:, cp * 128:(cp + 1) * 128], g2[:, cp * 128:(cp + 1) * 128], lrin[:, 2, :], True, True, [Blr, Blrin], [pbG[cp]])
                        gT, BgT = r_bg.next()
                        c.op("act", lambda e, gT=gT, psG=psG: e.activation(out=gT[:], in_=psG[:].rearrange("p (a b) -> p a b", b=128), func=AF.Copy),
                             reads=pbG, writes=[BgT])
                    KT, BKT = r_op.next()
                    BT, BBT = r_op.next()
                    AT, BAT = r_op.next()
                    vb, Bvb = r_op.next()
                    c.ew(lambda e, KT=KT, km=km, igam=igam: e.tensor_tensor(out=KT[:], in0=km[:], in1=igam[:], op=ALU.mult),
                         reads=[Bkm, Bigam], writes=[BKT])
                    c.ew(lambda e, icl=icl, kkt=kkt: e.tensor_tensor(out=icl[:], in0=icl[:], in1=kkt[:], op=ALU.mult), reads=[Bicl, Bkk], writes=[Bicl])
                    c.ew(lambda e, BT=BT, icl=icl, igam=igam: e.tensor_tensor(out=BT[:], in0=icl[:], in1=igam[:], op=ALU.mult),
                         reads=[Bicl, Bigam], writes=[BBT])
                    c.op("dve", lambda e, AT=AT, kkt=kkt, gprev=gprev: e.scalar_tensor_tensor(out=AT[:], in0=kkt[:], scalar=-1.0, in1=gprev[:],
                                                                                             op0=ALU.mult, op1=ALU.mult),
                         reads=[Bkk, Bgprev], writes=[BAT])
                    c.ew(lambda e, vb=vb, us=us: e.tensor_copy(out=vb[:], in_=us[:, 8:12, :]), reads=[Bus], writes=[Bvb])
                    c.ew(lambda e, AT=AT: e.tensor_copy(out=ARe_t[0:64, :, 0:128], in_=AT[0:64, :, :]), reads=[BAT], writes=[BAR_t])
                    c.ew(lambda e, AT=AT: e.tensor_copy(out=ARo_t[64:128, :, 0:128], in_=AT[64:128, :, :]), reads=[BAT], writes=[BAR_t])
                    c.ew(lambda e, us=us, gam=gam: e.tensor_tensor(out=ARe_t[0:64, :, 128:256], in0=us[0:64, 0:4, :], in1=gam[0:64, :, :],
                                                                   op=ALU.mult), reads=[Bus, Bgam], writes=[BAR_t])
                    c.ew(lambda e, us=us, gam=gam: e.tensor_tensor(out=ARo_t[64:128, :, 128:256], in0=us[64:128, 0:4, :], in1=gam[64:128, :, :],
                                                                   op=ALU.mult), reads=[Bus, Bgam], writes=[BAR_t])
                    if SUB < 4:
                        continue
                    if fwd:
                        src_t = NT - 1 - ti
                        ybS, BybS = r_ybl.next()
                        ybR, BybR = r_ybl.next()
                        for (dst_, Bdst_, kh, kl) in ((ybS, BybS, "ybs_h", "ybs_l"), (ybR, BybR, "ybr_h", "ybr_l")):
                            for q, key in enumerate((kh, kl)):
                                c.dma("sp", lambda e, dst_=dst_, q=q, key=key, src_t=src_t: e.dma_start(
                                    out=dst_[:, q, :], in_=ysc[key][src_t * 128:(src_t + 1) * 128, :]),
                                      reads=[B_ysc[key][src_t]], writes=[Bdst_])
                    heads_all = [(cp, x) for cp in range(4) for x in range(2)]
                    for g0 in range(0, 8, LS):
                        grp = heads_all[g0:g0 + LS]
                        cps = sorted(set(cp for cp, _ in grp))
                        toks, wus, st = {}, {}, {}
                        for cp in cps:
                            ps, pb = pp.next()
                            for q, src in enumerate((AT, BT, KT, vb)):
                                mm(ps[:, q * 128:(q + 1) * 128], src[:, cp, :], Kb("ident"), True, True, [BAT, BBT, BKT, Bvb, Bcsb], [pb[q]])
                            tok, Btok = r_tok.next()
                            evac(tok[:], ps[:].rearrange("p (a b) -> p a b", b=128), pb, [Btok])
                            toks[cp] = (tok, Btok)
                            wus[cp] = r_wu.next()
                        quads = []
                        for q0 in range(0, len(grp), 4):
                            zfq, Bzfq = r_zf.next()
                            zbq, Bzbq = r_zb.next()
                            quads.append(dict(heads=grp[q0:q0 + 4], zf=zfq, Bzf=Bzfq, zb=zbq, Bzb=Bzbq))
                        for hi_, (cp, x) in enumerate(grp):
                            qd = quads[hi_ // 4]
                            hq = hi_ % 4
                            tok, Btok = toks[cp]
                            AR_x = (ARe_t, ARo_t)[x]
                            hs = slice(x * 64, (x + 1) * 64)
                            psA, pbA = pp.next()
                            psB4, pbB4 = pp.next()
                            mm(psA[:, 0:128], AR_x[:, cp, 0:128], BT[:, cp, :], True, True, [BAR_t, BBT], [pbA[0]])
                            mm(psB4[:, 0:256], BT[:, cp, :], AR_x[:, cp, :], True, True, [BAR_t, BBT], pbB4[0:2])
                            mm(psB4[:, 256:512], KT[:, cp, :], AR_x[:, cp, :], True, True, [BAR_t, BKT], pbB4[2:4])
                            p0, Bp0 = r_p0.next()
                            c.op("dve", lambda e, p0=p0, psA=psA: e.tensor_tensor(out=p0[:], in0=psA[:, 0:128], in1=Kb("msl"), op=ALU.mult),
                                 reads=[pbA[0], Bcsb], writes=[Bp0])
                            e4, Be4 = r_e4.next()
                            c.op("dve", lambda e, e4=e4, psB4=psB4: e.tensor_tensor(out=e4[:], in0=psB4[:], in1=mask4[:], op=ALU.mult),
                                 reads=pbB4 + [Bm4], writes=[Be4])
                            mm(psA[:, 128:192], e4[:, 256:384], tok[:, 3, hs], True, True, [Be4, Btok], [pbA[1]])
                            zf, Bzf = qd["zf"][:, hq, :], qd["Bzf"]
                            c.op("act", lambda e, zf=zf, tok=tok, hs=hs: e.activation(out=zf[:, 0:64], in_=tok[:, 0, hs], func=AF.Copy),
                                 reads=[Btok], writes=[Bzf])
                            c.op("act", lambda e, zf=zf, psA=psA: e.activation(out=zf[:, 64:128], in_=psA[:, 128:192], func=AF.Copy),
                                 reads=[pbA[1]], writes=[Bzf])
                            st[(cp, x)] = dict(e4=e4, Be4=Be4, zf=zf, Bzf=Bzf, Pn=p0[:], PnT=e4[:, 0:128], BPn=[Bp0, Be4])
                        for qd in quads:
                            c.ew(lambda e, qd=qd: e.tensor_copy(out=qd["zb"][:], in_=qd["zf"][:]), reads=[qd["Bzf"]], writes=[qd["Bzb"]])
                        for it in range(7):
                            for qd in quads:
                                zfq, Bzfq, zbq, Bzbq = qd["zf"], qd["Bzf"], qd["zb"], qd["Bzb"]
                                psZ, pbZ = pp.next()
                                for hq, hd in enumerate(qd["heads"]):
                                    d_ = st[hd]
                                    mm(psZ[:, hq * 128:(hq + 1) * 128], d_["PnT"], zbq[:, hq, :], True, True, d_["BPn"] + [Bzbq], pbZ)
                                if it < 6:
                                    psS = [pp.next(), pp.next()]
                                    for hq, hd in enumerate(qd["heads"]):
                                        d_ = st[hd]
                                        ps_, pb_ = psS[hq // 2]
                                        o0 = (hq % 2) * 256
                                        mm(ps_[:, o0:o0 + 128], d_["PnT"], d_["Pn"], True, True, d_["BPn"], pb_)
                                        mm(ps_[:, o0 + 128:o0 + 256], d_["Pn"], d_["PnT"], True, True, d_["BPn"], pb_)
                                    c.op("dve", lambda e, zfq=zfq, zbq=zbq, psZ=psZ: e.tensor_tensor(
                                        out=zbq[:], in0=psZ[:].rearrange("p (a b) -> p a b", b=128), in1=zfq[:], op=ALU.add),
                                         reads=[pbZ, Bzfq], writes=[Bzbq])
                                c.op("dve", lambda e, zfq=zfq, psZ=psZ: e.tensor_tensor(
                                    out=zfq[:], in0=psZ[:].rearrange("p (a b) -> p a b", b=128), in1=zfq[:], op=ALU.add),
                                     reads=[pbZ, Bzfq], writes=[Bzfq])
                                if it < 6:
                                    ppq, Bppq = r_pp.next()
                                    for half in range(2):
                                        ps_, pb_ = psS[half]
                                        c.op("act", lambda e, ppq=ppq, ps_=ps_, half=half: e.activation(
                                            out=ppq[:, 2 * half:2 * half + 2, :], in_=ps_[:].rearrange("p (a b) -> p a b", b=256), func=AF.Copy),
                                             reads=pb_, writes=[Bppq])
                                    for hq, hd in enumerate(qd["heads"]):
                                        d_ = st[hd]
                                        d_["Pn"], d_["PnT"], d_["BPn"] = ppq[:, hq, 0:128], ppq[:, hq, 128:256], [Bppq]
                        for (cp, x) in grp:
                            d_ = st[(cp, x)]
                            wu, Bwu = wus[cp]
                            hs = slice(x * 64, (x + 1) * 64)
                            zf, Bzf = d_["zf"], d_["Bzf"]
                            c.op("act", lambda e, wu=wu, zf=zf, hs=hs: e.activation(out=wu[:, 0, hs], in_=zf[:, 0:64], func=AF.Copy),
                                 reads=[Bzf], writes=[Bwu])
                            c.ew(lambda e, wu=wu, zf=zf, hs=hs: e.tensor_copy(out=wu[:, 1, hs], in_=zf[:, 64:128]),
                                 reads=[Bzf], writes=[Bwu])
                        for cp in cps:
                            if (cp, 0) not in st or (cp, 1) not in st:
                                continue
                            tok, Btok = toks[cp]
                            wu, Bwu = wus[cp]
                            qt, Bqt = r_qt.next()
                            psQ, pbQ = pp.next()
                            for x in range(2):
                                e4, Be4 = st[(cp, x)]["e4"], st[(cp, x)]["Be4"]
                                AR_x = (ARe_t, ARo_t)[x]
                                hs = slice(x * 64, (x + 1) * 64)
                                mm(psQ[:, x * 128:(x + 1) * 128], wu[:, 0, :], e4[:, 128:256], True, True, [Bwu, Be4], [pbQ[x]])
                            for x in range(2):
                                AR_x = (ARe_t, ARo_t)[x]
                                hs = slice(x * 64, (x + 1) * 64)
                                c.op("dve", lambda e, qt=qt, psQ=psQ, x=x, hs=hs, AR_x=AR_x, cp=cp: e.tensor_tensor(
                                    out=qt[hs, :], in0=psQ[hs, x * 128:(x + 1) * 128], in1=AR_x[hs, cp, 128:256], op=ALU.add),
                                     reads=[pbQ[x], BAR_t], writes=[Bqt])
                            yo = YR_t[:, cp * 128:(cp + 1) * 128]
                            mm(yo, qt[:], Hb[:, cp, :], True, False, [Bqt, BH[cp]], [YR_b])
                            for x in range(2):
                                e4, Be4 = st[(cp, x)]["e4"], st[(cp, x)]["Be4"]
                                hs = slice(x * 64, (x + 1) * 64)
                                yx = YR_t[:, cp * 128 + x * 64:cp * 128 + (x + 1) * 64]
                                mm(yx, e4[:, 128:256], wu[:, 1, hs], False, False, [Be4, Bwu], [YR_b])
                                mm(yx, e4[:, 384:512], tok[:, 3, hs], False, x == 1, [Be4, Btok], [YR_b])
                            psM, pbM = pp.next()
                            mm(psM[:, 0:128], wu[:, 0, :], tok[:, 1, :], True, True, [Bwu, Btok], [pbM[0]])
                            mtb, Bmtb = r_mtb.next()
                            c.op("dve", lambda e, mtb=mtb, psM=psM: e.tensor_tensor(out=mtb[:], in0=psM[:, 0:128], in1=Kb("bmask"), op=ALU.mult),
                                 reads=[pbM[0], Bcsb], writes=[Bmtb])
                            psH, pbH = pp.next()
                            hsl = psH[:, 0:128]
                            mm(hsl, tok[:, 1, :], wu[:, 1, :], True, False, [Btok, Bwu], [pbH[0]])
                            mm(hsl, tok[:, 2, :], tok[:, 3, :], False, False, [Btok], [pbH[0]])
                            mm(hsl, Kb("ident"), Hb[:, cp, :], False, False, [Bcsb, BH[cp]], [pbH[0]])
                            mm(hsl, mtb[:], Hb[:, cp, :], False, True, [Bmtb, BH[cp]], [pbH[0]])
                            c.op("dve", lambda e, cp=cp, psH=psH, gam=gam: e.scalar_tensor_tensor(
                                out=Hb[:, cp, :], in0=psH[:, 0:128], scalar=gam[:, cp, 127:128], in1=Kb("bmask"), op0=ALU.mult, op1=ALU.mult),
                                 reads=[pbH[0], Bgam, Bcsb], writes=[BH[cp]])
                    yrh, Byrh = r_yhl.next()
                    split2(YR_t[:], yrh[:, 0, :], yrh[:, 1, :], [YR_b], Byrh, Byrh, psum=True)
                    if not fwd:
                        for q, key in enumerate(("ybr_h", "ybr_l")):
                            c.dma("sp", lambda e, yrh=yrh, q=q, key=key, ti=ti: e.dma_start(out=ysc[key][ti * 128:(ti + 1) * 128, :], in_=yrh[:, q, :]),
                                  reads=[Byrh], writes=[B_ysc[key][ti]])
                        continue
                    if SUB < 5:
                        continue

                    mix, Bmix = r_mix.next()

                    def ytrans(yh, Byh, yb, Byb):
                        ps, pb = pp.next()
                        for j in range(4):
                            o_ = ps[:, j * 128:(j + 1) * 128]
                            cs_ = slice(j * 128, (j + 1) * 128)
                            mm(o_, yh[:, 0, cs_], Kb("ident"), True, False, [Byh, Bcsb], [pb[j]])
                            mm(o_, yh[:, 1, cs_], Kb("ident"), False, False, [Byh, Bcsb], [pb[j]])
                            mm(o_, yb[:, 0, cs_], Kb("J"), False, False, [Byb, Bcsb], [pb[j]])
                            mm(o_, yb[:, 1, cs_], Kb("J"), False, True, [Byb, Bcsb], [pb[j]])
                        return ps, pb
                    ps, pb = ytrans(ysh, Bysh, ybS, BybS)
                    yg, Byg = AS(0)
                    for j in range(4):
                        c.op("dve", lambda e, j=j, yg=yg, xsf=xsf, ps=ps: e.scalar_tensor_tensor(
                            out=yg[:, j, :], in0=xsf[:, j, :], scalar=P("dch", j), in1=ps[:, j * 128:(j + 1) * 128], op0=ALU.mult, op1=ALU.add),
                             reads=[Bxsf, Bpv, pb[j]], writes=[Byg])
                    c.ew(lambda e, yg=yg, zs=zs: e.tensor_tensor(out=yg[:], in0=yg[:], in1=zs[:], op=ALU.mult), reads=[Byg, Bzs], writes=[Byg])
                    sq2, Bsq2 = AS(2)
                    c.ew(lambda e, sq2=sq2, yg=yg: e.tensor_tensor(out=sq2[:], in0=yg[:], in1=yg[:], op=ALU.mult), reads=[Byg], writes=[Bsq2])
                    hi, Bhi = r_sp.next()
                    lo, Blo = r_sp.next()
                    split2(sq2[:], hi[:], lo[:], [Bsq2], Bhi, Blo)
                    ps, pb = pp.next()
                    for g in range(2):
                        o_ = ps[:, g * 128:(g + 1) * 128]
                        for j in range(2):
                            mm(o_, Kb("o256"), hi[:, 2 * g + j, :], j == 0, False, [Bcsb, Bhi], [pb[g]])
                            mm(o_, Kb("o256"), lo[:, 2 * g + j, :], False, j == 1, [Bcsb, Blo], [pb[g]])
                    rs, Brs = r_st.next()
                    rs2, Brs2 = r_st.next()
                    rsqrt_eps(rs[:], ps[:, 0:128], RMS_EPS, [pb[0]], [Brs])
                    rsqrt_eps(rs2[:], ps[:, 128:256], RMS_EPS, [pb[1]], [Brs2])
                    for j in range(4):
                        rr_, Brr_ = (rs, Brs) if j < 2 else (rs2, Brs2)
                        c.op("dve", lambda e, j=j, mix=mix, yg=yg, rr_=rr_: e.scalar_tensor_tensor(
                            out=mix[:, j, :], in0=yg[:, j, :], scalar=P("nw", j), in1=rr_[:], op0=ALU.mult, op1=ALU.mult),
                             reads=[Byg, Bpv, Brr_], writes=[Bmix])
                    ps, pb = ytrans(yrh, Byrh, ybR, BybR)
                    yf, Byf = AS(1)
                    evac(yf[:], ps[:].rearrange("p (a b) -> p a b", b=128), pb, [Byf])
                    psm, pbm = blocksum4(yf[:], Byf, "bo64m")
                    c.op("dve", lambda e, yf=yf, psm=psm: e.tensor_tensor(out=yf[:], in0=yf[:], in1=psm, op=ALU.subtract), reads=[Byf] + pbm, writes=[Byf])
                    c.ew(lambda e, sq2=sq2, yf=yf: e.tensor_tensor(out=sq2[:], in0=yf[:], in1=yf[:], op=ALU.mult), reads=[Byf], writes=[Bsq2])
                    psv, pbv = blocksum4(sq2[:], Bsq2, "bo64m")
                    rsqrt_eps(sq2[:], psv, GN_EPS, pbv, [Bsq2])
                    c.ew(lambda e, yf=yf, sq2=sq2: e.tensor_tensor(out=yf[:], in0=yf[:], in1=sq2[:], op=ALU.mult), reads=[Byf, Bsq2], writes=[Byf])
                    c.ew(lambda e, yf=yf: e.tensor_tensor(out=yf[:], in0=yf[:], in1=bc4("lw"), op=ALU.mult), reads=[Byf, Bpv], writes=[Byf])
                    c.ew(lambda e, yf=yf: e.tensor_tensor(out=yf[:], in0=yf[:], in1=bc4("lb"), op=ALU.add), reads=[Byf, Bpv], writes=[Byf])
                    c.ew(lambda e, yf=yf, bonv=bonv: e.tensor_tensor(out=yf[:], in0=yf[:], in1=bonv[:], op=ALU.add), reads=[Byf, Bbonv], writes=[Byf])
                    c.ew(lambda e, mix=mix, yf=yf, gT=gT: e.tensor_tensor(out=mix[:, 4:8, :], in0=yf[:], in1=gT[:], op=ALU.mult),
                         reads=[Byf, BgT], writes=[Bmix])
                    xf, Bxf = AS(9, 2)
                    c.dma("sp", lambda e, xf=xf, col0=col0: e.dma_start(
                        out=xf, in_=xT[:, col0 + 2:col0 + 130].rearrange("(k p) t -> p k t", p=128)), writes=[Bxf])
                    pre, Bpre = AS(3, 2)
                    for half in range(2):
                        ps, pb = pp.next()
                        for j in range(4):
                            dc = half * 4 + j
                            for k in range(8):
                                mm(ps[:, j * 128:(j + 1) * 128], wo[:, k, dc * 128:(dc + 1) * 128], mix[:, k, :], k == 0, k == 7,
                                   [Bwo, Bmix], [pb[j]])
                        c.op("dve", lambda e, half=half, pre=pre, xf=xf, ps=ps: e.scalar_tensor_tensor(
                            out=pre[:, half * 4:(half + 1) * 4, :], in0=xf[:, half * 4:(half + 1) * 4, :], scalar=ALPHA,
                            in1=ps[:].rearrange("p (a b) -> p a b", b=128), op0=ALU.mult, op1=ALU.add), reads=[Bxf] + pb, writes=[Bpre])
                    hout, Bhout = AS(7, 2)
                    sqL, BsqL = AS(5, 2)
                    layer_norm(pp, Kb, pre, Bpre, hout, Bhout, sqL, BsqL, r_st, r_sps, P, "l1w", "l1b", Bcsb, Bpv, 128)
                    t0 = ti * C
                    c.dma("sp", lambda e, hout=hout, t0=t0: e.dma_start(
                        out=hT[:, t0:t0 + 128].rearrange("(k p) t -> p k t", p=128), in_=hout), reads=[Bhout], writes=[B_hT[ti]])
                c.full_barrier()
                c.run_block()

        def ffn_phase():
            NF = 256 if TT % 256 == 0 else 128
            with ExitStack() as es:
                sb = lambda n, s, dt=F32: es.enter_context(nc.sbuf_tensor(n, s, dt))
                rot = lambda n, s, dt=F32, k=2: Rot(nc, es, n, s, dt, k)
                pp = PsumPool(nc, es, ["pf%d" % i for i in range(8)])
                wfi = sb("wfi", [128, 8, 2 * DFF], BF16); Bwfi = Buf()
                wfo = sb("wfo", [128, NFC, D], BF16); Bwfo = Buf()
                for k in range(8):
                    c.dma("pool", lambda e, k=k: e.dma_start(out=wfi[:, k, :], in_=w_fi[k * 128:(k + 1) * 128, :]), writes=[Bwfi])
                for k in range(NFC):
                    c.dma("pool", lambda e, k=k: e.dma_start(out=wfo[:, k, :], in_=w_fo[k * 128:(k + 1) * 128, :]), writes=[Bwfo])
                pv = sb("pv2", [128, NPV]); Bpv = Buf()
                c.dma("sp", lambda e: e.dma_start(out=pv[:], in_=pvec), writes=[Bpv])
                csb = sb("csb2", [128, NCST * 128], BF16); Bcsb = Buf()
                c.dma("pool", lambda e: e.dma_start(out=csb[:], in_=cst), writes=[Bcsb])
                Kb = lambda n: csb[:, CST[n] * 128:(CST[n] + 1) * 128]
                P = lambda n, j=0, w=1: pv[:, PV[n][0] + j:PV[n][0] + j + w]
                r_hf = rot("hf", [128, 8, NF], F32, 1)
                r_hb = rot("hb", [128, 8, NF], BF16, 1)
                r_act = rot("actt", [128, NFC, NF], BF16, 1)
                r_sl = rot("sl", [128, NF], F32, 2)
                r_sq = rot("sq2", [128, 8, NF], F32, 1)
                r_st = rot("st2", [128, NF], F32, 4)
                r_sps = rot("sps2", [128, NF], BF16, 2)
                for fi in range(min(TT // NF, NTL)):
                    t0 = fi * NF
                    deps = [B_hT[t] for t in range(t0 // C, (t0 + NF) // C)]
                    hf, Bhf = r_hf.next()
                    hb, Bhb = r_hb.next()
                    c.dma("sp", lambda e, hf=hf, t0=t0: e.dma_start(out=hf[:], in_=hT[:, t0:t0 + NF].rearrange("(k p) t -> p k t", p=128)),
                          reads=deps, writes=[Bhf])
                    c.op("act", lambda e, hf=hf, hb=hb: e.activation(out=hb[:], in_=hf[:], func=AF.Copy), reads=[Bhf], writes=[Bhb])
                    actt, Bact = r_act.next()
                    for fc in range(NFC):
                        ps, pb = pp.next()
                        for k in range(8):
                            mm(ps[:, 0:NF], wfi[:, k, fc * 128:(fc + 1) * 128], hb[:, k, :], k == 0, k == 7, [Bwfi, Bhb], pb[0:2])
                        for k in range(8):
                            mm(ps[:, 256:256 + NF], wfi[:, k, DFF + fc * 128:DFF + (fc + 1) * 128], hb[:, k, :], k == 0, k == 7,
                               [Bwfi, Bhb], pb[2:4])
                        sl, Bsl = r_sl.next()
                        c.op("act", lambda e, sl=sl, ps=ps: e.activation(out=sl[:], in_=ps[:, 0:NF], func=AF.Silu), reads=pb[0:2], writes=[Bsl])
                        c.op("dve", lambda e, fc=fc, sl=sl, ps=ps, actt=actt: e.tensor_tensor(
                            out=actt[:, fc, :], in0=ps[:, 256:256 + NF], in1=sl[:], op=ALU.mult), reads=pb[2:4] + [Bsl], writes=[Bact])
                    for half in range(4):
                        ps, pb = pp.next()
                        for j in range(2):
                            dc = half * 2 + j
                            for k in range(NFC):
                                mm(ps[:, j * 256:j * 256 + NF], wfo[:, k, dc * 128:(dc + 1) * 128], actt[:, k, :], k == 0, k == NFC - 1,
                                   [Bwfo, Bact], pb[2 * j:2 * j + 2])
                        for j in range(2):
                            dc = half * 2 + j
                            c.op("dve", lambda e, dc=dc, j=j, hf=hf, ps=ps: e.scalar_tensor_tensor(
                                out=hf[:, dc, :], in0=hf[:, dc, :], scalar=ALPHA, in1=ps[:, j * 256:j * 256 + NF], op0=ALU.mult, op1=ALU.add),
                                 reads=[Bhf] + pb[2 * j:2 * j + 2], writes=[Bhf])
                    sq, Bsq = r_sq.next()
                    out, Bout = sq, Bsq
                    layer_norm(pp, Kb, hf[:], Bhf, out[:], Bout, sq[:], Bsq, r_st, r_sps, P, "l2w", "l2b", Bcsb, Bpv, NF)
                    Bo = Buf()
                    B_out.append(Bo)
                    c.dma("sp", lambda e, out=out, t0=t0: e.dma_start(out=yT[:, t0:t0 + NF].rearrange("(k p) t -> p k t", p=128), in_=out[:]),
                          reads=[Bout], writes=[Bo])
                c.wait_all("sp", B_out)
                c.run_block()

        mixer_phase(False)
        if stage >= 2:
            mixer_phase(True)
        if stage >= 3:
            ffn_phase()
    return nc


def _consts():
    i = np.arange(128)
    p, f = i[:, None], i[None, :]
    blocks = {
        "ident": (p == f), "J": (p + f == 127), "msl": (f < p), "msu": (p < f), "mui": (p <= f),
        "bmask": ((p // 64) == (f // 64)), "bo64": ((p // 64) == (f // 64)),
    }
    out = np.zeros((128, NCST * 128), np.float32)
    for n, m in blocks.items():
        out[:, CST[n] * 128:(CST[n] + 1) * 128] = m.astype(np.float32)
    out[:, CST["bo64m"] * 128:(CST["bo64m"] + 1) * 128] = blocks["bo64"].astype(np.float32) / 64.0
    out[:, CST["o1024"] * 128:(CST["o1024"] + 1) * 128] = 1.0 / 1024.0
    out[:, CST["o256"] * 128:(CST["o256"] + 1) * 128] = 1.0 / 256.0
    out[:, CST["ones"] * 128:(CST["ones"] + 1) * 128] = 1.0
    m8 = np.zeros((8, 1024), np.float32)
    for h in range(8):
        m8[h, h * 128:(h + 1) * 128] = 1.0
    return out, m8


def _chunkvec(v, n):
    return np.ascontiguousarray(np.asarray(v, np.float32).reshape(n, 128).T)


def prep_params(inp):
    g = lambda n: np.asarray(inp[n], np.float32)[0]
    w_in = g("w_in")
    M_W, CONV = 512, 1024
    o_z, o_xbc, o_dt = 0, 512, 1536
    o_r = 1544
    W = np.zeros((D, NCH_IN * 128), np.float32)
    W[:, 0:512] = w_in[:, o_z:o_z + 512]
    W[:, 512:1536] = w_in[:, o_xbc:o_xbc + 1024]
    W[:, 1536:3072] = w_in[:, o_r:o_r + 1536]
    W[:, CWL * 128:CWL * 128 + 64] = w_in[:, o_r + 1536:o_r + 1600]
    W[:, CAL * 128:CAL * 128 + 64] = w_in[:, o_r + 1600:o_r + 1664]
    W[:, CGL * 128:CGL * 128 + 128] = w_in[:, o_r + 1664:o_r + 1792]
    W[:, CDT * 128:CDT * 128 + 8] = w_in[:, o_dt:o_dt + 8]
    pv = np.zeros((128, NPV), np.float32)
    def put(name, arr):
        o, w = PV[name]
        pv[:, o:o + w] = arr
    cw = g("conv_w")
    cwp = np.zeros((128, 8, 5), np.float32)
    for j in range(8):
        cwp[:, j, :] = cw[:, j * 128:(j + 1) * 128].T
    put("cw", cwp.reshape(128, 40))
    put("cb", _chunkvec(g("conv_b"), 8))
    put("dch", _chunkvec(np.repeat(g("m_d"), 64), 4))
    put("nw", _chunkvec(g("m_norm_w"), 4))
    def mu15(v):
        o = np.zeros((128, 15), np.float32)
        o[:, 0:12] = _chunkvec(v[0:1536], 12)
        o[0:64, 12] = v[1536:1600]
        o[0:64, 13] = v[1600:1664]
        o[:, 14] = v[1664:1792]
        return o
    put("mp", mu15(g("r_mu_prev")))
    put("mn", mu15(g("r_mu_next")))
    put("w0f", _chunkvec(g("r_w0_f"), 4))
    put("w0b", _chunkvec(g("r_w0_b"), 4))
    put("a0", _chunkvec(g("r_a0"), 4))
    put("kk", _chunkvec(g("r_k_k"), 4))
    put("ka", _chunkvec(g("r_k_a"), 4))
    put("rk", _chunkvec(g("r_r_k").reshape(-1), 4))
    put("lw", _chunkvec(g("r_lnx_w"), 4))
    put("lb", _chunkvec(g("r_lnx_b"), 4))
    put("l1w", _chunkvec(g("ln1_w"), 8))
    put("l1b", _chunkvec(g("ln1_b"), 8))
    put("l2w", _chunkvec(g("ln2_w"), 8))
    put("l2b", _chunkvec(g("ln2_b"), 8))
    p8 = np.zeros((8, 8), np.float32)
    p8[:, 0] = g("m_dt_bias_f"); p8[:, 1] = g("m_dt_bias_b")
    p8[:, 2] = g("m_a_log_f"); p8[:, 3] = g("m_a_log_b")
    lowr = np.zeros((128, 4 * 512), np.float32)
    lowr[0:64, 0:512] = g("r_w2_f")
    lowr[0:64, 512:1024] = g("r_w2_b")
    lowr[0:64, 1024:1536] = g("r_a2")
    lowr[:, 1536:2048] = g("r_g2")
    cst, m8 = _consts()
    return dict(w_in=W, w_out=np.ascontiguousarray(g("w_out")), w_fi=np.ascontiguousarray(g("w_ffn_in")),
                w_fo=np.ascontiguousarray(g("w_ffn_out")), p8=p8, lowr=lowr, cst=cst, m8=m8), pv


def layout_core(units, link, TU):
    xT = np.zeros((D, 2 * (TU + 4)), np.float32)
    xTr = np.zeros((D, 2 * (TU + 4)), np.float32)
    for u, (seq, s) in enumerate(units):
        if seq is None:
            continue
        T = seq.shape[0]
        lo, hi = s - 2, s + TU + 2
        blk = np.zeros((TU + 4, D), np.float32)
        a, b = max(lo, 0), min(hi, T)
        blk[a - lo:b - lo] = seq[a:b]
        xT[:, u * (TU + 4):(u + 1) * (TU + 4)] = blk.T
        ur = 1 - u
        xTr[:, ur * (TU + 4):(ur + 1) * (TU + 4)] = blk[::-1].T
    return xT, xTr


_NC_CACHE = {}


def run_cores(assign, params, pv, TU, n_cores):
    if TU not in _NC_CACHE:
        import os
        _NC_CACHE[TU] = build(TU, stage=int(os.environ.get("MK_STAGE", "3")))
    nc = _NC_CACHE[TU]
    in_maps = []
    for units, link in assign:
        xT, xTr = layout_core(units, link, TU)
        pvc = pv.copy()
        pvc[:, PV["link"][0]] = float(link)
        m = dict(params)
        m.update(xT=xT, xTr=xTr, pvec=pvc)
        in_maps.append(m)
    res = run_bass_kernel_spmd(nc, in_maps, core_ids=list(range(n_cores)))
    return [r["yT"] for r in res.results]


def kernel(**inputs):
    xp = np.asarray(inputs["x_prompt"], np.float32)
    xs = np.asarray(inputs["x_sample"], np.float32)
    params, pv = prep_params(inputs)
    TU = xs.shape[1]
    assert xp.shape[1] == 2 * TU
    assign = []
    for b in range(xp.shape[0]):
        assign.append(([(xp[b], 0), (xp[b], TU)], 1))
    nb = xs.shape[0]
    slots = [[(xs[i], 0)] for i in range(nb)]
    n_rest = 8 - len(assign)
    per = [[] for _ in range(n_rest)]
    for i in range(nb):
        per[i % n_rest].append(i)
    for lst in per:
        u = [(xs[i], 0) for i in lst]
        while len(u) < 2:
            u.append((None, 0))
        assign.append((u, 0))
    outs = run_cores(assign, params, pv, TU, 8)
    y_p = np.zeros_like(xp)
    y_s = np.zeros_like(xs)
    for b in range(xp.shape[0]):
        y_p[b] = outs[b].T
    for ci, lst in enumerate(per):
        o = outs[xp.shape[0] + ci]
        for u, i in enumerate(lst):
            y_s[i] = o[:, u * TU:(u + 1) * TU].T
    return (y_p, y_s)
```

```python
import numpy as np
from contextlib import ExitStack
import concourse.bass as bass
import concourse.mybir as mybir
from concourse.bass_utils import run_bass_kernel_spmd

F32 = mybir.dt.float32
BF16 = mybir.dt.bfloat16
AF = mybir.ActivationFunctionType
ALU = mybir.AluOpType

D = 1024
C = 128
NCH_IN = 28
DFF = 2816
NFC = DFF // 128
ALPHA = 2.0 ** 0.25
LN_EPS = 1e-5
RMS_EPS = 1e-5
GN_EPS = 64e-5
CDEC = float(np.exp(-0.5))

CZ, CXS, CB, CC_, CR, CK, CV, CWL, CAL, CGL, CDT = 0, 4, 8, 10, 12, 16, 20, 24, 25, 26, 27

PV = {}
_o = 0
for _n, _w in [("cw", 40), ("cb", 8), ("dch", 4), ("nw", 4), ("mp", 15), ("mn", 15), ("w0f", 4), ("w0b", 4),
               ("a0", 4), ("kk", 4), ("ka", 4), ("rk", 4), ("lw", 4), ("lb", 4), ("l1w", 8), ("l1b", 8),
               ("l2w", 8), ("l2b", 8), ("link", 1)]:
    PV[_n] = (_o, _w)
    _o += _w
NPV = _o
CST = {n: i for i, n in enumerate(["ident", "J", "msl", "msu", "mui", "bmask", "bo64", "bo64m", "o1024", "o256", "ones"])}
NCST = len(CST)


class Buf:
    __slots__ = ("name", "w", "r", "psum")

    def __init__(self, name="", psum=False):
        self.name = name
        self.w = None
        self.r = []
        self.psum = psum


class Ctx:
    ENGS = ("pe", "act", "dve", "pool", "sp")

    def __init__(self, nc, es, n_dma_slots=6):
        self.nc = nc
        self.ops = {e: [] for e in self.ENGS}
        self.sems = {}
        self.count = {}
        for e in self.ENGS:
            self.sems[e] = es.enter_context(nc.semaphore("s_" + e))
            self.count[e] = 0
        self.waited = {e: {} for e in self.ENGS}
        self.slots = {}
        for q in ("sp", "pool"):
            self.slots[q] = []
            for i in range(n_dma_slots):
                key = "d_%s_%d" % (q, i)
                self.sems[key] = es.enter_context(nc.semaphore(key))
                self.count[key] = 0
                self.slots[q].append(key)
        self.slot_rr = {"sp": 0, "pool": 0}
        self.rr = 0
        import os
        self.ew_engs = tuple(os.environ.get("MK_EW", "dve").split(","))
        self.noself = tuple(x for x in os.environ.get("MK_NOSELF", "").split(",") if x)
        self.selfgap = int(os.environ.get("MK_SELFGAP", "0"))
        self.PEX = Buf("PEX")
        self.pex_on = os.environ.get("MK_PEX", "0") == "1"
        self.cut = int(os.environ.get("MK_CUT", "0"))
        self.nops = 0
        self.last_lines = None

    def _cut(self):
        self.nops += 1
        if self.cut and self.nops >= self.cut:
            if self.nops == self.cut:
                import sys
                f = sys._getframe(2)
                lines = []
                while f is not None and len(lines) < 5:
                    lines.append(f.f_lineno)
                    f = f.f_back
                print("MK_CUT: first dropped op #%d at lines %s" % (self.nops, lines), flush=True)
            return True
        return False

    def _deps(self, eng, reads, writes):
        need = {}
        for b in reads:
            ev = b.w
            if ev is not None and need.get(ev[0], 0) < ev[1]:
                need[ev[0]] = ev[1]
        for b in writes:
            ev = b.w
            if ev is not None and need.get(ev[0], 0) < ev[1]:
                need[ev[0]] = ev[1]
            for ev in b.r:
                if need.get(ev[0], 0) < ev[1]:
                    need[ev[0]] = ev[1]
        waits = []
        wd = self.waited[eng]
        for k, v in need.items():
            if k == eng and (eng == "pe" or eng in self.noself):
                continue
            if k == eng and self.selfgap and v <= self.count[eng] - self.selfgap:
                continue
            if wd.get(k, 0) >= v:
                continue
            wd[k] = v
            waits.append((k, v))
        return waits

    def _mark(self, ev, reads, writes):
        for b in reads:
            b.r.append(ev)
            if len(b.r) > 24:
                m = {}
                for k, v in b.r:
                    if m.get(k, 0) < v:
                        m[k] = v
                b.r = list(m.items())
        for b in writes:
            b.w = ev
            b.r = []

    @staticmethod
    def _flat(bs):
        out = []
        for b in bs:
            if isinstance(b, (list, tuple)):
                out.extend(Ctx._flat(b))
            else:
                out.append(b)
        return out

    def op(self, eng, fn, reads=(), writes=()):
        if self._cut():
            return
        reads, writes = self._flat(reads), self._flat(writes)
        pr = [b for b in reads if b.psum]
        if pr:
            reads = [b for b in reads if not b.psum]
            writes = list(writes) + pr
        if self.pex_on:
            if eng == "pe":
                reads = list(reads) + [self.PEX]
            elif eng == "dve" and any(b.psum for b in writes):
                writes = list(writes) + [self.PEX]
        waits = self._deps(eng, reads, writes)
        self.count[eng] += 1
        ev = (eng, self.count[eng])
        self.ops[eng].append((waits, fn, eng, 1))
        self._mark(ev, reads, writes)

    def dma(self, q, fn, reads=(), writes=()):
        if self._cut():
            return
        slot = self.slots[q][self.slot_rr[q] % len(self.slots[q])]
        self.slot_rr[q] += 1
        reads, writes = self._flat(reads), self._flat(writes)
        waits = self._deps(q, reads, writes)
        prev = self.count[slot]
        if prev > 0 and self.waited[q].get(slot, 0) < prev:
            self.waited[q][slot] = prev
            waits.append((slot, prev))
        self.count[slot] += 16
        ev = (slot, self.count[slot])
        self.ops[q].append((waits, fn, slot, 16))
        self._mark(ev, reads, writes)

    def wait_all(self, eng, bufs):
        if self.cut and self.nops >= self.cut:
            return
        waits = self._deps(eng, self._flat(bufs), ())
        self.ops[eng].append((waits, None, None, 0))

    def ew(self, fn, reads=(), writes=(), engs=None):
        if engs is None:
            engs = self.ew_engs
        e = engs[self.rr % len(engs)]
        self.rr += 1
        self.op(e, fn, reads, writes)

    def full_barrier(self):
        for eng in self.ENGS:
            waits = []
            for k, v in self.count.items():
                if v > 0 and self.waited[eng].get(k, 0) < v and not (k == eng):
                    self.waited[eng][k] = v
                    waits.append((k, v))
            self.ops[eng].append((waits, None, None, 0))

    def replay(self, engname, e):
        sems = self.sems
        for waits, fn, semkey, inc in self.ops[engname]:
            for k, v in waits:
                e.wait_ge(sems[k], v)
            if fn is not None:
                fn(e).then_inc(sems[semkey], inc)

    def run_block(self):
        nc = self.nc
        with nc.Block() as block:
            @block.tensor
            def _(e):
                self.replay("pe", e)

            @block.scalar
            def _(e):
                self.replay("act", e)

            @block.vector
            def _(e):
                self.replay("dve", e)

            @block.gpsimd
            def _(e):
                self.replay("pool", e)

            @block.sync
            def _(e):
                self.replay("sp", e)
        self.ops = {e: [] for e in self.ENGS}


class Rot:
    def __init__(self, nc, es, name, shape, dt, n=2):
        self.t = [es.enter_context(nc.sbuf_tensor("%s_%d" % (name, i), shape, dt)) for i in range(n)]
        self.b = [Buf(name) for _ in range(n)]
        self.i = 0

    def next(self):
        k = self.i % len(self.t)
        self.i += 1
        return self.t[k], self.b[k]


class PsumPool:
    def __init__(self, nc, es, names):
        self.t = [es.enter_context(nc.psum_tensor(n, [128, 512], F32)) for n in names]
        self.b = [[Buf(n, psum=True)] * 4 for n in names]
        self.i = 0

    def next(self):
        k = self.i % len(self.t)
        self.i += 1
        return self.t[k], self.b[k]


def build(TU, stage=3):
    import os
    SUB = float(os.environ.get('MK_SUB', '9'))
    NTL = int(os.environ.get('MK_NT', '999'))
    NTU = TU // C
    NT = 2 * NTU
    TT = 2 * TU
    XW = 2 * (TU + 4)
    nc = bass.Bass("TRN2", target_bir_lowering=False)
    dram = lambda n, s, dt=F32, kind="ExternalInput": nc.dram_tensor(n, s, dt, kind=kind).ap()
    xT = dram("xT", [D, XW])
    xTr = dram("xTr", [D, XW])
    w_in = dram("w_in", [D, NCH_IN * 128])
    w_out = dram("w_out", [D, D])
    w_fi = dram("w_fi", [D, 2 * DFF])
    w_fo = dram("w_fo", [DFF, D])
    pvec = dram("pvec", [128, NPV])
    p8 = dram("p8", [8, 8])
    lowr = dram("lowr", [128, 4 * 512])
    cst = dram("cst", [128, NCST * 128])
    m8 = dram("m8", [8, 1024])
    yT = dram("yT", [D, TT], kind="ExternalOutput")
    ysc = {k: dram(k, [NT * 128, 512], BF16, kind="Internal") for k in ("ybr_h", "ybr_l", "ybs_h", "ybs_l")}
    hT = dram("hT", [D, TT], kind="Internal")

    with ExitStack() as es0:
        c = Ctx(nc, es0)
        B_ysc = {k: [Buf() for _ in range(NT)] for k in ysc}
        B_hT = [Buf() for _ in range(NT)]
        B_out = []

        def mk_mm(c):
            def mm(out, lhsT, rhs, start, stop, reads, writes):
                c.op("pe", lambda e: e.matmul(out, lhsT=lhsT, rhs=rhs, start=start, stop=stop), reads=reads, writes=writes)
            return mm
        mm = mk_mm(c)

        def split2(src, hi, lo, reads, Bhi, Blo, psum=False):
            c.op("act", lambda e: e.activation(out=hi, in_=src, func=AF.Copy), reads=reads, writes=[Bhi])
            if psum:
                c.op("dve", lambda e: e.tensor_tensor(out=lo, in0=src, in1=hi, op=ALU.subtract), reads=list(reads) + [Bhi], writes=[Blo])
            else:
                c.ew(lambda e: e.tensor_tensor(out=lo, in0=src, in1=hi, op=ALU.subtract), reads=list(reads) + [Bhi], writes=[Blo])

        def rsqrt_eps(out, in_, eps, reads, writes):
            c.op("dve", lambda e: e.tensor_scalar(out=out, in0=in_, scalar1=float(eps), scalar2=None, op0=ALU.add), reads=reads, writes=writes)
            c.op("act", lambda e: e.activation(out=out, in_=out, func=AF.Sqrt), reads=writes, writes=writes)
            c.op("dve", lambda e: e.reciprocal(out=out, in_=out), reads=writes, writes=writes)

        def layer_norm(pp, Kb, pre, Bpre, out, Bout, sq, Bsq, r_st, r_sp, P, wn, bn, Bcsb, Bpv, N):
            def stat(src, Bsrc):
                red, Bred = r_st.next()
                c.op("dve", lambda e: e.tensor_reduce(out=red[:, 0:N], in_=src.rearrange("p a n -> p n a"), axis=mybir.AxisListType.X, op=ALU.add),
                     reads=[Bsrc], writes=[Bred])
                hi, Bhi = r_sp.next()
                lo, Blo = r_sp.next()
                split2(red[:, 0:N], hi[:, 0:N], lo[:, 0:N], [Bred], Bhi, Blo)
                ps, pb = pp.next()
                mm(ps[:, 0:N], Kb("o1024"), hi[:, 0:N], True, False, [Bcsb, Bhi], pb)
                mm(ps[:, 0:N], Kb("o1024"), lo[:, 0:N], False, True, [Bcsb, Blo], pb)
                return ps, pb
            ps, pb = stat(pre, Bpre)
            mu, Bmu = r_st.next()
            c.op("act", lambda e: e.activation(out=mu[:, 0:N], in_=ps[:, 0:N], func=AF.Copy), reads=pb, writes=[Bmu])
            bcN = lambda t: t[:, 0:N].unsqueeze(1).to_broadcast([128, 8, N])
            bcP = lambda n: P(n, 0, 8).unsqueeze(2).to_broadcast([128, 8, N])
            c.ew(lambda e: e.tensor_tensor(out=pre, in0=pre, in1=bcN(mu), op=ALU.subtract), reads=[Bpre, Bmu], writes=[Bpre])
            c.ew(lambda e: e.tensor_tensor(out=sq, in0=pre, in1=pre, op=ALU.mult), reads=[Bpre], writes=[Bsq])
            ps2, pb2 = stat(sq, Bsq)
            rs, Brs = r_st.next()
            rsqrt_eps(rs[:, 0:N], ps2[:, 0:N], LN_EPS, pb2, [Brs])
            c.ew(lambda e: e.tensor_tensor(out=pre, in0=pre, in1=bcN(rs), op=ALU.mult), reads=[Bpre, Brs], writes=[Bpre])
            c.ew(lambda e: e.tensor_tensor(out=pre, in0=pre, in1=bcP(wn), op=ALU.mult), reads=[Bpre, Bpv], writes=[Bpre])
            c.ew(lambda e: e.tensor_tensor(out=out, in0=pre, in1=bcP(bn), op=ALU.add), reads=[Bpre, Bpv], writes=[Bout])

        def mixer_phase(fwd):
            with ExitStack() as es:
                sfx = "F" if fwd else "B"
                sb = lambda n, s, dt=F32: es.enter_context(nc.sbuf_tensor(n + sfx, s, dt))
                rot = lambda n, s, dt=F32, k=2: Rot(nc, es, n + sfx, s, dt, k)
                pp = PsumPool(nc, es, ["pp%d%s" % (i, sfx) for i in range(6)])
                YR_t = es.enter_context(nc.psum_tensor("YR" + sfx, [128, 512], F32)); YR_b = Buf("YR", psum=True)
                YS_t = es.enter_context(nc.psum_tensor("YS" + sfx, [128, 512], F32)); YS_b = Buf("YS", psum=True)
                xsrc = xT if fwd else xTr

                class Fix:
                    def __init__(self, t, b):
                        self.t, self.b = t, b

                    def next(self):
                        return self.t, self.b
                one = lambda n, s, dt=F32: Fix(sb(n, s, dt), Buf(n))

                win = sb("win", [128, 8, NCH_IN * 128], BF16); Bwin = Buf()
                for k in range(8):
                    c.dma("pool", lambda e, k=k: e.dma_start(out=win[:, k, :], in_=w_in[k * 128:(k + 1) * 128, :]), writes=[Bwin])
                if fwd:
                    wo = sb("wo", [128, 8, D], BF16); Bwo = Buf()
                    for k in range(8):
                        c.dma("pool", lambda e, k=k: e.dma_start(out=wo[:, k, :], in_=w_out[k * 128:(k + 1) * 128, :]), writes=[Bwo])
                pv = sb("pv", [128, NPV]); Bpv = Buf()
                c.dma("sp", lambda e: e.dma_start(out=pv[:], in_=pvec), writes=[Bpv])
                p8t = sb("p8t", [8, 8]); Bp8 = Buf()
                c.dma("sp", lambda e: e.dma_start(out=p8t[:], in_=p8), writes=[Bp8])
                lrb = sb("lrb", [128, 3 * 512], BF16); Blr = Buf()
                w2o = 0 if fwd else 512
                c.dma("pool", lambda e: e.dma_start(out=lrb[:, 0:512], in_=lowr[:, w2o:w2o + 512]), writes=[Blr])
                c.dma("pool", lambda e: e.dma_start(out=lrb[:, 512:1536], in_=lowr[:, 1024:2048]), writes=[Blr])
                csb = sb("csb", [128, NCST * 128], BF16); Bcsb = Buf()
                c.dma("pool", lambda e: e.dma_start(out=csb[:], in_=cst), writes=[Bcsb])
                m8b = sb("m8b", [8, 1024], BF16); Bm8 = Buf()
                c.dma("pool", lambda e: e.dma_start(out=m8b[:], in_=m8), writes=[Bm8])
                Kb = lambda n: csb[:, CST[n] * 128:(CST[n] + 1) * 128]
                mask4 = sb("mask4", [128, 512], BF16); Bm4 = Buf()
                for q, n in enumerate(["msu", "mui", "msu", "mui"]):
                    c.op("dve", lambda e, q=q, n=n: e.tensor_copy(out=mask4[:, q * 128:(q + 1) * 128], in_=Kb(n)),
                         reads=[Bcsb], writes=[Bm4])
                P = lambda n, j=0, w=1: pv[:, PV[n][0] + j:PV[n][0] + j + w]
                der = sb("der", [128, 15 + 4]); Bder = Buf()
                c.op("dve", lambda e: e.tensor_tensor(out=der[:, 0:15], in0=P("mp", 0, 15), in1=P("mn", 0, 15), op=ALU.add),
                     reads=[Bpv], writes=[Bder])
                c.op("dve", lambda e: e.tensor_scalar(out=der[:, 0:15], in0=der[:, 0:15], scalar1=-1.0, scalar2=1.0,
                                                      op0=ALU.mult, op1=ALU.add), reads=[Bder], writes=[Bder])
                c.op("dve", lambda e: e.tensor_scalar(out=der[:, 15:19], in0=P("ka", 0, 4), scalar1=-1.0, scalar2=1.0,
                                                      op0=ALU.mult, op1=ALU.add), reads=[Bpv], writes=[Bder])
                a8 = sb("a8", [8, 1]); Ba8 = Buf()
                acol = 2 if fwd else 3
                c.op("act", lambda e: e.activation(out=a8[:], in_=p8t[:, acol:acol + 1], func=AF.Exp), reads=[Bp8], writes=[Ba8])
                c.op("dve", lambda e: e.tensor_scalar(out=a8[:], in0=a8[:], scalar1=-1.0, scalar2=None, op0=ALU.mult),
                     reads=[Ba8], writes=[Ba8])
                dtb = p8t[:, (0 if fwd else 1):(1 if fwd else 2)]
                w2 = lrb[:, 0:512]
                a2 = lrb[:, 512:1024]
                g2 = lrb[:, 1024:1536]
                w0n = "w0f" if fwd else "w0b"
                mpn, mnn = ("mp", "mn") if fwd else ("mn", "mp")

                Hb = sb("Hb", [128, 4, 128], BF16); BH = [Buf() for _ in range(4)]
                STf = sb("STf", [128, 512]); STb = sb("STb", [128, 512], BF16); BST = Buf()
                c.op("pool", lambda e: e.memset(Hb[:], 0.0), writes=BH)
                c.op("pool", lambda e: e.memset(STf[:], 0.0), writes=[BST])
                c.op("pool", lambda e: e.memset(STb[:], 0.0), writes=[BST])
                ARe_t = sb("ARe", [128, 4, 256], BF16)
                ARo_t = sb("ARo", [128, 4, 256], BF16)
                BAR_t = Buf()
                c.op("pool", lambda e: e.memset(ARe_t[:], 0.0), writes=[BAR_t])
                c.op("pool", lambda e: e.memset(ARo_t[:], 0.0), writes=[BAR_t])
                CSt = sb("CS", [128, 4, 129]); BCSt = Buf()
                c.op("pool", lambda e: e.memset(CSt[:], 0.0), writes=[BCSt])
                ptmp = sb("ptmp", [128, 128]); Bptmp = Buf()
                Bfence = Buf()
                ones_t = sb("ones_t", [128, 128]); Bones = Buf()
                c.op("pool", lambda e: e.memset(ones_t[:], 1.0), writes=[Bones])
                DT3 = sb("DT3", [128, 3, 2, 128], BF16); BDT3 = Buf()
                RB3 = sb("RB3", [128, 2, 8, 128], BF16); BRB3 = [Buf(), Buf()]
                c.op("pool", lambda e: e.memset(DT3[:], 0.0), writes=[BDT3])
                c.op("pool", lambda e: e.memset(RB3[:], 0.0), writes=BRB3)

                arena = sb("arena", [128, 11 * 512]); Bar = [Buf() for _ in range(11)]

                def AS(i, n=1, parts=128, nch=None):
                    nch = 4 * n if nch is None else nch
                    return (arena[0:parts, 512 * i:512 * i + 128 * nch].rearrange("p (a b) -> p a b", b=128), Bar[i:i + n])

                def AU(i, nchunks):
                    return (arena[:, 512 * i:512 * i + 132 * nchunks].rearrange("p (a b) -> p a b", b=132), Bar[i:i + 4])

                r_xb = rot("xb", [128, 8, 132], BF16, 1 if fwd else 2)
                if fwd:
                    U1t, BU1 = None, None
                else:
                    U1t = sb("U1", [128, 13, 132]); BU1 = Buf("U1")
                r_zs = one("zs", [128, 4, 128])
                r_xsf = one("xsf", [128, 4, 128])
                r_xbc = one("xbcb", [128, 8, 128], BF16)
                r_dtok = one("dtok", [128, 16])
                r_gm = one("gm", [128, 2, 128])
                r_MT = one("MT", [128, 8, 128], BF16)
                r_Cd = one("Cd", [128, 8, 128], BF16)
                r_xdt = one("xdt", [128, 2, 512], BF16)
                r_btok = one("btok", [128, 2, 128], BF16)
                r_lrin = one("lrin", [128, 3, 128], BF16)
                r_sp = rot("sp", [128, 4, 128], BF16, 2)
                r_op = rot("opb", [128, 4, 128], BF16, 4)
                LS = int(os.environ.get("MK_LS_F" if fwd else "MK_LS_B", "4" if fwd else "8"))
                NPR = max(1, LS // 2)
                r_tok = rot("tok", [128, 4, 128], BF16, max(2, NPR))
                r_p0 = rot("p0", [128, 128], BF16, LS)
                r_e4 = rot("e4", [128, 512], BF16, LS)
                NQ = LS // 4
                r_pp = rot("ppq", [128, 4, 256], BF16, 2 * NQ)
                r_zf = rot("zf", [128, 4, 128], F32, NQ)
                r_zb = rot("zb", [128, 4, 128], BF16, NQ)
                r_wu = rot("wu", [128, 2, 128], BF16, max(2, NPR))
                r_qt = rot("qt", [128, 128], BF16, 1)
                r_mtb = rot("mtb", [128, 128], BF16, 1)
                r_yhl = rot("yhl", [128, 2, 512], BF16, 2)
                if fwd:
                    r_ybl = rot("ybl", [128, 2, 512], BF16, 2)
                    r_bg = rot("bg", [128, 4, 128], F32, 2)
                    r_mix = one("mix", [128, 8, 128], BF16)
                    r_st = rot("stat", [128, 128], F32, 4)
                    r_sps = rot("sps", [128, 128], BF16, 2)

                evac_rr = [0]

                def evac(out, in_, reads, writes, engs=tuple(os.environ.get("MK_EVAC", "act,dve").split(","))):
                    e_ = engs[evac_rr[0] % len(engs)]
                    evac_rr[0] += 1
                    if e_ == "act":
                        c.op("act", lambda e: e.activation(out=out, in_=in_, func=AF.Copy), reads=reads, writes=writes)
                    else:
                        c.op(e_, lambda e: e.tensor_copy(out=out, in_=in_), reads=reads, writes=writes)

                def blocksum4(src, Bsrc, cname):
                    hi, Bhi = r_sp.next()
                    lo, Blo = r_sp.next()
                    split2(src, hi[:], lo[:], [Bsrc], Bhi, Blo)
                    ps, pb = pp.next()
                    for cp in range(4):
                        o_ = ps[:, cp * 128:(cp + 1) * 128]
                        mm(o_, Kb(cname), hi[:, cp, :], True, False, [Bcsb, Bhi], [pb[cp]])
                        mm(o_, Kb(cname), lo[:, cp, :], False, True, [Bcsb, Blo], [pb[cp]])
                    return ps[:].rearrange("p (a b) -> p a b", b=128), pb

                def link_state():
                    lk = P("link")
                    for cpr in range(4):
                        c.op("dve", lambda e, cpr=cpr: e.tensor_scalar(out=Hb[:, cpr, :], in0=Hb[:, cpr, :], scalar1=lk,
                                                                       scalar2=None, op0=ALU.mult),
                             reads=[BH[cpr], Bpv], writes=[BH[cpr]])
                    c.op("dve", lambda e: e.tensor_scalar(out=STf[:], in0=STf[:], scalar1=lk, scalar2=None, op0=ALU.mult),
                         reads=[BST, Bpv], writes=[BST])
                    c.op("dve", lambda e: e.tensor_scalar(out=STb[:], in0=STb[:], scalar1=lk, scalar2=None, op0=ALU.mult),
                         reads=[BST, Bpv], writes=[BST])

                for ti in range(min(NT, NTL)):
                    u_i, lt = divmod(ti, NTU)
                    if ti == NTU:
                        link_state()
                    col0 = u_i * (TU + 4) + lt * C
                    xb, Bxb = r_xb.next()
                    c.dma("pool", lambda e, xb=xb, col0=col0: e.dma_start(
                        out=xb[:], in_=xsrc[:, col0:col0 + 132].rearrange("(k p) t -> p k t", p=128)), writes=[Bxb])
                    if SUB < 1:
                        continue

                    def inproj(U, BU, chunks):
                        for g0 in range(0, len(chunks), 3):
                            grp = chunks[g0:g0 + 3]
                            n_g = len(grp)
                            ps, pb = pp.next()
                            for j, cc in enumerate(grp):
                                for k in range(8):
                                    mm(ps[:, j * 132:(j + 1) * 132], win[:, k, cc * 128:(cc + 1) * 128], xb[:, k, :],
                                       k == 0, k == 7, [Bwin, Bxb], pb)
                            evac(U[:, g0:g0 + n_g, :], ps[:, 0:n_g * 132].rearrange("p (a b) -> p a b", b=132), pb, [BU])

                    if U1t is not None:
                        U, BU = U1t[:], BU1
                    else:
                        U, BU = AU(6, 13)
                    inproj(U, BU, list(range(12)) + [CDT])
                    UDT = 12
                    if SUB < 2:
                        continue
                    zs, Bzs = r_zs.next()
                    c.op("act", lambda e, zs=zs, U=U: e.activation(out=zs[:], in_=U[:, CZ:CZ + 4, 2:130], func=AF.Silu),
                         reads=[BU], writes=[Bzs])
                    cv, Bcv = AS(0, 2)
                    for j in range(8):
                        eng = ("dve", "pool")[j % 2] if os.environ.get("MK_CONVP", "0") == "1" else "dve"
                        for k5 in range(5):
                            kk_ = k5 if fwd else 4 - k5
                            wk = pv[:, PV["cw"][0] + j * 5 + kk_:PV["cw"][0] + j * 5 + kk_ + 1]
                            src = U[:, CXS + j, k5:k5 + 128]
                            if k5 == 0:
                                c.op(eng, lambda e, j=j, wk=wk, src=src, cv=cv: e.tensor_scalar(
                                    out=cv[:, j, :], in0=src, scalar1=wk, scalar2=P("cb", j), op0=ALU.mult, op1=ALU.add),
                                     reads=[BU, Bpv], writes=[Bcv])
                            elif eng == "dve":
                                c.op(eng, lambda e, j=j, wk=wk, src=src, cv=cv: e.scalar_tensor_tensor(
                                    out=cv[:, j, :], in0=src, scalar=wk, in1=cv[:, j, :], op0=ALU.mult, op1=ALU.add),
                                     reads=[BU, Bpv, Bcv], writes=[Bcv])
                            else:
                                c.op(eng, lambda e, wk=wk, src=src: e.tensor_scalar(
                                    out=ptmp[:], in0=src, scalar1=wk, scalar2=None, op0=ALU.mult), reads=[BU, Bpv], writes=[Bptmp])
                                c.op(eng, lambda e, j=j, cv=cv: e.tensor_tensor(out=cv[:, j, :], in0=cv[:, j, :], in1=ptmp[:], op=ALU.add),
                                     reads=[Bptmp, Bcv], writes=[Bcv])
                    xsf, Bxsf = r_xsf.next()
                    xbc, Bxbc = r_xbc.next()
                    c.op("act", lambda e, xsf=xsf, cv=cv: e.activation(out=xsf[:], in_=cv[:, 0:4, :], func=AF.Silu),
                         reads=[Bcv], writes=[Bxsf])
                    c.op("act", lambda e, xbc=xbc, cv=cv: e.activation(out=xbc[:, 4:8, :], in_=cv[:, 4:8, :], func=AF.Silu),
                         reads=[Bcv], writes=[Bxbc])
                    c.op("act", lambda e, xbc=xbc, xsf=xsf: e.activation(out=xbc[:, 0:4, :], in_=xsf[:], func=AF.Copy),
                         reads=[Bxsf], writes=[Bxbc])
                    d8, Bd8 = AS(10, 1, parts=8)
                    c.op("act", lambda e, d8=d8, U=U: e.activation(out=d8[:, 0, :], in_=U[0:8, UDT, 2:130], func=AF.Exp,
                                                                   bias=dtb), reads=[BU, Bp8], writes=[Bd8])
                    c.op("act", lambda e, d8=d8: e.activation(out=d8[:, 1, :], in_=d8[:, 0, :], func=AF.Ln, bias=1.0),
                         reads=[Bd8], writes=[Bd8])
                    c.op("dve", lambda e, d8=d8: e.tensor_scalar(out=d8[:, 2, :], in0=d8[:, 1, :], scalar1=a8[:, 0:1],
                                                                 scalar2=None, op0=ALU.mult), reads=[Bd8, Ba8], writes=[Bd8])
                    c.op("dve", lambda e, d8=d8: e.tensor_tensor_scan(out=d8[:, 3, :], data0=ones_t[0:8, :], data1=d8[:, 2, :],
                                                                      initial=0.0, op0=ALU.mult, op1=ALU.add),
                         reads=[Bd8, Bones], writes=[Bd8])
                    for q, (src_i, r_i) in enumerate(((1, 0), (3, 2))):
                        x_ = d8[:, src_i, :]
                        r_ = d8[:, r_i, :]
                        c.op("dve", lambda e, q=q, x_=x_: e.tensor_copy(out=DT3[0:8, 0, q, :], in_=x_), reads=[Bd8], writes=[BDT3])
                        c.op("dve", lambda e, q=q, x_=x_, r_=r_: e.tensor_tensor(out=r_, in0=x_, in1=DT3[0:8, 0, q, :], op=ALU.subtract),
                             reads=[Bd8, BDT3], writes=[Bd8])
                        c.op("dve", lambda e, q=q, r_=r_: e.tensor_copy(out=DT3[0:8, 1, q, :], in_=r_), reads=[Bd8], writes=[BDT3])
                        c.op("dve", lambda e, q=q, r_=r_: e.tensor_tensor(out=r_, in0=r_, in1=DT3[0:8, 1, q, :], op=ALU.subtract),
                             reads=[Bd8, BDT3], writes=[Bd8])
                        c.op("dve", lambda e, q=q, r_=r_: e.tensor_copy(out=DT3[0:8, 2, q, :], in_=r_), reads=[Bd8], writes=[BDT3])
                    ps, pb = pp.next()
                    for q in range(2):
                        for i3 in range(3):
                            mm(ps[:, q * 128:(q + 1) * 128], DT3[:, i3, q, :], Kb("ident"), i3 == 0, i3 == 2, [BDT3, Bcsb], [pb[q]])
                    dtok, Bdtok = r_dtok.next()
                    c.op("dve", lambda e, dtok=dtok, ps=ps: e.tensor_copy(out=dtok[:, 0:8], in_=ps[:, 0:8]), reads=[pb[0], pb[1]], writes=[Bdtok])
                    c.op("dve", lambda e, dtok=dtok, ps=ps: e.tensor_copy(out=dtok[:, 8:16], in_=ps[:, 128:136]), reads=[pb[1]], writes=[Bdtok])
                    psR0, pbR0 = pp.next()
                    if os.environ.get("MK_SKIPB", "") == "1":
                        pp.next()
                    psR1, pbR1 = pp.next()
                    for i3 in range(3):
                        rb = i3 % 2
                        c.ew(lambda e, i3=i3, rb=rb: e.tensor_tensor(
                            out=RB3[0:8, rb, :, :], in0=m8b[:].rearrange("p (h l) -> p h l", l=128),
                            in1=DT3[0:8, i3, 1:2, :].to_broadcast([8, 8, 128]), op=ALU.mult), reads=[BDT3, Bm8], writes=[BRB3[rb]])
                        for (psR, pbR, h0) in ((psR0, pbR0, 0), (psR1, pbR1, 4)):
                            mm(psR[:], Kb("ones"), RB3[:, rb, h0:h0 + 4, :].rearrange("p a b -> p (a b)"), i3 == 0, i3 == 2, [Bcsb, BRB3[rb]], pbR)
                    seg, Bseg = AS(2, 2)
                    E, BE = AS(4, 2)
                    if os.environ.get("MK_FENCE", "-1") != "-1":
                        pbR0 = pbR0 + [Bfence]
                    for _d in range(int(os.environ.get("MK_DELAY", "0"))):
                        c.op("dve", lambda e: e.tensor_copy(out=ptmp[:], in_=ones_t[:]), reads=[pbR0, Bones], writes=[Bptmp])
                    for (psR, pbR, h0) in ((psR0, pbR0, 0), (psR1, pbR1, 4)):
                        c.op("dve", lambda e, psR=psR, h0=h0, seg=seg, dtok=dtok: e.tensor_tensor(
                            out=seg[:, h0:h0 + 4, :], in0=psR[:].rearrange("p (a b) -> p a b", b=128),
                            in1=dtok[:, 8 + h0:12 + h0].unsqueeze(2).to_broadcast([128, 4, 128]), op=ALU.subtract),
                             reads=[pbR, Bdtok], writes=[Bseg])
                    c.ew(lambda e, seg=seg: e.tensor_scalar(out=seg[:], in0=seg[:], scalar1=0.0, scalar2=None, op0=ALU.min), reads=[Bseg], writes=[Bseg])
                    c.op("act", lambda e, E=E, psR0=psR0: e.activation(out=E[:, 0:4, :], in_=psR0[:].rearrange("p (a b) -> p a b", b=128),
                                                                     func=AF.Exp), reads=pbR0, writes=[BE])
                    c.op("act", lambda e, E=E, psR1=psR1: e.activation(out=E[:, 4:8, :], in_=psR1[:].rearrange("p (a b) -> p a b", b=128),
                                                                     func=AF.Exp), reads=pbR1, writes=[BE])
                    c.op("act", lambda e, seg=seg: e.activation(out=seg[:], in_=seg[:], func=AF.Exp), reads=[Bseg], writes=[Bseg])
                    ps, pb = pp.next()
                    for g in range(2):
                        mm(ps[:, g * 128:(g + 1) * 128], xbc[:, 4 + g, :], xbc[:, 6 + g, :], True, True, [Bxbc], [pb[g]])
                    gm, Bgm = r_gm.next()
                    for g in range(2):
                        c.op("dve", lambda e, g=g, gm=gm, ps=ps: e.tensor_tensor(out=gm[:, g, :], in0=ps[:, g * 128:(g + 1) * 128],
                                                                                 in1=Kb("mui"), op=ALU.mult),
                             reads=[pb[g], Bcsb], writes=[Bgm])
                    MT, BMT = r_MT.next()
                    Cd, BCd = r_Cd.next()
                    for g in range(2):
                        c.ew(lambda e, g=g, MT=MT, seg=seg, gm=gm: e.tensor_tensor(
                            out=MT[:, 4 * g:4 * g + 4, :], in0=seg[:, 4 * g:4 * g + 4, :], in1=gm[:, g:g + 1, :].to_broadcast([128, 4, 128]), op=ALU.mult),
                             reads=[Bseg, Bgm], writes=[BMT])
                        c.ew(lambda e, g=g, Cd=Cd, E=E, xbc=xbc: e.tensor_tensor(
                            out=Cd[:, 4 * g:4 * g + 4, :], in0=E[:, 4 * g:4 * g + 4, :], in1=xbc[:, 6 + g:7 + g, :].to_broadcast([128, 4, 128]), op=ALU.mult),
                             reads=[BE, Bxbc], writes=[BCd])
                    ps, pb = pp.next()
                    for j in range(4):
                        mm(ps[:, j * 128:(j + 1) * 128], xbc[:, j, :], Kb("ident"), True, True, [Bxbc, Bcsb], [pb[j]])
                    xdt, Bxdt = r_xdt.next()
                    v864 = lambda ap: ap.rearrange("p (h q) -> p h q", q=64)
                    c.op("dve", lambda e, xdt=xdt, ps=ps, dtok=dtok: e.tensor_tensor(
                        out=v864(xdt[:, 0, :]), in0=v864(ps[:]), in1=dtok[:, 0:8].unsqueeze(2).to_broadcast([128, 8, 64]), op=ALU.mult),
                         reads=[pb, Bdtok], writes=[Bxdt])
                    c.ew(lambda e, xdt=xdt, seg=seg: e.tensor_tensor(
                        out=v864(xdt[:, 1, :]), in0=v864(xdt[:, 0, :]), in1=seg[:, :, 127:128].to_broadcast([128, 8, 64]), op=ALU.mult),
                         reads=[Bxdt, Bseg], writes=[Bxdt])
                    ps, pb = pp.next()
                    for g in range(2):
                        mm(ps[:, g * 128:(g + 1) * 128], xbc[:, 4 + g, :], Kb("ident"), True, True, [Bxbc, Bcsb], [pb[g]])
                    btok, Bbtok = r_btok.next()
                    evac(btok[:], ps[:, 0:256].rearrange("p (a b) -> p a b", b=128), pb[0:2], [Bbtok])
                    for h in range(8):
                        o_ = YS_t[:, h * 64:(h + 1) * 64]
                        mm(o_, MT[:, h, :], xdt[:, 0, h * 64:(h + 1) * 64], True, False, [BMT, Bxdt], [YS_b])
                        mm(o_, Cd[:, h, :], STb[:, h * 64:(h + 1) * 64], False, True, [BCd, BST], [YS_b])
                    ysh, Bysh = r_yhl.next()
                    split2(YS_t[:], ysh[:, 0, :], ysh[:, 1, :], [YS_b], Bysh, Bysh, psum=True)
                    ps, pb = pp.next()
                    for g in range(2):
                        mm(ps[:, g * 256:(g + 1) * 256], btok[:, g, :], xdt[:, 1, g * 256:(g + 1) * 256], True, True,
                           [Bbtok, Bxdt], pb[2 * g:2 * g + 2])
                    c.ew(lambda e, E=E: e.tensor_tensor(out=v864(STf[:]), in0=v864(STf[:]), in1=E[:, :, 127:128].to_broadcast([128, 8, 64]), op=ALU.mult),
                         reads=[BST, BE], writes=[BST])
                    c.op("dve", lambda e, ps=ps: e.tensor_tensor(out=STf[:], in0=STf[:], in1=ps[:], op=ALU.add), reads=[BST, pb], writes=[BST])
                    c.op("act", lambda e: e.activation(out=STb[:], in_=STf[:], func=AF.Copy), reads=[BST], writes=[BST])
                    if not fwd:
                        for q, key in enumerate(("ybs_h", "ybs_l")):
                            c.dma("sp", lambda e, ysh=ysh, q=q, key=key, ti=ti: e.dma_start(out=ysc[key][ti * 128:(ti + 1) * 128, :], in_=ysh[:, q, :]),
                                  reads=[Bysh], writes=[B_ysc[key][ti]])
                    if SUB < 3:
                        continue

                    U, BU = AU(4, 15)
                    inproj(U, BU, list(range(CR, CR + 15)))
                    us, Bus = AS(0, 4, nch=15)
                    tsh, Btsh = AS(8, 2)
                    bc15 = lambda ap, j0, n: ap[:, j0:j0 + n].unsqueeze(2).to_broadcast([128, n, 128])
                    c.ew(lambda e, us=us, U=U: e.tensor_tensor(out=us, in0=U[:, 0:15, 2:130], in1=bc15(der, 0, 15), op=ALU.mult),
                         reads=[BU, Bder], writes=[Bus])
                    for (j0, n) in ((0, 8), (8, 7)):
                        for (sl_, mun) in (((1, 129), mpn), ((3, 131), mnn)):
                            mu_ap = pv[:, PV[mun][0]:PV[mun][0] + 15]
                            c.ew(lambda e, j0=j0, n=n, sl_=sl_, mu_ap=mu_ap, U=U, tsh=tsh: e.tensor_tensor(
                                out=tsh[:, 0:n, :], in0=U[:, j0:j0 + n, sl_[0]:sl_[1]], in1=bc15(mu_ap, j0, n), op=ALU.mult),
                                 reads=[BU, Bpv], writes=[Btsh])
                            c.ew(lambda e, j0=j0, n=n, us=us, tsh=tsh: e.tensor_tensor(
                                out=us[:, j0:j0 + n, :], in0=us[:, j0:j0 + n, :], in1=tsh[:, 0:n, :], op=ALU.add),
                                 reads=[Bus, Btsh], writes=[Bus])
                    lrin, Blrin = r_lrin.next()
                    c.op("act", lambda e, us=us, lrin=lrin: e.activation(out=lrin[:, 0, :], in_=us[:, 12, :], func=AF.Tanh), reads=[Bus], writes=[Blrin])
                    c.op("act", lambda e, us=us, lrin=lrin: e.activation(out=lrin[:, 1, :], in_=us[:, 13, :], func=AF.Copy), reads=[Bus], writes=[Blrin])
                    c.op("act", lambda e, us=us, lrin=lrin: e.activation(out=lrin[:, 2, :], in_=us[:, 14, :], func=AF.Sigmoid), reads=[Bus], writes=[Blrin])
                    psL, pbL = pp.next()
                    psI, pbI = pp.next()
                    for cp in range(4):
                        mm(psL[:, cp * 128:(cp + 1) * 128], w2[:, cp * 128:(cp + 1) * 128], lrin[:, 0, :], True, True, [Blr, Blrin], [pbL[cp]])
                        mm(psI[:, cp * 128:(cp + 1) * 128], a2[:, cp * 128:(cp + 1) * 128], lrin[:, 1, :], True, True, [Blr, Blrin], [pbI[cp]])
                    sg, Bsg = AS(4)
                    icl, Bicl = AS(5)
                    bc4 = lambda n: P(n, 0, 4).unsqueeze(2).to_broadcast([128, 4, 128])
                    v4 = lambda ps_: ps_[:].rearrange("p (a b) -> p a b", b=128)
                    c.op("dve", lambda e, sg=sg, psL=psL: e.tensor_tensor(out=sg[:], in0=v4(psL), in1=bc4(w0n), op=ALU.add), reads=[pbL, Bpv], writes=[Bsg])
                    c.op("act", lambda e, sg=sg: e.activation(out=sg[:], in_=sg[:], func=AF.Sigmoid), reads=[Bsg], writes=[Bsg])
                    c.op("dve", lambda e, icl=icl, psI=psI: e.tensor_tensor(out=icl[:], in0=v4(psI), in1=bc4("a0"), op=ALU.add), reads=[pbI, Bpv], writes=[Bicl])
                    c.op("act", lambda e, icl=icl: e.activation(out=icl[:], in_=icl[:], func=AF.Sigmoid), reads=[Bicl], writes=[Bicl])
                    for cp in range(4):
                        c.op("dve", lambda e, cp=cp, sg=sg: e.tensor_tensor_scan(
                            out=CSt[:, cp, 1:129], data0=ones_t[:], data1=sg[:, cp, :], initial=0.0, op0=ALU.mult, op1=ALU.add),
                             reads=[Bsg, Bones, BCSt], writes=[BCSt])
                    gam, Bgam = AS(6)
                    igam, Bigam = AS(7)
                    gprev, Bgprev = AS(8)
                    c.op("act", lambda e, gam=gam: e.activation(out=gam[:], in_=CSt[:, :, 1:129], func=AF.Exp, scale=-CDEC),
                         reads=[BCSt], writes=[Bgam])
                    c.op("act", lambda e, igam=igam: e.activation(out=igam[:], in_=CSt[:, :, 1:129], func=AF.Exp, scale=CDEC),
                         reads=[BCSt], writes=[Bigam])
                    c.op("act", lambda e, gprev=gprev: e.activation(out=gprev[:], in_=CSt[:, :, 0:128], func=AF.Exp, scale=-CDEC),
                         reads=[BCSt], writes=[Bgprev])
                    kkt, Bkk = AS(9)
                    c.ew(lambda e, kkt=kkt, us=us: e.tensor_tensor(out=kkt[:], in0=us[:, 4:8, :], in1=bc4("kk"), op=ALU.mult), reads=[Bus, Bpv], writes=[Bkk])
                    sq, Bsq = sg, Bsg
                    c.ew(lambda e, sq=sq, kkt=kkt: e.tensor_tensor(out=sq[:], in0=kkt[:], in1=kkt[:], op=ALU.mult), reads=[Bkk], writes=[Bsq])
                    psS, pbS = blocksum4(sq[:], Bsq, "bo64")
                    rn, Brn = AS(10)
                    c.op("dve", lambda e, rn=rn, psS=psS: e.tensor_scalar(out=rn[:], in0=psS, scalar1=1e-24, scalar2=None, op0=ALU.max),
                         reads=pbS, writes=[Brn])
                    c.op("act", lambda e, rn=rn: e.activation(out=rn[:], in_=rn[:], func=AF.Sqrt), reads=[Brn], writes=[Brn])
                    c.op("dve", lambda e, rn=rn: e.reciprocal(out=rn[:], in_=rn[:]), reads=[Brn], writes=[Brn])
                    c.ew(lambda e, kkt=kkt, rn=rn: e.tensor_tensor(out=kkt[:], in0=kkt[:], in1=rn[:], op=ALU.mult), reads=[Bkk, Brn], writes=[Bkk])
                    km, Bkm = rn, Brn
                    c.ew(lambda e, km=km, icl=icl: e.tensor_tensor(out=km[:], in0=icl[:], in1=bc4("ka"), op=ALU.mult), reads=[Bicl, Bpv], writes=[Bkm])
                    c.ew(lambda e, km=km: e.tensor_tensor(out=km[:], in0=km[:], in1=der[:, 15:19].unsqueeze(2).to_broadcast([128, 4, 128]), op=ALU.add),
                         reads=[Bkm, Bder], writes=[Bkm])
                    c.ew(lambda e, km=km, us=us: e.tensor_tensor(out=km[:], in0=km[:], in1=us[:, 4:8, :], op=ALU.mult), reads=[Bkm, Bus], writes=[Bkm])
                    if fwd:
                        rk_, Brk = sq, Bsq
                        c.ew(lambda e, rk_=rk_, us=us: e.tensor_tensor(out=rk_[:], in0=us[:, 0:4, :], in1=bc4("rk"), op=ALU.mult), reads=[Bus, Bpv], writes=[Brk])
                        c.ew(lambda e, rk_=rk_, km=km: e.tensor_tensor(out=rk_[:], in0=rk_[:], in1=km[:], op=ALU.mult), reads=[Brk, Bkm], writes=[Brk])
                        psB, pbB = blocksum4(rk_[:], Brk, "bo64")
                        bonv, Bbonv = r_bg.next()
                        c.op("dve", lambda e, bonv=bonv, psB=psB, us=us: e.tensor_tensor(out=bonv[:], in0=psB, in1=us[:, 8:12, :], op=ALU.mult),
                             reads=pbB + [Bus], writes=[Bbonv])
                        psG, pbG = pp.next()
                        for cp in range(4):
                            mm(psG[:, cp * 128:(cp + 1) * 128], g2[:, cp * 128:(cp + 1) * 128], lrin[:, 2, :], True, True, [Blr, Blrin], [pbG[cp]])
                        gT, BgT = r_bg.next()
                        c.op("act", lambda e, gT=gT, psG=psG: e.activation(out=gT[:], in_=psG[:].rearrange("p (a b) -> p a b", b=128), func=AF.Copy),
                             reads=pbG, writes=[BgT])
                    KT, BKT = r_op.next()
                    BT, BBT = r_op.next()
                    AT, BAT = r_op.next()
                    vb, Bvb = r_op.next()
                    c.ew(lambda e, KT=KT, km=km, igam=igam: e.tensor_tensor(out=KT[:], in0=km[:], in1=igam[:], op=ALU.mult),
                         reads=[Bkm, Bigam], writes=[BKT])
                    c.ew(lambda e, icl=icl, kkt=kkt: e.tensor_tensor(out=icl[:], in0=icl[:], in1=kkt[:], op=ALU.mult), reads=[Bicl, Bkk], writes=[Bicl])
                    c.ew(lambda e, BT=BT, icl=icl, igam=igam: e.tensor_tensor(out=BT[:], in0=icl[:], in1=igam[:], op=ALU.mult),
                         reads=[Bicl, Bigam], writes=[BBT])
                    c.op("dve", lambda e, AT=AT, kkt=kkt, gprev=gprev: e.scalar_tensor_tensor(out=AT[:], in0=kkt[:], scalar=-1.0, in1=gprev[:],
                                                                                             op0=ALU.mult, op1=ALU.mult),
                         reads=[Bkk, Bgprev], writes=[BAT])
                    c.ew(lambda e, vb=vb, us=us: e.tensor_copy(out=vb[:], in_=us[:, 8:12, :]), reads=[Bus], writes=[Bvb])
                    c.ew(lambda e, AT=AT: e.tensor_copy(out=ARe_t[0:64, :, 0:128], in_=AT[0:64, :, :]), reads=[BAT], writes=[BAR_t])
                    c.ew(lambda e, AT=AT: e.tensor_copy(out=ARo_t[64:128, :, 0:128], in_=AT[64:128, :, :]), reads=[BAT], writes=[BAR_t])
                    c.ew(lambda e, us=us, gam=gam: e.tensor_tensor(out=ARe_t[0:64, :, 128:256], in0=us[0:64, 0:4, :], in1=gam[0:64, :, :],
                                                                   op=ALU.mult), reads=[Bus, Bgam], writes=[BAR_t])
                    c.ew(lambda e, us=us, gam=gam: e.tensor_tensor(out=ARo_t[64:128, :, 128:256], in0=us[64:128, 0:4, :], in1=gam[64:128, :, :],
                                                                   op=ALU.mult), reads=[Bus, Bgam], writes=[BAR_t])
                    if SUB < 4:
                        continue
                    if fwd:
                        src_t = NT - 1 - ti
                        ybS, BybS = r_ybl.next()
                        ybR, BybR = r_ybl.next()
                        for (dst_, Bdst_, kh, kl) in ((ybS, BybS, "ybs_h", "ybs_l"), (ybR, BybR, "ybr_h", "ybr_l")):
                            for q, key in enumerate((kh, kl)):
                                c.dma("sp", lambda e, dst_=dst_, q=q, key=key, src_t=src_t: e.dma_start(
                                    out=dst_[:, q, :], in_=ysc[key][src_t * 128:(src_t + 1) * 128, :]),
                                      reads=[B_ysc[key][src_t]], writes=[Bdst_])
                    heads_all = [(cp, x) for cp in range(4) for x in range(2)]
                    for g0 in range(0, 8, LS):
                        grp = heads_all[g0:g0 + LS]
                        cps = sorted(set(cp for cp, _ in grp))
                        toks, wus, st = {}, {}, {}
                        for cp in cps:
                            ps, pb = pp.next()
                            for q, src in enumerate((AT, BT, KT, vb)):
                                mm(ps[:, q * 128:(q + 1) * 128], src[:, cp, :], Kb("ident"), True, True, [BAT, BBT, BKT, Bvb, Bcsb], [pb[q]])
                            tok, Btok = r_tok.next()
                            evac(tok[:], ps[:].rearrange("p (a b) -> p a b", b=128), pb, [Btok])
                            toks[cp] = (tok, Btok)
                            wus[cp] = r_wu.next()
                        quads = []
                        for q0 in range(0, len(grp), 4):
                            zfq, Bzfq = r_zf.next()
                            zbq, Bzbq = r_zb.next()
                            quads.append(dict(heads=grp[q0:q0 + 4], zf=zfq, Bzf=Bzfq, zb=zbq, Bzb=Bzbq))
                        for hi_, (cp, x) in enumerate(grp):
                            qd = quads[hi_ // 4]
                            hq = hi_ % 4
                            tok, Btok = toks[cp]
                            AR_x = (ARe_t, ARo_t)[x]
                            hs = slice(x * 64, (x + 1) * 64)
                            psA, pbA = pp.next()
                            psB4, pbB4 = pp.next()
                            mm(psA[:, 0:128], AR_x[:, cp, 0:128], BT[:, cp, :], True, True, [BAR_t, BBT], [pbA[0]])
                            mm(psB4[:, 0:256], BT[:, cp, :], AR_x[:, cp, :], True, True, [BAR_t, BBT], pbB4[0:2])
                            mm(psB4[:, 256:512], KT[:, cp, :], AR_x[:, cp, :], True, True, [BAR_t, BKT], pbB4[2:4])
                            p0, Bp0 = r_p0.next()
                            c.op("dve", lambda e, p0=p0, psA=psA: e.tensor_tensor(out=p0[:], in0=psA[:, 0:128], in1=Kb("msl"), op=ALU.mult),
                                 reads=[pbA[0], Bcsb], writes=[Bp0])
                            e4, Be4 = r_e4.next()
                            c.op("dve", lambda e, e4=e4, psB4=psB4: e.tensor_tensor(out=e4[:], in0=psB4[:], in1=mask4[:], op=ALU.mult),
                                 reads=pbB4 + [Bm4], writes=[Be4])
                            mm(psA[:, 128:192], e4[:, 256:384], tok[:, 3, hs], True, True, [Be4, Btok], [pbA[1]])
                            zf, Bzf = qd["zf"][:, hq, :], qd["Bzf"]
                            c.op("act", lambda e, zf=zf, tok=tok, hs=hs: e.activation(out=zf[:, 0:64], in_=tok[:, 0, hs], func=AF.Copy),
                                 reads=[Btok], writes=[Bzf])
                            c.op("act", lambda e, zf=zf, psA=psA: e.activation(out=zf[:, 64:128], in_=psA[:, 128:192], func=AF.Copy),
                                 reads=[pbA[1]], writes=[Bzf])
                            st[(cp, x)] = dict(e4=e4, Be4=Be4, zf=zf, Bzf=Bzf, Pn=p0[:], PnT=e4[:, 0:128], BPn=[Bp0, Be4])
                        for qd in quads:
                            c.ew(lambda e, qd=qd: e.tensor_copy(out=qd["zb"][:], in_=qd["zf"][:]), reads=[qd["Bzf"]], writes=[qd["Bzb"]])
                        for it in range(7):
                            for qd in quads:
                                zfq, Bzfq, zbq, Bzbq = qd["zf"], qd["Bzf"], qd["zb"], qd["Bzb"]
                                psZ, pbZ = pp.next()
                                for hq, hd in enumerate(qd["heads"]):
                                    d_ = st[hd]
                                    mm(psZ[:, hq * 128:(hq + 1) * 128], d_["PnT"], zbq[:, hq, :], True, True, d_["BPn"] + [Bzbq], pbZ)
                                if it < 6:
                                    psS = [pp.next(), pp.next()]
                                    for hq, hd in enumerate(qd["heads"]):
                                        d_ = st[hd]
                                        ps_, pb_ = psS[hq // 2]
                                        o0 = (hq % 2) * 256
                                        mm(ps_[:, o0:o0 + 128], d_["PnT"], d_["Pn"], True, True, d_["BPn"], pb_)
                                        mm(ps_[:, o0 + 128:o0 + 256], d_["Pn"], d_["PnT"], True, True, d_["BPn"], pb_)
                                    c.op("dve", lambda e, zfq=zfq, zbq=zbq, psZ=psZ: e.tensor_tensor(
                                        out=zbq[:], in0=psZ[:].rearrange("p (a b) -> p a b", b=128), in1=zfq[:], op=ALU.add),
                                         reads=[pbZ, Bzfq], writes=[Bzbq])
                                c.op("dve", lambda e, zfq=zfq, psZ=psZ: e.tensor_tensor(
                                    out=zfq[:], in0=psZ[:].rearrange("p (a b) -> p a b", b=128), in1=zfq[:], op=ALU.add),
                                     reads=[pbZ, Bzfq], writes=[Bzfq])
                                if it < 6:
                                    ppq, Bppq = r_pp.next()
                                    for half in range(2):
                                        ps_, pb_ = psS[half]
                                        c.op("act", lambda e, ppq=ppq, ps_=ps_, half=half: e.activation(
                                            out=ppq[:, 2 * half:2 * half + 2, :], in_=ps_[:].rearrange("p (a b) -> p a b", b=256), func=AF.Copy),
                                             reads=pb_, writes=[Bppq])
                                    for hq, hd in enumerate(qd["heads"]):
                                        d_ = st[hd]
                                        d_["Pn"], d_["PnT"], d_["BPn"] = ppq[:, hq, 0:128], ppq[:, hq, 128:256], [Bppq]
                        for (cp, x) in grp:
                            d_ = st[(cp, x)]
                            wu, Bwu = wus[cp]
                            hs = slice(x * 64, (x + 1) * 64)
                            zf, Bzf = d_["zf"], d_["Bzf"]
                            c.op("act", lambda e, wu=wu, zf=zf, hs=hs: e.activation(out=wu[:, 0, hs], in_=zf[:, 0:64], func=AF.Copy),
                                 reads=[Bzf], writes=[Bwu])
                            c.ew(lambda e, wu=wu, zf=zf, hs=hs: e.tensor_copy(out=wu[:, 1, hs], in_=zf[:, 64:128]),
                                 reads=[Bzf], writes=[Bwu])
                        for cp in cps:
                            if (cp, 0) not in st or (cp, 1) not in st:
                                continue
                            tok, Btok = toks[cp]
                            wu, Bwu = wus[cp]
                            qt, Bqt = r_qt.next()
                            psQ, pbQ = pp.next()
                            for x in range(2):
                                e4, Be4 = st[(cp, x)]["e4"], st[(cp, x)]["Be4"]
                                AR_x = (ARe_t, ARo_t)[x]
                                hs = slice(x * 64, (x + 1) * 64)
                                mm(psQ[:, x * 128:(x + 1) * 128], wu[:, 0, :], e4[:, 128:256], True, True, [Bwu, Be4], [pbQ[x]])
                            for x in range(2):
                                AR_x = (ARe_t, ARo_t)[x]
                                hs = slice(x * 64, (x + 1) * 64)
                                c.op("dve", lambda e, qt=qt, psQ=psQ, x=x, hs=hs, AR_x=AR_x, cp=cp: e.tensor_tensor(
                                    out=qt[hs, :], in0=psQ[hs, x * 128:(x + 1) * 128], in1=AR_x[hs, cp, 128:256], op=ALU.add),
                                     reads=[pbQ[x], BAR_t], writes=[Bqt])
                            yo = YR_t[:, cp * 128:(cp + 1) * 128]
                            mm(yo, qt[:], Hb[:, cp, :], True, False, [Bqt, BH[cp]], [YR_b])
                            for x in range(2):
                                e4, Be4 = st[(cp, x)]["e4"], st[(cp, x)]["Be4"]
                                hs = slice(x * 64, (x + 1) * 64)
                                yx = YR_t[:, cp * 128 + x * 64:cp * 128 + (x + 1) * 64]
                                mm(yx, e4[:, 128:256], wu[:, 1, hs], False, False, [Be4, Bwu], [YR_b])
                                mm(yx, e4[:, 384:512], tok[:, 3, hs], False, x == 1, [Be4, Btok], [YR_b])
                            psM, pbM = pp.next()
                            mm(psM[:, 0:128], wu[:, 0, :], tok[:, 1, :], True, True, [Bwu, Btok], [pbM[0]])
                            mtb, Bmtb = r_mtb.next()
                            c.op("dve", lambda e, mtb=mtb, psM=psM: e.tensor_tensor(out=mtb[:], in0=psM[:, 0:128], in1=Kb("bmask"), op=ALU.mult),
                                 reads=[pbM[0], Bcsb], writes=[Bmtb])
                            psH, pbH = pp.next()
                            hsl = psH[:, 0:128]
                            mm(hsl, tok[:, 1, :], wu[:, 1, :], True, False, [Btok, Bwu], [pbH[0]])
                            mm(hsl, tok[:, 2, :], tok[:, 3, :], False, False, [Btok], [pbH[0]])
                            mm(hsl, Kb("ident"), Hb[:, cp, :], False, False, [Bcsb, BH[cp]], [pbH[0]])
                            mm(hsl, mtb[:], Hb[:, cp, :], False, True, [Bmtb, BH[cp]], [pbH[0]])
                            c.op("dve", lambda e, cp=cp, psH=psH, gam=gam: e.scalar_tensor_tensor(
                                out=Hb[:, cp, :], in0=psH[:, 0:128], scalar=gam[:, cp, 127:128], in1=Kb("bmask"), op0=ALU.mult, op1=ALU.mult),
                                 reads=[pbH[0], Bgam, Bcsb], writes=[BH[cp]])
                    yrh, Byrh = r_yhl.next()
                    split2(YR_t[:], yrh[:, 0, :], yrh[:, 1, :], [YR_b], Byrh, Byrh, psum=True)
                    if not fwd:
                        for q, key in enumerate(("ybr_h", "ybr_l")):
                            c.dma("sp", lambda e, yrh=yrh, q=q, key=key, ti=ti: e.dma_start(out=ysc[key][ti * 128:(ti + 1) * 128, :], in_=yrh[:, q, :]),
                                  reads=[Byrh], writes=[B_ysc[key][ti]])
                        continue
                    if SUB < 5:
                        continue

                    mix, Bmix = r_mix.next()

                    def ytrans(yh, Byh, yb, Byb):
                        ps, pb = pp.next()
                        for j in range(4):
                            o_ = ps[:, j * 128:(j + 1) * 128]
                            cs_ = slice(j * 128, (j + 1) * 128)
                            mm(o_, yh[:, 0, cs_], Kb("ident"), True, False, [Byh, Bcsb], [pb[j]])
                            mm(o_, yh[:, 1, cs_], Kb("ident"), False, False, [Byh, Bcsb], [pb[j]])
                            mm(o_, yb[:, 0, cs_], Kb("J"), False, False, [Byb, Bcsb], [pb[j]])
                            mm(o_, yb[:, 1, cs_], Kb("J"), False, True, [Byb, Bcsb], [pb[j]])
                        return ps, pb
                    def ch_ssd():
                        ps, pb = ytrans(ysh, Bysh, ybS, BybS)
                        yield
                        yg, Byg = AS(0)
                        yield
                        for j in range(4):
                            c.op("dve", lambda e, j=j, yg=yg, xsf=xsf, ps=ps: e.scalar_tensor_tensor(
                                out=yg[:, j, :], in0=xsf[:, j, :], scalar=P("dch", j), in1=ps[:, j * 128:(j + 1) * 128], op0=ALU.mult, op1=ALU.add),
                                 reads=[Bxsf, Bpv, pb[j]], writes=[Byg])
                        c.ew(lambda e, yg=yg, zs=zs: e.tensor_tensor(out=yg[:], in0=yg[:], in1=zs[:], op=ALU.mult), reads=[Byg, Bzs], writes=[Byg])
                        yield
                        sq2, Bsq2 = AS(2)
                        yield
                        c.ew(lambda e, sq2=sq2, yg=yg: e.tensor_tensor(out=sq2[:], in0=yg[:], in1=yg[:], op=ALU.mult), reads=[Byg], writes=[Bsq2])
                        yield
                        hi, Bhi = r_sp.next()
                        yield
                        lo, Blo = r_sp.next()
                        yield
                        split2(sq2[:], hi[:], lo[:], [Bsq2], Bhi, Blo)
                        yield
                        ps, pb = pp.next()
                        yield
                        for g in range(2):
                            o_ = ps[:, g * 128:(g + 1) * 128]
                            for j in range(2):
                                mm(o_, Kb("o256"), hi[:, 2 * g + j, :], j == 0, False, [Bcsb, Bhi], [pb[g]])
                                mm(o_, Kb("o256"), lo[:, 2 * g + j, :], False, j == 1, [Bcsb, Blo], [pb[g]])
                        rs, Brs = r_st.next()
                        yield
                        rs2, Brs2 = r_st.next()
                        yield
                        rsqrt_eps(rs[:], ps[:, 0:128], RMS_EPS, [pb[0]], [Brs])
                        yield
                        rsqrt_eps(rs2[:], ps[:, 128:256], RMS_EPS, [pb[1]], [Brs2])
                        yield
                        for j in range(4):
                            rr_, Brr_ = (rs, Brs) if j < 2 else (rs2, Brs2)
                            c.op("dve", lambda e, j=j, mix=mix, yg=yg, rr_=rr_: e.scalar_tensor_tensor(
                                out=mix[:, j, :], in0=yg[:, j, :], scalar=P("nw", j), in1=rr_[:], op0=ALU.mult, op1=ALU.mult),
                                 reads=[Byg, Bpv, Brr_], writes=[Bmix])
                        yield
                    def ch_rwkv():
                        ps, pb = ytrans(yrh, Byrh, ybR, BybR)
                        yield
                        yf, Byf = AS(1)
                        sq2r, Bsq2r = AS(5)
                        yield
                        evac(yf[:], ps[:].rearrange("p (a b) -> p a b", b=128), pb, [Byf])
                        yield
                        psm, pbm = blocksum4(yf[:], Byf, "bo64m")
                        yield
                        c.op("dve", lambda e, yf=yf, psm=psm: e.tensor_tensor(out=yf[:], in0=yf[:], in1=psm, op=ALU.subtract), reads=[Byf] + pbm, writes=[Byf])
                        yield
                        c.ew(lambda e, sq2r=sq2r, yf=yf: e.tensor_tensor(out=sq2r[:], in0=yf[:], in1=yf[:], op=ALU.mult), reads=[Byf], writes=[Bsq2r])
                        yield
                        psv, pbv = blocksum4(sq2r[:], Bsq2r, "bo64m")
                        yield
                        rsqrt_eps(sq2r[:], psv, GN_EPS, pbv, [Bsq2r])
                        yield
                        c.ew(lambda e, yf=yf, sq2r=sq2r: e.tensor_tensor(out=yf[:], in0=yf[:], in1=sq2r[:], op=ALU.mult), reads=[Byf, Bsq2r], writes=[Byf])
                        yield
                        c.ew(lambda e, yf=yf: e.tensor_tensor(out=yf[:], in0=yf[:], in1=bc4("lw"), op=ALU.mult), reads=[Byf, Bpv], writes=[Byf])
                        yield
                        c.ew(lambda e, yf=yf: e.tensor_tensor(out=yf[:], in0=yf[:], in1=bc4("lb"), op=ALU.add), reads=[Byf, Bpv], writes=[Byf])
                        yield
                        c.ew(lambda e, yf=yf, bonv=bonv: e.tensor_tensor(out=yf[:], in0=yf[:], in1=bonv[:], op=ALU.add), reads=[Byf, Bbonv], writes=[Byf])
                        yield
                        c.ew(lambda e, mix=mix, yf=yf, gT=gT: e.tensor_tensor(out=mix[:, 4:8, :], in0=yf[:], in1=gT[:], op=ALU.mult),
                             reads=[Byf, BgT], writes=[Bmix])
                        yield
                    gens_ = [ch_ssd(), ch_rwkv()]
                    while gens_:
                        for g_ in list(gens_):
                            try:
                                next(g_)
                            except StopIteration:
                                gens_.remove(g_)
                    xf, Bxf = AS(9, 2)
                    c.dma("sp", lambda e, xf=xf, col0=col0: e.dma_start(
                        out=xf, in_=xT[:, col0 + 2:col0 + 130].rearrange("(k p) t -> p k t", p=128)), writes=[Bxf])
                    pre, Bpre = AS(3, 2)
                    for half in range(2):
                        ps, pb = pp.next()
                        for j in range(4):
                            dc = half * 4 + j
                            for k in range(8):
                                mm(ps[:, j * 128:(j + 1) * 128], wo[:, k, dc * 128:(dc + 1) * 128], mix[:, k, :], k == 0, k == 7,
                                   [Bwo, Bmix], [pb[j]])
                        c.op("dve", lambda e, half=half, pre=pre, xf=xf, ps=ps: e.scalar_tensor_tensor(
                            out=pre[:, half * 4:(half + 1) * 4, :], in0=xf[:, half * 4:(half + 1) * 4, :], scalar=ALPHA,
                            in1=ps[:].rearrange("p (a b) -> p a b", b=128), op0=ALU.mult, op1=ALU.add), reads=[Bxf] + pb, writes=[Bpre])
                    hout, Bhout = AS(7, 2)
                    sqL, BsqL = AS(5, 2)
                    layer_norm(pp, Kb, pre, Bpre, hout, Bhout, sqL, BsqL, r_st, r_sps, P, "l1w", "l1b", Bcsb, Bpv, 128)
                    t0 = ti * C
                    c.dma("sp", lambda e, hout=hout, t0=t0: e.dma_start(
                        out=hT[:, t0:t0 + 128].rearrange("(k p) t -> p k t", p=128), in_=hout), reads=[Bhout], writes=[B_hT[ti]])
                c.full_barrier()
                c.run_block()

        def ffn_phase():
            NF = 256 if TT % 256 == 0 else 128
            with ExitStack() as es:
                sb = lambda n, s, dt=F32: es.enter_context(nc.sbuf_tensor(n, s, dt))
                rot = lambda n, s, dt=F32, k=2: Rot(nc, es, n, s, dt, k)
                pp = PsumPool(nc, es, ["pf%d" % i for i in range(8)])
                wfi = sb("wfi", [128, 8, 2 * DFF], BF16); Bwfi = Buf()
                wfo = sb("wfo", [128, NFC, D], BF16); Bwfo = Buf()
                for k in range(8):
                    c.dma("pool", lambda e, k=k: e.dma_start(out=wfi[:, k, :], in_=w_fi[k * 128:(k + 1) * 128, :]), writes=[Bwfi])
                for k in range(NFC):
                    c.dma("pool", lambda e, k=k: e.dma_start(out=wfo[:, k, :], in_=w_fo[k * 128:(k + 1) * 128, :]), writes=[Bwfo])
                pv = sb("pv2", [128, NPV]); Bpv = Buf()
                c.dma("sp", lambda e: e.dma_start(out=pv[:], in_=pvec), writes=[Bpv])
                csb = sb("csb2", [128, NCST * 128], BF16); Bcsb = Buf()
                c.dma("pool", lambda e: e.dma_start(out=csb[:], in_=cst), writes=[Bcsb])
                Kb = lambda n: csb[:, CST[n] * 128:(CST[n] + 1) * 128]
                P = lambda n, j=0, w=1: pv[:, PV[n][0] + j:PV[n][0] + j + w]
                r_hf = rot("hf", [128, 8, NF], F32, 1)
                r_hb = rot("hb", [128, 8, NF], BF16, 1)
                r_act = rot("actt", [128, NFC, NF], BF16, 1)
                r_sl = rot("sl", [128, NF], F32, 2)
                r_sq = rot("sq2", [128, 8, NF], F32, 1)
                r_st = rot("st2", [128, NF], F32, 4)
                r_sps = rot("sps2", [128, NF], BF16, 2)
                for fi in range(min(TT // NF, NTL)):
                    t0 = fi * NF
                    deps = [B_hT[t] for t in range(t0 // C, (t0 + NF) // C)]
                    hf, Bhf = r_hf.next()
                    hb, Bhb = r_hb.next()
                    c.dma("sp", lambda e, hf=hf, t0=t0: e.dma_start(out=hf[:], in_=hT[:, t0:t0 + NF].rearrange("(k p) t -> p k t", p=128)),
                          reads=deps, writes=[Bhf])
                    c.op("act", lambda e, hf=hf, hb=hb: e.activation(out=hb[:], in_=hf[:], func=AF.Copy), reads=[Bhf], writes=[Bhb])
                    actt, Bact = r_act.next()
                    for fc in range(NFC):
                        ps, pb = pp.next()
                        for k in range(8):
                            mm(ps[:, 0:NF], wfi[:, k, fc * 128:(fc + 1) * 128], hb[:, k, :], k == 0, k == 7, [Bwfi, Bhb], pb[0:2])
                        for k in range(8):
                            mm(ps[:, 256:256 + NF], wfi[:, k, DFF + fc * 128:DFF + (fc + 1) * 128], hb[:, k, :], k == 0, k == 7,
                               [Bwfi, Bhb], pb[2:4])
                        sl, Bsl = r_sl.next()
                        c.op("act", lambda e, sl=sl, ps=ps: e.activation(out=sl[:], in_=ps[:, 0:NF], func=AF.Silu), reads=pb[0:2], writes=[Bsl])
                        c.op("dve", lambda e, fc=fc, sl=sl, ps=ps, actt=actt: e.tensor_tensor(
                            out=actt[:, fc, :], in0=ps[:, 256:256 + NF], in1=sl[:], op=ALU.mult), reads=pb[2:4] + [Bsl], writes=[Bact])
                    for half in range(4):
                        ps, pb = pp.next()
                        for j in range(2):
                            dc = half * 2 + j
                            for k in range(NFC):
                                mm(ps[:, j * 256:j * 256 + NF], wfo[:, k, dc * 128:(dc + 1) * 128], actt[:, k, :], k == 0, k == NFC - 1,
                                   [Bwfo, Bact], pb[2 * j:2 * j + 2])
                        for j in range(2):
                            dc = half * 2 + j
                            c.op("dve", lambda e, dc=dc, j=j, hf=hf, ps=ps: e.scalar_tensor_tensor(
                                out=hf[:, dc, :], in0=hf[:, dc, :], scalar=ALPHA, in1=ps[:, j * 256:j * 256 + NF], op0=ALU.mult, op1=ALU.add),
                                 reads=[Bhf] + pb[2 * j:2 * j + 2], writes=[Bhf])
                    sq, Bsq = r_sq.next()
                    out, Bout = sq, Bsq
                    layer_norm(pp, Kb, hf[:], Bhf, out[:], Bout, sq[:], Bsq, r_st, r_sps, P, "l2w", "l2b", Bcsb, Bpv, NF)
                    Bo = Buf()
                    B_out.append(Bo)
                    c.dma("sp", lambda e, out=out, t0=t0: e.dma_start(out=yT[:, t0:t0 + NF].rearrange("(k p) t -> p k t", p=128), in_=out[:]),
                          reads=[Bout], writes=[Bo])
                c.wait_all("sp", B_out)
                c.run_block()

        mixer_phase(False)
        if stage >= 2:
            mixer_phase(True)
        if stage >= 3:
            ffn_phase()
    return nc


def _consts():
    i = np.arange(128)
    p, f = i[:, None], i[None, :]
    blocks = {
        "ident": (p == f), "J": (p + f == 127), "msl": (f < p), "msu": (p < f), "mui": (p <= f),
        "bmask": ((p // 64) == (f // 64)), "bo64": ((p // 64) == (f // 64)),
    }
    out = np.zeros((128, NCST * 128), np.float32)
    for n, m in blocks.items():
        out[:, CST[n] * 128:(CST[n] + 1) * 128] = m.astype(np.float32)
    out[:, CST["bo64m"] * 128:(CST["bo64m"] + 1) * 128] = blocks["bo64"].astype(np.float32) / 64.0
    out[:, CST["o1024"] * 128:(CST["o1024"] + 1) * 128] = 1.0 / 1024.0
    out[:, CST["o256"] * 128:(CST["o256"] + 1) * 128] = 1.0 / 256.0
    out[:, CST["ones"] * 128:(CST["ones"] + 1) * 128] = 1.0
    m8 = np.zeros((8, 1024), np.float32)
    for h in range(8):
        m8[h, h * 128:(h + 1) * 128] = 1.0
    return out, m8


def _chunkvec(v, n):
    return np.ascontiguousarray(np.asarray(v, np.float32).reshape(n, 128).T)


def prep_params(inp):
    g = lambda n: np.asarray(inp[n], np.float32)[0]
    w_in = g("w_in")
    M_W, CONV = 512, 1024
    o_z, o_xbc, o_dt = 0, 512, 1536
    o_r = 1544
    W = np.zeros((D, NCH_IN * 128), np.float32)
    W[:, 0:512] = w_in[:, o_z:o_z + 512]
    W[:, 512:1536] = w_in[:, o_xbc:o_xbc + 1024]
    W[:, 1536:3072] = w_in[:, o_r:o_r + 1536]
    W[:, CWL * 128:CWL * 128 + 64] = w_in[:, o_r + 1536:o_r + 1600]
    W[:, CAL * 128:CAL * 128 + 64] = w_in[:, o_r + 1600:o_r + 1664]
    W[:, CGL * 128:CGL * 128 + 128] = w_in[:, o_r + 1664:o_r + 1792]
    W[:, CDT * 128:CDT * 128 + 8] = w_in[:, o_dt:o_dt + 8]
    pv = np.zeros((128, NPV), np.float32)
    def put(name, arr):
        o, w = PV[name]
        pv[:, o:o + w] = arr
    cw = g("conv_w")
    cwp = np.zeros((128, 8, 5), np.float32)
    for j in range(8):
        cwp[:, j, :] = cw[:, j * 128:(j + 1) * 128].T
    put("cw", cwp.reshape(128, 40))
    put("cb", _chunkvec(g("conv_b"), 8))
    put("dch", _chunkvec(np.repeat(g("m_d"), 64), 4))
    put("nw", _chunkvec(g("m_norm_w"), 4))
    def mu15(v):
        o = np.zeros((128, 15), np.float32)
        o[:, 0:12] = _chunkvec(v[0:1536], 12)
        o[0:64, 12] = v[1536:1600]
        o[0:64, 13] = v[1600:1664]
        o[:, 14] = v[1664:1792]
        return o
    put("mp", mu15(g("r_mu_prev")))
    put("mn", mu15(g("r_mu_next")))
    put("w0f", _chunkvec(g("r_w0_f"), 4))
    put("w0b", _chunkvec(g("r_w0_b"), 4))
    put("a0", _chunkvec(g("r_a0"), 4))
    put("kk", _chunkvec(g("r_k_k"), 4))
    put("ka", _chunkvec(g("r_k_a"), 4))
    put("rk", _chunkvec(g("r_r_k").reshape(-1), 4))
    put("lw", _chunkvec(g("r_lnx_w"), 4))
    put("lb", _chunkvec(g("r_lnx_b"), 4))
    put("l1w", _chunkvec(g("ln1_w"), 8))
    put("l1b", _chunkvec(g("ln1_b"), 8))
    put("l2w", _chunkvec(g("ln2_w"), 8))
    put("l2b", _chunkvec(g("ln2_b"), 8))
    p8 = np.zeros((8, 8), np.float32)
    p8[:, 0] = g("m_dt_bias_f"); p8[:, 1] = g("m_dt_bias_b")
    p8[:, 2] = g("m_a_log_f"); p8[:, 3] = g("m_a_log_b")
    lowr = np.zeros((128, 4 * 512), np.float32)
    lowr[0:64, 0:512] = g("r_w2_f")
    lowr[0:64, 512:1024] = g("r_w2_b")
    lowr[0:64, 1024:1536] = g("r_a2")
    lowr[:, 1536:2048] = g("r_g2")
    cst, m8 = _consts()
    return dict(w_in=W, w_out=np.ascontiguousarray(g("w_out")), w_fi=np.ascontiguousarray(g("w_ffn_in")),
                w_fo=np.ascontiguousarray(g("w_ffn_out")), p8=p8, lowr=lowr, cst=cst, m8=m8), pv


def layout_core(units, link, TU):
    xT = np.zeros((D, 2 * (TU + 4)), np.float32)
    xTr = np.zeros((D, 2 * (TU + 4)), np.float32)
    for u, (seq, s) in enumerate(units):
        if seq is None:
            continue
        T = seq.shape[0]
        lo, hi = s - 2, s + TU + 2
        blk = np.zeros((TU + 4, D), np.float32)
        a, b = max(lo, 0), min(hi, T)
        blk[a - lo:b - lo] = seq[a:b]
        xT[:, u * (TU + 4):(u + 1) * (TU + 4)] = blk.T
        ur = 1 - u
        xTr[:, ur * (TU + 4):(ur + 1) * (TU + 4)] = blk[::-1].T
    return xT, xTr


_NC_CACHE = {}


def run_cores(assign, params, pv, TU, n_cores):
    if TU not in _NC_CACHE:
        import os
        _NC_CACHE[TU] = build(TU, stage=int(os.environ.get("MK_STAGE", "3")))
    nc = _NC_CACHE[TU]
    in_maps = []
    for units, link in assign:
        xT, xTr = layout_core(units, link, TU)
        pvc = pv.copy()
        pvc[:, PV["link"][0]] = float(link)
        m = dict(params)
        m.update(xT=xT, xTr=xTr, pvec=pvc)
        in_maps.append(m)
    res = run_bass_kernel_spmd(nc, in_maps, core_ids=list(range(n_cores)))
    return [r["yT"] for r in res.results]


def kernel(**inputs):
    xp = np.asarray(inputs["x_prompt"], np.float32)
    xs = np.asarray(inputs["x_sample"], np.float32)
    params, pv = prep_params(inputs)
    TU = xs.shape[1]
    assert xp.shape[1] == 2 * TU
    assign = []
    for b in range(xp.shape[0]):
        assign.append(([(xp[b], 0), (xp[b], TU)], 1))
    nb = xs.shape[0]
    slots = [[(xs[i], 0)] for i in range(nb)]
    n_rest = 8 - len(assign)
    per = [[] for _ in range(n_rest)]
    for i in range(nb):
        per[i % n_rest].append(i)
    for lst in per:
        u = [(xs[i], 0) for i in lst]
        while len(u) < 2:
            u.append((None, 0))
        assign.append((u, 0))
    outs = run_cores(assign, params, pv, TU, 8)
    y_p = np.zeros_like(xp)
    y_s = np.zeros_like(xs)
    for b in range(xp.shape[0]):
        y_p[b] = outs[b].T
    for ci, lst in enumerate(per):
        o = outs[xp.shape[0] + ci]
        for u, i in enumerate(lst):
            y_s[i] = o[:, u * TU:(u + 1) * TU].T
    return (y_p, y_s)
```

```python
import numpy as np
from contextlib import ExitStack
import concourse.bass as bass
import concourse.mybir as mybir
from concourse.bass_utils import run_bass_kernel_spmd

F32 = mybir.dt.float32
BF16 = mybir.dt.bfloat16
AF = mybir.ActivationFunctionType
ALU = mybir.AluOpType

D = 1024
C = 128
NCH_IN = 28
DFF = 2816
NFC = DFF // 128
ALPHA = 2.0 ** 0.25
LN_EPS = 1e-5
RMS_EPS = 1e-5
GN_EPS = 64e-5
CDEC = float(np.exp(-0.5))

CZ, CXS, CB, CC_, CR, CK, CV, CWL, CAL, CGL, CDT = 0, 4, 8, 10, 12, 16, 20, 24, 25, 26, 27

PV = {}
_o = 0
for _n, _w in [("cw", 40), ("cb", 8), ("dch", 4), ("nw", 4), ("mp", 15), ("mn", 15), ("w0f", 4), ("w0b", 4),
               ("a0", 4), ("kk", 4), ("ka", 4), ("rk", 4), ("lw", 4), ("lb", 4), ("l1w", 8), ("l1b", 8),
               ("l2w", 8), ("l2b", 8), ("link", 1)]:
    PV[_n] = (_o, _w)
    _o += _w
NPV = _o
CST = {n: i for i, n in enumerate(["ident", "J", "msl", "msu", "mui", "bmask", "bo64", "bo64m", "o1024", "o256", "ones"])}
NCST = len(CST)


class Buf:
    __slots__ = ("name", "w", "r", "psum")

    def __init__(self, name="", psum=False):
        self.name = name
        self.w = None
        self.r = []
        self.psum = psum


class Ctx:
    ENGS = ("pe", "act", "dve", "pool", "sp")

    def __init__(self, nc, es, n_dma_slots=6):
        self.nc = nc
        self.ops = {e: [] for e in self.ENGS}
        self.sems = {}
        self.count = {}
        for e in self.ENGS:
            self.sems[e] = es.enter_context(nc.semaphore("s_" + e))
            self.count[e] = 0
        self.waited = {e: {} for e in self.ENGS}
        self.slots = {}
        for q in ("sp", "pool"):
            self.slots[q] = []
            for i in range(n_dma_slots):
                key = "d_%s_%d" % (q, i)
                self.sems[key] = es.enter_context(nc.semaphore(key))
                self.count[key] = 0
                self.slots[q].append(key)
        self.slot_rr = {"sp": 0, "pool": 0}
        self.rr = 0
        import os
        self.ew_engs = tuple(os.environ.get("MK_EW", "dve").split(","))
        self.noself = tuple(x for x in os.environ.get("MK_NOSELF", "").split(",") if x)
        self.selfgap = int(os.environ.get("MK_SELFGAP", "0"))
        self.PEX = Buf("PEX")
        self.pex_on = os.environ.get("MK_PEX", "0") == "1"
        self.cut = int(os.environ.get("MK_CUT", "0"))
        self.nops = 0
        self.last_lines = None

    def _cut(self):
        self.nops += 1
        if self.cut and self.nops >= self.cut:
            if self.nops == self.cut:
                import sys
                f = sys._getframe(2)
                lines = []
                while f is not None and len(lines) < 5:
                    lines.append(f.f_lineno)
                    f = f.f_back
                print("MK_CUT: first dropped op #%d at lines %s" % (self.nops, lines), flush=True)
            return True
        return False

    def _deps(self, eng, reads, writes):
        need = {}
        for b in reads:
            ev = b.w
            if ev is not None and need.get(ev[0], 0) < ev[1]:
                need[ev[0]] = ev[1]
        for b in writes:
            ev = b.w
            if ev is not None and need.get(ev[0], 0) < ev[1]:
                need[ev[0]] = ev[1]
            for ev in b.r:
                if need.get(ev[0], 0) < ev[1]:
                    need[ev[0]] = ev[1]
        waits = []
        wd = self.waited[eng]
        for k, v in need.items():
            if k == eng and (eng == "pe" or eng in self.noself):
                continue
            if k == eng and self.selfgap and v <= self.count[eng] - self.selfgap:
                continue
            if wd.get(k, 0) >= v:
                continue
            wd[k] = v
            waits.append((k, v))
        return waits

    def _mark(self, ev, reads, writes):
        for b in reads:
            b.r.append(ev)
            if len(b.r) > 24:
                m = {}
                for k, v in b.r:
                    if m.get(k, 0) < v:
                        m[k] = v
                b.r = list(m.items())
        for b in writes:
            b.w = ev
            b.r = []

    @staticmethod
    def _flat(bs):
        out = []
        for b in bs:
            if isinstance(b, (list, tuple)):
                out.extend(Ctx._flat(b))
            else:
                out.append(b)
        return out

    def op(self, eng, fn, reads=(), writes=()):
        if self._cut():
            return
        reads, writes = self._flat(reads), self._flat(writes)
        pr = [b for b in reads if b.psum]
        if pr:
            reads = [b for b in reads if not b.psum]
            writes = list(writes) + pr
        if self.pex_on:
            if eng == "pe":
                reads = list(reads) + [self.PEX]
            elif eng == "dve" and any(b.psum for b in writes):
                writes = list(writes) + [self.PEX]
        waits = self._deps(eng, reads, writes)
        self.count[eng] += 1
        ev = (eng, self.count[eng])
        self.ops[eng].append((waits, fn, eng, 1))
        self._mark(ev, reads, writes)

    def dma(self, q, fn, reads=(), writes=()):
        if self._cut():
            return
        slot = self.slots[q][self.slot_rr[q] % len(self.slots[q])]
        self.slot_rr[q] += 1
        reads, writes = self._flat(reads), self._flat(writes)
        waits = self._deps(q, reads, writes)
        prev = self.count[slot]
        if prev > 0 and self.waited[q].get(slot, 0) < prev:
            self.waited[q][slot] = prev
            waits.append((slot, prev))
        self.count[slot] += 16
        ev = (slot, self.count[slot])
        self.ops[q].append((waits, fn, slot, 16))
        self._mark(ev, reads, writes)

    def wait_all(self, eng, bufs):
        if self.cut and self.nops >= self.cut:
            return
        waits = self._deps(eng, self._flat(bufs), ())
        self.ops[eng].append((waits, None, None, 0))

    def ew(self, fn, reads=(), writes=(), engs=None):
        if engs is None:
            engs = self.ew_engs
        e = engs[self.rr % len(engs)]
        self.rr += 1
        self.op(e, fn, reads, writes)

    def full_barrier(self):
        for eng in self.ENGS:
            waits = []
            for k, v in self.count.items():
                if v > 0 and self.waited[eng].get(k, 0) < v and not (k == eng):
                    self.waited[eng][k] = v
                    waits.append((k, v))
            self.ops[eng].append((waits, None, None, 0))

    def replay(self, engname, e):
        sems = self.sems
        for waits, fn, semkey, inc in self.ops[engname]:
            for k, v in waits:
                e.wait_ge(sems[k], v)
            if fn is not None:
                fn(e).then_inc(sems[semkey], inc)

    def run_block(self):
        nc = self.nc
        with nc.Block() as block:
            @block.tensor
            def _(e):
                self.replay("pe", e)

            @block.scalar
            def _(e):
                self.replay("act", e)

            @block.vector
            def _(e):
                self.replay("dve", e)

            @block.gpsimd
            def _(e):
                self.replay("pool", e)

            @block.sync
            def _(e):
                self.replay("sp", e)
        self.ops = {e: [] for e in self.ENGS}


class Rot:
    def __init__(self, nc, es, name, shape, dt, n=2):
        self.t = [es.enter_context(nc.sbuf_tensor("%s_%d" % (name, i), shape, dt)) for i in range(n)]
        self.b = [Buf(name) for _ in range(n)]
        self.i = 0

    def next(self):
        k = self.i % len(self.t)
        self.i += 1
        return self.t[k], self.b[k]


class PsumPool:
    def __init__(self, nc, es, names):
        self.t = [es.enter_context(nc.psum_tensor(n, [128, 512], F32)) for n in names]
        self.b = [[Buf(n, psum=True)] * 4 for n in names]
        self.i = 0

    def next(self):
        k = self.i % len(self.t)
        self.i += 1
        return self.t[k], self.b[k]


def build(TU, stage=3):
    import os
    SUB = float(os.environ.get('MK_SUB', '9'))
    NTL = int(os.environ.get('MK_NT', '999'))
    NTU = TU // C
    NT = 2 * NTU
    TT = 2 * TU
    XW = 2 * (TU + 4)
    nc = bass.Bass("TRN2", target_bir_lowering=False)
    dram = lambda n, s, dt=F32, kind="ExternalInput": nc.dram_tensor(n, s, dt, kind=kind).ap()
    xT = dram("xT", [D, XW])
    xTr = dram("xTr", [D, XW])
    w_in = dram("w_in", [D, NCH_IN * 128])
    w_out = dram("w_out", [D, D])
    w_fi = dram("w_fi", [D, 2 * DFF])
    w_fo = dram("w_fo", [DFF, D])
    pvec = dram("pvec", [128, NPV])
    p8 = dram("p8", [8, 8])
    lowr = dram("lowr", [128, 4 * 512])
    cst = dram("cst", [128, NCST * 128])
    m8 = dram("m8", [8, 1024])
    yT = dram("yT", [D, TT], kind="ExternalOutput")
    ysc = {k: dram(k, [NT * 128, 512], BF16, kind="Internal") for k in ("ybr_h", "ybr_l", "ybs_h", "ybs_l")}
    hT = dram("hT", [D, TT], kind="Internal")

    with ExitStack() as es0:
        c = Ctx(nc, es0)
        B_ysc = {k: [Buf() for _ in range(NT)] for k in ysc}
        B_hT = [Buf() for _ in range(NT)]
        B_out = []

        def mk_mm(c):
            def mm(out, lhsT, rhs, start, stop, reads, writes):
                c.op("pe", lambda e: e.matmul(out, lhsT=lhsT, rhs=rhs, start=start, stop=stop), reads=reads, writes=writes)
            return mm
        mm = mk_mm(c)

        def split2(src, hi, lo, reads, Bhi, Blo, psum=False):
            c.op("act", lambda e: e.activation(out=hi, in_=src, func=AF.Copy), reads=reads, writes=[Bhi])
            if psum:
                c.op("dve", lambda e: e.tensor_tensor(out=lo, in0=src, in1=hi, op=ALU.subtract), reads=list(reads) + [Bhi], writes=[Blo])
            else:
                c.ew(lambda e: e.tensor_tensor(out=lo, in0=src, in1=hi, op=ALU.subtract), reads=list(reads) + [Bhi], writes=[Blo])

        def rsqrt_eps(out, in_, eps, reads, writes):
            c.op("dve", lambda e: e.tensor_scalar(out=out, in0=in_, scalar1=float(eps), scalar2=None, op0=ALU.add), reads=reads, writes=writes)
            c.op("act", lambda e: e.activation(out=out, in_=out, func=AF.Sqrt), reads=writes, writes=writes)
            c.op("dve", lambda e: e.reciprocal(out=out, in_=out), reads=writes, writes=writes)

        def layer_norm(pp, Kb, pre, Bpre, out, Bout, sq, Bsq, r_st, r_sp, P, wn, bn, Bcsb, Bpv, N):
            def stat(src, Bsrc):
                red, Bred = r_st.next()
                c.op("dve", lambda e: e.tensor_reduce(out=red[:, 0:N], in_=src.rearrange("p a n -> p n a"), axis=mybir.AxisListType.X, op=ALU.add),
                     reads=[Bsrc], writes=[Bred])
                hi, Bhi = r_sp.next()
                lo, Blo = r_sp.next()
                split2(red[:, 0:N], hi[:, 0:N], lo[:, 0:N], [Bred], Bhi, Blo)
                ps, pb = pp.next()
                mm(ps[:, 0:N], Kb("o1024"), hi[:, 0:N], True, False, [Bcsb, Bhi], pb)
                mm(ps[:, 0:N], Kb("o1024"), lo[:, 0:N], False, True, [Bcsb, Blo], pb)
                return ps, pb
            ps, pb = stat(pre, Bpre)
            mu, Bmu = r_st.next()
            c.op("act", lambda e: e.activation(out=mu[:, 0:N], in_=ps[:, 0:N], func=AF.Copy), reads=pb, writes=[Bmu])
            bcN = lambda t: t[:, 0:N].unsqueeze(1).to_broadcast([128, 8, N])
            bcP = lambda n: P(n, 0, 8).unsqueeze(2).to_broadcast([128, 8, N])
            c.ew(lambda e: e.tensor_tensor(out=pre, in0=pre, in1=bcN(mu), op=ALU.subtract), reads=[Bpre, Bmu], writes=[Bpre])
            c.ew(lambda e: e.tensor_tensor(out=sq, in0=pre, in1=pre, op=ALU.mult), reads=[Bpre], writes=[Bsq])
            ps2, pb2 = stat(sq, Bsq)
            rs, Brs = r_st.next()
            rsqrt_eps(rs[:, 0:N], ps2[:, 0:N], LN_EPS, pb2, [Brs])
            c.ew(lambda e: e.tensor_tensor(out=pre, in0=pre, in1=bcN(rs), op=ALU.mult), reads=[Bpre, Brs], writes=[Bpre])
            c.ew(lambda e: e.tensor_tensor(out=pre, in0=pre, in1=bcP(wn), op=ALU.mult), reads=[Bpre, Bpv], writes=[Bpre])
            c.ew(lambda e: e.tensor_tensor(out=out, in0=pre, in1=bcP(bn), op=ALU.add), reads=[Bpre, Bpv], writes=[Bout])

        def mixer_phase(fwd):
            with ExitStack() as es:
                sfx = "F" if fwd else "B"
                sb = lambda n, s, dt=F32: es.enter_context(nc.sbuf_tensor(n + sfx, s, dt))
                rot = lambda n, s, dt=F32, k=2: Rot(nc, es, n + sfx, s, dt, k)
                pp = PsumPool(nc, es, ["pp%d%s" % (i, sfx) for i in range(6)])
                YR_t = es.enter_context(nc.psum_tensor("YR" + sfx, [128, 512], F32)); YR_b = Buf("YR", psum=True)
                YS_t = es.enter_context(nc.psum_tensor("YS" + sfx, [128, 512], F32)); YS_b = Buf("YS", psum=True)
                xsrc = xT if fwd else xTr

                class Fix:
                    def __init__(self, t, b):
                        self.t, self.b = t, b

                    def next(self):
                        return self.t, self.b
                one = lambda n, s, dt=F32: Fix(sb(n, s, dt), Buf(n))

                win = sb("win", [128, 8, NCH_IN * 128], BF16); Bwin = Buf()
                for k in range(8):
                    c.dma("pool", lambda e, k=k: e.dma_start(out=win[:, k, :], in_=w_in[k * 128:(k + 1) * 128, :]), writes=[Bwin])
                if fwd:
                    wo = sb("wo", [128, 8, D], BF16); Bwo = Buf()
                    for k in range(8):
                        c.dma("pool", lambda e, k=k: e.dma_start(out=wo[:, k, :], in_=w_out[k * 128:(k + 1) * 128, :]), writes=[Bwo])
                pv = sb("pv", [128, NPV]); Bpv = Buf()
                c.dma("sp", lambda e: e.dma_start(out=pv[:], in_=pvec), writes=[Bpv])
                p8t = sb("p8t", [8, 8]); Bp8 = Buf()
                c.dma("sp", lambda e: e.dma_start(out=p8t[:], in_=p8), writes=[Bp8])
                lrb = sb("lrb", [128, 3 * 512], BF16); Blr = Buf()
                w2o = 0 if fwd else 512
                c.dma("pool", lambda e: e.dma_start(out=lrb[:, 0:512], in_=lowr[:, w2o:w2o + 512]), writes=[Blr])
                c.dma("pool", lambda e: e.dma_start(out=lrb[:, 512:1536], in_=lowr[:, 1024:2048]), writes=[Blr])
                csb = sb("csb", [128, NCST * 128], BF16); Bcsb = Buf()
                c.dma("pool", lambda e: e.dma_start(out=csb[:], in_=cst), writes=[Bcsb])
                m8b = sb("m8b", [8, 1024], BF16); Bm8 = Buf()
                c.dma("pool", lambda e: e.dma_start(out=m8b[:], in_=m8), writes=[Bm8])
                Kb = lambda n: csb[:, CST[n] * 128:(CST[n] + 1) * 128]
                mask4 = sb("mask4", [128, 512], BF16); Bm4 = Buf()
                for q, n in enumerate(["msu", "mui", "msu", "mui"]):
                    c.op("dve", lambda e, q=q, n=n: e.tensor_copy(out=mask4[:, q * 128:(q + 1) * 128], in_=Kb(n)),
                         reads=[Bcsb], writes=[Bm4])
                P = lambda n, j=0, w=1: pv[:, PV[n][0] + j:PV[n][0] + j + w]
                der = sb("der", [128, 15 + 4]); Bder = Buf()
                c.op("dve", lambda e: e.tensor_tensor(out=der[:, 0:15], in0=P("mp", 0, 15), in1=P("mn", 0, 15), op=ALU.add),
                     reads=[Bpv], writes=[Bder])
                c.op("dve", lambda e: e.tensor_scalar(out=der[:, 0:15], in0=der[:, 0:15], scalar1=-1.0, scalar2=1.0,
                                                      op0=ALU.mult, op1=ALU.add), reads=[Bder], writes=[Bder])
                c.op("dve", lambda e: e.tensor_scalar(out=der[:, 15:19], in0=P("ka", 0, 4), scalar1=-1.0, scalar2=1.0,
                                                      op0=ALU.mult, op1=ALU.add), reads=[Bpv], writes=[Bder])
                a8 = sb("a8", [8, 1]); Ba8 = Buf()
                acol = 2 if fwd else 3
                c.op("act", lambda e: e.activation(out=a8[:], in_=p8t[:, acol:acol + 1], func=AF.Exp), reads=[Bp8], writes=[Ba8])
                c.op("dve", lambda e: e.tensor_scalar(out=a8[:], in0=a8[:], scalar1=-1.0, scalar2=None, op0=ALU.mult),
                     reads=[Ba8], writes=[Ba8])
                dtb = p8t[:, (0 if fwd else 1):(1 if fwd else 2)]
                w2 = lrb[:, 0:512]
                a2 = lrb[:, 512:1024]
                g2 = lrb[:, 1024:1536]
                w0n = "w0f" if fwd else "w0b"
                mpn, mnn = ("mp", "mn") if fwd else ("mn", "mp")

                Hb = sb("Hb", [128, 4, 128], BF16); BH = [Buf() for _ in range(4)]
                STf = sb("STf", [128, 512]); STb = sb("STb", [128, 512], BF16); BST = Buf()
                c.op("pool", lambda e: e.memset(Hb[:], 0.0), writes=BH)
                c.op("pool", lambda e: e.memset(STf[:], 0.0), writes=[BST])
                c.op("pool", lambda e: e.memset(STb[:], 0.0), writes=[BST])
                ARe_t = sb("ARe", [128, 4, 256], BF16)
                ARo_t = sb("ARo", [128, 4, 256], BF16)
                BAR_t = Buf()
                c.op("pool", lambda e: e.memset(ARe_t[:], 0.0), writes=[BAR_t])
                c.op("pool", lambda e: e.memset(ARo_t[:], 0.0), writes=[BAR_t])
                CSt = sb("CS", [128, 4, 129]); BCSt = Buf()
                c.op("pool", lambda e: e.memset(CSt[:], 0.0), writes=[BCSt])
                ptmp = sb("ptmp", [128, 128]); Bptmp = Buf()
                Bfence = Buf()
                ones_t = sb("ones_t", [128, 128]); Bones = Buf()
                c.op("pool", lambda e: e.memset(ones_t[:], 1.0), writes=[Bones])
                DT3 = sb("DT3", [128, 3, 2, 128], BF16); BDT3 = Buf()
                RB3 = sb("RB3", [128, 2, 8, 128], BF16); BRB3 = [Buf(), Buf()]
                c.op("pool", lambda e: e.memset(DT3[:], 0.0), writes=[BDT3])
                c.op("pool", lambda e: e.memset(RB3[:], 0.0), writes=BRB3)

                arena = sb("arena", [128, 11 * 512]); Bar = [Buf() for _ in range(11)]

                def AS(i, n=1, parts=128, nch=None):
                    nch = 4 * n if nch is None else nch
                    return (arena[0:parts, 512 * i:512 * i + 128 * nch].rearrange("p (a b) -> p a b", b=128), Bar[i:i + n])

                def AU(i, nchunks):
                    return (arena[:, 512 * i:512 * i + 132 * nchunks].rearrange("p (a b) -> p a b", b=132), Bar[i:i + 4])

                r_xb = rot("xb", [128, 8, 132], BF16, 1 if fwd else 2)
                if fwd:
                    U1t, BU1 = None, None
                else:
                    U1t = sb("U1", [128, 13, 132]); BU1 = Buf("U1")
                r_zs = one("zs", [128, 4, 128])
                r_xsf = one("xsf", [128, 4, 128])
                r_xbc = one("xbcb", [128, 8, 128], BF16)
                r_dtok = one("dtok", [128, 16])
                r_gm = one("gm", [128, 2, 128])
                r_MT = one("MT", [128, 8, 128], BF16)
                r_Cd = one("Cd", [128, 8, 128], BF16)
                r_xdt = one("xdt", [128, 2, 512], BF16)
                r_btok = one("btok", [128, 2, 128], BF16)
                r_lrin = one("lrin", [128, 3, 128], BF16)
                r_sp = rot("sp", [128, 4, 128], BF16, 2)
                r_op = rot("opb", [128, 4, 128], BF16, 4)
                LS = int(os.environ.get("MK_LS_F" if fwd else "MK_LS_B", "4" if fwd else "8"))
                NPR = max(1, LS // 2)
                r_tok = rot("tok", [128, 4, 128], BF16, max(2, NPR))
                r_p0 = rot("p0", [128, 128], BF16, LS)
                r_e4 = rot("e4", [128, 512], BF16, LS)
                NQ = LS // 4
                r_pp = rot("ppq", [128, 4, 256], BF16, 2 * NQ)
                r_zf = rot("zf", [128, 4, 128], F32, NQ)
                r_zb = rot("zb", [128, 4, 128], BF16, NQ)
                r_wu = rot("wu", [128, 2, 128], BF16, max(2, NPR))
                r_qt = rot("qt", [128, 128], BF16, 1)
                r_mtb = rot("mtb", [128, 128], BF16, 1)
                r_yhl = rot("yhl", [128, 2, 512], BF16, 2)
                if fwd:
                    r_ybl = rot("ybl", [128, 2, 512], BF16, 2)
                    r_bg = rot("bg", [128, 4, 128], F32, 2)
                    r_mix = one("mix", [128, 8, 128], BF16)
                    r_st = rot("stat", [128, 128], F32, 4)
                    r_sps = rot("sps", [128, 128], BF16, 2)

                evac_rr = [0]

                def evac(out, in_, reads, writes, engs=tuple(os.environ.get("MK_EVAC", "act,dve").split(","))):
                    e_ = engs[evac_rr[0] % len(engs)]
                    evac_rr[0] += 1
                    if e_ == "act":
                        c.op("act", lambda e: e.activation(out=out, in_=in_, func=AF.Copy), reads=reads, writes=writes)
                    else:
                        c.op(e_, lambda e: e.tensor_copy(out=out, in_=in_), reads=reads, writes=writes)

                def blocksum4(src, Bsrc, cname):
                    hi, Bhi = r_sp.next()
                    lo, Blo = r_sp.next()
                    split2(src, hi[:], lo[:], [Bsrc], Bhi, Blo)
                    ps, pb = pp.next()
                    for cp in range(4):
                        o_ = ps[:, cp * 128:(cp + 1) * 128]
                        mm(o_, Kb(cname), hi[:, cp, :], True, False, [Bcsb, Bhi], [pb[cp]])
                        mm(o_, Kb(cname), lo[:, cp, :], False, True, [Bcsb, Blo], [pb[cp]])
                    return ps[:].rearrange("p (a b) -> p a b", b=128), pb

                def link_state():
                    lk = P("link")
                    for cpr in range(4):
                        c.op("dve", lambda e, cpr=cpr: e.tensor_scalar(out=Hb[:, cpr, :], in0=Hb[:, cpr, :], scalar1=lk,
                                                                       scalar2=None, op0=ALU.mult),
                             reads=[BH[cpr], Bpv], writes=[BH[cpr]])
                    c.op("dve", lambda e: e.tensor_scalar(out=STf[:], in0=STf[:], scalar1=lk, scalar2=None, op0=ALU.mult),
                         reads=[BST, Bpv], writes=[BST])
                    c.op("dve", lambda e: e.tensor_scalar(out=STb[:], in0=STb[:], scalar1=lk, scalar2=None, op0=ALU.mult),
                         reads=[BST, Bpv], writes=[BST])

                for ti in range(min(NT, NTL)):
                    u_i, lt = divmod(ti, NTU)
                    if ti == NTU:
                        link_state()
                    col0 = u_i * (TU + 4) + lt * C
                    xb, Bxb = r_xb.next()
                    c.dma("pool", lambda e, xb=xb, col0=col0: e.dma_start(
                        out=xb[:], in_=xsrc[:, col0:col0 + 132].rearrange("(k p) t -> p k t", p=128)), writes=[Bxb])
                    if SUB < 1:
                        continue

                    def inproj(U, BU, chunks):
                        for g0 in range(0, len(chunks), 3):
                            grp = chunks[g0:g0 + 3]
                            n_g = len(grp)
                            ps, pb = pp.next()
                            for j, cc in enumerate(grp):
                                for k in range(8):
                                    mm(ps[:, j * 132:(j + 1) * 132], win[:, k, cc * 128:(cc + 1) * 128], xb[:, k, :],
                                       k == 0, k == 7, [Bwin, Bxb], pb)
                            evac(U[:, g0:g0 + n_g, :], ps[:, 0:n_g * 132].rearrange("p (a b) -> p a b", b=132), pb, [BU])

                    if U1t is not None:
                        U, BU = U1t[:], BU1
                    else:
                        U, BU = AU(6, 13)
                    inproj(U, BU, list(range(12)) + [CDT])
                    UDT = 12
                    if U1t is not None and SUB >= 3:
                        U2, BU2 = AU(6, 15)
                        inproj(U2, BU2, list(range(CR, CR + 15)))
                    if SUB < 2:
                        continue
                    zs, Bzs = r_zs.next()
                    c.op("act", lambda e, zs=zs, U=U: e.activation(out=zs[:], in_=U[:, CZ:CZ + 4, 2:130], func=AF.Silu),
                         reads=[BU], writes=[Bzs])
                    cv, Bcv = AS(0, 2)
                    for j in range(8):
                        eng = ("dve", "pool")[j % 2] if os.environ.get("MK_CONVP", "0") == "1" else "dve"
                        for k5 in range(5):
                            kk_ = k5 if fwd else 4 - k5
                            wk = pv[:, PV["cw"][0] + j * 5 + kk_:PV["cw"][0] + j * 5 + kk_ + 1]
                            src = U[:, CXS + j, k5:k5 + 128]
                            if k5 == 0:
                                c.op(eng, lambda e, j=j, wk=wk, src=src, cv=cv: e.tensor_scalar(
                                    out=cv[:, j, :], in0=src, scalar1=wk, scalar2=P("cb", j), op0=ALU.mult, op1=ALU.add),
                                     reads=[BU, Bpv], writes=[Bcv])
                            elif eng == "dve":
                                c.op(eng, lambda e, j=j, wk=wk, src=src, cv=cv: e.scalar_tensor_tensor(
                                    out=cv[:, j, :], in0=src, scalar=wk, in1=cv[:, j, :], op0=ALU.mult, op1=ALU.add),
                                     reads=[BU, Bpv, Bcv], writes=[Bcv])
                            else:
                                c.op(eng, lambda e, wk=wk, src=src: e.tensor_scalar(
                                    out=ptmp[:], in0=src, scalar1=wk, scalar2=None, op0=ALU.mult), reads=[BU, Bpv], writes=[Bptmp])
                                c.op(eng, lambda e, j=j, cv=cv: e.tensor_tensor(out=cv[:, j, :], in0=cv[:, j, :], in1=ptmp[:], op=ALU.add),
                                     reads=[Bptmp, Bcv], writes=[Bcv])
                    xsf, Bxsf = r_xsf.next()
                    xbc, Bxbc = r_xbc.next()
                    c.op("act", lambda e, xsf=xsf, cv=cv: e.activation(out=xsf[:], in_=cv[:, 0:4, :], func=AF.Silu),
                         reads=[Bcv], writes=[Bxsf])
                    c.op("act", lambda e, xbc=xbc, cv=cv: e.activation(out=xbc[:, 4:8, :], in_=cv[:, 4:8, :], func=AF.Silu),
                         reads=[Bcv], writes=[Bxbc])
                    c.op("act", lambda e, xbc=xbc, xsf=xsf: e.activation(out=xbc[:, 0:4, :], in_=xsf[:], func=AF.Copy),
                         reads=[Bxsf], writes=[Bxbc])
                    d8, Bd8 = AS(10, 1, parts=8)
                    c.op("act", lambda e, d8=d8, U=U: e.activation(out=d8[:, 0, :], in_=U[0:8, UDT, 2:130], func=AF.Exp,
                                                                   bias=dtb), reads=[BU, Bp8], writes=[Bd8])
                    c.op("act", lambda e, d8=d8: e.activation(out=d8[:, 1, :], in_=d8[:, 0, :], func=AF.Ln, bias=1.0),
                         reads=[Bd8], writes=[Bd8])
                    c.op("dve", lambda e, d8=d8: e.tensor_scalar(out=d8[:, 2, :], in0=d8[:, 1, :], scalar1=a8[:, 0:1],
                                                                 scalar2=None, op0=ALU.mult), reads=[Bd8, Ba8], writes=[Bd8])
                    c.op("dve", lambda e, d8=d8: e.tensor_tensor_scan(out=d8[:, 3, :], data0=ones_t[0:8, :], data1=d8[:, 2, :],
                                                                      initial=0.0, op0=ALU.mult, op1=ALU.add),
                         reads=[Bd8, Bones], writes=[Bd8])
                    for q, (src_i, r_i) in enumerate(((1, 0), (3, 2))):
                        x_ = d8[:, src_i, :]
                        r_ = d8[:, r_i, :]
                        c.op("dve", lambda e, q=q, x_=x_: e.tensor_copy(out=DT3[0:8, 0, q, :], in_=x_), reads=[Bd8], writes=[BDT3])
                        c.op("dve", lambda e, q=q, x_=x_, r_=r_: e.tensor_tensor(out=r_, in0=x_, in1=DT3[0:8, 0, q, :], op=ALU.subtract),
                             reads=[Bd8, BDT3], writes=[Bd8])
                        c.op("dve", lambda e, q=q, r_=r_: e.tensor_copy(out=DT3[0:8, 1, q, :], in_=r_), reads=[Bd8], writes=[BDT3])
                        c.op("dve", lambda e, q=q, r_=r_: e.tensor_tensor(out=r_, in0=r_, in1=DT3[0:8, 1, q, :], op=ALU.subtract),
                             reads=[Bd8, BDT3], writes=[Bd8])
                        c.op("dve", lambda e, q=q, r_=r_: e.tensor_copy(out=DT3[0:8, 2, q, :], in_=r_), reads=[Bd8], writes=[BDT3])
                    ps, pb = pp.next()
                    for q in range(2):
                        for i3 in range(3):
                            mm(ps[:, q * 128:(q + 1) * 128], DT3[:, i3, q, :], Kb("ident"), i3 == 0, i3 == 2, [BDT3, Bcsb], [pb[q]])
                    dtok, Bdtok = r_dtok.next()
                    c.op("dve", lambda e, dtok=dtok, ps=ps: e.tensor_copy(out=dtok[:, 0:8], in_=ps[:, 0:8]), reads=[pb[0], pb[1]], writes=[Bdtok])
                    c.op("dve", lambda e, dtok=dtok, ps=ps: e.tensor_copy(out=dtok[:, 8:16], in_=ps[:, 128:136]), reads=[pb[1]], writes=[Bdtok])
                    psR0, pbR0 = pp.next()
                    if os.environ.get("MK_SKIPB", "") == "1":
                        pp.next()
                    psR1, pbR1 = pp.next()
                    for i3 in range(3):
                        rb = i3 % 2
                        c.ew(lambda e, i3=i3, rb=rb: e.tensor_tensor(
                            out=RB3[0:8, rb, :, :], in0=m8b[:].rearrange("p (h l) -> p h l", l=128),
                            in1=DT3[0:8, i3, 1:2, :].to_broadcast([8, 8, 128]), op=ALU.mult), reads=[BDT3, Bm8], writes=[BRB3[rb]])
                        for (psR, pbR, h0) in ((psR0, pbR0, 0), (psR1, pbR1, 4)):
                            mm(psR[:], Kb("ones"), RB3[:, rb, h0:h0 + 4, :].rearrange("p a b -> p (a b)"), i3 == 0, i3 == 2, [Bcsb, BRB3[rb]], pbR)
                    seg, Bseg = AS(2, 2)
                    E, BE = AS(4, 2)
                    if os.environ.get("MK_FENCE", "-1") != "-1":
                        pbR0 = pbR0 + [Bfence]
                    for _d in range(int(os.environ.get("MK_DELAY", "0"))):
                        c.op("dve", lambda e: e.tensor_copy(out=ptmp[:], in_=ones_t[:]), reads=[pbR0, Bones], writes=[Bptmp])
                    for (psR, pbR, h0) in ((psR0, pbR0, 0), (psR1, pbR1, 4)):
                        c.op("dve", lambda e, psR=psR, h0=h0, seg=seg, dtok=dtok: e.tensor_tensor(
                            out=seg[:, h0:h0 + 4, :], in0=psR[:].rearrange("p (a b) -> p a b", b=128),
                            in1=dtok[:, 8 + h0:12 + h0].unsqueeze(2).to_broadcast([128, 4, 128]), op=ALU.subtract),
                             reads=[pbR, Bdtok], writes=[Bseg])
                    c.ew(lambda e, seg=seg: e.tensor_scalar(out=seg[:], in0=seg[:], scalar1=0.0, scalar2=None, op0=ALU.min), reads=[Bseg], writes=[Bseg])
                    c.op("act", lambda e, E=E, psR0=psR0: e.activation(out=E[:, 0:4, :], in_=psR0[:].rearrange("p (a b) -> p a b", b=128),
                                                                     func=AF.Exp), reads=pbR0, writes=[BE])
                    c.op("act", lambda e, E=E, psR1=psR1: e.activation(out=E[:, 4:8, :], in_=psR1[:].rearrange("p (a b) -> p a b", b=128),
                                                                     func=AF.Exp), reads=pbR1, writes=[BE])
                    c.op("act", lambda e, seg=seg: e.activation(out=seg[:], in_=seg[:], func=AF.Exp), reads=[Bseg], writes=[Bseg])
                    ps, pb = pp.next()
                    for g in range(2):
                        mm(ps[:, g * 128:(g + 1) * 128], xbc[:, 4 + g, :], xbc[:, 6 + g, :], True, True, [Bxbc], [pb[g]])
                    gm, Bgm = r_gm.next()
                    for g in range(2):
                        c.op("dve", lambda e, g=g, gm=gm, ps=ps: e.tensor_tensor(out=gm[:, g, :], in0=ps[:, g * 128:(g + 1) * 128],
                                                                                 in1=Kb("mui"), op=ALU.mult),
                             reads=[pb[g], Bcsb], writes=[Bgm])
                    MT, BMT = r_MT.next()
                    Cd, BCd = r_Cd.next()
                    for g in range(2):
                        c.ew(lambda e, g=g, MT=MT, seg=seg, gm=gm: e.tensor_tensor(
                            out=MT[:, 4 * g:4 * g + 4, :], in0=seg[:, 4 * g:4 * g + 4, :], in1=gm[:, g:g + 1, :].to_broadcast([128, 4, 128]), op=ALU.mult),
                             reads=[Bseg, Bgm], writes=[BMT])
                        c.ew(lambda e, g=g, Cd=Cd, E=E, xbc=xbc: e.tensor_tensor(
                            out=Cd[:, 4 * g:4 * g + 4, :], in0=E[:, 4 * g:4 * g + 4, :], in1=xbc[:, 6 + g:7 + g, :].to_broadcast([128, 4, 128]), op=ALU.mult),
                             reads=[BE, Bxbc], writes=[BCd])
                    ps, pb = pp.next()
                    for j in range(4):
                        mm(ps[:, j * 128:(j + 1) * 128], xbc[:, j, :], Kb("ident"), True, True, [Bxbc, Bcsb], [pb[j]])
                    xdt, Bxdt = r_xdt.next()
                    v864 = lambda ap: ap.rearrange("p (h q) -> p h q", q=64)
                    c.op("dve", lambda e, xdt=xdt, ps=ps, dtok=dtok: e.tensor_tensor(
                        out=v864(xdt[:, 0, :]), in0=v864(ps[:]), in1=dtok[:, 0:8].unsqueeze(2).to_broadcast([128, 8, 64]), op=ALU.mult),
                         reads=[pb, Bdtok], writes=[Bxdt])
                    c.ew(lambda e, xdt=xdt, seg=seg: e.tensor_tensor(
                        out=v864(xdt[:, 1, :]), in0=v864(xdt[:, 0, :]), in1=seg[:, :, 127:128].to_broadcast([128, 8, 64]), op=ALU.mult),
                         reads=[Bxdt, Bseg], writes=[Bxdt])
                    ps, pb = pp.next()
                    for g in range(2):
                        mm(ps[:, g * 128:(g + 1) * 128], xbc[:, 4 + g, :], Kb("ident"), True, True, [Bxbc, Bcsb], [pb[g]])
                    btok, Bbtok = r_btok.next()
                    evac(btok[:], ps[:, 0:256].rearrange("p (a b) -> p a b", b=128), pb[0:2], [Bbtok])
                    for h in range(8):
                        o_ = YS_t[:, h * 64:(h + 1) * 64]
                        mm(o_, MT[:, h, :], xdt[:, 0, h * 64:(h + 1) * 64], True, False, [BMT, Bxdt], [YS_b])
                        mm(o_, Cd[:, h, :], STb[:, h * 64:(h + 1) * 64], False, True, [BCd, BST], [YS_b])
                    ysh, Bysh = r_yhl.next()
                    split2(YS_t[:], ysh[:, 0, :], ysh[:, 1, :], [YS_b], Bysh, Bysh, psum=True)
                    ps, pb = pp.next()
                    for g in range(2):
                        mm(ps[:, g * 256:(g + 1) * 256], btok[:, g, :], xdt[:, 1, g * 256:(g + 1) * 256], True, True,
                           [Bbtok, Bxdt], pb[2 * g:2 * g + 2])
                    c.ew(lambda e, E=E: e.tensor_tensor(out=v864(STf[:]), in0=v864(STf[:]), in1=E[:, :, 127:128].to_broadcast([128, 8, 64]), op=ALU.mult),
                         reads=[BST, BE], writes=[BST])
                    c.op("dve", lambda e, ps=ps: e.tensor_tensor(out=STf[:], in0=STf[:], in1=ps[:], op=ALU.add), reads=[BST, pb], writes=[BST])
                    c.op("act", lambda e: e.activation(out=STb[:], in_=STf[:], func=AF.Copy), reads=[BST], writes=[BST])
                    if not fwd:
                        for q, key in enumerate(("ybs_h", "ybs_l")):
                            c.dma("sp", lambda e, ysh=ysh, q=q, key=key, ti=ti: e.dma_start(out=ysc[key][ti * 128:(ti + 1) * 128, :], in_=ysh[:, q, :]),
                                  reads=[Bysh], writes=[B_ysc[key][ti]])
                    if SUB < 3:
                        continue

                    if U1t is not None:
                        U, BU = U2, BU2
                    else:
                        U, BU = AU(4, 15)
                        inproj(U, BU, list(range(CR, CR + 15)))
                    us, Bus = AS(0, 4, nch=15)
                    tsh, Btsh = AS(4, 2) if U1t is not None else AS(8, 2)
                    bc15 = lambda ap, j0, n: ap[:, j0:j0 + n].unsqueeze(2).to_broadcast([128, n, 128])
                    c.ew(lambda e, us=us, U=U: e.tensor_tensor(out=us, in0=U[:, 0:15, 2:130], in1=bc15(der, 0, 15), op=ALU.mult),
                         reads=[BU, Bder], writes=[Bus])
                    for (j0, n) in ((0, 8), (8, 7)):
                        for (sl_, mun) in (((1, 129), mpn), ((3, 131), mnn)):
                            mu_ap = pv[:, PV[mun][0]:PV[mun][0] + 15]
                            c.ew(lambda e, j0=j0, n=n, sl_=sl_, mu_ap=mu_ap, U=U, tsh=tsh: e.tensor_tensor(
                                out=tsh[:, 0:n, :], in0=U[:, j0:j0 + n, sl_[0]:sl_[1]], in1=bc15(mu_ap, j0, n), op=ALU.mult),
                                 reads=[BU, Bpv], writes=[Btsh])
                            c.ew(lambda e, j0=j0, n=n, us=us, tsh=tsh: e.tensor_tensor(
                                out=us[:, j0:j0 + n, :], in0=us[:, j0:j0 + n, :], in1=tsh[:, 0:n, :], op=ALU.add),
                                 reads=[Bus, Btsh], writes=[Bus])
                    lrin, Blrin = r_lrin.next()
                    c.op("act", lambda e, us=us, lrin=lrin: e.activation(out=lrin[:, 0, :], in_=us[:, 12, :], func=AF.Tanh), reads=[Bus], writes=[Blrin])
                    c.op("act", lambda e, us=us, lrin=lrin: e.activation(out=lrin[:, 1, :], in_=us[:, 13, :], func=AF.Copy), reads=[Bus], writes=[Blrin])
                    c.op("act", lambda e, us=us, lrin=lrin: e.activation(out=lrin[:, 2, :], in_=us[:, 14, :], func=AF.Sigmoid), reads=[Bus], writes=[Blrin])
                    psL, pbL = pp.next()
                    psI, pbI = pp.next()
                    for cp in range(4):
                        mm(psL[:, cp * 128:(cp + 1) * 128], w2[:, cp * 128:(cp + 1) * 128], lrin[:, 0, :], True, True, [Blr, Blrin], [pbL[cp]])
                        mm(psI[:, cp * 128:(cp + 1) * 128], a2[:, cp * 128:(cp + 1) * 128], lrin[:, 1, :], True, True, [Blr, Blrin], [pbI[cp]])
                    sg, Bsg = AS(4)
                    icl, Bicl = AS(5)
                    bc4 = lambda n: P(n, 0, 4).unsqueeze(2).to_broadcast([128, 4, 128])
                    v4 = lambda ps_: ps_[:].rearrange("p (a b) -> p a b", b=128)
                    c.op("dve", lambda e, sg=sg, psL=psL: e.tensor_tensor(out=sg[:], in0=v4(psL), in1=bc4(w0n), op=ALU.add), reads=[pbL, Bpv], writes=[Bsg])
                    c.op("act", lambda e, sg=sg: e.activation(out=sg[:], in_=sg[:], func=AF.Sigmoid), reads=[Bsg], writes=[Bsg])
                    c.op("dve", lambda e, icl=icl, psI=psI: e.tensor_tensor(out=icl[:], in0=v4(psI), in1=bc4("a0"), op=ALU.add), reads=[pbI, Bpv], writes=[Bicl])
                    c.op("act", lambda e, icl=icl: e.activation(out=icl[:], in_=icl[:], func=AF.Sigmoid), reads=[Bicl], writes=[Bicl])
                    for cp in range(4):
                        c.op("dve", lambda e, cp=cp, sg=sg: e.tensor_tensor_scan(
                            out=CSt[:, cp, 1:129], data0=ones_t[:], data1=sg[:, cp, :], initial=0.0, op0=ALU.mult, op1=ALU.add),
                             reads=[Bsg, Bones, BCSt], writes=[BCSt])
                    gam, Bgam = AS(6)
                    igam, Bigam = AS(7)
                    gprev, Bgprev = AS(8)
                    c.op("act", lambda e, gam=gam: e.activation(out=gam[:], in_=CSt[:, :, 1:129], func=AF.Exp, scale=-CDEC),
                         reads=[BCSt], writes=[Bgam])
                    c.op("act", lambda e, igam=igam: e.activation(out=igam[:], in_=CSt[:, :, 1:129], func=AF.Exp, scale=CDEC),
                         reads=[BCSt], writes=[Bigam])
                    c.op("act", lambda e, gprev=gprev: e.activation(out=gprev[:], in_=CSt[:, :, 0:128], func=AF.Exp, scale=-CDEC),
                         reads=[BCSt], writes=[Bgprev])
                    kkt, Bkk = AS(9)
                    c.ew(lambda e, kkt=kkt, us=us: e.tensor_tensor(out=kkt[:], in0=us[:, 4:8, :], in1=bc4("kk"), op=ALU.mult), reads=[Bus, Bpv], writes=[Bkk])
                    sq, Bsq = sg, Bsg
                    c.ew(lambda e, sq=sq, kkt=kkt: e.tensor_tensor(out=sq[:], in0=kkt[:], in1=kkt[:], op=ALU.mult), reads=[Bkk], writes=[Bsq])
                    psS, pbS = blocksum4(sq[:], Bsq, "bo64")
                    rn, Brn = AS(10)
                    c.op("dve", lambda e, rn=rn, psS=psS: e.tensor_scalar(out=rn[:], in0=psS, scalar1=1e-24, scalar2=None, op0=ALU.max),
                         reads=pbS, writes=[Brn])
                    c.op("act", lambda e, rn=rn: e.activation(out=rn[:], in_=rn[:], func=AF.Sqrt), reads=[Brn], writes=[Brn])
                    c.op("dve", lambda e, rn=rn: e.reciprocal(out=rn[:], in_=rn[:]), reads=[Brn], writes=[Brn])
                    c.ew(lambda e, kkt=kkt, rn=rn: e.tensor_tensor(out=kkt[:], in0=kkt[:], in1=rn[:], op=ALU.mult), reads=[Bkk, Brn], writes=[Bkk])
                    km, Bkm = rn, Brn
                    c.ew(lambda e, km=km, icl=icl: e.tensor_tensor(out=km[:], in0=icl[:], in1=bc4("ka"), op=ALU.mult), reads=[Bicl, Bpv], writes=[Bkm])
                    c.ew(lambda e, km=km: e.tensor_tensor(out=km[:], in0=km[:], in1=der[:, 15:19].unsqueeze(2).to_broadcast([128, 4, 128]), op=ALU.add),
                         reads=[Bkm, Bder], writes=[Bkm])
                    c.ew(lambda e, km=km, us=us: e.tensor_tensor(out=km[:], in0=km[:], in1=us[:, 4:8, :], op=ALU.mult), reads=[Bkm, Bus], writes=[Bkm])
                    if fwd:
                        rk_, Brk = sq, Bsq
                        c.ew(lambda e, rk_=rk_, us=us: e.tensor_tensor(out=rk_[:], in0=us[:, 0:4, :], in1=bc4("rk"), op=ALU.mult), reads=[Bus, Bpv], writes=[Brk])
                        c.ew(lambda e, rk_=rk_, km=km: e.tensor_tensor(out=rk_[:], in0=rk_[:], in1=km[:], op=ALU.mult), reads=[Brk, Bkm], writes=[Brk])
                        psB, pbB = blocksum4(rk_[:], Brk, "bo64")
                        bonv, Bbonv = r_bg.next()
                        c.op("dve", lambda e, bonv=bonv, psB=psB, us=us: e.tensor_tensor(out=bonv[:], in0=psB, in1=us[:, 8:12, :], op=ALU.mult),
                             reads=pbB + [Bus], writes=[Bbonv])
                        psG, pbG = pp.next()
                        for cp in range(4):
                            mm(psG[:, cp * 128:(cp + 1) * 128], g2[:, cp * 128:(cp + 1) * 128], lrin[:, 2, :], True, True, [Blr, Blrin], [pbG[cp]])
                        gT, BgT = r_bg.next()
                        c.op("act", lambda e, gT=gT, psG=psG: e.activation(out=gT[:], in_=psG[:].rearrange("p (a b) -> p a b", b=128), func=AF.Copy),
                             reads=pbG, writes=[BgT])
                    KT, BKT = r_op.next()
                    BT, BBT = r_op.next()
                    AT, BAT = r_op.next()
                    vb, Bvb = r_op.next()
                    c.ew(lambda e, KT=KT, km=km, igam=igam: e.tensor_tensor(out=KT[:], in0=km[:], in1=igam[:], op=ALU.mult),
                         reads=[Bkm, Bigam], writes=[BKT])
                    c.ew(lambda e, icl=icl, kkt=kkt: e.tensor_tensor(out=icl[:], in0=icl[:], in1=kkt[:], op=ALU.mult), reads=[Bicl, Bkk], writes=[Bicl])
                    c.ew(lambda e, BT=BT, icl=icl, igam=igam: e.tensor_tensor(out=BT[:], in0=icl[:], in1=igam[:], op=ALU.mult),
                         reads=[Bicl, Bigam], writes=[BBT])
                    c.op("dve", lambda e, AT=AT, kkt=kkt, gprev=gprev: e.scalar_tensor_tensor(out=AT[:], in0=kkt[:], scalar=-1.0, in1=gprev[:],
                                                                                             op0=ALU.mult, op1=ALU.mult),
                         reads=[Bkk, Bgprev], writes=[BAT])
                    c.ew(lambda e, vb=vb, us=us: e.tensor_copy(out=vb[:], in_=us[:, 8:12, :]), reads=[Bus], writes=[Bvb])
                    c.ew(lambda e, AT=AT: e.tensor_copy(out=ARe_t[0:64, :, 0:128], in_=AT[0:64, :, :]), reads=[BAT], writes=[BAR_t])
                    c.ew(lambda e, AT=AT: e.tensor_copy(out=ARo_t[64:128, :, 0:128], in_=AT[64:128, :, :]), reads=[BAT], writes=[BAR_t])
                    c.ew(lambda e, us=us, gam=gam: e.tensor_tensor(out=ARe_t[0:64, :, 128:256], in0=us[0:64, 0:4, :], in1=gam[0:64, :, :],
                                                                   op=ALU.mult), reads=[Bus, Bgam], writes=[BAR_t])
                    c.ew(lambda e, us=us, gam=gam: e.tensor_tensor(out=ARo_t[64:128, :, 128:256], in0=us[64:128, 0:4, :], in1=gam[64:128, :, :],
                                                                   op=ALU.mult), reads=[Bus, Bgam], writes=[BAR_t])
                    if SUB < 4:
                        continue
                    if fwd:
                        src_t = NT - 1 - ti
                        ybS, BybS = r_ybl.next()
                        ybR, BybR = r_ybl.next()
                        for (dst_, Bdst_, kh, kl) in ((ybS, BybS, "ybs_h", "ybs_l"), (ybR, BybR, "ybr_h", "ybr_l")):
                            for q, key in enumerate((kh, kl)):
                                c.dma("sp", lambda e, dst_=dst_, q=q, key=key, src_t=src_t: e.dma_start(
                                    out=dst_[:, q, :], in_=ysc[key][src_t * 128:(src_t + 1) * 128, :]),
                                      reads=[B_ysc[key][src_t]], writes=[Bdst_])
                    heads_all = [(cp, x) for cp in range(4) for x in range(2)]
                    for g0 in range(0, 8, LS):
                        grp = heads_all[g0:g0 + LS]
                        cps = sorted(set(cp for cp, _ in grp))
                        toks, wus, st = {}, {}, {}
                        for cp in cps:
                            ps, pb = pp.next()
                            for q, src in enumerate((AT, BT, KT, vb)):
                                mm(ps[:, q * 128:(q + 1) * 128], src[:, cp, :], Kb("ident"), True, True, [BAT, BBT, BKT, Bvb, Bcsb], [pb[q]])
                            tok, Btok = r_tok.next()
                            evac(tok[:], ps[:].rearrange("p (a b) -> p a b", b=128), pb, [Btok])
                            toks[cp] = (tok, Btok)
                            wus[cp] = r_wu.next()
                        quads = []
                        for q0 in range(0, len(grp), 4):
                            zfq, Bzfq = r_zf.next()
                            zbq, Bzbq = r_zb.next()
                            quads.append(dict(heads=grp[q0:q0 + 4], zf=zfq, Bzf=Bzfq, zb=zbq, Bzb=Bzbq))
                        for hi_, (cp, x) in enumerate(grp):
                            qd = quads[hi_ // 4]
                            hq = hi_ % 4
                            tok, Btok = toks[cp]
                            AR_x = (ARe_t, ARo_t)[x]
                            hs = slice(x * 64, (x + 1) * 64)
                            psA, pbA = pp.next()
                            psB4, pbB4 = pp.next()
                            mm(psA[:, 0:128], AR_x[:, cp, 0:128], BT[:, cp, :], True, True, [BAR_t, BBT], [pbA[0]])
                            mm(psB4[:, 0:256], BT[:, cp, :], AR_x[:, cp, :], True, True, [BAR_t, BBT], pbB4[0:2])
                            mm(psB4[:, 256:512], KT[:, cp, :], AR_x[:, cp, :], True, True, [BAR_t, BKT], pbB4[2:4])
                            p0, Bp0 = r_p0.next()
                            c.op("dve", lambda e, p0=p0, psA=psA: e.tensor_tensor(out=p0[:], in0=psA[:, 0:128], in1=Kb("msl"), op=ALU.mult),
                                 reads=[pbA[0], Bcsb], writes=[Bp0])
                            e4, Be4 = r_e4.next()
                            c.op("dve", lambda e, e4=e4, psB4=psB4: e.tensor_tensor(out=e4[:], in0=psB4[:], in1=mask4[:], op=ALU.mult),
                                 reads=pbB4 + [Bm4], writes=[Be4])
                            mm(psA[:, 128:192], e4[:, 256:384], tok[:, 3, hs], True, True, [Be4, Btok], [pbA[1]])
                            zf, Bzf = qd["zf"][:, hq, :], qd["Bzf"]
                            c.op("act", lambda e, zf=zf, tok=tok, hs=hs: e.activation(out=zf[:, 0:64], in_=tok[:, 0, hs], func=AF.Copy),
                                 reads=[Btok], writes=[Bzf])
                            c.op("act", lambda e, zf=zf, psA=psA: e.activation(out=zf[:, 64:128], in_=psA[:, 128:192], func=AF.Copy),
                                 reads=[pbA[1]], writes=[Bzf])
                            st[(cp, x)] = dict(e4=e4, Be4=Be4, zf=zf, Bzf=Bzf, Pn=p0[:], PnT=e4[:, 0:128], BPn=[Bp0, Be4])
                        for qd in quads:
                            c.ew(lambda e, qd=qd: e.tensor_copy(out=qd["zb"][:], in_=qd["zf"][:]), reads=[qd["Bzf"]], writes=[qd["Bzb"]])
                        for it in range(7):
                            for qd in quads:
                                zfq, Bzfq, zbq, Bzbq = qd["zf"], qd["Bzf"], qd["zb"], qd["Bzb"]
                                psZ, pbZ = pp.next()
                                for hq, hd in enumerate(qd["heads"]):
                                    d_ = st[hd]
                                    mm(psZ[:, hq * 128:(hq + 1) * 128], d_["PnT"], zbq[:, hq, :], True, True, d_["BPn"] + [Bzbq], pbZ)
                                if it < 6:
                                    psS = [pp.next(), pp.next()]
                                    for hq, hd in enumerate(qd["heads"]):
                                        d_ = st[hd]
                                        ps_, pb_ = psS[hq // 2]
                                        o0 = (hq % 2) * 256
                                        if it < 5:
                                            mm(ps_[:, o0:o0 + 128], d_["PnT"], d_["Pn"], True, True, d_["BPn"], pb_)
                                        mm(ps_[:, o0 + 128:o0 + 256], d_["Pn"], d_["PnT"], True, True, d_["BPn"], pb_)
                                    c.op("dve", lambda e, zfq=zfq, zbq=zbq, psZ=psZ: e.tensor_tensor(
                                        out=zbq[:], in0=psZ[:].rearrange("p (a b) -> p a b", b=128), in1=zfq[:], op=ALU.add),
                                         reads=[pbZ, Bzfq], writes=[Bzbq])
                                c.op("dve", lambda e, zfq=zfq, psZ=psZ: e.tensor_tensor(
                                    out=zfq[:], in0=psZ[:].rearrange("p (a b) -> p a b", b=128), in1=zfq[:], op=ALU.add),
                                     reads=[pbZ, Bzfq], writes=[Bzfq])
                                if it < 6:
                                    ppq, Bppq = r_pp.next()
                                    for half in range(2):
                                        ps_, pb_ = psS[half]
                                        c.op("act", lambda e, ppq=ppq, ps_=ps_, half=half: e.activation(
                                            out=ppq[:, 2 * half:2 * half + 2, :], in_=ps_[:].rearrange("p (a b) -> p a b", b=256), func=AF.Copy),
                                             reads=pb_, writes=[Bppq])
                                    for hq, hd in enumerate(qd["heads"]):
                                        d_ = st[hd]
                                        d_["Pn"], d_["PnT"], d_["BPn"] = ppq[:, hq, 0:128], ppq[:, hq, 128:256], [Bppq]
                        for (cp, x) in grp:
                            d_ = st[(cp, x)]
                            wu, Bwu = wus[cp]
                            hs = slice(x * 64, (x + 1) * 64)
                            zf, Bzf = d_["zf"], d_["Bzf"]
                            c.op("act", lambda e, wu=wu, zf=zf, hs=hs: e.activation(out=wu[:, 0, hs], in_=zf[:, 0:64], func=AF.Copy),
                                 reads=[Bzf], writes=[Bwu])
                            c.ew(lambda e, wu=wu, zf=zf, hs=hs: e.tensor_copy(out=wu[:, 1, hs], in_=zf[:, 64:128]),
                                 reads=[Bzf], writes=[Bwu])
                        for cp in cps:
                            if (cp, 0) not in st or (cp, 1) not in st:
                                continue
                            tok, Btok = toks[cp]
                            wu, Bwu = wus[cp]
                            qt, Bqt = r_qt.next()
                            psQ, pbQ = pp.next()
                            for x in range(2):
                                e4, Be4 = st[(cp, x)]["e4"], st[(cp, x)]["Be4"]
                                AR_x = (ARe_t, ARo_t)[x]
                                hs = slice(x * 64, (x + 1) * 64)
                                mm(psQ[:, x * 128:(x + 1) * 128], wu[:, 0, :], e4[:, 128:256], True, True, [Bwu, Be4], [pbQ[x]])
                            for x in range(2):
                                AR_x = (ARe_t, ARo_t)[x]
                                hs = slice(x * 64, (x + 1) * 64)
                                c.op("dve", lambda e, qt=qt, psQ=psQ, x=x, hs=hs, AR_x=AR_x, cp=cp: e.tensor_tensor(
                                    out=qt[hs, :], in0=psQ[hs, x * 128:(x + 1) * 128], in1=AR_x[hs, cp, 128:256], op=ALU.add),
                                     reads=[pbQ[x], BAR_t], writes=[Bqt])
                            yo = YR_t[:, cp * 128:(cp + 1) * 128]
                            mm(yo, qt[:], Hb[:, cp, :], True, False, [Bqt, BH[cp]], [YR_b])
                            for x in range(2):
                                e4, Be4 = st[(cp, x)]["e4"], st[(cp, x)]["Be4"]
                                hs = slice(x * 64, (x + 1) * 64)
                                yx = YR_t[:, cp * 128 + x * 64:cp * 128 + (x + 1) * 64]
                                mm(yx, e4[:, 128:256], wu[:, 1, hs], False, False, [Be4, Bwu], [YR_b])
                                mm(yx, e4[:, 384:512], tok[:, 3, hs], False, x == 1, [Be4, Btok], [YR_b])
                            psM, pbM = pp.next()
                            mm(psM[:, 0:128], wu[:, 0, :], tok[:, 1, :], True, True, [Bwu, Btok], [pbM[0]])
                            mtb, Bmtb = r_mtb.next()
                            c.op("dve", lambda e, mtb=mtb, psM=psM: e.tensor_tensor(out=mtb[:], in0=psM[:, 0:128], in1=Kb("bmask"), op=ALU.mult),
                                 reads=[pbM[0], Bcsb], writes=[Bmtb])
                            psH, pbH = pp.next()
                            hsl = psH[:, 0:128]
                            mm(hsl, tok[:, 1, :], wu[:, 1, :], True, False, [Btok, Bwu], [pbH[0]])
                            mm(hsl, tok[:, 2, :], tok[:, 3, :], False, False, [Btok], [pbH[0]])
                            mm(hsl, Kb("ident"), Hb[:, cp, :], False, False, [Bcsb, BH[cp]], [pbH[0]])
                            mm(hsl, mtb[:], Hb[:, cp, :], False, True, [Bmtb, BH[cp]], [pbH[0]])
                            c.op("dve", lambda e, cp=cp, psH=psH, gam=gam: e.scalar_tensor_tensor(
                                out=Hb[:, cp, :], in0=psH[:, 0:128], scalar=gam[:, cp, 127:128], in1=Kb("bmask"), op0=ALU.mult, op1=ALU.mult),
                                 reads=[pbH[0], Bgam, Bcsb], writes=[BH[cp]])
                    yrh, Byrh = r_yhl.next()
                    split2(YR_t[:], yrh[:, 0, :], yrh[:, 1, :], [YR_b], Byrh, Byrh, psum=True)
                    if not fwd:
                        for q, key in enumerate(("ybr_h", "ybr_l")):
                            c.dma("sp", lambda e, yrh=yrh, q=q, key=key, ti=ti: e.dma_start(out=ysc[key][ti * 128:(ti + 1) * 128, :], in_=yrh[:, q, :]),
                                  reads=[Byrh], writes=[B_ysc[key][ti]])
                        continue
                    if SUB < 5:
                        continue

                    mix, Bmix = r_mix.next()

                    def ytrans(yh, Byh, yb, Byb):
                        ps, pb = pp.next()
                        for j in range(4):
                            o_ = ps[:, j * 128:(j + 1) * 128]
                            cs_ = slice(j * 128, (j + 1) * 128)
                            mm(o_, yh[:, 0, cs_], Kb("ident"), True, False, [Byh, Bcsb], [pb[j]])
                            mm(o_, yh[:, 1, cs_], Kb("ident"), False, False, [Byh, Bcsb], [pb[j]])
                            mm(o_, yb[:, 0, cs_], Kb("J"), False, False, [Byb, Bcsb], [pb[j]])
                            mm(o_, yb[:, 1, cs_], Kb("J"), False, True, [Byb, Bcsb], [pb[j]])
                        return ps, pb
                    def ch_ssd():
                        ps, pb = ytrans(ysh, Bysh, ybS, BybS)
                        yield
                        yg, Byg = AS(0)
                        yield
                        for j in range(4):
                            c.op("dve", lambda e, j=j, yg=yg, xsf=xsf, ps=ps: e.scalar_tensor_tensor(
                                out=yg[:, j, :], in0=xsf[:, j, :], scalar=P("dch", j), in1=ps[:, j * 128:(j + 1) * 128], op0=ALU.mult, op1=ALU.add),
                                 reads=[Bxsf, Bpv, pb[j]], writes=[Byg])
                        c.ew(lambda e, yg=yg, zs=zs: e.tensor_tensor(out=yg[:], in0=yg[:], in1=zs[:], op=ALU.mult), reads=[Byg, Bzs], writes=[Byg])
                        yield
                        sq2, Bsq2 = AS(2)
                        yield
                        c.ew(lambda e, sq2=sq2, yg=yg: e.tensor_tensor(out=sq2[:], in0=yg[:], in1=yg[:], op=ALU.mult), reads=[Byg], writes=[Bsq2])
                        yield
                        hi, Bhi = r_sp.next()
                        yield
                        lo, Blo = r_sp.next()
                        yield
                        split2(sq2[:], hi[:], lo[:], [Bsq2], Bhi, Blo)
                        yield
                        ps, pb = pp.next()
                        yield
                        for g in range(2):
                            o_ = ps[:, g * 128:(g + 1) * 128]
                            for j in range(2):
                                mm(o_, Kb("o256"), hi[:, 2 * g + j, :], j == 0, False, [Bcsb, Bhi], [pb[g]])
                                mm(o_, Kb("o256"), lo[:, 2 * g + j, :], False, j == 1, [Bcsb, Blo], [pb[g]])
                        rs, Brs = r_st.next()
                        yield
                        rs2, Brs2 = r_st.next()
                        yield
                        rsqrt_eps(rs[:], ps[:, 0:128], RMS_EPS, [pb[0]], [Brs])
                        yield
                        rsqrt_eps(rs2[:], ps[:, 128:256], RMS_EPS, [pb[1]], [Brs2])
                        yield
                        for j in range(4):
                            rr_, Brr_ = (rs, Brs) if j < 2 else (rs2, Brs2)
                            c.op("dve", lambda e, j=j, mix=mix, yg=yg, rr_=rr_: e.scalar_tensor_tensor(
                                out=mix[:, j, :], in0=yg[:, j, :], scalar=P("nw", j), in1=rr_[:], op0=ALU.mult, op1=ALU.mult),
                                 reads=[Byg, Bpv, Brr_], writes=[Bmix])
                        yield
                    def ch_rwkv():
                        ps, pb = ytrans(yrh, Byrh, ybR, BybR)
                        yield
                        yf, Byf = AS(1)
                        sq2r, Bsq2r = AS(5)
                        yield
                        evac(yf[:], ps[:].rearrange("p (a b) -> p a b", b=128), pb, [Byf])
                        yield
                        psm, pbm = blocksum4(yf[:], Byf, "bo64m")
                        yield
                        c.op("dve", lambda e, yf=yf, psm=psm: e.tensor_tensor(out=yf[:], in0=yf[:], in1=psm, op=ALU.subtract), reads=[Byf] + pbm, writes=[Byf])
                        yield
                        c.ew(lambda e, sq2r=sq2r, yf=yf: e.tensor_tensor(out=sq2r[:], in0=yf[:], in1=yf[:], op=ALU.mult), reads=[Byf], writes=[Bsq2r])
                        yield
                        psv, pbv = blocksum4(sq2r[:], Bsq2r, "bo64m")
                        yield
                        rsqrt_eps(sq2r[:], psv, GN_EPS, pbv, [Bsq2r])
                        yield
                        c.ew(lambda e, yf=yf, sq2r=sq2r: e.tensor_tensor(out=yf[:], in0=yf[:], in1=sq2r[:], op=ALU.mult), reads=[Byf, Bsq2r], writes=[Byf])
                        yield
                        c.ew(lambda e, yf=yf: e.tensor_tensor(out=yf[:], in0=yf[:], in1=bc4("lw"), op=ALU.mult), reads=[Byf, Bpv], writes=[Byf])
                        yield
                        c.ew(lambda e, yf=yf: e.tensor_tensor(out=yf[:], in0=yf[:], in1=bc4("lb"), op=ALU.add), reads=[Byf, Bpv], writes=[Byf])
                        yield
                        c.ew(lambda e, yf=yf, bonv=bonv: e.tensor_tensor(out=yf[:], in0=yf[:], in1=bonv[:], op=ALU.add), reads=[Byf, Bbonv], writes=[Byf])
                        yield
                        c.ew(lambda e, mix=mix, yf=yf, gT=gT: e.tensor_tensor(out=mix[:, 4:8, :], in0=yf[:], in1=gT[:], op=ALU.mult),
                             reads=[Byf, BgT], writes=[Bmix])
                        yield
                    gens_ = [ch_ssd(), ch_rwkv()]
                    while gens_:
                        for g_ in list(gens_):
                            try:
                                next(g_)
                            except StopIteration:
                                gens_.remove(g_)
                    xf, Bxf = AS(9, 2)
                    c.dma("sp", lambda e, xf=xf, col0=col0: e.dma_start(
                        out=xf, in_=xT[:, col0 + 2:col0 + 130].rearrange("(k p) t -> p k t", p=128)), writes=[Bxf])
                    pre, Bpre = AS(3, 2)
                    for half in range(2):
                        ps, pb = pp.next()
                        for j in range(4):
                            dc = half * 4 + j
                            for k in range(8):
                                mm(ps[:, j * 128:(j + 1) * 128], wo[:, k, dc * 128:(dc + 1) * 128], mix[:, k, :], k == 0, k == 7,
                                   [Bwo, Bmix], [pb[j]])
                        c.op("dve", lambda e, half=half, pre=pre, xf=xf, ps=ps: e.scalar_tensor_tensor(
                            out=pre[:, half * 4:(half + 1) * 4, :], in0=xf[:, half * 4:(half + 1) * 4, :], scalar=ALPHA,
                            in1=ps[:].rearrange("p (a b) -> p a b", b=128), op0=ALU.mult, op1=ALU.add), reads=[Bxf] + pb, writes=[Bpre])
                    hout, Bhout = AS(7, 2)
                    sqL, BsqL = AS(5, 2)
                    layer_norm(pp, Kb, pre, Bpre, hout, Bhout, sqL, BsqL, r_st, r_sps, P, "l1w", "l1b", Bcsb, Bpv, 128)
                    t0 = ti * C
                    c.dma("sp", lambda e, hout=hout, t0=t0: e.dma_start(
                        out=hT[:, t0:t0 + 128].rearrange("(k p) t -> p k t", p=128), in_=hout), reads=[Bhout], writes=[B_hT[ti]])
                c.full_barrier()
                c.run_block()

        def ffn_phase():
            NF = 256 if TT % 256 == 0 else 128
            with ExitStack() as es:
                sb = lambda n, s, dt=F32: es.enter_context(nc.sbuf_tensor(n, s, dt))
                rot = lambda n, s, dt=F32, k=2: Rot(nc, es, n, s, dt, k)
                pp = PsumPool(nc, es, ["pf%d" % i for i in range(8)])
                wfi = sb("wfi", [128, 8, 2 * DFF], BF16); Bwfi = Buf()
                wfo = sb("wfo", [128, NFC, D], BF16); Bwfo = Buf()
                for k in range(8):
                    c.dma("pool", lambda e, k=k: e.dma_start(out=wfi[:, k, :], in_=w_fi[k * 128:(k + 1) * 128, :]), writes=[Bwfi])
                for k in range(NFC):
                    c.dma("pool", lambda e, k=k: e.dma_start(out=wfo[:, k, :], in_=w_fo[k * 128:(k + 1) * 128, :]), writes=[Bwfo])
                pv = sb("pv2", [128, NPV]); Bpv = Buf()
                c.dma("sp", lambda e: e.dma_start(out=pv[:], in_=pvec), writes=[Bpv])
                csb = sb("csb2", [128, NCST * 128], BF16); Bcsb = Buf()
                c.dma("pool", lambda e: e.dma_start(out=csb[:], in_=cst), writes=[Bcsb])
                Kb = lambda n: csb[:, CST[n] * 128:(CST[n] + 1) * 128]
                P = lambda n, j=0, w=1: pv[:, PV[n][0] + j:PV[n][0] + j + w]
                r_hf = rot("hf", [128, 8, NF], F32, 1)
                r_hb = rot("hb", [128, 8, NF], BF16, 1)
                r_act = rot("actt", [128, NFC, NF], BF16, 1)
                r_sl = rot("sl", [128, NF], F32, 2)
                r_sq = rot("sq2", [128, 8, NF], F32, 1)
                r_st = rot("st2", [128, NF], F32, 4)
                r_sps = rot("sps2", [128, NF], BF16, 2)
                for fi in range(min(TT // NF, NTL)):
                    t0 = fi * NF
                    deps = [B_hT[t] for t in range(t0 // C, (t0 + NF) // C)]
                    hf, Bhf = r_hf.next()
                    hb, Bhb = r_hb.next()
                    c.dma("sp", lambda e, hf=hf, t0=t0: e.dma_start(out=hf[:], in_=hT[:, t0:t0 + NF].rearrange("(k p) t -> p k t", p=128)),
                          reads=deps, writes=[Bhf])
                    c.op("act", lambda e, hf=hf, hb=hb: e.activation(out=hb[:], in_=hf[:], func=AF.Copy), reads=[Bhf], writes=[Bhb])
                    actt, Bact = r_act.next()
                    for fc in range(NFC):
                        ps, pb = pp.next()
                        for k in range(8):
                            mm(ps[:, 0:NF], wfi[:, k, fc * 128:(fc + 1) * 128], hb[:, k, :], k == 0, k == 7, [Bwfi, Bhb], pb[0:2])
                        for k in range(8):
                            mm(ps[:, 256:256 + NF], wfi[:, k, DFF + fc * 128:DFF + (fc + 1) * 128], hb[:, k, :], k == 0, k == 7,
                               [Bwfi, Bhb], pb[2:4])
                        sl, Bsl = r_sl.next()
                        c.op("act", lambda e, sl=sl, ps=ps: e.activation(out=sl[:], in_=ps[:, 0:NF], func=AF.Silu), reads=pb[0:2], writes=[Bsl])
                        c.op("dve", lambda e, fc=fc, sl=sl, ps=ps, actt=actt: e.tensor_tensor(
                            out=actt[:, fc, :], in0=ps[:, 256:256 + NF], in1=sl[:], op=ALU.mult), reads=pb[2:4] + [Bsl], writes=[Bact])
                    for half in range(4):
                        ps, pb = pp.next()
                        for j in range(2):
                            dc = half * 2 + j
                            for k in range(NFC):
                                mm(ps[:, j * 256:j * 256 + NF], wfo[:, k, dc * 128:(dc + 1) * 128], actt[:, k, :], k == 0, k == NFC - 1,
                                   [Bwfo, Bact], pb[2 * j:2 * j + 2])
                        for j in range(2):
                            dc = half * 2 + j
                            c.op("dve", lambda e, dc=dc, j=j, hf=hf, ps=ps: e.scalar_tensor_tensor(
                                out=hf[:, dc, :], in0=hf[:, dc, :], scalar=ALPHA, in1=ps[:, j * 256:j * 256 + NF], op0=ALU.mult, op1=ALU.add),
                                 reads=[Bhf] + pb[2 * j:2 * j + 2], writes=[Bhf])
                    sq, Bsq = r_sq.next()
                    out, Bout = sq, Bsq
                    layer_norm(pp, Kb, hf[:], Bhf, out[:], Bout, sq[:], Bsq, r_st, r_sps, P, "l2w", "l2b", Bcsb, Bpv, NF)
                    Bo = Buf()
                    B_out.append(Bo)
                    c.dma("sp", lambda e, out=out, t0=t0: e.dma_start(out=yT[:, t0:t0 + NF].rearrange("(k p) t -> p k t", p=128), in_=out[:]),
                          reads=[Bout], writes=[Bo])
                c.wait_all("sp", B_out)
                c.run_block()

        mixer_phase(False)
        if stage >= 2:
            mixer_phase(True)
        if stage >= 3:
            ffn_phase()
    return nc


def _consts():
    i = np.arange(128)
    p, f = i[:, None], i[None, :]
    blocks = {
        "ident": (p == f), "J": (p + f == 127), "msl": (f < p), "msu": (p < f), "mui": (p <= f),
        "bmask": ((p // 64) == (f // 64)), "bo64": ((p // 64) == (f // 64)),
    }
    out = np.zeros((128, NCST * 128), np.float32)
    for n, m in blocks.items():
        out[:, CST[n] * 128:(CST[n] + 1) * 128] = m.astype(np.float32)
    out[:, CST["bo64m"] * 128:(CST["bo64m"] + 1) * 128] = blocks["bo64"].astype(np.float32) / 64.0
    out[:, CST["o1024"] * 128:(CST["o1024"] + 1) * 128] = 1.0 / 1024.0
    out[:, CST["o256"] * 128:(CST["o256"] + 1) * 128] = 1.0 / 256.0
    out[:, CST["ones"] * 128:(CST["ones"] + 1) * 128] = 1.0
    m8 = np.zeros((8, 1024), np.float32)
    for h in range(8):
        m8[h, h * 128:(h + 1) * 128] = 1.0
    return out, m8


def _chunkvec(v, n):
    return np.ascontiguousarray(np.asarray(v, np.float32).reshape(n, 128).T)


def prep_params(inp):
    g = lambda n: np.asarray(inp[n], np.float32)[0]
    w_in = g("w_in")
    M_W, CONV = 512, 1024
    o_z, o_xbc, o_dt = 0, 512, 1536
    o_r = 1544
    W = np.zeros((D, NCH_IN * 128), np.float32)
    W[:, 0:512] = w_in[:, o_z:o_z + 512]
    W[:, 512:1536] = w_in[:, o_xbc:o_xbc + 1024]
    W[:, 1536:3072] = w_in[:, o_r:o_r + 1536]
    W[:, CWL * 128:CWL * 128 + 64] = w_in[:, o_r + 1536:o_r + 1600]
    W[:, CAL * 128:CAL * 128 + 64] = w_in[:, o_r + 1600:o_r + 1664]
    W[:, CGL * 128:CGL * 128 + 128] = w_in[:, o_r + 1664:o_r + 1792]
    W[:, CDT * 128:CDT * 128 + 8] = w_in[:, o_dt:o_dt + 8]
    pv = np.zeros((128, NPV), np.float32)
    def put(name, arr):
        o, w = PV[name]
        pv[:, o:o + w] = arr
    cw = g("conv_w")
    cwp = np.zeros((128, 8, 5), np.float32)
    for j in range(8):
        cwp[:, j, :] = cw[:, j * 128:(j + 1) * 128].T
    put("cw", cwp.reshape(128, 40))
    put("cb", _chunkvec(g("conv_b"), 8))
    put("dch", _chunkvec(np.repeat(g("m_d"), 64), 4))
    put("nw", _chunkvec(g("m_norm_w"), 4))
    def mu15(v):
        o = np.zeros((128, 15), np.float32)
        o[:, 0:12] = _chunkvec(v[0:1536], 12)
        o[0:64, 12] = v[1536:1600]
        o[0:64, 13] = v[1600:1664]
        o[:, 14] = v[1664:1792]
        return o
    put("mp", mu15(g("r_mu_prev")))
    put("mn", mu15(g("r_mu_next")))
    put("w0f", _chunkvec(g("r_w0_f"), 4))
    put("w0b", _chunkvec(g("r_w0_b"), 4))
    put("a0", _chunkvec(g("r_a0"), 4))
    put("kk", _chunkvec(g("r_k_k"), 4))
    put("ka", _chunkvec(g("r_k_a"), 4))
    put("rk", _chunkvec(g("r_r_k").reshape(-1), 4))
    put("lw", _chunkvec(g("r_lnx_w"), 4))
    put("lb", _chunkvec(g("r_lnx_b"), 4))
    put("l1w", _chunkvec(g("ln1_w"), 8))
    put("l1b", _chunkvec(g("ln1_b"), 8))
    put("l2w", _chunkvec(g("ln2_w"), 8))
    put("l2b", _chunkvec(g("ln2_b"), 8))
    p8 = np.zeros((8, 8), np.float32)
    p8[:, 0] = g("m_dt_bias_f"); p8[:, 1] = g("m_dt_bias_b")
    p8[:, 2] = g("m_a_log_f"); p8[:, 3] = g("m_a_log_b")
    lowr = np.zeros((128, 4 * 512), np.float32)
    lowr[0:64, 0:512] = g("r_w2_f")
    lowr[0:64, 512:1024] = g("r_w2_b")
    lowr[0:64, 1024:1536] = g("r_a2")
    lowr[:, 1536:2048] = g("r_g2")
    cst, m8 = _consts()
    return dict(w_in=W, w_out=np.ascontiguousarray(g("w_out")), w_fi=np.ascontiguousarray(g("w_ffn_in")),
                w_fo=np.ascontiguousarray(g("w_ffn_out")), p8=p8, lowr=lowr, cst=cst, m8=m8), pv


def layout_core(units, link, TU):
    xT = np.zeros((D, 2 * (TU + 4)), np.float32)
    xTr = np.zeros((D, 2 * (TU + 4)), np.float32)
    for u, (seq, s) in enumerate(units):
        if seq is None:
            continue
        T = seq.shape[0]
        lo, hi = s - 2, s + TU + 2
        blk = np.zeros((TU + 4, D), np.float32)
        a, b = max(lo, 0), min(hi, T)
        blk[a - lo:b - lo] = seq[a:b]
        xT[:, u * (TU + 4):(u + 1) * (TU + 4)] = blk.T
        ur = 1 - u
        xTr[:, ur * (TU + 4):(ur + 1) * (TU + 4)] = blk[::-1].T
    return xT, xTr


_NC_CACHE = {}


def run_cores(assign, params, pv, TU, n_cores):
    if TU not in _NC_CACHE:
        import os
        _NC_CACHE[TU] = build(TU, stage=int(os.environ.get("MK_STAGE", "3")))
    nc = _NC_CACHE[TU]
    in_maps = []
    for units, link in assign:
        xT, xTr = layout_core(units, link, TU)
        pvc = pv.copy()
        pvc[:, PV["link"][0]] = float(link)
        m = dict(params)
        m.update(xT=xT, xTr=xTr, pvec=pvc)
        in_maps.append(m)
    res = run_bass_kernel_spmd(nc, in_maps, core_ids=list(range(n_cores)))
    return [r["yT"] for r in res.results]


def kernel(**inputs):
    xp = np.asarray(inputs["x_prompt"], np.float32)
    xs = np.asarray(inputs["x_sample"], np.float32)
    params, pv = prep_params(inputs)
    TU = xs.shape[1]
    assert xp.shape[1] == 2 * TU
    assign = []
    for b in range(xp.shape[0]):
        assign.append(([(xp[b], 0), (xp[b], TU)], 1))
    nb = xs.shape[0]
    slots = [[(xs[i], 0)] for i in range(nb)]
    n_rest = 8 - len(assign)
    per = [[] for _ in range(n_rest)]
    for i in range(nb):
        per[i % n_rest].append(i)
    for lst in per:
        u = [(xs[i], 0) for i in lst]
        while len(u) < 2:
            u.append((None, 0))
        assign.append((u, 0))
    outs = run_cores(assign, params, pv, TU, 8)
    y_p = np.zeros_like(xp)
    y_s = np.zeros_like(xs)
    for b in range(xp.shape[0]):
        y_p[b] = outs[b].T
    for ci, lst in enumerate(per):
        o = outs[xp.shape[0] + ci]
        for u, i in enumerate(lst):
            y_s[i] = o[:, u * TU:(u + 1) * TU].T
    return (y_p, y_s)
```

```python
import numpy as np
from contextlib import ExitStack
import concourse.bass as bass
import concourse.mybir as mybir
from concourse.bass_utils import run_bass_kernel_spmd

F32 = mybir.dt.float32
BF16 = mybir.dt.bfloat16
AF = mybir.ActivationFunctionType
ALU = mybir.AluOpType

D = 1024
C = 128
NCH_IN = 28
DFF = 2816
NFC = DFF // 128
ALPHA = 2.0 ** 0.25
LN_EPS = 1e-5
RMS_EPS = 1e-5
GN_EPS = 64e-5
CDEC = float(np.exp(-0.5))

CZ, CXS, CB, CC_, CR, CK, CV, CWL, CAL, CGL, CDT = 0, 4, 8, 10, 12, 16, 20, 24, 25, 26, 27

PV = {}
_o = 0
for _n, _w in [("cw", 40), ("cb", 8), ("dch", 4), ("nw", 4), ("mp", 15), ("mn", 15), ("w0f", 4), ("w0b", 4),
               ("a0", 4), ("kk", 4), ("ka", 4), ("rk", 4), ("lw", 4), ("lb", 4), ("l1w", 8), ("l1b", 8),
               ("l2w", 8), ("l2b", 8), ("link", 1)]:
    PV[_n] = (_o, _w)
    _o += _w
NPV = _o
CST = {n: i for i, n in enumerate(["ident", "J", "msl", "msu", "mui", "bmask", "bo64", "bo64m", "o1024", "o256", "ones"])}
NCST = len(CST)


class Buf:
    __slots__ = ("name", "w", "r", "psum")

    def __init__(self, name="", psum=False):
        self.name = name
        self.w = None
        self.r = []
        self.psum = psum


class Ctx:
    ENGS = ("pe", "act", "dve", "pool", "sp")

    def __init__(self, nc, es, n_dma_slots=6):
        self.nc = nc
        self.ops = {e: [] for e in self.ENGS}
        self.sems = {}
        self.count = {}
        for e in self.ENGS:
            self.sems[e] = es.enter_context(nc.semaphore("s_" + e))
            self.count[e] = 0
        self.waited = {e: {} for e in self.ENGS}
        self.slots = {}
        for q in ("sp", "pool"):
            self.slots[q] = []
            for i in range(n_dma_slots):
                key = "d_%s_%d" % (q, i)
                self.sems[key] = es.enter_context(nc.semaphore(key))
                self.count[key] = 0
                self.slots[q].append(key)
        self.slot_rr = {"sp": 0, "pool": 0}
        self.rr = 0
        import os
        self.ew_engs = tuple(os.environ.get("MK_EW", "dve").split(","))
        self.noself = tuple(x for x in os.environ.get("MK_NOSELF", "").split(",") if x)
        self.selfgap = int(os.environ.get("MK_SELFGAP", "0"))
        self.PEX = Buf("PEX")
        self.pex_on = os.environ.get("MK_PEX", "0") == "1"
        self.cut = int(os.environ.get("MK_CUT", "0"))
        self.nops = 0
        self.last_lines = None

    def _cut(self):
        self.nops += 1
        if self.cut and self.nops >= self.cut:
            if self.nops == self.cut:
                import sys
                f = sys._getframe(2)
                lines = []
                while f is not None and len(lines) < 5:
                    lines.append(f.f_lineno)
                    f = f.f_back
                print("MK_CUT: first dropped op #%d at lines %s" % (self.nops, lines), flush=True)
            return True
        return False

    def _deps(self, eng, reads, writes):
        need = {}
        for b in reads:
            ev = b.w
            if ev is not None and need.get(ev[0], 0) < ev[1]:
                need[ev[0]] = ev[1]
        for b in writes:
            ev = b.w
            if ev is not None and need.get(ev[0], 0) < ev[1]:
                need[ev[0]] = ev[1]
            for ev in b.r:
                if need.get(ev[0], 0) < ev[1]:
                    need[ev[0]] = ev[1]
        waits = []
        wd = self.waited[eng]
        for k, v in need.items():
            if k == eng and (eng == "pe" or eng in self.noself):
                continue
            if k == eng and self.selfgap and v <= self.count[eng] - self.selfgap:
                continue
            if wd.get(k, 0) >= v:
                continue
            wd[k] = v
            waits.append((k, v))
        return waits

    def _mark(self, ev, reads, writes):
        for b in reads:
            b.r.append(ev)
            if len(b.r) > 24:
                m = {}
                for k, v in b.r:
                    if m.get(k, 0) < v:
                        m[k] = v
                b.r = list(m.items())
        for b in writes:
            b.w = ev
            b.r = []

    @staticmethod
    def _flat(bs):
        out = []
        for b in bs:
            if isinstance(b, (list, tuple)):
                out.extend(Ctx._flat(b))
            else:
                out.append(b)
        return out

    def op(self, eng, fn, reads=(), writes=()):
        if self._cut():
            return
        reads, writes = self._flat(reads), self._flat(writes)
        pr = [b for b in reads if b.psum]
        if pr:
            reads = [b for b in reads if not b.psum]
            writes = list(writes) + pr
        if self.pex_on:
            if eng == "pe":
                reads = list(reads) + [self.PEX]
            elif eng == "dve" and any(b.psum for b in writes):
                writes = list(writes) + [self.PEX]
        waits = self._deps(eng, reads, writes)
        self.count[eng] += 1
        ev = (eng, self.count[eng])
        self.ops[eng].append((waits, fn, eng, 1))
        self._mark(ev, reads, writes)

    def dma(self, q, fn, reads=(), writes=()):
        if self._cut():
            return
        slot = self.slots[q][self.slot_rr[q] % len(self.slots[q])]
        self.slot_rr[q] += 1
        reads, writes = self._flat(reads), self._flat(writes)
        waits = self._deps(q, reads, writes)
        prev = self.count[slot]
        if prev > 0 and self.waited[q].get(slot, 0) < prev:
            self.waited[q][slot] = prev
            waits.append((slot, prev))
        self.count[slot] += 16
        ev = (slot, self.count[slot])
        self.ops[q].append((waits, fn, slot, 16))
        self._mark(ev, reads, writes)

    def wait_all(self, eng, bufs):
        if self.cut and self.nops >= self.cut:
            return
        waits = self._deps(eng, self._flat(bufs), ())
        self.ops[eng].append((waits, None, None, 0))

    def ew(self, fn, reads=(), writes=(), engs=None):
        if engs is None:
            engs = self.ew_engs
        e = engs[self.rr % len(engs)]
        self.rr += 1
        self.op(e, fn, reads, writes)

    def full_barrier(self):
        for eng in self.ENGS:
            waits = []
            for k, v in self.count.items():
                if v > 0 and self.waited[eng].get(k, 0) < v and not (k == eng):
                    self.waited[eng][k] = v
                    waits.append((k, v))
            self.ops[eng].append((waits, None, None, 0))

    def replay(self, engname, e):
        sems = self.sems
        for waits, fn, semkey, inc in self.ops[engname]:
            for k, v in waits:
                e.wait_ge(sems[k], v)
            if fn is not None:
                fn(e).then_inc(sems[semkey], inc)

    def run_block(self):
        nc = self.nc
        with nc.Block() as block:
            @block.tensor
            def _(e):
                self.replay("pe", e)

            @block.scalar
            def _(e):
                self.replay("act", e)

            @block.vector
            def _(e):
                self.replay("dve", e)

            @block.gpsimd
            def _(e):
                self.replay("pool", e)

            @block.sync
            def _(e):
                self.replay("sp", e)
        self.ops = {e: [] for e in self.ENGS}


class Rot:
    def __init__(self, nc, es, name, shape, dt, n=2):
        self.t = [es.enter_context(nc.sbuf_tensor("%s_%d" % (name, i), shape, dt)) for i in range(n)]
        self.b = [Buf(name) for _ in range(n)]
        self.i = 0

    def next(self):
        k = self.i % len(self.t)
        self.i += 1
        return self.t[k], self.b[k]


class PsumPool:
    def __init__(self, nc, es, names):
        self.t = [es.enter_context(nc.psum_tensor(n, [128, 512], F32)) for n in names]
        self.b = [[Buf(n, psum=True)] * 4 for n in names]
        self.i = 0

    def next(self):
        k = self.i % len(self.t)
        self.i += 1
        return self.t[k], self.b[k]


def build(TU, stage=3):
    import os
    SUB = float(os.environ.get('MK_SUB', '9'))
    NTL = int(os.environ.get('MK_NT', '999'))
    NTU = TU // C
    NT = 2 * NTU
    TT = 2 * TU
    XW = 2 * (TU + 4)
    nc = bass.Bass("TRN2", target_bir_lowering=False)
    dram = lambda n, s, dt=F32, kind="ExternalInput": nc.dram_tensor(n, s, dt, kind=kind).ap()
    xT = dram("xT", [D, XW])
    xTr = dram("xTr", [D, XW])
    w_in = dram("w_in", [D, NCH_IN * 128])
    w_out = dram("w_out", [D, D])
    w_fi = dram("w_fi", [D, 2 * DFF])
    w_fo = dram("w_fo", [DFF, D])
    pvec = dram("pvec", [128, NPV])
    p8 = dram("p8", [8, 8])
    lowr = dram("lowr", [128, 4 * 512])
    cst = dram("cst", [128, NCST * 128])
    m8 = dram("m8", [8, 1024])
    yT = dram("yT", [D, TT], kind="ExternalOutput")
    ysc = {k: dram(k, [NT * 128, 512], BF16, kind="Internal") for k in ("ybr_h", "ybr_l", "ybs_h", "ybs_l")}
    hT = dram("hT", [D, TT], kind="Internal")

    with ExitStack() as es0:
        c = Ctx(nc, es0)
        B_ysc = {k: [Buf() for _ in range(NT)] for k in ysc}
        B_hT = [Buf() for _ in range(NT)]
        B_out = []

        def mk_mm(c):
            def mm(out, lhsT, rhs, start, stop, reads, writes):
                c.op("pe", lambda e: e.matmul(out, lhsT=lhsT, rhs=rhs, start=start, stop=stop), reads=reads, writes=writes)
            return mm
        mm = mk_mm(c)

        def split2(src, hi, lo, reads, Bhi, Blo, psum=False):
            c.op("act", lambda e: e.activation(out=hi, in_=src, func=AF.Copy), reads=reads, writes=[Bhi])
            if psum:
                c.op("dve", lambda e: e.tensor_tensor(out=lo, in0=src, in1=hi, op=ALU.subtract), reads=list(reads) + [Bhi], writes=[Blo])
            else:
                c.ew(lambda e: e.tensor_tensor(out=lo, in0=src, in1=hi, op=ALU.subtract), reads=list(reads) + [Bhi], writes=[Blo])

        def rsqrt_eps(out, in_, eps, reads, writes):
            c.op("dve", lambda e: e.tensor_scalar(out=out, in0=in_, scalar1=float(eps), scalar2=None, op0=ALU.add), reads=reads, writes=writes)
            c.op("act", lambda e: e.activation(out=out, in_=out, func=AF.Sqrt), reads=writes, writes=writes)
            c.op("dve", lambda e: e.reciprocal(out=out, in_=out), reads=writes, writes=writes)

        def layer_norm(pp, Kb, pre, Bpre, out, Bout, sq, Bsq, r_st, r_sp, P, wn, bn, Bcsb, Bpv, N):
            def stat(src, Bsrc):
                red, Bred = r_st.next()
                c.op("dve", lambda e: e.tensor_reduce(out=red[:, 0:N], in_=src.rearrange("p a n -> p n a"), axis=mybir.AxisListType.X, op=ALU.add),
                     reads=[Bsrc], writes=[Bred])
                hi, Bhi = r_sp.next()
                lo, Blo = r_sp.next()
                split2(red[:, 0:N], hi[:, 0:N], lo[:, 0:N], [Bred], Bhi, Blo)
                ps, pb = pp.next()
                mm(ps[:, 0:N], Kb("o1024"), hi[:, 0:N], True, False, [Bcsb, Bhi], pb)
                mm(ps[:, 0:N], Kb("o1024"), lo[:, 0:N], False, True, [Bcsb, Blo], pb)
                return ps, pb
            ps, pb = stat(pre, Bpre)
            mu, Bmu = r_st.next()
            c.op("act", lambda e: e.activation(out=mu[:, 0:N], in_=ps[:, 0:N], func=AF.Copy), reads=pb, writes=[Bmu])
            bcN = lambda t: t[:, 0:N].unsqueeze(1).to_broadcast([128, 8, N])
            bcP = lambda n: P(n, 0, 8).unsqueeze(2).to_broadcast([128, 8, N])
            c.ew(lambda e: e.tensor_tensor(out=pre, in0=pre, in1=bcN(mu), op=ALU.subtract), reads=[Bpre, Bmu], writes=[Bpre])
            c.ew(lambda e: e.tensor_tensor(out=sq, in0=pre, in1=pre, op=ALU.mult), reads=[Bpre], writes=[Bsq])
            ps2, pb2 = stat(sq, Bsq)
            rs, Brs = r_st.next()
            rsqrt_eps(rs[:, 0:N], ps2[:, 0:N], LN_EPS, pb2, [Brs])
            c.ew(lambda e: e.tensor_tensor(out=pre, in0=pre, in1=bcN(rs), op=ALU.mult), reads=[Bpre, Brs], writes=[Bpre])
            c.ew(lambda e: e.tensor_tensor(out=pre, in0=pre, in1=bcP(wn), op=ALU.mult), reads=[Bpre, Bpv], writes=[Bpre])
            c.ew(lambda e: e.tensor_tensor(out=out, in0=pre, in1=bcP(bn), op=ALU.add), reads=[Bpre, Bpv], writes=[Bout])

        def mixer_phase(fwd):
            with ExitStack() as es:
                sfx = "F" if fwd else "B"
                sb = lambda n, s, dt=F32: es.enter_context(nc.sbuf_tensor(n + sfx, s, dt))
                rot = lambda n, s, dt=F32, k=2: Rot(nc, es, n + sfx, s, dt, k)
                pp = PsumPool(nc, es, ["pp%d%s" % (i, sfx) for i in range(7)])
                YR_t = es.enter_context(nc.psum_tensor("YR" + sfx, [128, 512], F32)); YR_b = Buf("YR", psum=True)
                xsrc = xT if fwd else xTr

                class Fix:
                    def __init__(self, t, b):
                        self.t, self.b = t, b

                    def next(self):
                        return self.t, self.b
                one = lambda n, s, dt=F32: Fix(sb(n, s, dt), Buf(n))

                win = sb("win", [128, 8, NCH_IN * 128], BF16); Bwin = Buf()
                for k in range(8):
                    c.dma("pool", lambda e, k=k: e.dma_start(out=win[:, k, :], in_=w_in[k * 128:(k + 1) * 128, :]), writes=[Bwin])
                if fwd:
                    wo = sb("wo", [128, 8, D], BF16); Bwo = Buf()
                    for k in range(8):
                        c.dma("pool", lambda e, k=k: e.dma_start(out=wo[:, k, :], in_=w_out[k * 128:(k + 1) * 128, :]), writes=[Bwo])
                pv = sb("pv", [128, NPV]); Bpv = Buf()
                c.dma("sp", lambda e: e.dma_start(out=pv[:], in_=pvec), writes=[Bpv])
                p8t = sb("p8t", [8, 8]); Bp8 = Buf()
                c.dma("sp", lambda e: e.dma_start(out=p8t[:], in_=p8), writes=[Bp8])
                lrb = sb("lrb", [128, 3 * 512], BF16); Blr = Buf()
                w2o = 0 if fwd else 512
                c.dma("pool", lambda e: e.dma_start(out=lrb[:, 0:512], in_=lowr[:, w2o:w2o + 512]), writes=[Blr])
                c.dma("pool", lambda e: e.dma_start(out=lrb[:, 512:1536], in_=lowr[:, 1024:2048]), writes=[Blr])
                csb = sb("csb", [128, NCST * 128], BF16); Bcsb = Buf()
                c.dma("pool", lambda e: e.dma_start(out=csb[:], in_=cst), writes=[Bcsb])
                m8b = sb("m8b", [8, 1024], BF16); Bm8 = Buf()
                c.dma("pool", lambda e: e.dma_start(out=m8b[:], in_=m8), writes=[Bm8])
                Kb = lambda n: csb[:, CST[n] * 128:(CST[n] + 1) * 128]
                mask4 = sb("mask4", [128, 512], BF16); Bm4 = Buf()
                for q, n in enumerate(["msu", "mui", "msu", "mui"]):
                    c.op("dve", lambda e, q=q, n=n: e.tensor_copy(out=mask4[:, q * 128:(q + 1) * 128], in_=Kb(n)),
                         reads=[Bcsb], writes=[Bm4])
                P = lambda n, j=0, w=1: pv[:, PV[n][0] + j:PV[n][0] + j + w]
                der = sb("der", [128, 15 + 4]); Bder = Buf()
                c.op("dve", lambda e: e.tensor_tensor(out=der[:, 0:15], in0=P("mp", 0, 15), in1=P("mn", 0, 15), op=ALU.add),
                     reads=[Bpv], writes=[Bder])
                c.op("dve", lambda e: e.tensor_scalar(out=der[:, 0:15], in0=der[:, 0:15], scalar1=-1.0, scalar2=1.0,
                                                      op0=ALU.mult, op1=ALU.add), reads=[Bder], writes=[Bder])
                c.op("dve", lambda e: e.tensor_scalar(out=der[:, 15:19], in0=P("ka", 0, 4), scalar1=-1.0, scalar2=1.0,
                                                      op0=ALU.mult, op1=ALU.add), reads=[Bpv], writes=[Bder])
                a8 = sb("a8", [8, 1]); Ba8 = Buf()
                acol = 2 if fwd else 3
                c.op("act", lambda e: e.activation(out=a8[:], in_=p8t[:, acol:acol + 1], func=AF.Exp), reads=[Bp8], writes=[Ba8])
                c.op("dve", lambda e: e.tensor_scalar(out=a8[:], in0=a8[:], scalar1=-1.0, scalar2=None, op0=ALU.mult),
                     reads=[Ba8], writes=[Ba8])
                dtb = p8t[:, (0 if fwd else 1):(1 if fwd else 2)]
                w2 = lrb[:, 0:512]
                a2 = lrb[:, 512:1024]
                g2 = lrb[:, 1024:1536]
                w0n = "w0f" if fwd else "w0b"
                mpn, mnn = ("mp", "mn") if fwd else ("mn", "mp")

                Hb = sb("Hb", [128, 4, 128], BF16); BH = [Buf() for _ in range(4)]
                STf = sb("STf", [128, 512]); STb = sb("STb", [128, 512], BF16); BST = Buf()
                c.op("pool", lambda e: e.memset(Hb[:], 0.0), writes=BH)
                c.op("pool", lambda e: e.memset(STf[:], 0.0), writes=[BST])
                c.op("pool", lambda e: e.memset(STb[:], 0.0), writes=[BST])
                ARe_t = sb("ARe", [128, 4, 256], BF16)
                ARo_t = sb("ARo", [128, 4, 256], BF16)
                BAR_t = Buf()
                c.op("pool", lambda e: e.memset(ARe_t[:], 0.0), writes=[BAR_t])
                c.op("pool", lambda e: e.memset(ARo_t[:], 0.0), writes=[BAR_t])
                CSt = sb("CS", [128, 4, 129]); BCSt = Buf()
                c.op("pool", lambda e: e.memset(CSt[:], 0.0), writes=[BCSt])
                ptmp = sb("ptmp", [128, 128]); Bptmp = Buf()
                Bfence = Buf()
                ones_t = sb("ones_t", [128, 128]); Bones = Buf()
                c.op("pool", lambda e: e.memset(ones_t[:], 1.0), writes=[Bones])
                DT3 = sb("DT3", [128, 3, 2, 128], BF16); BDT3 = Buf()
                RB3 = sb("RB3", [128, 2, 8, 128], BF16); BRB3 = [Buf(), Buf()]
                c.op("pool", lambda e: e.memset(DT3[:], 0.0), writes=[BDT3])
                c.op("pool", lambda e: e.memset(RB3[:], 0.0), writes=BRB3)

                arena = sb("arena", [128, 11 * 512]); Bar = [Buf() for _ in range(11)]

                def AS(i, n=1, parts=128, nch=None):
                    nch = 4 * n if nch is None else nch
                    return (arena[0:parts, 512 * i:512 * i + 128 * nch].rearrange("p (a b) -> p a b", b=128), Bar[i:i + n])

                def AU(i, nchunks):
                    return (arena[:, 512 * i:512 * i + 132 * nchunks].rearrange("p (a b) -> p a b", b=132), Bar[i:i + 4])

                r_xb = rot("xb", [128, 8, 132], BF16, 1 if fwd else 2)
                if fwd:
                    U1t, BU1 = None, None
                else:
                    U1t = sb("U1", [128, 13, 132]); BU1 = Buf("U1")
                r_zs = one("zs", [128, 4, 128])
                r_xsf = one("xsf", [128, 4, 128])
                r_xbc = one("xbcb", [128, 8, 128], BF16)
                r_dtok = one("dtok", [128, 16])
                r_gm = one("gm", [128, 2, 128])
                r_MT = one("MT", [128, 8, 128], BF16)
                r_Cd = one("Cd", [128, 8, 128], BF16)
                r_xdt = one("xdt", [128, 2, 512], BF16)
                r_btok = one("btok", [128, 2, 128], BF16)
                r_lrin = one("lrin", [128, 3, 128], BF16)
                r_sp = rot("sp", [128, 4, 128], BF16, 2)
                r_op = rot("opb", [128, 4, 128], BF16, 4)
                LS = int(os.environ.get("MK_LS_F" if fwd else "MK_LS_B", "4" if fwd else "8"))
                NPR = max(1, LS // 2)
                r_tok = rot("tok", [128, 4, 128], BF16, max(2, NPR))
                r_p0 = rot("p0", [128, 128], BF16, LS)
                r_e4 = rot("e4", [128, 512], BF16, LS)
                NQ = LS // 4
                r_pp = rot("ppq", [128, 4, 256], BF16, 2 * NQ)
                r_zf = rot("zf", [128, 4, 128], F32, NQ)
                r_zb = rot("zb", [128, 4, 128], BF16, NQ)
                r_wu = rot("wu", [128, 2, 128], BF16, max(2, NPR))
                r_qt = rot("qt", [128, 128], BF16, 1)
                r_mtb = rot("mtb", [128, 128], BF16, 1)
                r_yhl = rot("yhl", [128, 2, 512], BF16, 2)
                if fwd:
                    r_ybl = rot("ybl", [128, 2, 512], BF16, 2)
                    r_bg = rot("bg", [128, 4, 128], F32, 2)
                    r_mix = one("mix", [128, 8, 128], BF16)
                    r_st = rot("stat", [128, 128], F32, 4)
                    r_sps = rot("sps", [128, 128], BF16, 2)

                evac_rr = [0]

                def evac(out, in_, reads, writes, engs=tuple(os.environ.get("MK_EVAC", "act,dve").split(","))):
                    e_ = engs[evac_rr[0] % len(engs)]
                    evac_rr[0] += 1
                    if e_ == "act":
                        c.op("act", lambda e: e.activation(out=out, in_=in_, func=AF.Copy), reads=reads, writes=writes)
                    else:
                        c.op(e_, lambda e: e.tensor_copy(out=out, in_=in_), reads=reads, writes=writes)

                def blocksum4(src, Bsrc, cname):
                    hi, Bhi = r_sp.next()
                    lo, Blo = r_sp.next()
                    split2(src, hi[:], lo[:], [Bsrc], Bhi, Blo)
                    ps, pb = pp.next()
                    for cp in range(4):
                        o_ = ps[:, cp * 128:(cp + 1) * 128]
                        mm(o_, Kb(cname), hi[:, cp, :], True, False, [Bcsb, Bhi], [pb[cp]])
                        mm(o_, Kb(cname), lo[:, cp, :], False, True, [Bcsb, Blo], [pb[cp]])
                    return ps[:].rearrange("p (a b) -> p a b", b=128), pb

                def link_state():
                    lk = P("link")
                    for cpr in range(4):
                        c.op("dve", lambda e, cpr=cpr: e.tensor_scalar(out=Hb[:, cpr, :], in0=Hb[:, cpr, :], scalar1=lk,
                                                                       scalar2=None, op0=ALU.mult),
                             reads=[BH[cpr], Bpv], writes=[BH[cpr]])
                    c.op("dve", lambda e: e.tensor_scalar(out=STf[:], in0=STf[:], scalar1=lk, scalar2=None, op0=ALU.mult),
                         reads=[BST, Bpv], writes=[BST])
                    c.op("dve", lambda e: e.tensor_scalar(out=STb[:], in0=STb[:], scalar1=lk, scalar2=None, op0=ALU.mult),
                         reads=[BST, Bpv], writes=[BST])

                for ti in range(min(NT, NTL)):
                    u_i, lt = divmod(ti, NTU)
                    if ti == NTU:
                        link_state()
                    col0 = u_i * (TU + 4) + lt * C
                    xb, Bxb = r_xb.next()
                    c.dma("pool", lambda e, xb=xb, col0=col0: e.dma_start(
                        out=xb[:], in_=xsrc[:, col0:col0 + 132].rearrange("(k p) t -> p k t", p=128)), writes=[Bxb])
                    if SUB < 1:
                        continue

                    def inproj(U, BU, chunks):
                        for g0 in range(0, len(chunks), 3):
                            grp = chunks[g0:g0 + 3]
                            n_g = len(grp)
                            ps, pb = pp.next()
                            for j, cc in enumerate(grp):
                                for k in range(8):
                                    mm(ps[:, j * 132:(j + 1) * 132], win[:, k, cc * 128:(cc + 1) * 128], xb[:, k, :],
                                       k == 0, k == 7, [Bwin, Bxb], pb)
                            evac(U[:, g0:g0 + n_g, :], ps[:, 0:n_g * 132].rearrange("p (a b) -> p a b", b=132), pb, [BU])

                    if U1t is not None:
                        U, BU = U1t[:], BU1
                    else:
                        U, BU = AU(6, 13)
                    inproj(U, BU, list(range(12)) + [CDT])
                    UDT = 12
                    if U1t is not None and SUB >= 3:
                        U2, BU2 = AU(6, 15)
                        inproj(U2, BU2, list(range(CR, CR + 15)))
                    if SUB < 2:
                        continue
                    zs, Bzs = r_zs.next()
                    c.op("act", lambda e, zs=zs, U=U: e.activation(out=zs[:], in_=U[:, CZ:CZ + 4, 2:130], func=AF.Silu),
                         reads=[BU], writes=[Bzs])
                    cv, Bcv = AS(0, 2)
                    for j in range(8):
                        eng = ("dve", "pool")[j % 2] if os.environ.get("MK_CONVP", "0") == "1" else "dve"
                        for k5 in range(5):
                            kk_ = k5 if fwd else 4 - k5
                            wk = pv[:, PV["cw"][0] + j * 5 + kk_:PV["cw"][0] + j * 5 + kk_ + 1]
                            src = U[:, CXS + j, k5:k5 + 128]
                            if k5 == 0:
                                c.op(eng, lambda e, j=j, wk=wk, src=src, cv=cv: e.tensor_scalar(
                                    out=cv[:, j, :], in0=src, scalar1=wk, scalar2=P("cb", j), op0=ALU.mult, op1=ALU.add),
                                     reads=[BU, Bpv], writes=[Bcv])
                            elif eng == "dve":
                                c.op(eng, lambda e, j=j, wk=wk, src=src, cv=cv: e.scalar_tensor_tensor(
                                    out=cv[:, j, :], in0=src, scalar=wk, in1=cv[:, j, :], op0=ALU.mult, op1=ALU.add),
                                     reads=[BU, Bpv, Bcv], writes=[Bcv])
                            else:
                                c.op(eng, lambda e, wk=wk, src=src: e.tensor_scalar(
                                    out=ptmp[:], in0=src, scalar1=wk, scalar2=None, op0=ALU.mult), reads=[BU, Bpv], writes=[Bptmp])
                                c.op(eng, lambda e, j=j, cv=cv: e.tensor_tensor(out=cv[:, j, :], in0=cv[:, j, :], in1=ptmp[:], op=ALU.add),
                                     reads=[Bptmp, Bcv], writes=[Bcv])
                    xsf, Bxsf = r_xsf.next()
                    xbc, Bxbc = r_xbc.next()
                    c.op("act", lambda e, xsf=xsf, cv=cv: e.activation(out=xsf[:], in_=cv[:, 0:4, :], func=AF.Silu),
                         reads=[Bcv], writes=[Bxsf])
                    c.op("act", lambda e, xbc=xbc, cv=cv: e.activation(out=xbc[:, 4:8, :], in_=cv[:, 4:8, :], func=AF.Silu),
                         reads=[Bcv], writes=[Bxbc])
                    c.op("act", lambda e, xbc=xbc, xsf=xsf: e.activation(out=xbc[:, 0:4, :], in_=xsf[:], func=AF.Copy),
                         reads=[Bxsf], writes=[Bxbc])
                    d8, Bd8 = AS(10, 1, parts=8)
                    c.op("act", lambda e, d8=d8, U=U: e.activation(out=d8[:, 0, :], in_=U[0:8, UDT, 2:130], func=AF.Exp,
                                                                   bias=dtb), reads=[BU, Bp8], writes=[Bd8])
                    c.op("act", lambda e, d8=d8: e.activation(out=d8[:, 1, :], in_=d8[:, 0, :], func=AF.Ln, bias=1.0),
                         reads=[Bd8], writes=[Bd8])
                    c.op("dve", lambda e, d8=d8: e.tensor_scalar(out=d8[:, 2, :], in0=d8[:, 1, :], scalar1=a8[:, 0:1],
                                                                 scalar2=None, op0=ALU.mult), reads=[Bd8, Ba8], writes=[Bd8])
                    c.op("dve", lambda e, d8=d8: e.tensor_tensor_scan(out=d8[:, 3, :], data0=ones_t[0:8, :], data1=d8[:, 2, :],
                                                                      initial=0.0, op0=ALU.mult, op1=ALU.add),
                         reads=[Bd8, Bones], writes=[Bd8])
                    for q, (src_i, r_i) in enumerate(((1, 0), (3, 2))):
                        x_ = d8[:, src_i, :]
                        r_ = d8[:, r_i, :]
                        c.op("dve", lambda e, q=q, x_=x_: e.tensor_copy(out=DT3[0:8, 0, q, :], in_=x_), reads=[Bd8], writes=[BDT3])
                        c.op("dve", lambda e, q=q, x_=x_, r_=r_: e.tensor_tensor(out=r_, in0=x_, in1=DT3[0:8, 0, q, :], op=ALU.subtract),
                             reads=[Bd8, BDT3], writes=[Bd8])
                        c.op("dve", lambda e, q=q, r_=r_: e.tensor_copy(out=DT3[0:8, 1, q, :], in_=r_), reads=[Bd8], writes=[BDT3])
                        c.op("dve", lambda e, q=q, r_=r_: e.tensor_tensor(out=r_, in0=r_, in1=DT3[0:8, 1, q, :], op=ALU.subtract),
                             reads=[Bd8, BDT3], writes=[Bd8])
                        c.op("dve", lambda e, q=q, r_=r_: e.tensor_copy(out=DT3[0:8, 2, q, :], in_=r_), reads=[Bd8], writes=[BDT3])
                    ps, pb = pp.next()
                    for q in range(2):
                        for i3 in range(3):
                            mm(ps[:, q * 128:(q + 1) * 128], DT3[:, i3, q, :], Kb("ident"), i3 == 0, i3 == 2, [BDT3, Bcsb], [pb[q]])
                    dtok, Bdtok = r_dtok.next()
                    c.op("dve", lambda e, dtok=dtok, ps=ps: e.tensor_copy(out=dtok[:, 0:8], in_=ps[:, 0:8]), reads=[pb[0], pb[1]], writes=[Bdtok])
                    c.op("dve", lambda e, dtok=dtok, ps=ps: e.tensor_copy(out=dtok[:, 8:16], in_=ps[:, 128:136]), reads=[pb[1]], writes=[Bdtok])
                    psR0, pbR0 = pp.next()
                    if os.environ.get("MK_SKIPB", "") == "1":
                        pp.next()
                    psR1, pbR1 = pp.next()
                    for i3 in range(3):
                        rb = i3 % 2
                        c.ew(lambda e, i3=i3, rb=rb: e.tensor_tensor(
                            out=RB3[0:8, rb, :, :], in0=m8b[:].rearrange("p (h l) -> p h l", l=128),
                            in1=DT3[0:8, i3, 1:2, :].to_broadcast([8, 8, 128]), op=ALU.mult), reads=[BDT3, Bm8], writes=[BRB3[rb]])
                        for (psR, pbR, h0) in ((psR0, pbR0, 0), (psR1, pbR1, 4)):
                            mm(psR[:], Kb("ones"), RB3[:, rb, h0:h0 + 4, :].rearrange("p a b -> p (a b)"), i3 == 0, i3 == 2, [Bcsb, BRB3[rb]], pbR)
                    seg, Bseg = AS(2, 2)
                    E, BE = AS(4, 2)
                    if os.environ.get("MK_FENCE", "-1") != "-1":
                        pbR0 = pbR0 + [Bfence]
                    for _d in range(int(os.environ.get("MK_DELAY", "0"))):
                        c.op("dve", lambda e: e.tensor_copy(out=ptmp[:], in_=ones_t[:]), reads=[pbR0, Bones], writes=[Bptmp])
                    for (psR, pbR, h0) in ((psR0, pbR0, 0), (psR1, pbR1, 4)):
                        c.op("dve", lambda e, psR=psR, h0=h0, seg=seg, dtok=dtok: e.tensor_tensor(
                            out=seg[:, h0:h0 + 4, :], in0=psR[:].rearrange("p (a b) -> p a b", b=128),
                            in1=dtok[:, 8 + h0:12 + h0].unsqueeze(2).to_broadcast([128, 4, 128]), op=ALU.subtract),
                             reads=[pbR, Bdtok], writes=[Bseg])
                    c.ew(lambda e, seg=seg: e.tensor_scalar(out=seg[:], in0=seg[:], scalar1=0.0, scalar2=None, op0=ALU.min), reads=[Bseg], writes=[Bseg])
                    c.op("act", lambda e, E=E, psR0=psR0: e.activation(out=E[:, 0:4, :], in_=psR0[:].rearrange("p (a b) -> p a b", b=128),
                                                                     func=AF.Exp), reads=pbR0, writes=[BE])
                    c.op("act", lambda e, E=E, psR1=psR1: e.activation(out=E[:, 4:8, :], in_=psR1[:].rearrange("p (a b) -> p a b", b=128),
                                                                     func=AF.Exp), reads=pbR1, writes=[BE])
                    c.op("act", lambda e, seg=seg: e.activation(out=seg[:], in_=seg[:], func=AF.Exp), reads=[Bseg], writes=[Bseg])
                    ps, pb = pp.next()
                    for g in range(2):
                        mm(ps[:, g * 128:(g + 1) * 128], xbc[:, 4 + g, :], xbc[:, 6 + g, :], True, True, [Bxbc], [pb[g]])
                    gm, Bgm = r_gm.next()
                    for g in range(2):
                        c.op("dve", lambda e, g=g, gm=gm, ps=ps: e.tensor_tensor(out=gm[:, g, :], in0=ps[:, g * 128:(g + 1) * 128],
                                                                                 in1=Kb("mui"), op=ALU.mult),
                             reads=[pb[g], Bcsb], writes=[Bgm])
                    MT, BMT = r_MT.next()
                    Cd, BCd = r_Cd.next()
                    for g in range(2):
                        c.ew(lambda e, g=g, MT=MT, seg=seg, gm=gm: e.tensor_tensor(
                            out=MT[:, 4 * g:4 * g + 4, :], in0=seg[:, 4 * g:4 * g + 4, :], in1=gm[:, g:g + 1, :].to_broadcast([128, 4, 128]), op=ALU.mult),
                             reads=[Bseg, Bgm], writes=[BMT])
                        c.ew(lambda e, g=g, Cd=Cd, E=E, xbc=xbc: e.tensor_tensor(
                            out=Cd[:, 4 * g:4 * g + 4, :], in0=E[:, 4 * g:4 * g + 4, :], in1=xbc[:, 6 + g:7 + g, :].to_broadcast([128, 4, 128]), op=ALU.mult),
                             reads=[BE, Bxbc], writes=[BCd])
                    ps, pb = pp.next()
                    for j in range(4):
                        mm(ps[:, j * 128:(j + 1) * 128], xbc[:, j, :], Kb("ident"), True, True, [Bxbc, Bcsb], [pb[j]])
                    xdt, Bxdt = r_xdt.next()
                    v864 = lambda ap: ap.rearrange("p (h q) -> p h q", q=64)
                    c.op("dve", lambda e, xdt=xdt, ps=ps, dtok=dtok: e.tensor_tensor(
                        out=v864(xdt[:, 0, :]), in0=v864(ps[:]), in1=dtok[:, 0:8].unsqueeze(2).to_broadcast([128, 8, 64]), op=ALU.mult),
                         reads=[pb, Bdtok], writes=[Bxdt])
                    c.ew(lambda e, xdt=xdt, seg=seg: e.tensor_tensor(
                        out=v864(xdt[:, 1, :]), in0=v864(xdt[:, 0, :]), in1=seg[:, :, 127:128].to_broadcast([128, 8, 64]), op=ALU.mult),
                         reads=[Bxdt, Bseg], writes=[Bxdt])
                    ps, pb = pp.next()
                    for g in range(2):
                        mm(ps[:, g * 128:(g + 1) * 128], xbc[:, 4 + g, :], Kb("ident"), True, True, [Bxbc, Bcsb], [pb[g]])
                    btok, Bbtok = r_btok.next()
                    evac(btok[:], ps[:, 0:256].rearrange("p (a b) -> p a b", b=128), pb[0:2], [Bbtok])
                    YS_t, YS_pb = pp.next()
                    YS_b = YS_pb[0]
                    for h in range(8):
                        o_ = YS_t[:, h * 64:(h + 1) * 64]
                        mm(o_, MT[:, h, :], xdt[:, 0, h * 64:(h + 1) * 64], True, False, [BMT, Bxdt], [YS_b])
                        mm(o_, Cd[:, h, :], STb[:, h * 64:(h + 1) * 64], False, True, [BCd, BST], [YS_b])
                    ysh, Bysh = r_yhl.next()
                    split2(YS_t[:], ysh[:, 0, :], ysh[:, 1, :], [YS_b], Bysh, Bysh, psum=True)
                    ps, pb = pp.next()
                    for g in range(2):
                        mm(ps[:, g * 256:(g + 1) * 256], btok[:, g, :], xdt[:, 1, g * 256:(g + 1) * 256], True, True,
                           [Bbtok, Bxdt], pb[2 * g:2 * g + 2])
                    c.ew(lambda e, E=E: e.tensor_tensor(out=v864(STf[:]), in0=v864(STf[:]), in1=E[:, :, 127:128].to_broadcast([128, 8, 64]), op=ALU.mult),
                         reads=[BST, BE], writes=[BST])
                    c.op("dve", lambda e, ps=ps: e.tensor_tensor(out=STf[:], in0=STf[:], in1=ps[:], op=ALU.add), reads=[BST, pb], writes=[BST])
                    c.op("act", lambda e: e.activation(out=STb[:], in_=STf[:], func=AF.Copy), reads=[BST], writes=[BST])
                    if not fwd:
                        for q, key in enumerate(("ybs_h", "ybs_l")):
                            c.dma("sp", lambda e, ysh=ysh, q=q, key=key, ti=ti: e.dma_start(out=ysc[key][ti * 128:(ti + 1) * 128, :], in_=ysh[:, q, :]),
                                  reads=[Bysh], writes=[B_ysc[key][ti]])
                    if SUB < 3:
                        continue

                    if U1t is not None:
                        U, BU = U2, BU2
                    else:
                        U, BU = AU(4, 15)
                        inproj(U, BU, list(range(CR, CR + 15)))
                    us, Bus = AS(0, 4, nch=15)
                    tsh, Btsh = AS(4, 2) if U1t is not None else AS(8, 2)
                    bc15 = lambda ap, j0, n: ap[:, j0:j0 + n].unsqueeze(2).to_broadcast([128, n, 128])
                    c.ew(lambda e, us=us, U=U: e.tensor_tensor(out=us, in0=U[:, 0:15, 2:130], in1=bc15(der, 0, 15), op=ALU.mult),
                         reads=[BU, Bder], writes=[Bus])
                    for (j0, n) in ((0, 8), (8, 7)):
                        for (sl_, mun) in (((1, 129), mpn), ((3, 131), mnn)):
                            mu_ap = pv[:, PV[mun][0]:PV[mun][0] + 15]
                            c.ew(lambda e, j0=j0, n=n, sl_=sl_, mu_ap=mu_ap, U=U, tsh=tsh: e.tensor_tensor(
                                out=tsh[:, 0:n, :], in0=U[:, j0:j0 + n, sl_[0]:sl_[1]], in1=bc15(mu_ap, j0, n), op=ALU.mult),
                                 reads=[BU, Bpv], writes=[Btsh])
                            c.ew(lambda e, j0=j0, n=n, us=us, tsh=tsh: e.tensor_tensor(
                                out=us[:, j0:j0 + n, :], in0=us[:, j0:j0 + n, :], in1=tsh[:, 0:n, :], op=ALU.add),
                                 reads=[Bus, Btsh], writes=[Bus])
                    lrin, Blrin = r_lrin.next()
                    c.op("act", lambda e, us=us, lrin=lrin: e.activation(out=lrin[:, 0, :], in_=us[:, 12, :], func=AF.Tanh), reads=[Bus], writes=[Blrin])
                    c.op("act", lambda e, us=us, lrin=lrin: e.activation(out=lrin[:, 1, :], in_=us[:, 13, :], func=AF.Copy), reads=[Bus], writes=[Blrin])
                    c.op("act", lambda e, us=us, lrin=lrin: e.activation(out=lrin[:, 2, :], in_=us[:, 14, :], func=AF.Sigmoid), reads=[Bus], writes=[Blrin])
                    psL, pbL = pp.next()
                    psI, pbI = pp.next()
                    for cp in range(4):
                        mm(psL[:, cp * 128:(cp + 1) * 128], w2[:, cp * 128:(cp + 1) * 128], lrin[:, 0, :], True, True, [Blr, Blrin], [pbL[cp]])
                        mm(psI[:, cp * 128:(cp + 1) * 128], a2[:, cp * 128:(cp + 1) * 128], lrin[:, 1, :], True, True, [Blr, Blrin], [pbI[cp]])
                    sg, Bsg = AS(4)
                    icl, Bicl = AS(5)
                    bc4 = lambda n: P(n, 0, 4).unsqueeze(2).to_broadcast([128, 4, 128])
                    v4 = lambda ps_: ps_[:].rearrange("p (a b) -> p a b", b=128)
                    c.op("dve", lambda e, sg=sg, psL=psL: e.tensor_tensor(out=sg[:], in0=v4(psL), in1=bc4(w0n), op=ALU.add), reads=[pbL, Bpv], writes=[Bsg])
                    c.op("act", lambda e, sg=sg: e.activation(out=sg[:], in_=sg[:], func=AF.Sigmoid), reads=[Bsg], writes=[Bsg])
                    c.op("dve", lambda e, icl=icl, psI=psI: e.tensor_tensor(out=icl[:], in0=v4(psI), in1=bc4("a0"), op=ALU.add), reads=[pbI, Bpv], writes=[Bicl])
                    c.op("act", lambda e, icl=icl: e.activation(out=icl[:], in_=icl[:], func=AF.Sigmoid), reads=[Bicl], writes=[Bicl])
                    for cp in range(4):
                        c.op("dve", lambda e, cp=cp, sg=sg: e.tensor_tensor_scan(
                            out=CSt[:, cp, 1:129], data0=ones_t[:], data1=sg[:, cp, :], initial=0.0, op0=ALU.mult, op1=ALU.add),
                             reads=[Bsg, Bones, BCSt], writes=[BCSt])
                    gam, Bgam = AS(6)
                    igam, Bigam = AS(7)
                    gprev, Bgprev = AS(8)
                    c.op("act", lambda e, gam=gam: e.activation(out=gam[:], in_=CSt[:, :, 1:129], func=AF.Exp, scale=-CDEC),
                         reads=[BCSt], writes=[Bgam])
                    c.op("act", lambda e, igam=igam: e.activation(out=igam[:], in_=CSt[:, :, 1:129], func=AF.Exp, scale=CDEC),
                         reads=[BCSt], writes=[Bigam])
                    c.op("act", lambda e, gprev=gprev: e.activation(out=gprev[:], in_=CSt[:, :, 0:128], func=AF.Exp, scale=-CDEC),
                         reads=[BCSt], writes=[Bgprev])
                    kkt, Bkk = AS(9)
                    c.ew(lambda e, kkt=kkt, us=us: e.tensor_tensor(out=kkt[:], in0=us[:, 4:8, :], in1=bc4("kk"), op=ALU.mult), reads=[Bus, Bpv], writes=[Bkk])
                    sq, Bsq = sg, Bsg
                    c.ew(lambda e, sq=sq, kkt=kkt: e.tensor_tensor(out=sq[:], in0=kkt[:], in1=kkt[:], op=ALU.mult), reads=[Bkk], writes=[Bsq])
                    psS, pbS = blocksum4(sq[:], Bsq, "bo64")
                    rn, Brn = AS(10)
                    c.op("dve", lambda e, rn=rn, psS=psS: e.tensor_scalar(out=rn[:], in0=psS, scalar1=1e-24, scalar2=None, op0=ALU.max),
                         reads=pbS, writes=[Brn])
                    c.op("act", lambda e, rn=rn: e.activation(out=rn[:], in_=rn[:], func=AF.Sqrt), reads=[Brn], writes=[Brn])
                    c.op("dve", lambda e, rn=rn: e.reciprocal(out=rn[:], in_=rn[:]), reads=[Brn], writes=[Brn])
                    c.ew(lambda e, kkt=kkt, rn=rn: e.tensor_tensor(out=kkt[:], in0=kkt[:], in1=rn[:], op=ALU.mult), reads=[Bkk, Brn], writes=[Bkk])
                    km, Bkm = rn, Brn
                    c.ew(lambda e, km=km, icl=icl: e.tensor_tensor(out=km[:], in0=icl[:], in1=bc4("ka"), op=ALU.mult), reads=[Bicl, Bpv], writes=[Bkm])
                    c.ew(lambda e, km=km: e.tensor_tensor(out=km[:], in0=km[:], in1=der[:, 15:19].unsqueeze(2).to_broadcast([128, 4, 128]), op=ALU.add),
                         reads=[Bkm, Bder], writes=[Bkm])
                    c.ew(lambda e, km=km, us=us: e.tensor_tensor(out=km[:], in0=km[:], in1=us[:, 4:8, :], op=ALU.mult), reads=[Bkm, Bus], writes=[Bkm])
                    if fwd:
                        rk_, Brk = sq, Bsq
                        c.ew(lambda e, rk_=rk_, us=us: e.tensor_tensor(out=rk_[:], in0=us[:, 0:4, :], in1=bc4("rk"), op=ALU.mult), reads=[Bus, Bpv], writes=[Brk])
                        c.ew(lambda e, rk_=rk_, km=km: e.tensor_tensor(out=rk_[:], in0=rk_[:], in1=km[:], op=ALU.mult), reads=[Brk, Bkm], writes=[Brk])
                        psB, pbB = blocksum4(rk_[:], Brk, "bo64")
                        bonv, Bbonv = r_bg.next()
                        c.op("dve", lambda e, bonv=bonv, psB=psB, us=us: e.tensor_tensor(out=bonv[:], in0=psB, in1=us[:, 8:12, :], op=ALU.mult),
                             reads=pbB + [Bus], writes=[Bbonv])
                        psG, pbG = pp.next()
                        for cp in range(4):
                            mm(psG[:, cp * 128:(cp + 1) * 128], g2[:, cp * 128:(cp + 1) * 128], lrin[:, 2, :], True, True, [Blr, Blrin], [pbG[cp]])
                        gT, BgT = r_bg.next()
                        c.op("act", lambda e, gT=gT, psG=psG: e.activation(out=gT[:], in_=psG[:].rearrange("p (a b) -> p a b", b=128), func=AF.Copy),
                             reads=pbG, writes=[BgT])
                    KT, BKT = r_op.next()
                    BT, BBT = r_op.next()
                    AT, BAT = r_op.next()
                    vb, Bvb = r_op.next()
                    c.ew(lambda e, KT=KT, km=km, igam=igam: e.tensor_tensor(out=KT[:], in0=km[:], in1=igam[:], op=ALU.mult),
                         reads=[Bkm, Bigam], writes=[BKT])
                    c.ew(lambda e, icl=icl, kkt=kkt: e.tensor_tensor(out=icl[:], in0=icl[:], in1=kkt[:], op=ALU.mult), reads=[Bicl, Bkk], writes=[Bicl])
                    c.ew(lambda e, BT=BT, icl=icl, igam=igam: e.tensor_tensor(out=BT[:], in0=icl[:], in1=igam[:], op=ALU.mult),
                         reads=[Bicl, Bigam], writes=[BBT])
                    c.op("dve", lambda e, AT=AT, kkt=kkt, gprev=gprev: e.scalar_tensor_tensor(out=AT[:], in0=kkt[:], scalar=-1.0, in1=gprev[:],
                                                                                             op0=ALU.mult, op1=ALU.mult),
                         reads=[Bkk, Bgprev], writes=[BAT])
                    c.ew(lambda e, vb=vb, us=us: e.tensor_copy(out=vb[:], in_=us[:, 8:12, :]), reads=[Bus], writes=[Bvb])
                    c.ew(lambda e, AT=AT: e.tensor_copy(out=ARe_t[0:64, :, 0:128], in_=AT[0:64, :, :]), reads=[BAT], writes=[BAR_t])
                    c.ew(lambda e, AT=AT: e.tensor_copy(out=ARo_t[64:128, :, 0:128], in_=AT[64:128, :, :]), reads=[BAT], writes=[BAR_t])
                    c.ew(lambda e, us=us, gam=gam: e.tensor_tensor(out=ARe_t[0:64, :, 128:256], in0=us[0:64, 0:4, :], in1=gam[0:64, :, :],
                                                                   op=ALU.mult), reads=[Bus, Bgam], writes=[BAR_t])
                    c.ew(lambda e, us=us, gam=gam: e.tensor_tensor(out=ARo_t[64:128, :, 128:256], in0=us[64:128, 0:4, :], in1=gam[64:128, :, :],
                                                                   op=ALU.mult), reads=[Bus, Bgam], writes=[BAR_t])
                    if SUB < 4:
                        continue
                    if fwd:
                        src_t = NT - 1 - ti
                        ybS, BybS = r_ybl.next()
                        ybR, BybR = r_ybl.next()
                        for (dst_, Bdst_, kh, kl) in ((ybS, BybS, "ybs_h", "ybs_l"), (ybR, BybR, "ybr_h", "ybr_l")):
                            for q, key in enumerate((kh, kl)):
                                c.dma("sp", lambda e, dst_=dst_, q=q, key=key, src_t=src_t: e.dma_start(
                                    out=dst_[:, q, :], in_=ysc[key][src_t * 128:(src_t + 1) * 128, :]),
                                      reads=[B_ysc[key][src_t]], writes=[Bdst_])
                    heads_all = [(cp, x) for cp in range(4) for x in range(2)]
                    for g0 in range(0, 8, LS):
                        grp = heads_all[g0:g0 + LS]
                        cps = sorted(set(cp for cp, _ in grp))
                        toks, wus, st = {}, {}, {}
                        for cp in cps:
                            ps, pb = pp.next()
                            for q, src in enumerate((AT, BT, KT, vb)):
                                mm(ps[:, q * 128:(q + 1) * 128], src[:, cp, :], Kb("ident"), True, True, [BAT, BBT, BKT, Bvb, Bcsb], [pb[q]])
                            tok, Btok = r_tok.next()
                            evac(tok[:], ps[:].rearrange("p (a b) -> p a b", b=128), pb, [Btok])
                            toks[cp] = (tok, Btok)
                            wus[cp] = r_wu.next()
                        quads = []
                        for q0 in range(0, len(grp), 4):
                            zfq, Bzfq = r_zf.next()
                            zbq, Bzbq = r_zb.next()
                            quads.append(dict(heads=grp[q0:q0 + 4], zf=zfq, Bzf=Bzfq, zb=zbq, Bzb=Bzbq))
                        for hi_, (cp, x) in enumerate(grp):
                            qd = quads[hi_ // 4]
                            hq = hi_ % 4
                            tok, Btok = toks[cp]
                            AR_x = (ARe_t, ARo_t)[x]
                            hs = slice(x * 64, (x + 1) * 64)
                            psA, pbA = pp.next()
                            psB4, pbB4 = pp.next()
                            mm(psA[:, 0:128], AR_x[:, cp, 0:128], BT[:, cp, :], True, True, [BAR_t, BBT], [pbA[0]])
                            mm(psB4[:, 0:256], BT[:, cp, :], AR_x[:, cp, :], True, True, [BAR_t, BBT], pbB4[0:2])
                            mm(psB4[:, 256:512], KT[:, cp, :], AR_x[:, cp, :], True, True, [BAR_t, BKT], pbB4[2:4])
                            p0, Bp0 = r_p0.next()
                            c.op("dve", lambda e, p0=p0, psA=psA: e.tensor_tensor(out=p0[:], in0=psA[:, 0:128], in1=Kb("msl"), op=ALU.mult),
                                 reads=[pbA[0], Bcsb], writes=[Bp0])
                            e4, Be4 = r_e4.next()
                            c.op("dve", lambda e, e4=e4, psB4=psB4: e.tensor_tensor(out=e4[:], in0=psB4[:], in1=mask4[:], op=ALU.mult),
                                 reads=pbB4 + [Bm4], writes=[Be4])
                            mm(psA[:, 128:192], e4[:, 256:384], tok[:, 3, hs], True, True, [Be4, Btok], [pbA[1]])
                            zf, Bzf = qd["zf"][:, hq, :], qd["Bzf"]
                            c.op("act", lambda e, zf=zf, tok=tok, hs=hs: e.activation(out=zf[:, 0:64], in_=tok[:, 0, hs], func=AF.Copy),
                                 reads=[Btok], writes=[Bzf])
                            c.op("act", lambda e, zf=zf, psA=psA: e.activation(out=zf[:, 64:128], in_=psA[:, 128:192], func=AF.Copy),
                                 reads=[pbA[1]], writes=[Bzf])
                            st[(cp, x)] = dict(e4=e4, Be4=Be4, zf=zf, Bzf=Bzf, Pn=p0[:], PnT=e4[:, 0:128], BPn=[Bp0, Be4])
                        for qd in quads:
                            c.ew(lambda e, qd=qd: e.tensor_copy(out=qd["zb"][:], in_=qd["zf"][:]), reads=[qd["Bzf"]], writes=[qd["Bzb"]])
                        for it in range(7):
                            for qd in quads:
                                zfq, Bzfq, zbq, Bzbq = qd["zf"], qd["Bzf"], qd["zb"], qd["Bzb"]
                                psZ, pbZ = pp.next()
                                for hq, hd in enumerate(qd["heads"]):
                                    d_ = st[hd]
                                    mm(psZ[:, hq * 128:(hq + 1) * 128], d_["PnT"], zbq[:, hq, :], True, True, d_["BPn"] + [Bzbq], pbZ)
                                if it < 6:
                                    psS = [pp.next(), pp.next()]
                                    for hq, hd in enumerate(qd["heads"]):
                                        d_ = st[hd]
                                        ps_, pb_ = psS[hq // 2]
                                        o0 = (hq % 2) * 256
                                        if it < 5:
                                            mm(ps_[:, o0:o0 + 128], d_["PnT"], d_["Pn"], True, True, d_["BPn"], pb_)
                                        mm(ps_[:, o0 + 128:o0 + 256], d_["Pn"], d_["PnT"], True, True, d_["BPn"], pb_)
                                    c.op("dve", lambda e, zfq=zfq, zbq=zbq, psZ=psZ: e.tensor_tensor(
                                        out=zbq[:], in0=psZ[:].rearrange("p (a b) -> p a b", b=128), in1=zfq[:], op=ALU.add),
                                         reads=[pbZ, Bzfq], writes=[Bzbq])
                                c.op("dve", lambda e, zfq=zfq, psZ=psZ: e.tensor_tensor(
                                    out=zfq[:], in0=psZ[:].rearrange("p (a b) -> p a b", b=128), in1=zfq[:], op=ALU.add),
                                     reads=[pbZ, Bzfq], writes=[Bzfq])
                                if it < 6:
                                    ppq, Bppq = r_pp.next()
                                    for half in range(2):
                                        ps_, pb_ = psS[half]
                                        c.op("act", lambda e, ppq=ppq, ps_=ps_, half=half: e.activation(
                                            out=ppq[:, 2 * half:2 * half + 2, :], in_=ps_[:].rearrange("p (a b) -> p a b", b=256), func=AF.Copy),
                                             reads=pb_, writes=[Bppq])
                                    for hq, hd in enumerate(qd["heads"]):
                                        d_ = st[hd]
                                        d_["Pn"], d_["PnT"], d_["BPn"] = ppq[:, hq, 0:128], ppq[:, hq, 128:256], [Bppq]
                        for (cp, x) in grp:
                            d_ = st[(cp, x)]
                            wu, Bwu = wus[cp]
                            hs = slice(x * 64, (x + 1) * 64)
                            zf, Bzf = d_["zf"], d_["Bzf"]
                            c.op("act", lambda e, wu=wu, zf=zf, hs=hs: e.activation(out=wu[:, 0, hs], in_=zf[:, 0:64], func=AF.Copy),
                                 reads=[Bzf], writes=[Bwu])
                            c.ew(lambda e, wu=wu, zf=zf, hs=hs: e.tensor_copy(out=wu[:, 1, hs], in_=zf[:, 64:128]),
                                 reads=[Bzf], writes=[Bwu])
                        for cp in cps:
                            if (cp, 0) not in st or (cp, 1) not in st:
                                continue
                            tok, Btok = toks[cp]
                            wu, Bwu = wus[cp]
                            qt, Bqt = r_qt.next()
                            psQ, pbQ = pp.next()
                            for x in range(2):
                                e4, Be4 = st[(cp, x)]["e4"], st[(cp, x)]["Be4"]
                                AR_x = (ARe_t, ARo_t)[x]
                                hs = slice(x * 64, (x + 1) * 64)
                                mm(psQ[:, x * 128:(x + 1) * 128], wu[:, 0, :], e4[:, 128:256], True, True, [Bwu, Be4], [pbQ[x]])
                            for x in range(2):
                                AR_x = (ARe_t, ARo_t)[x]
                                hs = slice(x * 64, (x + 1) * 64)
                                c.op("dve", lambda e, qt=qt, psQ=psQ, x=x, hs=hs, AR_x=AR_x, cp=cp: e.tensor_tensor(
                                    out=qt[hs, :], in0=psQ[hs, x * 128:(x + 1) * 128], in1=AR_x[hs, cp, 128:256], op=ALU.add),
                                     reads=[pbQ[x], BAR_t], writes=[Bqt])
                            yo = YR_t[:, cp * 128:(cp + 1) * 128]
                            mm(yo, qt[:], Hb[:, cp, :], True, False, [Bqt, BH[cp]], [YR_b])
                            for x in range(2):
                                e4, Be4 = st[(cp, x)]["e4"], st[(cp, x)]["Be4"]
                                hs = slice(x * 64, (x + 1) * 64)
                                yx = YR_t[:, cp * 128 + x * 64:cp * 128 + (x + 1) * 64]
                                mm(yx, e4[:, 128:256], wu[:, 1, hs], False, False, [Be4, Bwu], [YR_b])
                                mm(yx, e4[:, 384:512], tok[:, 3, hs], False, x == 1, [Be4, Btok], [YR_b])
                            psM, pbM = pp.next()
                            mm(psM[:, 0:128], wu[:, 0, :], tok[:, 1, :], True, True, [Bwu, Btok], [pbM[0]])
                            mtb, Bmtb = r_mtb.next()
                            c.op("dve", lambda e, mtb=mtb, psM=psM: e.tensor_tensor(out=mtb[:], in0=psM[:, 0:128], in1=Kb("bmask"), op=ALU.mult),
                                 reads=[pbM[0], Bcsb], writes=[Bmtb])
                            psH, pbH = pp.next()
                            hsl = psH[:, 0:128]
                            mm(hsl, tok[:, 1, :], wu[:, 1, :], True, False, [Btok, Bwu], [pbH[0]])
                            mm(hsl, tok[:, 2, :], tok[:, 3, :], False, False, [Btok], [pbH[0]])
                            mm(hsl, Kb("ident"), Hb[:, cp, :], False, False, [Bcsb, BH[cp]], [pbH[0]])
                            mm(hsl, mtb[:], Hb[:, cp, :], False, True, [Bmtb, BH[cp]], [pbH[0]])
                            c.op("dve", lambda e, cp=cp, psH=psH, gam=gam: e.scalar_tensor_tensor(
                                out=Hb[:, cp, :], in0=psH[:, 0:128], scalar=gam[:, cp, 127:128], in1=Kb("bmask"), op0=ALU.mult, op1=ALU.mult),
                                 reads=[pbH[0], Bgam, Bcsb], writes=[BH[cp]])
                    yrh, Byrh = r_yhl.next()
                    split2(YR_t[:], yrh[:, 0, :], yrh[:, 1, :], [YR_b], Byrh, Byrh, psum=True)
                    if not fwd:
                        for q, key in enumerate(("ybr_h", "ybr_l")):
                            c.dma("sp", lambda e, yrh=yrh, q=q, key=key, ti=ti: e.dma_start(out=ysc[key][ti * 128:(ti + 1) * 128, :], in_=yrh[:, q, :]),
                                  reads=[Byrh], writes=[B_ysc[key][ti]])
                        continue
                    if SUB < 5:
                        continue

                    mix, Bmix = r_mix.next()

                    def ytrans(yh, Byh, yb, Byb):
                        ps, pb = pp.next()
                        for j in range(4):
                            o_ = ps[:, j * 128:(j + 1) * 128]
                            cs_ = slice(j * 128, (j + 1) * 128)
                            mm(o_, yh[:, 0, cs_], Kb("ident"), True, False, [Byh, Bcsb], [pb[j]])
                            mm(o_, yh[:, 1, cs_], Kb("ident"), False, False, [Byh, Bcsb], [pb[j]])
                            mm(o_, yb[:, 0, cs_], Kb("J"), False, False, [Byb, Bcsb], [pb[j]])
                            mm(o_, yb[:, 1, cs_], Kb("J"), False, True, [Byb, Bcsb], [pb[j]])
                        return ps, pb
                    def ch_ssd():
                        ps, pb = ytrans(ysh, Bysh, ybS, BybS)
                        yield
                        yg, Byg = AS(0)
                        yield
                        for j in range(4):
                            c.op("dve", lambda e, j=j, yg=yg, xsf=xsf, ps=ps: e.scalar_tensor_tensor(
                                out=yg[:, j, :], in0=xsf[:, j, :], scalar=P("dch", j), in1=ps[:, j * 128:(j + 1) * 128], op0=ALU.mult, op1=ALU.add),
                                 reads=[Bxsf, Bpv, pb[j]], writes=[Byg])
                        c.ew(lambda e, yg=yg, zs=zs: e.tensor_tensor(out=yg[:], in0=yg[:], in1=zs[:], op=ALU.mult), reads=[Byg, Bzs], writes=[Byg])
                        yield
                        sq2, Bsq2 = AS(2)
                        yield
                        c.ew(lambda e, sq2=sq2, yg=yg: e.tensor_tensor(out=sq2[:], in0=yg[:], in1=yg[:], op=ALU.mult), reads=[Byg], writes=[Bsq2])
                        yield
                        hi, Bhi = r_sp.next()
                        yield
                        lo, Blo = r_sp.next()
                        yield
                        split2(sq2[:], hi[:], lo[:], [Bsq2], Bhi, Blo)
                        yield
                        ps, pb = pp.next()
                        yield
                        for g in range(2):
                            o_ = ps[:, g * 128:(g + 1) * 128]
                            for j in range(2):
                                mm(o_, Kb("o256"), hi[:, 2 * g + j, :], j == 0, False, [Bcsb, Bhi], [pb[g]])
                                mm(o_, Kb("o256"), lo[:, 2 * g + j, :], False, j == 1, [Bcsb, Blo], [pb[g]])
                        rs, Brs = r_st.next()
                        yield
                        rs2, Brs2 = r_st.next()
                        yield
                        rsqrt_eps(rs[:], ps[:, 0:128], RMS_EPS, [pb[0]], [Brs])
                        yield
                        rsqrt_eps(rs2[:], ps[:, 128:256], RMS_EPS, [pb[1]], [Brs2])
                        yield
                        for j in range(4):
                            rr_, Brr_ = (rs, Brs) if j < 2 else (rs2, Brs2)
                            c.op("dve", lambda e, j=j, mix=mix, yg=yg, rr_=rr_: e.scalar_tensor_tensor(
                                out=mix[:, j, :], in0=yg[:, j, :], scalar=P("nw", j), in1=rr_[:], op0=ALU.mult, op1=ALU.mult),
                                 reads=[Byg, Bpv, Brr_], writes=[Bmix])
                        yield
                    def ch_rwkv():
                        ps, pb = ytrans(yrh, Byrh, ybR, BybR)
                        yield
                        yf, Byf = AS(1)
                        sq2r, Bsq2r = AS(5)
                        yield
                        evac(yf[:], ps[:].rearrange("p (a b) -> p a b", b=128), pb, [Byf])
                        yield
                        psm, pbm = blocksum4(yf[:], Byf, "bo64m")
                        yield
                        c.op("dve", lambda e, yf=yf, psm=psm: e.tensor_tensor(out=yf[:], in0=yf[:], in1=psm, op=ALU.subtract), reads=[Byf] + pbm, writes=[Byf])
                        yield
                        c.ew(lambda e, sq2r=sq2r, yf=yf: e.tensor_tensor(out=sq2r[:], in0=yf[:], in1=yf[:], op=ALU.mult), reads=[Byf], writes=[Bsq2r])
                        yield
                        psv, pbv = blocksum4(sq2r[:], Bsq2r, "bo64m")
                        yield
                        rsqrt_eps(sq2r[:], psv, GN_EPS, pbv, [Bsq2r])
                        yield
                        c.ew(lambda e, yf=yf, sq2r=sq2r: e.tensor_tensor(out=yf[:], in0=yf[:], in1=sq2r[:], op=ALU.mult), reads=[Byf, Bsq2r], writes=[Byf])
                        yield
                        c.ew(lambda e, yf=yf: e.tensor_tensor(out=yf[:], in0=yf[:], in1=bc4("lw"), op=ALU.mult), reads=[Byf, Bpv], writes=[Byf])
                        yield
                        c.ew(lambda e, yf=yf: e.tensor_tensor(out=yf[:], in0=yf[:], in1=bc4("lb"), op=ALU.add), reads=[Byf, Bpv], writes=[Byf])
                        yield
                        c.ew(lambda e, yf=yf, bonv=bonv: e.tensor_tensor(out=yf[:], in0=yf[:], in1=bonv[:], op=ALU.add), reads=[Byf, Bbonv], writes=[Byf])
                        yield
                        c.ew(lambda e, mix=mix, yf=yf, gT=gT: e.tensor_tensor(out=mix[:, 4:8, :], in0=yf[:], in1=gT[:], op=ALU.mult),
                             reads=[Byf, BgT], writes=[Bmix])
                        yield
                    gens_ = [ch_ssd(), ch_rwkv()]
                    while gens_:
                        for g_ in list(gens_):
                            try:
                                next(g_)
                            except StopIteration:
                                gens_.remove(g_)
                    xf, Bxf = AS(9, 2)
                    c.dma("sp", lambda e, xf=xf, col0=col0: e.dma_start(
                        out=xf, in_=xT[:, col0 + 2:col0 + 130].rearrange("(k p) t -> p k t", p=128)), writes=[Bxf])
                    pre, Bpre = AS(3, 2)
                    for half in range(2):
                        ps, pb = pp.next()
                        for j in range(4):
                            dc = half * 4 + j
                            for k in range(8):
                                mm(ps[:, j * 128:(j + 1) * 128], wo[:, k, dc * 128:(dc + 1) * 128], mix[:, k, :], k == 0, k == 7,
                                   [Bwo, Bmix], [pb[j]])
                        c.op("dve", lambda e, half=half, pre=pre, xf=xf, ps=ps: e.scalar_tensor_tensor(
                            out=pre[:, half * 4:(half + 1) * 4, :], in0=xf[:, half * 4:(half + 1) * 4, :], scalar=ALPHA,
                            in1=ps[:].rearrange("p (a b) -> p a b", b=128), op0=ALU.mult, op1=ALU.add), reads=[Bxf] + pb, writes=[Bpre])
                    hout, Bhout = AS(7, 2)
                    sqL, BsqL = AS(5, 2)
                    layer_norm(pp, Kb, pre, Bpre, hout, Bhout, sqL, BsqL, r_st, r_sps, P, "l1w", "l1b", Bcsb, Bpv, 128)
                    t0 = ti * C
                    c.dma("sp", lambda e, hout=hout, t0=t0: e.dma_start(
                        out=hT[:, t0:t0 + 128].rearrange("(k p) t -> p k t", p=128), in_=hout), reads=[Bhout], writes=[B_hT[ti]])
                c.full_barrier()
                c.run_block()

        def ffn_phase():
            NF = 256 if TT % 256 == 0 else 128
            with ExitStack() as es:
                sb = lambda n, s, dt=F32: es.enter_context(nc.sbuf_tensor(n, s, dt))
                rot = lambda n, s, dt=F32, k=2: Rot(nc, es, n, s, dt, k)
                pp = PsumPool(nc, es, ["pf%d" % i for i in range(8)])
                wfi = sb("wfi", [128, 8, 2 * DFF], BF16); Bwfi = Buf()
                wfo = sb("wfo", [128, NFC, D], BF16); Bwfo = Buf()
                for k in range(8):
                    c.dma("pool", lambda e, k=k: e.dma_start(out=wfi[:, k, :], in_=w_fi[k * 128:(k + 1) * 128, :]), writes=[Bwfi])
                for k in range(NFC):
                    c.dma("pool", lambda e, k=k: e.dma_start(out=wfo[:, k, :], in_=w_fo[k * 128:(k + 1) * 128, :]), writes=[Bwfo])
                pv = sb("pv2", [128, NPV]); Bpv = Buf()
                c.dma("sp", lambda e: e.dma_start(out=pv[:], in_=pvec), writes=[Bpv])
                csb = sb("csb2", [128, NCST * 128], BF16); Bcsb = Buf()
                c.dma("pool", lambda e: e.dma_start(out=csb[:], in_=cst), writes=[Bcsb])
                Kb = lambda n: csb[:, CST[n] * 128:(CST[n] + 1) * 128]
                P = lambda n, j=0, w=1: pv[:, PV[n][0] + j:PV[n][0] + j + w]
                r_hf = rot("hf", [128, 8, NF], F32, 1)
                r_hb = rot("hb", [128, 8, NF], BF16, 1)
                r_act = rot("actt", [128, NFC, NF], BF16, 1)
                r_sl = rot("sl", [128, NF], F32, 2)
                r_sq = rot("sq2", [128, 8, NF], F32, 1)
                r_st = rot("st2", [128, NF], F32, 4)
                r_sps = rot("sps2", [128, NF], BF16, 2)
                for fi in range(min(TT // NF, NTL)):
                    t0 = fi * NF
                    deps = [B_hT[t] for t in range(t0 // C, (t0 + NF) // C)]
                    hf, Bhf = r_hf.next()
                    hb, Bhb = r_hb.next()
                    c.dma("sp", lambda e, hf=hf, t0=t0: e.dma_start(out=hf[:], in_=hT[:, t0:t0 + NF].rearrange("(k p) t -> p k t", p=128)),
                          reads=deps, writes=[Bhf])
                    c.op("act", lambda e, hf=hf, hb=hb: e.activation(out=hb[:], in_=hf[:], func=AF.Copy), reads=[Bhf], writes=[Bhb])
                    actt, Bact = r_act.next()
                    for fc in range(NFC):
                        ps, pb = pp.next()
                        for k in range(8):
                            mm(ps[:, 0:NF], wfi[:, k, fc * 128:(fc + 1) * 128], hb[:, k, :], k == 0, k == 7, [Bwfi, Bhb], pb[0:2])
                        for k in range(8):
                            mm(ps[:, 256:256 + NF], wfi[:, k, DFF + fc * 128:DFF + (fc + 1) * 128], hb[:, k, :], k == 0, k == 7,
                               [Bwfi, Bhb], pb[2:4])
                        sl, Bsl = r_sl.next()
                        c.op("act", lambda e, sl=sl, ps=ps: e.activation(out=sl[:], in_=ps[:, 0:NF], func=AF.Silu), reads=pb[0:2], writes=[Bsl])
                        c.op("dve", lambda e, fc=fc, sl=sl, ps=ps, actt=actt: e.tensor_tensor(
                            out=actt[:, fc, :], in0=ps[:, 256:256 + NF], in1=sl[:], op=ALU.mult), reads=pb[2:4] + [Bsl], writes=[Bact])
                    for half in range(4):
                        ps, pb = pp.next()
                        for j in range(2):
                            dc = half * 2 + j
                            for k in range(NFC):
                                mm(ps[:, j * 256:j * 256 + NF], wfo[:, k, dc * 128:(dc + 1) * 128], actt[:, k, :], k == 0, k == NFC - 1,
                                   [Bwfo, Bact], pb[2 * j:2 * j + 2])
                        for j in range(2):
                            dc = half * 2 + j
                            c.op("dve", lambda e, dc=dc, j=j, hf=hf, ps=ps: e.scalar_tensor_tensor(
                                out=hf[:, dc, :], in0=hf[:, dc, :], scalar=ALPHA, in1=ps[:, j * 256:j * 256 + NF], op0=ALU.mult, op1=ALU.add),
                                 reads=[Bhf] + pb[2 * j:2 * j + 2], writes=[Bhf])
                    sq, Bsq = r_sq.next()
                    out, Bout = sq, Bsq
                    layer_norm(pp, Kb, hf[:], Bhf, out[:], Bout, sq[:], Bsq, r_st, r_sps, P, "l2w", "l2b", Bcsb, Bpv, NF)
                    Bo = Buf()
                    B_out.append(Bo)
                    c.dma("sp", lambda e, out=out, t0=t0: e.dma_start(out=yT[:, t0:t0 + NF].rearrange("(k p) t -> p k t", p=128), in_=out[:]),
                          reads=[Bout], writes=[Bo])
                c.wait_all("sp", B_out)
                c.run_block()

        mixer_phase(False)
        if stage >= 2:
            mixer_phase(True)
        if stage >= 3:
            ffn_phase()
    return nc


def _consts():
    i = np.arange(128)
    p, f = i[:, None], i[None, :]
    blocks = {
        "ident": (p == f), "J": (p + f == 127), "msl": (f < p), "msu": (p < f), "mui": (p <= f),
        "bmask": ((p // 64) == (f // 64)), "bo64": ((p // 64) == (f // 64)),
    }
    out = np.zeros((128, NCST * 128), np.float32)
    for n, m in blocks.items():
        out[:, CST[n] * 128:(CST[n] + 1) * 128] = m.astype(np.float32)
    out[:, CST["bo64m"] * 128:(CST["bo64m"] + 1) * 128] = blocks["bo64"].astype(np.float32) / 64.0
    out[:, CST["o1024"] * 128:(CST["o1024"] + 1) * 128] = 1.0 / 1024.0
    out[:, CST["o256"] * 128:(CST["o256"] + 1) * 128] = 1.0 / 256.0
    out[:, CST["ones"] * 128:(CST["ones"] + 1) * 128] = 1.0
    m8 = np.zeros((8, 1024), np.float32)
    for h in range(8):
        m8[h, h * 128:(h + 1) * 128] = 1.0
    return out, m8


def _chunkvec(v, n):
    return np.ascontiguousarray(np.asarray(v, np.float32).reshape(n, 128).T)


def prep_params(inp):
    g = lambda n: np.asarray(inp[n], np.float32)[0]
    w_in = g("w_in")
    M_W, CONV = 512, 1024
    o_z, o_xbc, o_dt = 0, 512, 1536
    o_r = 1544
    W = np.zeros((D, NCH_IN * 128), np.float32)
    W[:, 0:512] = w_in[:, o_z:o_z + 512]
    W[:, 512:1536] = w_in[:, o_xbc:o_xbc + 1024]
    W[:, 1536:3072] = w_in[:, o_r:o_r + 1536]
    W[:, CWL * 128:CWL * 128 + 64] = w_in[:, o_r + 1536:o_r + 1600]
    W[:, CAL * 128:CAL * 128 + 64] = w_in[:, o_r + 1600:o_r + 1664]
    W[:, CGL * 128:CGL * 128 + 128] = w_in[:, o_r + 1664:o_r + 1792]
    W[:, CDT * 128:CDT * 128 + 8] = w_in[:, o_dt:o_dt + 8]
    pv = np.zeros((128, NPV), np.float32)
    def put(name, arr):
        o, w = PV[name]
        pv[:, o:o + w] = arr
    cw = g("conv_w")
    cwp = np.zeros((128, 8, 5), np.float32)
    for j in range(8):
        cwp[:, j, :] = cw[:, j * 128:(j + 1) * 128].T
    put("cw", cwp.reshape(128, 40))
    put("cb", _chunkvec(g("conv_b"), 8))
    put("dch", _chunkvec(np.repeat(g("m_d"), 64), 4))
    put("nw", _chunkvec(g("m_norm_w"), 4))
    def mu15(v):
        o = np.zeros((128, 15), np.float32)
        o[:, 0:12] = _chunkvec(v[0:1536], 12)
        o[0:64, 12] = v[1536:1600]
        o[0:64, 13] = v[1600:1664]
        o[:, 14] = v[1664:1792]
        return o
    put("mp", mu15(g("r_mu_prev")))
    put("mn", mu15(g("r_mu_next")))
    put("w0f", _chunkvec(g("r_w0_f"), 4))
    put("w0b", _chunkvec(g("r_w0_b"), 4))
    put("a0", _chunkvec(g("r_a0"), 4))
    put("kk", _chunkvec(g("r_k_k"), 4))
    put("ka", _chunkvec(g("r_k_a"), 4))
    put("rk", _chunkvec(g("r_r_k").reshape(-1), 4))
    put("lw", _chunkvec(g("r_lnx_w"), 4))
    put("lb", _chunkvec(g("r_lnx_b"), 4))
    put("l1w", _chunkvec(g("ln1_w"), 8))
    put("l1b", _chunkvec(g("ln1_b"), 8))
    put("l2w", _chunkvec(g("ln2_w"), 8))
    put("l2b", _chunkvec(g("ln2_b"), 8))
    p8 = np.zeros((8, 8), np.float32)
    p8[:, 0] = g("m_dt_bias_f"); p8[:, 1] = g("m_dt_bias_b")
    p8[:, 2] = g("m_a_log_f"); p8[:, 3] = g("m_a_log_b")
    lowr = np.zeros((128, 4 * 512), np.float32)
    lowr[0:64, 0:512] = g("r_w2_f")
    lowr[0:64, 512:1024] = g("r_w2_b")
    lowr[0:64, 1024:1536] = g("r_a2")
    lowr[:, 1536:2048] = g("r_g2")
    cst, m8 = _consts()
    return dict(w_in=W, w_out=np.ascontiguousarray(g("w_out")), w_fi=np.ascontiguousarray(g("w_ffn_in")),
                w_fo=np.ascontiguousarray(g("w_ffn_out")), p8=p8, lowr=lowr, cst=cst, m8=m8), pv


def layout_core(units, link, TU):
    xT = np.zeros((D, 2 * (TU + 4)), np.float32)
    xTr = np.zeros((D, 2 * (TU + 4)), np.float32)
    for u, (seq, s) in enumerate(units):
        if seq is None:
            continue
        T = seq.shape[0]
        lo, hi = s - 2, s + TU + 2
        blk = np.zeros((TU + 4, D), np.float32)
        a, b = max(lo, 0), min(hi, T)
        blk[a - lo:b - lo] = seq[a:b]
        xT[:, u * (TU + 4):(u + 1) * (TU + 4)] = blk.T
        ur = 1 - u
        xTr[:, ur * (TU + 4):(ur + 1) * (TU + 4)] = blk[::-1].T
    return xT, xTr


_NC_CACHE = {}


def run_cores(assign, params, pv, TU, n_cores):
    if TU not in _NC_CACHE:
        import os
        _NC_CACHE[TU] = build(TU, stage=int(os.environ.get("MK_STAGE", "3")))
    nc = _NC_CACHE[TU]
    in_maps = []
    for units, link in assign:
        xT, xTr = layout_core(units, link, TU)
        pvc = pv.copy()
        pvc[:, PV["link"][0]] = float(link)
        m = dict(params)
        m.update(xT=xT, xTr=xTr, pvec=pvc)
        in_maps.append(m)
    res = run_bass_kernel_spmd(nc, in_maps, core_ids=list(range(n_cores)))
    return [r["yT"] for r in res.results]


def kernel(**inputs):
    xp = np.asarray(inputs["x_prompt"], np.float32)
    xs = np.asarray(inputs["x_sample"], np.float32)
    params, pv = prep_params(inputs)
    TU = xs.shape[1]
    assert xp.shape[1] == 2 * TU
    assign = []
    for b in range(xp.shape[0]):
        assign.append(([(xp[b], 0), (xp[b], TU)], 1))
    nb = xs.shape[0]
    slots = [[(xs[i], 0)] for i in range(nb)]
    n_rest = 8 - len(assign)
    per = [[] for _ in range(n_rest)]
    for i in range(nb):
        per[i % n_rest].append(i)
    for lst in per:
        u = [(xs[i], 0) for i in lst]
        while len(u) < 2:
            u.append((None, 0))
        assign.append((u, 0))
    outs = run_cores(assign, params, pv, TU, 8)
    y_p = np.zeros_like(xp)
    y_s = np.zeros_like(xs)
    for b in range(xp.shape[0]):
        y_p[b] = outs[b].T
    for ci, lst in enumerate(per):
        o = outs[xp.shape[0] + ci]
        for u, i in enumerate(lst):
            y_s[i] = o[:, u * TU:(u + 1) * TU].T
    return (y_p, y_s)
```
